# Optimizing a Trainium2 kernel written in Bass

```python
import math, functools
import jax, jax.numpy as jnp
from jax import lax
import numpy as np

D_MODEL = 2048
BATCH = 1
SEQ = 16384
DEPTH = 4
DEC_BATCH = 16
DEC_SEQ = 32
PAST_LEN = 4096

CHUNK = 64
D_MIX = D_MODEL // 2
W_GROUP = D_MIX // 4
GDN_HEADS = 4
GDN_HD = W_GROUP // GDN_HEADS
CONV_W = 4
ML_HEADS = 4
ML_HD = W_GROUP // ML_HEADS
S5_CH = 16
S5_GROUPS = W_GROUP // S5_CH
S5_P = 64
SB_HEADS = 4
SB_HD = W_GROUP // SB_HEADS
SB_BLOCK = 128
SB_KB = 64
SB_SEGMENTS = 16
D_FF = 2048
EPS = 1e-6
GDN_IN = 3 * W_GROUP + 2 * GDN_HEADS + W_GROUP
ML_IN = 3 * W_GROUP + 2 * ML_HEADS + W_GROUP
S5_IN = W_GROUP
SB_IN = 3 * W_GROUP
N_IN = GDN_IN + ML_IN + S5_IN + SB_IN

kernel_name = "hybrid_streaming_encoder_step"


def rms_norm(x, g):
    xf = x.astype(jnp.float32)
    y = xf * lax.rsqrt(jnp.mean(xf * xf, axis=-1, keepdims=True) + EPS)
    return (y * g.astype(jnp.float32)).astype(x.dtype)


def l2_normalize(x):
    return x * lax.rsqrt(jnp.sum(x * x, axis=-1, keepdims=True) + EPS)


def swiglu_ffn(x, w_gate, w_up, w_down):
    return (jax.nn.silu(x @ w_gate) * (x @ w_up)) @ w_down


def causal_conv(x, buf, w):
    L = x.shape[1]
    xp = jnp.concatenate([buf.astype(x.dtype), x], axis=1)
    y = sum(xp[:, i:i + L] * w[i] for i in range(CONV_W))
    return y, xp[:, L:]


def to_chunks(x, c):
    B, L = x.shape[:2]
    x = x.reshape((B, L // c, c) + x.shape[2:])
    if x.ndim == 5:
        return x.transpose(1, 0, 3, 2, 4)
    return x.transpose(1, 0, 3, 2)


def from_chunks(y):
    n, B, H, c, d = y.shape
    return y.transpose(1, 0, 3, 2, 4).reshape(B, n * c, H, d)


def gated_delta_chunked(q, k, v, log_a, beta, s0):
    c = min(CHUNK, q.shape[1])
    dv = v.shape[-1]
    qc, kc, vc = to_chunks(q, c), to_chunks(k, c), to_chunks(v, c)
    gc = jnp.cumsum(to_chunks(log_a, c), axis=-1)
    bc = to_chunks(beta, c)
    idx = jnp.arange(c)
    incl = idx[:, None] >= idx[None, :]
    strict = idx[:, None] > idx[None, :]
    decay = jnp.exp(jnp.where(incl, gc[..., :, None] - gc[..., None, :], -jnp.inf))
    kk = jnp.einsum('nbhid,nbhjd->nbhij', kc, kc)
    m = jnp.where(strict, bc[..., :, None] * kk * decay, 0.0)
    gamma = jnp.exp(gc)
    rhs = jnp.concatenate([bc[..., None] * vc, (bc * gamma)[..., None] * kc], axis=-1)
    w = lax.linalg.triangular_solve(jnp.eye(c, dtype=m.dtype) + m, rhs,
                                    left_side=True, lower=True, unit_diagonal=True)
    w_v, w_k = w[..., :dv], w[..., dv:]
    p = jnp.einsum('nbhid,nbhjd->nbhij', qc, kc) * decay
    qg = qc * gamma[..., None]
    kg = kc * jnp.exp(gc[..., -1:] - gc)[..., None]
    g_last = jnp.exp(gc[..., -1])

    def step(s, xs):
        wv, wk, pp, qq, kq, gl = xs
        u = wv - jnp.einsum('bhik,bhkv->bhiv', wk, s)
        o = jnp.einsum('bhik,bhkv->bhiv', qq, s) + jnp.einsum('bhij,bhjv->bhiv', pp, u)
        s = gl[..., None, None] * s + jnp.einsum('bhjk,bhjv->bhkv', kq, u)
        return s, o

    s_fin, o = lax.scan(step, s0, (w_v, w_k, p, qg, kg, g_last))
    return from_chunks(o), s_fin


def mlstm_chunked(q, k, v, i_pre, log_f, c0, n0, m0):
    c = min(CHUNK, q.shape[1])
    qc, kc, vc = to_chunks(q, c), to_chunks(k, c), to_chunks(v, c)
    ic = to_chunks(i_pre, c)
    bcum = jnp.cumsum(to_chunks(log_f, c), axis=-1)
    idx = jnp.arange(c)
    incl = idx[:, None] >= idx[None, :]
    dmat = jnp.where(incl, bcum[..., :, None] - bcum[..., None, :] + ic[..., None, :], -jnp.inf)
    dmax = jnp.max(dmat, axis=-1)
    qk = jnp.einsum('nbhid,nbhjd->nbhij', qc, kc)
    dlast = bcum[..., -1:] - bcum + ic

    def step(carry, xs):
        cm, nv, m = carry
        qq, kq, vq, qkq, dm, dmx, bq, dl = xs
        inter = bq + m[..., None]
        mt = jnp.maximum(inter, dmx)
        w_int = jnp.exp(inter - mt)
        wqk = jnp.exp(dm - mt[..., None]) * qkq
        num = w_int[..., None] * jnp.einsum('bhik,bhkv->bhiv', qq, cm) + jnp.einsum('bhij,bhjv->bhiv', wqk, vq)
        den = w_int * jnp.einsum('bhik,bhk->bhi', qq, nv) + jnp.sum(wqk, axis=-1)
        h = num / jnp.maximum(jnp.abs(den), jnp.exp(-mt))[..., None]
        m_new = mt[..., -1]
        f_old = jnp.exp(bq[..., -1] + m - m_new)
        kw = kq * jnp.exp(dl - m_new[..., None])[..., None]
        cm = f_old[..., None, None] * cm + jnp.einsum('bhjk,bhjv->bhkv', kw, vq)
        nv = f_old[..., None] * nv + jnp.sum(kw, axis=2)
        return (cm, nv, m_new), h

    (c_fin, n_fin, m_fin), h = lax.scan(step, (c0, n0, m0), (qc, kc, vc, qk, dmat, dmax, bcum, dlast))
    return from_chunks(h), c_fin, n_fin, m_fin


def s5_ssm(u, x0_re, x0_im, a_re, a_im, b_re, b_im, c_re, c_im, d, log_step):
    f32 = jnp.float32
    B, L, _ = u.shape
    c = min(CHUNK, L)
    nc = L // c
    ug = u.reshape(B, L, S5_GROUPS, S5_CH)
    step = jnp.exp(log_step)[:, None]
    da_re, da_im = step * a_re, step * a_im
    mag = jnp.exp(da_re)
    ab_re, ab_im = mag * jnp.cos(da_im), mag * jnp.sin(da_im)
    den = a_re * a_re + a_im * a_im
    nr, ni = ab_re - 1.0, ab_im
    f_re = (nr * a_re + ni * a_im) / den
    f_im = (ni * a_re - nr * a_im) / den
    bb_re = f_re[..., None] * b_re - f_im[..., None] * b_im
    bb_im = f_re[..., None] * b_im + f_im[..., None] * b_re
    bu_re = jnp.einsum('blgc,gpc->blgp', ug, bb_re)
    bu_im = jnp.einsum('blgc,gpc->blgp', ug, bb_im)
    bu_re = bu_re.at[:, 0].add(ab_re * x0_re - ab_im * x0_im)
    bu_im = bu_im.at[:, 0].add(ab_re * x0_im + ab_im * x0_re)

    def power(n):
        mg = jnp.exp(n * da_re)
        return mg * jnp.cos(n * da_im), mg * jnp.sin(n * da_im)

    def combine(e1, e2):
        n1, r1, i1 = e1
        n2, r2, i2 = e2
        pr, pi = power(n2)
        return (n1 + n2, pr * r1 - pi * i1 + r2, pr * i1 + pi * r1 + i2)

    br = bu_re.reshape(B, nc, c, S5_GROUPS, S5_P)
    bi = bu_im.reshape(B, nc, c, S5_GROUPS, S5_P)
    n_loc = jnp.ones((B, nc, c, 1, 1), f32)
    _, lr, li = lax.associative_scan(combine, (n_loc, br, bi), axis=2)
    n_chk = jnp.full((B, nc, 1, 1), float(c), f32)
    _, sr, si = lax.associative_scan(combine, (n_chk, lr[:, :, -1], li[:, :, -1]), axis=1)
    cr = jnp.concatenate([jnp.zeros_like(sr[:, :1]), sr[:, :-1]], axis=1)[:, :, None]
    ci = jnp.concatenate([jnp.zeros_like(si[:, :1]), si[:, :-1]], axis=1)[:, :, None]
    pw_r, pw_i = power(jnp.arange(1, c + 1, dtype=f32)[:, None, None])
    xr = (lr + pw_r * cr - pw_i * ci).reshape(B, L, S5_GROUPS, S5_P)
    xi = (li + pw_r * ci + pw_i * cr).reshape(B, L, S5_GROUPS, S5_P)
    y = jnp.einsum('blgp,gcp->blgc', xr, c_re) - jnp.einsum('blgp,gcp->blgc', xi, c_im)
    y = y.reshape(B, L, W_GROUP) + d * u
    return y, sr[:, -1], si[:, -1]


def sb_block(args, ks, vs):
    qblk, start = args
    B, blk, H, dh = qblk.shape
    Lk = ks.shape[1]
    nk = Lk // SB_KB
    z = jnp.einsum('bqhd,bkhd->bhqk', qblk, ks) * (dh ** -0.5)
    valid = jnp.arange(Lk)[None, :] < (start + jnp.arange(blk))[:, None]
    ls = jax.nn.log_sigmoid(z)
    log_not = jnp.where(valid, ls - z, 0.0)
    ln = log_not.reshape(B, H, blk, nk, SB_KB)
    ii = jnp.arange(SB_KB)
    tri_kb = (ii[:, None] > ii[None, :]).astype(z.dtype)
    jj = jnp.arange(nk)
    tri_nk = (jj[:, None] > jj[None, :]).astype(z.dtype)
    within = jnp.einsum('bhqnj,js->bhqns', ln, tri_kb)
    suffix = jnp.einsum('bhqm,mn->bhqn', jnp.sum(ln, axis=-1), tri_nk)
    after = (within + suffix[..., None]).reshape(B, H, blk, Lk)
    a = jnp.where(valid, jnp.exp(ls + after), 0.0)
    return jnp.einsum('bhqk,bkhd->bqhd', a, vs)


def stick_breaking(q, k, v, q_offset):
    B, Lq, H, dh = q.shape
    Lk = k.shape[1]
    blk = min(SB_BLOCK, Lq)
    nb = Lq // blk
    n_seg = math.gcd(nb, SB_SEGMENTS)
    seg = Lq // n_seg
    pad = (-Lk) % SB_KB
    k = jnp.pad(k, ((0, 0), (0, pad), (0, 0), (0, 0)))
    v = jnp.pad(v, ((0, 0), (0, pad), (0, 0), (0, 0)))
    outs = []
    for s in range(n_seg):
        k_lim = -(-(q_offset + (s + 1) * seg) // SB_KB) * SB_KB
        qs = q[:, s * seg:(s + 1) * seg].reshape(B, seg // blk, blk, H, dh).transpose(1, 0, 2, 3, 4)
        starts = q_offset + s * seg + jnp.arange(seg // blk) * blk
        o = lax.map(functools.partial(sb_block, ks=k[:, :k_lim], vs=v[:, :k_lim]), (qs, starts))
        outs.append(o.transpose(1, 0, 2, 3, 4).reshape(B, seg, H, dh))
    return jnp.concatenate(outs, axis=1)


def hybrid_mixing(h, lw, ls):
    (w_in, conv_w, a_log, dt_bias, gdn_norm, b_i, b_f, ml_norm, a_re, a_im, b_re, b_im,
     c_re, c_im, s5_d, log_step, w_glu, b_glu, s5_norm, sb_norm, w_out) = lw
    (k_past, v_past, gdn_s, gdn_conv, ml_c, ml_n, ml_m, s5_re, s5_im) = ls
    f32 = jnp.float32
    B, L, _ = h.shape
    z = h @ w_in
    z_a, z_b, z_c, z_d = jnp.split(z, [GDN_IN, GDN_IN + ML_IN, GDN_IN + ML_IN + S5_IN], axis=-1)
    qkv, conv_new = causal_conv(z_a[..., :3 * W_GROUP], gdn_conv, conv_w)
    qkv = jax.nn.silu(qkv.astype(f32)).reshape(B, L, 3, GDN_HEADS, GDN_HD)
    q_a = l2_normalize(qkv[:, :, 0]) * GDN_HD ** -0.5
    k_a = l2_normalize(qkv[:, :, 1])
    v_a = qkv[:, :, 2]
    ga = z_a[..., 3 * W_GROUP:].astype(f32)
    log_a = -jnp.exp(a_log.astype(f32)) * jax.nn.softplus(ga[..., :GDN_HEADS] + dt_bias.astype(f32))
    beta = jax.nn.sigmoid(ga[..., GDN_HEADS:2 * GDN_HEADS])
    o_a, s_a = gated_delta_chunked(q_a, k_a, v_a, log_a, beta, gdn_s.astype(f32))
    g_out = ga[..., 2 * GDN_HEADS:].reshape(B, L, GDN_HEADS, GDN_HD)
    o_a = (rms_norm(o_a, gdn_norm) * jax.nn.silu(g_out)).reshape(B, L, W_GROUP)
    qkv_b = z_b[..., :3 * W_GROUP].astype(f32).reshape(B, L, 3, ML_HEADS, ML_HD)
    gb = z_b[..., 3 * W_GROUP:].astype(f32)
    i_pre = gb[..., :ML_HEADS] + b_i.astype(f32)
    log_f = jax.nn.log_sigmoid(gb[..., ML_HEADS:2 * ML_HEADS] + b_f.astype(f32))
    o_gate = jax.nn.sigmoid(gb[..., 2 * ML_HEADS:])
    h_b, c_b, n_b, m_b = mlstm_chunked(qkv_b[:, :, 0], qkv_b[:, :, 1] * ML_HD ** -0.5, qkv_b[:, :, 2],
                                       i_pre, log_f, ml_c.astype(f32), ml_n.astype(f32), ml_m.astype(f32))
    o_b = o_gate * rms_norm(h_b, ml_norm.reshape(ML_HEADS, ML_HD)).reshape(B, L, W_GROUP)
    y_c, x_re, x_im = s5_ssm(z_c.astype(f32), s5_re.astype(f32), s5_im.astype(f32),
                             a_re.astype(f32), a_im.astype(f32), b_re.astype(f32), b_im.astype(f32),
                             c_re.astype(f32), c_im.astype(f32), s5_d.astype(f32), log_step.astype(f32))
    y_c = jax.nn.gelu(y_c)
    o_c = rms_norm(y_c * jax.nn.sigmoid(y_c @ w_glu.astype(f32) + b_glu.astype(f32)), s5_norm)
    qkv_d = z_d.reshape(B, L, 3, SB_HEADS, SB_HD)
    k_new, v_new = qkv_d[:, :, 1], qkv_d[:, :, 2]
    k_all = jnp.concatenate([k_past.astype(k_new.dtype), k_new], axis=1).astype(f32)
    v_all = jnp.concatenate([v_past.astype(v_new.dtype), v_new], axis=1).astype(f32)
    o_d = stick_breaking(qkv_d[:, :, 0].astype(f32), k_all, v_all, k_past.shape[1])
    o_d = rms_norm(o_d.reshape(B, L, W_GROUP), sb_norm)
    mixed = jnp.concatenate([o_a, o_b, o_c, o_d], axis=-1).astype(h.dtype) @ w_out
    return mixed, (k_new, v_new, s_a, conv_new, c_b, n_b, m_b, x_re, x_im)


def run_trunk(x, states, weights):
    (sb_k, sb_v, gdn_s, gdn_conv, ml_c, ml_n, ml_m, s5_re, s5_im) = states
    (ffn1_norm, ffn1_w_gate, ffn1_w_up, ffn1_w_down, mix_norm, w_in, gdn_conv_w, gdn_a_log,
     gdn_dt_bias, gdn_norm, ml_b_i, ml_b_f, ml_norm, s5_a_re, s5_a_im, s5_b_re, s5_b_im,
     s5_c_re, s5_c_im, s5_d, s5_log_step, s5_w_glu, s5_b_glu, s5_norm, sb_norm, w_out,
     ffn2_norm, ffn2_w_gate, ffn2_w_up, ffn2_w_down, final_norm) = weights
    new = [[] for _ in range(9)]
    for l in range(DEPTH):
        x = x + 0.5 * swiglu_ffn(rms_norm(x, ffn1_norm[l]), ffn1_w_gate[l], ffn1_w_up[l], ffn1_w_down[l])
        lw = (w_in[l], gdn_conv_w[l], gdn_a_log[l], gdn_dt_bias[l], gdn_norm[l], ml_b_i[l], ml_b_f[l],
              ml_norm[l], s5_a_re[l], s5_a_im[l], s5_b_re[l], s5_b_im[l], s5_c_re[l], s5_c_im[l], s5_d[l],
              s5_log_step[l], s5_w_glu[l], s5_b_glu[l], s5_norm[l], sb_norm[l], w_out[l])
        ls = (sb_k[l], sb_v[l], gdn_s[l], gdn_conv[l], ml_c[l], ml_n[l], ml_m[l], s5_re[l], s5_im[l])
        mixed, layer_new = hybrid_mixing(rms_norm(x, mix_norm[l]), lw, ls)
        x = x + mixed
        x = x + 0.5 * swiglu_ffn(rms_norm(x, ffn2_norm[l]), ffn2_w_gate[l], ffn2_w_up[l], ffn2_w_down[l])
        for lst, arr in zip(new, layer_new):
            lst.append(arr)
    return rms_norm(x, final_norm), [jnp.stack(lst) for lst in new]


def setup_inputs(seed: int = 0) -> dict:
    key = jax.random.key(seed)
    ks = iter(jax.random.split(key, 64))
    nrm = lambda shape, s=1.0: jax.random.normal(next(ks), shape, jnp.float32) * s
    gain = lambda shape: 1.0 + 0.02 * jax.random.normal(next(ks), shape, jnp.float32)
    unif = lambda shape, lo, hi: jax.random.uniform(next(ks), shape, jnp.float32, lo, hi)
    dt = jnp.exp(unif((DEPTH, GDN_HEADS), math.log(1e-3), math.log(1e-1)))
    return {
        "x_prompt": nrm((BATCH, SEQ, D_MODEL)),
        "x_sample": nrm((DEC_BATCH, DEC_SEQ, D_MODEL)),
        "cache_sb_k": nrm((DEPTH, DEC_BATCH, PAST_LEN, SB_HEADS, SB_HD)),
        "cache_sb_v": nrm((DEPTH, DEC_BATCH, PAST_LEN, SB_HEADS, SB_HD)),
        "state_gdn_s": nrm((DEPTH, DEC_BATCH, GDN_HEADS, GDN_HD, GDN_HD), 0.1),
        "state_gdn_conv": nrm((DEPTH, DEC_BATCH, CONV_W - 1, 3 * W_GROUP)),
        "state_mlstm_c": nrm((DEPTH, DEC_BATCH, ML_HEADS, ML_HD, ML_HD), 0.1),
        "state_mlstm_n": nrm((DEPTH, DEC_BATCH, ML_HEADS, ML_HD), 0.1),
        "state_mlstm_m": nrm((DEPTH, DEC_BATCH, ML_HEADS)),
        "state_s5_re": nrm((DEPTH, DEC_BATCH, S5_GROUPS, S5_P), 0.1),
        "state_s5_im": nrm((DEPTH, DEC_BATCH, S5_GROUPS, S5_P), 0.1),
        "ffn1_norm": gain((DEPTH, D_MODEL)),
        "ffn1_w_gate": nrm((DEPTH, D_MODEL, D_FF), D_MODEL ** -0.5),
        "ffn1_w_up": nrm((DEPTH, D_MODEL, D_FF), D_MODEL ** -0.5),
        "ffn1_w_down": nrm((DEPTH, D_FF, D_MODEL), D_FF ** -0.5),
        "mix_norm": gain((DEPTH, D_MODEL)),
        "w_in": nrm((DEPTH, D_MODEL, N_IN), D_MODEL ** -0.5),
        "gdn_conv_w": nrm((DEPTH, CONV_W, 3 * W_GROUP), CONV_W ** -0.5),
        "gdn_a_log": jnp.log(unif((DEPTH, GDN_HEADS), 1.0, 16.0)),
        "gdn_dt_bias": dt + jnp.log(-jnp.expm1(-dt)),
        "gdn_norm": gain((DEPTH, GDN_HD)),
        "ml_b_i": nrm((DEPTH, ML_HEADS), 0.1),
        "ml_b_f": jnp.linspace(3.0, 6.0, ML_HEADS, dtype=jnp.float32)[None, :] + nrm((DEPTH, ML_HEADS), 0.1),
        "ml_norm": gain((DEPTH, W_GROUP)),
        "s5_a_re": -0.5 + nrm((DEPTH, S5_GROUPS, S5_P), 0.01),
        "s5_a_im": math.pi * jnp.arange(S5_P, dtype=jnp.float32) + nrm((DEPTH, S5_GROUPS, S5_P), 0.01),
        "s5_b_re": nrm((DEPTH, S5_GROUPS, S5_P, S5_CH), (2 * S5_CH) ** -0.5),
        "s5_b_im": nrm((DEPTH, S5_GROUPS, S5_P, S5_CH), (2 * S5_CH) ** -0.5),
        "s5_c_re": nrm((DEPTH, S5_GROUPS, S5_CH, S5_P), (2 * S5_P) ** -0.5),
        "s5_c_im": nrm((DEPTH, S5_GROUPS, S5_CH, S5_P), (2 * S5_P) ** -0.5),
        "s5_d": nrm((DEPTH, W_GROUP)),
        "s5_log_step": unif((DEPTH, S5_GROUPS), math.log(1e-3), math.log(1e-1)),
        "s5_w_glu": nrm((DEPTH, W_GROUP, W_GROUP), W_GROUP ** -0.5),
        "s5_b_glu": nrm((DEPTH, W_GROUP), 0.01),
        "s5_norm": gain((DEPTH, W_GROUP)),
        "sb_norm": gain((DEPTH, W_GROUP)),
        "w_out": nrm((DEPTH, D_MIX, D_MODEL), D_MIX ** -0.5),
        "ffn2_norm": gain((DEPTH, D_MODEL)),
        "ffn2_w_gate": nrm((DEPTH, D_MODEL, D_FF), D_MODEL ** -0.5),
        "ffn2_w_up": nrm((DEPTH, D_MODEL, D_FF), D_MODEL ** -0.5),
        "ffn2_w_down": nrm((DEPTH, D_FF, D_MODEL), D_FF ** -0.5),
        "final_norm": gain((D_MODEL,)),
    }


def reference(x_prompt, x_sample, cache_sb_k, cache_sb_v, state_gdn_s, state_gdn_conv, state_mlstm_c,
              state_mlstm_n, state_mlstm_m, state_s5_re, state_s5_im,
              ffn1_norm, ffn1_w_gate, ffn1_w_up, ffn1_w_down, mix_norm, w_in, gdn_conv_w, gdn_a_log,
              gdn_dt_bias, gdn_norm, ml_b_i, ml_b_f, ml_norm, s5_a_re, s5_a_im, s5_b_re, s5_b_im,
              s5_c_re, s5_c_im, s5_d, s5_log_step, s5_w_glu, s5_b_glu, s5_norm, sb_norm, w_out,
              ffn2_norm, ffn2_w_gate, ffn2_w_up, ffn2_w_down, final_norm):
    weights = (ffn1_norm, ffn1_w_gate, ffn1_w_up, ffn1_w_down, mix_norm, w_in, gdn_conv_w, gdn_a_log,
               gdn_dt_bias, gdn_norm, ml_b_i, ml_b_f, ml_norm, s5_a_re, s5_a_im, s5_b_re, s5_b_im,
               s5_c_re, s5_c_im, s5_d, s5_log_step, s5_w_glu, s5_b_glu, s5_norm, sb_norm, w_out,
               ffn2_norm, ffn2_w_gate, ffn2_w_up, ffn2_w_down, final_norm)
    f32 = jnp.float32
    dtp = x_prompt.dtype
    fresh = (jnp.zeros((DEPTH, BATCH, 0, SB_HEADS, SB_HD), dtp),
             jnp.zeros((DEPTH, BATCH, 0, SB_HEADS, SB_HD), dtp),
             jnp.zeros((DEPTH, BATCH, GDN_HEADS, GDN_HD, GDN_HD), f32),
             jnp.zeros((DEPTH, BATCH, CONV_W - 1, 3 * W_GROUP), dtp),
             jnp.zeros((DEPTH, BATCH, ML_HEADS, ML_HD, ML_HD), f32),
             jnp.zeros((DEPTH, BATCH, ML_HEADS, ML_HD), f32),
             jnp.zeros((DEPTH, BATCH, ML_HEADS), f32),
             jnp.zeros((DEPTH, BATCH, S5_GROUPS, S5_P), f32),
             jnp.zeros((DEPTH, BATCH, S5_GROUPS, S5_P), f32))
    y_prompt, (p_sb_k, p_sb_v, p_gdn_s, p_gdn_conv, p_ml_c, p_ml_n, p_ml_m, p_s5_re, p_s5_im) = \
        run_trunk(x_prompt, fresh, weights)
    running = (cache_sb_k, cache_sb_v, state_gdn_s, state_gdn_conv, state_mlstm_c, state_mlstm_n,
               state_mlstm_m, state_s5_re, state_s5_im)
    y_sample, (s_sb_k, s_sb_v, s_gdn_s, s_gdn_conv, s_ml_c, s_ml_n, s_ml_m, s_s5_re, s_s5_im) = \
        run_trunk(x_sample, running, weights)
    return (y_prompt, y_sample,
            p_sb_k, p_sb_v, p_gdn_s, p_gdn_conv, p_ml_c, p_ml_n, p_ml_m, p_s5_re, p_s5_im,
            s_sb_k, s_sb_v, s_gdn_s, s_gdn_conv, s_ml_c, s_ml_n, s_ml_m, s_s5_re, s_s5_im)
```

```python
import numpy as np
from contextlib import ExitStack
import concourse.bass as bass
import concourse.mybir as mybir
from concourse.bass_utils import run_bass_kernel_spmd

F32 = mybir.dt.float32; BF16 = mybir.dt.bfloat16; I32 = mybir.dt.int32
AF = mybir.ActivationFunctionType
ALU = mybir.AluOpType
AX = mybir.AxisListType

class Buf:
    __slots__ = ("name", "w", "r")
    def __init__(self, name):
        self.name = name; self.w = None; self.r = []

class KB:
    EPOCH = 30000
    NDMA = 24
    def __init__(self, nc, stack):
        self.nc = nc; self.stack = stack
        self.eng = {"pe": nc.tensor, "act": nc.scalar, "dve": nc.vector, "pool": nc.gpsimd, "sp": nc.sync}
        self.cur = {}
        self.nsem = 0
        for e in self.eng: self._newsem(e)
        self.seen = {}
        self.dsem = [self._sem("d%d" % i) for i in range(self.NDMA)]
        self.dcnt = [0] * self.NDMA
        self.drr = 0
        self.ninst = 0
    def _sem(self, name):
        self.nsem += 1
        return self.stack.enter_context(self.nc.semaphore("%s_%d" % (name, self.nsem)))
    def _newsem(self, e):
        self.cur[e] = [self._sem("c" + e), 0]
    def wait(self, e, tok):
        if tok is None: return
        sem, val = tok
        k = (e, id(sem))
        if self.seen.get(k, 0) >= val: return
        self.seen[k] = val
        self.eng[e].wait_ge(sem, val)
    def deps(self, e, reads, writes):
        for b in reads:
            self.wait(e, b.w)
        for b in writes:
            self.wait(e, b.w)
            for t in b.r: self.wait(e, t)
    def mark(self, tok, reads, writes):
        for b in reads: b.r.append(tok)
        for b in writes:
            b.w = tok; b.r = []
    def op(self, e, fn, reads=(), writes=(), **kw):
        self.deps(e, reads, writes)
        c = self.cur[e]
        if c[1] >= self.EPOCH:
            self._newsem(e); c = self.cur[e]
        ins = fn(**kw)
        c[1] += 1
        ins.then_inc(c[0], 1)
        tok = (c[0], c[1])
        self.seen[(e, id(c[0]))] = c[1] if e == "pe" else self.seen.get((e, id(c[0])), 0)
        self.mark(tok, reads, writes)
        self.ninst += 1
        return tok
    def dma(self, q, out, in_, reads=(), writes=(), **kw):
        slot = self.drr % self.NDMA; self.drr += 1
        sem = self.dsem[slot]
        if self.dcnt[slot] > 0:
            self.wait(q, (sem, 16 * self.dcnt[slot]))
        self.deps(q, reads, writes)
        self.eng[q].dma_start(out=out, in_=in_, **kw).then_inc(sem, 16)
        self.dcnt[slot] += 1
        tok = (sem, 16 * self.dcnt[slot])
        self.mark(tok, reads, writes)
        self.ninst += 1
        return tok
    def finish(self, bufs):
        for b in bufs:
            self.wait("sp", b.w)


NCOL = 2176
REG = [(0, 2048, 0), (2048, 32, 2048), (2080, 32, 2112)]
RAWOFF = [3, 2054, 2089]
GROUPS = [(0, 8), (8, 8), (16, 8), (24, 8), (32, 2)]
SEGW = 2304
EPS = 1e-6

def bc_last(ap, n):
    return bass.AP(ap.tensor, ap.offset, [list(x) for x in ap.ap] + [[0, n]])
def bc_mid(ap, n):
    a = [list(x) for x in ap.ap]
    return bass.AP(ap.tensor, ap.offset, [a[0], [0, n]] + a[1:])

class Ctx:
    pass

def b_setup(nc, st, kb):
    c = Ctx(); c.nc = nc; c.st = st; c.kb = kb
    def sb(name, shape, dt=F32):
        return st.enter_context(nc.sbuf_tensor("s_" + name, list(shape), dt)), Buf(name)
    c.sb = sb
    c.ps = [(st.enter_context(nc.psum_tensor("bps%d" % i, [128, 512], F32)), Buf("bps%d" % i)) for i in range(8)]
    return c

def load_consts(c, cm_d, rows_d):
    kb = c.kb
    c.cm, c.bcm = c.sb("cm", [64, 6, 64])
    c.rows, c.brows = c.sb("rows", [1, 2, NCOL])
    kb.dma("sp", c.cm[:], cm_d, writes=[c.bcm])
    kb.dma("sp", c.rows[:], rows_d, writes=[c.brows])
    c.eps, c.beps = c.sb("epsb", [128, 1])
    kb.op("pool", c.nc.gpsimd.memset, writes=[c.beps], ap=c.eps[:], constant=EPS)
    c.ident = c.cm[:, 0, :]; c.maskUn = c.cm[:, 1, :]; c.maskLn = c.cm[:, 2, :]; c.maskI = c.cm[:, 3, :]; c.ones64 = c.cm[:, 4, :]

def gather_rows(c, dst_ap, bdst, zg, idx_ap, bidx, bzg=None):
    kb = c.kb; nc = c.nc
    reads = [bidx] + ([bzg] if bzg is not None else [])
    kb.deps("pool", reads, [bdst])
    slot = kb.drr % kb.NDMA; kb.drr += 1; sem = kb.dsem[slot]
    if kb.dcnt[slot] > 0: kb.wait("pool", (sem, 16 * kb.dcnt[slot]))
    nc.gpsimd.indirect_dma_start(out=dst_ap, out_offset=None, in_=zg, in_offset=bass.IndirectOffsetOnAxis(ap=idx_ap, axis=0)).then_inc(sem, 16)
    kb.dcnt[slot] += 1
    kb.mark((sem, 16 * kb.dcnt[slot]), reads, [bdst]); kb.ninst += 1

def scatter_rows(c, src_ap, bsrc, og, idx_ap, bidx, bog):
    kb = c.kb; nc = c.nc
    kb.deps("pool", [bsrc, bidx], [bog])
    slot = kb.drr % kb.NDMA; kb.drr += 1; sem = kb.dsem[slot]
    if kb.dcnt[slot] > 0: kb.wait("pool", (sem, 16 * kb.dcnt[slot]))
    nc.gpsimd.indirect_dma_start(out=og, out_offset=bass.IndirectOffsetOnAxis(ap=idx_ap, axis=0), in_=src_ap, in_offset=None).then_inc(sem, 16)
    kb.dcnt[slot] += 1
    kb.mark((sem, 16 * kb.dcnt[slot]), [bsrc, bidx], [bog]); kb.ninst += 1

def row_to_col(c, col, bcol, row_ap, brow, n, one11, bone):
    kb = c.kb; nc = c.nc
    p, pb = c.ps[6]
    for k in range(n):
        kb.op("pe", nc.tensor.matmul, reads=[brow, bone], writes=[pb], out=p[:64, k:k + 1], lhsT=row_ap[0:1, k * 64:(k + 1) * 64], rhs=one11, start=True, stop=True)
    kb.op("act", nc.scalar.copy, reads=[pb], writes=[bcol], out=col[:, :n], in_=p[:64, :n])

def bcast_rows(c, dst, bdst, row_ap, brow, ncols, func=None, ones_row=None, bones=None, psi=6):
    kb = c.kb; nc = c.nc
    t = 0
    while t < ncols:
        w = min(512, ncols - t)
        p, pb = c.ps[psi]
        kb.op("pe", nc.tensor.matmul, reads=[brow, bones], writes=[pb], out=p[:64, :w], lhsT=ones_row, rhs=row_ap[:, t:t + w], start=True, stop=True)
        if func is None:
            kb.op("act", nc.scalar.copy, reads=[pb], writes=[bdst], out=dst[:, t:t + w], in_=p[:64, :w])
        else:
            kb.op("act", nc.scalar.activation, reads=[pb], writes=[bdst], out=dst[:, t:t + w], in_=p[:64, :w], func=func)
        t += w

class LinAttn:
    def __init__(self, c, name, delta, DV):
        self.c = c; self.name = name; self.delta = delta; self.DV = DV
        sb = lambda n, s, dt=F32: c.sb(name + n, s, dt)
        self.QT, self.bQT = sb("QT", [64, SEGW]); self.KT, self.bKT = sb("KT", [64, SEGW]); self.VT, self.bVT = sb("VT", [33, SEGW])
        self.grow, self.bgrow = sb("grow", [1, NCOL])
        self.g2row, self.bg2row = sb("g2row", [1, NCOL])
        self.gcol, self.bgcol = sb("gcol", [64, 34]); self.g2col, self.bg2col = sb("g2col", [64, 34])
        self.kwcol, self.bkwcol = sb("kwcol", [64, 34])
        self.rkcol, self.brkcol = sb("rkcol", [64, 34])
        self.gbc, self.bgbc = sb("gbc", [64, NCOL]); self.gam, self.bgam = sb("gam", [64, NCOL])
        if delta: self.g2bc, self.bg2bc = sb("g2bc", [64, NCOL])
        self.onesrow, self.bonesrow = sb("onesrow", [1, 64])
        c.kb.op("pool", c.nc.gpsimd.memset, writes=[self.bonesrow], ap=self.onesrow[:], constant=1.0)
        names = ["D", "DM", "N0", "NT0", "N1", "NT1", "P", "pT", "qgT", "KG", "tmp"] if delta else ["D", "DM", "pT", "qgT", "KG", "tmp"]
        self.w = {n: sb("w" + n, [64, 512]) for n in names}
        if delta:
            self.RK = sb("RK", [64, 512]); self.wkT = sb("wkT", [64, 512])
            self.RV = sb("RV", [64, 8, DV]); self.wv = sb("wv", [64, 8, DV]); self.u = [sb("u%d" % i, [64, DV]) for i in range(2)]
        else:
            self.RV = sb("RV", [64, 8, DV])
            c.kb.op("pool", c.nc.gpsimd.memset, writes=[self.RV[1]], ap=self.RV[0][:], constant=1.0)
        self.S, self.bS = sb("S", [64, 17, DV])
        self.oT, self.boT = sb("oT", [DV, SEGW])
        c.kb.op("pool", c.nc.gpsimd.memset, writes=[self.boT], ap=self.oT[:], constant=0.0)

    def mats(self, ci, nch):
        c = self.c; kb = c.kb; nc = c.nc; W = nch * 64; c0 = ci * 64
        v3 = lambda t: t[:, :W].rearrange("p (n i) -> p n i", i=64)
        D, bD = self.w["D"]; DM, bDM = self.w["DM"]; tmp, btmp = self.w["tmp"]
        gbc3 = v3(self.gbc[:, c0:c0 + W])
        kb.op("dve", nc.vector.tensor_tensor, reads=[self.bgbc, self.bgcol], writes=[bD], out=v3(D), in0=gbc3, in1=bc_last(self.gcol[:, ci:ci + nch], 64), op=ALU.subtract)
        kb.op("dve", nc.vector.tensor_scalar, reads=[bD], writes=[bD], out=D[:, :W], in0=D[:, :W], scalar1=0.0, scalar2=None, op0=ALU.min)
        kb.op("act", nc.scalar.activation, reads=[bD], writes=[bD], out=D[:, :W], in_=D[:, :W], func=AF.Exp)
        kb.op("dve", nc.vector.tensor_tensor, reads=[bD, c.bcm], writes=[bDM], out=v3(DM), in0=v3(D), in1=bc_mid(c.maskI, nch), op=ALU.mult)
        if not self.delta:
            kb.op("dve", nc.vector.tensor_tensor, reads=[bDM, self.bg2col], writes=[bDM], out=v3(DM), in0=v3(DM), in1=bc_last(self.g2col[:, ci:ci + nch], 64), op=ALU.mult)
        p, pb = c.ps[5]
        for n in range(nch):
            sl = slice(c0 + n * 64, c0 + n * 64 + 64)
            kb.op("pe", nc.tensor.matmul, reads=[self.bKT, self.bQT], writes=[pb], out=p[:64, n * 64:n * 64 + 64], lhsT=self.KT[:, sl], rhs=self.QT[:, sl], start=True, stop=True)
        pT, bpT = self.w["pT"]
        kb.op("dve", nc.vector.tensor_tensor, reads=[pb, bDM], writes=[bpT], out=pT[:, :W], in0=p[:64, :W], in1=DM[:, :W], op=ALU.mult)
        qg, bqg = self.w["qgT"]
        kb.op("pool", nc.gpsimd.tensor_tensor, reads=[self.bQT, self.bgam], writes=[bqg], out=qg[:, :W], in0=self.QT[:, c0:c0 + W], in1=self.gam[:, c0:c0 + W], op=ALU.mult)
        p2, pb2 = c.ps[2]
        for n in range(nch):
            sl = slice(c0 + n * 64, c0 + n * 64 + 64)
            kb.op("pe", nc.tensor.transpose, reads=[self.bKT, c.bcm], writes=[pb2], out=p2[:64, n * 64:n * 64 + 64], in_=self.KT[:, sl], identity=c.ident)
        KG, bKG = self.w["KG"]
        kb.op("dve", nc.vector.tensor_tensor, reads=[pb2, self.bkwcol], writes=[bKG], out=v3(KG), in0=v3(p2[:64, :]), in1=bc_last(self.kwcol[:, ci:ci + nch], 64), op=ALU.mult)
        if self.delta:
            RK, bRK = self.RK
            kb.op("dve", nc.vector.tensor_tensor, reads=[pb2, self.brkcol], writes=[bRK], out=v3(RK), in0=v3(p2[:64, :]), in1=bc_last(self.rkcol[:, ci:ci + nch], 64), op=ALU.mult)
        DV = self.DV
        p3, pb3 = c.ps[3]
        nv = 32
        for n in range(nch):
            sl = slice(c0 + n * 64, c0 + n * 64 + 64)
            kb.op("pe", nc.tensor.transpose, reads=[self.bVT, c.bcm], writes=[pb3], out=p3[:64, n * 32:n * 32 + 32], in_=self.VT[:32, sl], identity=c.ident[:32, :32])
        RV, bRV = self.RV
        pv3 = p3[:64, :nch * 32].rearrange("p (n d) -> p n d", d=32)
        if self.delta:
            kb.op("dve", nc.vector.tensor_tensor, reads=[pb3, self.bg2col], writes=[bRV], out=RV[:, :nch, :], in0=pv3, in1=bc_last(self.g2col[:, ci:ci + nch], 32), op=ALU.mult)
        else:
            kb.op("act", nc.scalar.copy, reads=[pb3], writes=[bRV], out=RV[:, :nch, 0:32], in_=pv3)
        if not self.delta:
            return
        p0, pb0 = c.ps[0]
        for n in range(nch):
            sl = slice(c0 + n * 64, c0 + n * 64 + 64)
            kb.op("pe", nc.tensor.matmul, reads=[self.bKT], writes=[pb0], out=p0[:64, n * 64:n * 64 + 64], lhsT=self.KT[:, sl], rhs=self.KT[:, sl], start=True, stop=True)
        N0, bN0 = self.w["N0"]; NT0, bNT0 = self.w["NT0"]; N1, bN1 = self.w["N1"]; NT1, bNT1 = self.w["NT1"]; P, bP = self.w["P"]
        kb.op("dve", nc.vector.tensor_tensor, reads=[self.bg2bc, c.bcm], writes=[btmp], out=v3(tmp), in0=v3(self.g2bc[:, c0:c0 + W]), in1=bc_mid(c.maskUn, nch), op=ALU.mult)
        kb.op("dve", nc.vector.tensor_tensor, reads=[btmp, bD], writes=[btmp], out=tmp[:, :W], in0=tmp[:, :W], in1=D[:, :W], op=ALU.mult)
        kb.op("dve", nc.vector.tensor_tensor, reads=[pb0, btmp], writes=[bN0], out=N0[:, :W], in0=p0[:64, :W], in1=tmp[:, :W], op=ALU.mult)
        kb.op("dve", nc.vector.tensor_tensor, reads=[self.bgbc, self.bgcol], writes=[btmp], out=v3(tmp), in0=bc_last(self.gcol[:, ci:ci + nch], 64), in1=gbc3, op=ALU.subtract)
        kb.op("dve", nc.vector.tensor_scalar, reads=[btmp], writes=[btmp], out=tmp[:, :W], in0=tmp[:, :W], scalar1=0.0, scalar2=None, op0=ALU.min)
        kb.op("act", nc.scalar.activation, reads=[btmp], writes=[btmp], out=tmp[:, :W], in_=tmp[:, :W], func=AF.Exp)
        kb.op("dve", nc.vector.tensor_tensor, reads=[btmp, c.bcm], writes=[btmp], out=v3(tmp), in0=v3(tmp), in1=bc_mid(c.maskLn, nch), op=ALU.mult)
        kb.op("dve", nc.vector.tensor_tensor, reads=[btmp, self.bg2col], writes=[btmp], out=v3(tmp), in0=v3(tmp), in1=bc_last(self.g2col[:, ci:ci + nch], 64), op=ALU.mult)
        kb.op("dve", nc.vector.tensor_tensor, reads=[pb0, btmp], writes=[bNT0], out=NT0[:, :W], in0=p0[:64, :W], in1=tmp[:, :W], op=ALU.mult)
        kb.op("dve", nc.vector.tensor_tensor, reads=[bN0, c.bcm], writes=[bP], out=v3(P), in0=v3(N0), in1=bc_mid(c.ident, nch), op=ALU.add)
        A, bA, AT, bAT = N0, bN0, NT0, bNT0
        A2, bA2, AT2, bAT2 = N1, bN1, NT1, bNT1
        pa, pab = c.ps[0]; pat, patb = c.ps[1]; pp, ppb = c.ps[4]
        for r in range(5):
            last = (r == 4)
            if not last:
                for n in range(nch):
                    s = slice(n * 64, n * 64 + 64)
                    kb.op("pe", nc.tensor.matmul, reads=[bA, bAT], writes=[pab], out=pa[:64, s], lhsT=AT[:, s], rhs=A[:, s], start=True, stop=True)
            for n in range(nch):
                s = slice(n * 64, n * 64 + 64)
                kb.op("pe", nc.tensor.matmul, reads=[bA, bAT], writes=[patb], out=pat[:64, s], lhsT=A[:, s], rhs=AT[:, s], start=True, stop=True)
            if not last:
                kb.op("act", nc.scalar.copy, reads=[pab], writes=[bA2], out=A2[:, :W], in_=pa[:64, :W])
            kb.op("dve", nc.vector.tensor_copy, reads=[patb], writes=[bAT2], out=AT2[:, :W], in_=pat[:64, :W])
            for n in range(nch):
                s = slice(n * 64, n * 64 + 64)
                kb.op("pe", nc.tensor.matmul, reads=[bAT2, bP], writes=[ppb], out=pp[:64, s], lhsT=AT2[:, s], rhs=P[:, s], start=True, stop=True)
            kb.op("dve", nc.vector.tensor_tensor, reads=[ppb, bP], writes=[bP], out=P[:, :W], in0=pp[:64, :W], in1=P[:, :W], op=ALU.add)
            A, bA, AT, bAT, A2, bA2, AT2, bAT2 = A2, bA2, AT2, bAT2, A, bA, AT, bAT
        pw, pwb = c.ps[3]; pk, pkb = c.ps[2]
        RK, bRK = self.RK
        for n in range(nch):
            s = slice(n * 64, n * 64 + 64)
            kb.op("pe", nc.tensor.matmul, reads=[bP, bRV], writes=[pwb], out=pw[:64, n * 32:n * 32 + 32], lhsT=P[:, s], rhs=RV[:, n, :], start=True, stop=True)
        for n in range(nch):
            s = slice(n * 64, n * 64 + 64)
            kb.op("pe", nc.tensor.matmul, reads=[bP, bRK], writes=[pkb], out=pk[:64, s], lhsT=RK[:, s], rhs=P[:, s], start=True, stop=True)
        wv, bwv = self.wv; wkT, bwkT = self.wkT
        kb.op("act", nc.scalar.copy, reads=[pwb], writes=[bwv], out=wv[:, :nch, :], in_=pw[:64, :nch * 32].rearrange("p (n d) -> p n d", d=32))
        kb.op("act", nc.scalar.copy, reads=[pkb], writes=[bwkT], out=wkT[:, :W], in_=pk[:64, :W])

    def scan(self, ci, nch, slots):
        c = self.c; kb = c.kb; nc = c.nc; DV = self.DV; c0 = ci * 64
        pT, bpT = self.w["pT"]; qg, bqg = self.w["qgT"]; KG, bKG = self.w["KG"]; RV, bRV = self.RV
        po, pob = c.ps[7]; psm, psmb = c.ps[6]
        for n in range(nch):
            s = slice(n * 64, n * 64 + 64); sl = slots[n]
            S = self.S[:, sl, :]
            if self.delta:
                wv, bwv = self.wv; wkT, bwkT = self.wkT; u, bu = self.u[n % 2]
                kb.op("pe", nc.tensor.matmul, reads=[bwkT, self.bS], writes=[psmb], out=psm[:64, 0:DV], lhsT=wkT[:, s], rhs=S, start=True, stop=True)
                kb.op("pe", nc.tensor.matmul, reads=[bqg, self.bS], writes=[pob], out=po[:DV, s], lhsT=S, rhs=qg[:, s], start=True, stop=False)
                kb.op("dve", nc.vector.tensor_tensor, reads=[psmb, bwv], writes=[bu], out=u[:], in0=wv[:, n, :], in1=psm[:64, 0:DV], op=ALU.subtract)
                uu, buu = u[:], bu
            else:
                kb.op("pe", nc.tensor.matmul, reads=[bqg, self.bS], writes=[pob], out=po[:DV, s], lhsT=S, rhs=qg[:, s], start=True, stop=False)
                uu, buu = RV[:, n, :], bRV
            kb.op("pe", nc.tensor.matmul, reads=[bpT, buu], writes=[pob], out=po[:DV, s], lhsT=uu, rhs=pT[:, s], start=False, stop=True)
            kb.op("pe", nc.tensor.matmul, reads=[bKG, buu], writes=[psmb], out=psm[:64, 64:64 + DV], lhsT=KG[:, s], rhs=uu, start=True, stop=True)
            glast = self.gam[:, c0 + n * 64 + 63:c0 + n * 64 + 64]
            kb.op("dve", nc.vector.scalar_tensor_tensor, reads=[psmb, self.bS, self.bgam], writes=[self.bS], out=S, in0=S, scalar=glast, in1=psm[:64, 64:64 + DV], op0=ALU.mult, op1=ALU.add)
        kb.op("act", nc.scalar.copy, reads=[pob], writes=[self.boT], out=self.oT[:, c0:c0 + nch * 64], in_=po[:DV, :nch * 64])

def gate_cols(c, la, ci0=0):
    kb = c.kb; nc = c.nc
    row_to_col(c, la.gcol, la.bgcol, la.grow[0:1, :], la.bgrow, 34, la.onesrow[0:1, 0:1], la.bonesrow)
    row_to_col(c, la.g2col, la.bg2col, la.g2row[0:1, :], la.bg2row, 34, la.onesrow[0:1, 0:1], la.bonesrow)
    bcast_rows(c, la.gbc, la.bgbc, la.grow, la.bgrow, NCOL, ones_row=la.onesrow[:], bones=la.bonesrow)
    bcast_rows(c, la.gam, la.bgam, la.grow, la.bgrow, NCOL, func=AF.Exp, ones_row=la.onesrow[:], bones=la.bonesrow)

class GDN:
    def __init__(self, c, prm_d):
        self.c = c; kb = c.kb; nc = c.nc
        self.la = LinAttn(c, "gdn", True, 32)
        sb = c.sb
        self.R, self.bR = sb("gR", [64, 3 + SEGW]); self.H = [sb("gH%d" % i, [64, 3]) for i in range(3)]
        self.SR, self.bSR = sb("gSR", [64, 70])
        self.cw = [sb("gcw%d" % i, [64, 4]) for i in range(3)]
        self.cst = [sb("gcst%d" % i, [64, 16, 3]) for i in range(3)]
        self.prm, self.bprm = sb("gprm", [1, 4])
        self.gt, self.bgt = sb("ggt", [2, SEGW]); self.g1, self.bg1 = sb("gg1", [1, NCOL])
        self.ysq, self.bysq = sb("gysq", [64, 512]); self.rinv, self.brinv = sb("grinv", [64, 512])
        for i, (r0, nr) in enumerate([(0, 64), (64, 64), (128, 32)]):
            kb.dma("sp", self.cw[i][0][:nr, :], prm_d["convw"][r0:r0 + nr, :], writes=[self.cw[i][1]])
            kb.dma("sp", self.cst[i][0][:nr], prm_d["convst"][r0:r0 + nr], writes=[self.cst[i][1]])
        kb.dma("sp", self.prm[:, 0:1], prm_d["alog"], writes=[self.bprm]); kb.dma("sp", self.prm[:, 1:2], prm_d["dtb"], writes=[self.bprm])
        kb.op("act", nc.scalar.activation, reads=[self.bprm], writes=[self.bprm], out=self.prm[:, 2:3], in_=self.prm[:, 0:1], func=AF.Exp)
        kb.op("dve", nc.vector.tensor_scalar, reads=[self.bprm], writes=[self.bprm], out=self.prm[:, 2:3], in0=self.prm[:, 2:3], scalar1=-1.0, scalar2=None, op0=ALU.mult)
        la = self.la
        kb.dma("sp", la.S[:, 1:17, :], prm_d["s0"], writes=[la.bS])
        kb.op("pool", nc.gpsimd.memset, writes=[la.bS], ap=la.S[:, 0, :], constant=0.0)
        for t, b in [(la.QT, la.bQT), (la.KT, la.bKT), (la.VT, la.bVT), (self.gt, self.bgt)]:
            kb.op("pool", nc.gpsimd.memset, writes=[b], ap=t[:], constant=0.0)
        for t, b in self.H:
            kb.op("pool", nc.gpsimd.memset, writes=[b], ap=t[:], constant=0.0)

    def segment(self, s, zg, bzg, idx, bidx, icol):
        c = self.c; kb = c.kb; nc = c.nc; la = self.la
        raws = [(self.R, self.bR, 64, la.QT, la.bQT), (self.R, self.bR, 64, la.KT, la.bKT), (self.R, self.bR, 32, la.VT, la.bVT)]
        for i, (R, bR, nr, Y, bY) in enumerate(raws):
            gather_rows(c, R[:nr, 3:3 + SEGW], bR, zg, idx[:nr, icol["qkv"[i]]:icol["qkv"[i]] + 1], bidx, bzg=bzg)
            Hh, bHh = self.H[i]
            kb.op("pool", nc.gpsimd.tensor_copy, reads=[bHh], writes=[bR], out=R[:nr, 0:3], in_=Hh[:nr, :])
            kb.op("pool", nc.gpsimd.tensor_copy, reads=[bR], writes=[bHh], out=Hh[:nr, :], in_=R[:nr, 2048:2051])
            cst, bcst = self.cst[i]; SR, bSR = self.SR, self.bSR
            kb.op("pool", nc.gpsimd.tensor_copy, reads=[bcst], writes=[bSR], out=SR[:nr, 0:3], in_=cst[:nr, 2 * s, :])
            kb.op("pool", nc.gpsimd.tensor_copy, reads=[bR], writes=[bSR], out=SR[:nr, 3:35], in_=R[:nr, 3 + 2048:3 + 2080])
            kb.op("pool", nc.gpsimd.tensor_copy, reads=[bcst], writes=[bSR], out=SR[:nr, 35:38], in_=cst[:nr, 2 * s + 1, :])
            kb.op("pool", nc.gpsimd.tensor_copy, reads=[bR], writes=[bSR], out=SR[:nr, 38:70], in_=R[:nr, 3 + 2112:3 + 2144])
            cw, bcw = self.cw[i]
            for (X, bX, r0, ln, d0) in [(R, bR, 3, 2048, 0), (SR, bSR, 3, 32, 2048), (SR, bSR, 38, 32, 2112)]:
                kb.op("dve", nc.vector.tensor_scalar, reads=[bX, bcw], writes=[bY], out=Y[:nr, d0:d0 + ln], in0=X[:nr, r0:r0 + ln], scalar1=cw[:nr, 3:4], scalar2=None, op0=ALU.mult)
                for t in range(3):
                    kb.op("dve", nc.vector.scalar_tensor_tensor, reads=[bX, bcw, bY], writes=[bY], out=Y[:nr, d0:d0 + ln], in0=X[:nr, r0 - 3 + t:r0 - 3 + t + ln],
                          scalar=cw[:nr, t:t + 1], in1=Y[:nr, d0:d0 + ln], op0=ALU.mult, op1=ALU.add)
            kb.op("act", nc.scalar.activation, reads=[bY], writes=[bY], out=Y[:nr, :], in_=Y[:nr, :], func=AF.Silu)
            if i < 2:
                for t0 in range(0, NCOL, 512):
                    w = min(512, NCOL - t0)
                    kb.op("act", nc.scalar.activation, reads=[bY], writes=[self.bysq], out=self.ysq[:, :w], in_=Y[:, t0:t0 + w], func=AF.Square)
                    p, pb = c.ps[6]
                    kb.op("pe", nc.tensor.matmul, reads=[self.bysq, c.bcm], writes=[pb], out=p[:64, :w], lhsT=c.ones64, rhs=self.ysq[:, :w], start=True, stop=True)
                    kb.op("act", nc.scalar.activation, reads=[pb, c.beps], writes=[self.brinv], out=self.rinv[:, :w], in_=p[:64, :w], func=AF.Sqrt, bias=c.eps[:64, 0:1])
                    kb.op("dve", nc.vector.reciprocal, reads=[self.brinv], writes=[self.brinv], out=self.rinv[:, :w], in_=self.rinv[:, :w])
                    kb.op("dve", nc.vector.scalar_tensor_tensor, reads=[bY, self.brinv], writes=[bY], out=Y[:, t0:t0 + w], in0=Y[:, t0:t0 + w], scalar=(0.125 if i == 0 else 1.0),
                          in1=self.rinv[:, :w], op0=ALU.mult, op1=ALU.mult)
        gather_rows(c, self.gt[:2, :], self.bgt, zg, idx[:2, icol["g"]:icol["g"] + 1], bidx, bzg=bzg)
        kb.dma("sp", self.g1[:], self.gt[1:2, :NCOL], reads=[self.bgt], writes=[self.bg1])
        valid = c.rows[:, 1, :]
        kb.op("act", nc.scalar.activation, reads=[self.bgt, self.bprm], writes=[la.bgrow], out=la.grow[:], in_=self.gt[0:1, :NCOL], func=AF.Exp, bias=self.prm[:, 1:2])
        kb.op("act", nc.scalar.activation, reads=[la.bgrow], writes=[la.bgrow], out=la.grow[:], in_=la.grow[:], func=AF.Ln, bias=1.0)
        kb.op("dve", nc.vector.scalar_tensor_tensor, reads=[la.bgrow, self.bprm, c.brows], writes=[la.bgrow], out=la.grow[:], in0=la.grow[:], scalar=self.prm[:, 2:3], in1=valid, op0=ALU.mult, op1=ALU.mult)
        kb.op("dve", nc.vector.tensor_tensor_scan, reads=[la.bgrow, c.brows], writes=[la.bgrow], out=la.grow[:], data0=c.rows[:, 0, :], data1=la.grow[:], initial=0.0, op0=ALU.mult, op1=ALU.add)
        kb.op("act", nc.scalar.activation, reads=[self.bg1], writes=[la.bg2row], out=la.g2row[:], in_=self.g1[:], func=AF.Sigmoid)
        kb.op("dve", nc.vector.tensor_tensor, reads=[la.bg2row, c.brows], writes=[la.bg2row], out=la.g2row[:], in0=la.g2row[:], in1=valid, op=ALU.mult)
        gate_cols(c, la)
        bcast_rows(c, la.g2bc, la.bg2bc, la.g2row, la.bg2row, NCOL, ones_row=la.onesrow[:], bones=la.bonesrow)
        glast = la.gbc[:, 63:NCOL:64]
        kb.op("dve", nc.vector.tensor_tensor, reads=[la.bgbc, la.bgcol], writes=[la.bkwcol], out=la.kwcol[:], in0=glast, in1=la.gcol[:], op=ALU.subtract)
        kb.op("act", nc.scalar.activation, reads=[la.bkwcol], writes=[la.bkwcol], out=la.kwcol[:], in_=la.kwcol[:], func=AF.Exp)
        kb.op("act", nc.scalar.activation, reads=[la.bgcol], writes=[la.brkcol], out=la.rkcol[:], in_=la.gcol[:], func=AF.Exp)
        kb.op("dve", nc.vector.tensor_tensor, reads=[la.brkcol, la.bg2col], writes=[la.brkcol], out=la.rkcol[:], in0=la.rkcol[:], in1=la.g2col[:], op=ALU.mult)
        for (ci, nch) in GROUPS:
            la.mats(ci, nch)
            slots = [0] * nch if ci < 32 else [1 + 2 * s, 2 + 2 * s]
            la.scan(ci, nch, slots)


class MLSTM:
    def __init__(self, c, prm_d):
        self.c = c; kb = c.kb; nc = c.nc
        self.la = la = LinAttn(c, "ml", False, 33)
        sb = c.sb
        self.prm, self.bprm = sb("mprm", [1, 4])
        self.gt, self.bgt = sb("mgt", [2, SEGW]); self.g1, self.bg1 = sb("mg1", [1, NCOL])
        self.padb, self.bpadb = sb("mpadb", [1, NCOL])
        self.mrow, self.bmrow = sb("mmrow", [1, NCOL]); self.mfin, self.bmfin = sb("mmfin", [1, 17]); self.em, self.bem = sb("mem", [1, 17])
        self.embc, self.bembc = sb("membc", [64, 17]); self.Sout, self.bSout = sb("mSout", [64, 17, 33])
        self.lf, self.blf = sb("mlf", [1, NCOL])
        self.on33, self.bon33 = sb("mon33", [33, 32]); self.rrow, self.brrow = sb("mrrow", [33, 512]); self.hT, self.bhT = sb("mhT", [32, SEGW])
        kb.op("pool", nc.gpsimd.memset, writes=[self.bhT], ap=self.hT[:], constant=0.0)
        kb.op("pool", nc.gpsimd.memset, writes=[self.bon33], ap=self.on33[:], constant=1.0)
        kb.dma("sp", self.prm[:, 0:1], prm_d["bi"], writes=[self.bprm]); kb.dma("sp", self.prm[:, 1:2], prm_d["bf"], writes=[self.bprm])
        kb.op("dve", nc.vector.tensor_scalar, reads=[self.bprm], writes=[self.bprm], out=self.prm[:, 2:3], in0=self.prm[:, 1:2], scalar1=-1.0, scalar2=None, op0=ALU.mult)
        kb.op("dve", nc.vector.tensor_scalar, reads=[c.brows], writes=[self.bpadb], out=self.padb[:], in0=c.rows[:, 1, :], scalar1=1.0, scalar2=30000.0, op0=ALU.subtract, op1=ALU.mult)
        kb.dma("sp", la.S[:, 1:17, :], prm_d["s0"], writes=[la.bS])
        kb.op("pool", nc.gpsimd.memset, writes=[la.bS], ap=la.S[:, 0, :], constant=0.0)
        kb.op("pool", nc.gpsimd.memset, writes=[self.bmfin], ap=self.mfin[:], constant=0.0)
        kb.dma("sp", self.mfin[:, 1:17], prm_d["m0"], writes=[self.bmfin])
        kb.op("act", nc.scalar.activation, reads=[self.bmfin], writes=[self.bem], out=self.em[:], in_=self.mfin[:], func=AF.Exp)
        p, pb = c.ps[6]
        kb.op("pe", nc.tensor.matmul, reads=[self.bem, la.bonesrow], writes=[pb], out=p[:64, :17], lhsT=la.onesrow[:], rhs=self.em[:], start=True, stop=True)
        kb.op("act", nc.scalar.copy, reads=[pb], writes=[self.bembc], out=self.embc[:], in_=p[:64, :17])
        kb.op("dve", nc.vector.tensor_tensor, reads=[la.bS, self.bembc], writes=[la.bS], out=la.S[:, 1:17, :], in0=la.S[:, 1:17, :], in1=bc_last(self.embc[:, 1:17], 33), op=ALU.mult)
        kb.op("pool", nc.gpsimd.memset, writes=[la.bVT], ap=la.VT[:], constant=1.0)

    def segment(self, s, zg, bzg, idx, bidx, icol):
        c = self.c; kb = c.kb; nc = c.nc; la = self.la
        gather_rows(c, la.QT[:, :], la.bQT, zg, idx[:64, icol["q"]:icol["q"] + 1], bidx, bzg=bzg)
        gather_rows(c, la.KT[:, :], la.bKT, zg, idx[:64, icol["k"]:icol["k"] + 1], bidx, bzg=bzg)
        gather_rows(c, la.VT[:32, :], la.bVT, zg, idx[:32, icol["v"]:icol["v"] + 1], bidx, bzg=bzg)
        gather_rows(c, self.gt[:2, :], self.bgt, zg, idx[:2, icol["g"]:icol["g"] + 1], bidx, bzg=bzg)
        kb.op("act", nc.scalar.mul, reads=[la.bKT], writes=[la.bKT], out=la.KT[:, :NCOL], in_=la.KT[:, :NCOL], mul=0.125)
        kb.dma("sp", self.g1[:], self.gt[1:2, :NCOL], reads=[self.bgt], writes=[self.bg1])
        valid = c.rows[:, 1, :]
        kb.op("dve", nc.vector.scalar_tensor_tensor, reads=[self.bgt, self.bprm, c.brows], writes=[la.bg2row], out=la.g2row[:], in0=self.gt[0:1, :NCOL], scalar=self.prm[:, 0:1], in1=valid, op0=ALU.add, op1=ALU.mult)
        kb.op("dve", nc.vector.tensor_tensor, reads=[la.bg2row, self.bpadb], writes=[la.bg2row], out=la.g2row[:], in0=la.g2row[:], in1=self.padb[:], op=ALU.add)
        kb.op("act", nc.scalar.activation, reads=[self.bg1, self.bprm], writes=[self.blf], out=self.lf[:], in_=self.g1[:], func=AF.Exp, scale=-1.0, bias=self.prm[:, 2:3])
        kb.op("act", nc.scalar.activation, reads=[self.blf], writes=[self.blf], out=self.lf[:], in_=self.lf[:], func=AF.Ln, bias=1.0)
        kb.op("dve", nc.vector.scalar_tensor_tensor, reads=[self.blf, c.brows], writes=[self.blf], out=self.lf[:], in0=self.lf[:], scalar=-1.0, in1=valid, op0=ALU.mult, op1=ALU.mult)
        for (c0, ln, slot) in [(0, 2048, 0), (2048, 32, 1 + 2 * s), (2112, 32, 2 + 2 * s)]:
            kb.op("dve", nc.vector.tensor_tensor_scan, reads=[self.blf, la.bg2row, self.bmfin], writes=[self.bmrow], out=self.mrow[:, c0:c0 + ln], data0=self.lf[:, c0:c0 + ln], data1=la.g2row[:, c0:c0 + ln],
                  initial=self.mfin[:, slot:slot + 1], op0=ALU.add, op1=ALU.max)
            kb.op("dve", nc.vector.tensor_copy, reads=[self.bmrow], writes=[self.bmfin], out=self.mfin[:, slot:slot + 1], in_=self.mrow[:, c0 + ln - 1:c0 + ln])
        kb.op("dve", nc.vector.tensor_tensor_scan, reads=[self.blf, c.brows], writes=[la.bgrow], out=la.grow[:], data0=c.rows[:, 0, :], data1=self.lf[:], initial=0.0, op0=ALU.mult, op1=ALU.add)
        gate_cols(c, la)
        kb.op("act", nc.scalar.activation, reads=[la.bg2col], writes=[la.bg2col], out=la.g2col[:], in_=la.g2col[:], func=AF.Exp)
        glast = la.gbc[:, 63:NCOL:64]
        kb.op("dve", nc.vector.tensor_tensor, reads=[la.bgbc, la.bgcol], writes=[la.bkwcol], out=la.kwcol[:], in0=glast, in1=la.gcol[:], op=ALU.subtract)
        kb.op("act", nc.scalar.activation, reads=[la.bkwcol], writes=[la.bkwcol], out=la.kwcol[:], in_=la.kwcol[:], func=AF.Exp)
        kb.op("dve", nc.vector.tensor_tensor, reads=[la.bkwcol, la.bg2col], writes=[la.bkwcol], out=la.kwcol[:], in0=la.kwcol[:], in1=la.g2col[:], op=ALU.mult)
        for (ci, nch) in GROUPS:
            la.mats(ci, nch)
            slots = [0] * nch if ci < 32 else [1 + 2 * s, 2 + 2 * s]
            la.scan(ci, nch, slots)
            c0 = ci * 64; W = nch * 64
            kb.op("dve", nc.vector.scalar_tensor_tensor, reads=[la.boT], writes=[self.brrow], out=self.rrow[32:33, :W], in0=la.oT[32:33, c0:c0 + W], scalar=-1.0, in1=la.oT[32:33, c0:c0 + W], op0=ALU.mult, op1=ALU.max)
            kb.op("dve", nc.vector.tensor_scalar, reads=[self.brrow], writes=[self.brrow], out=self.rrow[32:33, :W], in0=self.rrow[32:33, :W], scalar1=1.0, scalar2=None, op0=ALU.max)
            kb.op("dve", nc.vector.reciprocal, reads=[self.brrow], writes=[self.brrow], out=self.rrow[32:33, :W], in_=self.rrow[32:33, :W])
            p, pb = c.ps[6]
            kb.op("pe", nc.tensor.matmul, reads=[self.brrow, self.bon33], writes=[pb], out=p[:32, :W], lhsT=self.on33[32:33, :], rhs=self.rrow[32:33, :W], start=True, stop=True)
            kb.op("dve", nc.vector.tensor_tensor, reads=[pb, la.boT], writes=[self.bhT], out=self.hT[:, c0:c0 + W], in0=la.oT[0:32, c0:c0 + W], in1=p[:32, :W], op=ALU.mult)

    def finish(self):
        c = self.c; kb = c.kb; nc = c.nc; la = self.la
        kb.op("act", nc.scalar.activation, reads=[self.bmfin], writes=[self.bem], out=self.em[:], in_=self.mfin[:], func=AF.Exp, scale=-1.0)
        p, pb = c.ps[6]
        kb.op("pe", nc.tensor.matmul, reads=[self.bem, la.bonesrow], writes=[pb], out=p[:64, :17], lhsT=la.onesrow[:], rhs=self.em[:], start=True, stop=True)
        kb.op("act", nc.scalar.copy, reads=[pb], writes=[self.bembc], out=self.embc[:], in_=p[:64, :17])
        kb.op("dve", nc.vector.tensor_tensor, reads=[la.bS, self.bembc], writes=[self.bSout], out=self.Sout[:], in0=la.S[:], in1=bc_last(self.embc[:], 33), op=ALU.mult)

PI = 3.141592653589793

class S5:
    def __init__(self, c, prm_d):
        self.c = c; kb = c.kb; nc = c.nc; sb = c.sb
        V = nc.vector
        self.pv, self.bpv = sb("s5pv", [128, 32])
        self.BT, self.bBT = sb("s5BT", [32, 2, 128]); self.CT, self.bCT = sb("s5CT", [128, 2, 32]); self.dv, self.bdv = sb("s5dv", [32, 1])
        self.X, self.bX = sb("s5X", [128, 2, 17])
        self.U, self.bU = sb("s5U", [128, 2, 2048]); self.L, self.bL = sb("s5L", [128, 2, 2048])
        self.rho, self.brho = sb("s5rho", [128, NCOL])
        self.uT, self.buT = sb("s5uT", [32, SEGW])
        self.bu, self.bbu = sb("s5bu", [128, 2, NCOL]); self.rr, self.brr = sb("s5rr", [128, 2, NCOL]); self.ww, self.bww = sb("s5ww", [128, 2, NCOL])
        self.t1, self.bt1 = sb("s5t1", [128, NCOL]); self.yT, self.byT = sb("s5yT", [32, SEGW])
        kb.op("pool", nc.gpsimd.memset, writes=[self.byT], ap=self.yT[:], constant=0.0)
        self.pw, self.bpw = sb("s5pw", [128, 8])
        pv = self.pv; bpv = self.bpv
        kb.dma("sp", pv[:, 0:3], prm_d["vec"], writes=[bpv])
        kb.dma("sp", self.BT[:, 0, :], prm_d["BreT"], writes=[self.bBT]); kb.dma("sp", self.BT[:, 1, :], prm_d["BimT"], writes=[self.bBT])
        kb.dma("sp", self.CT[:, 0, :], prm_d["CreT"], writes=[self.bCT]); kb.dma("sp", self.CT[:, 1, :], prm_d["CimT"], writes=[self.bCT])
        kb.dma("sp", self.dv[:], prm_d["dvec"], writes=[self.bdv])
        kb.op("pool", nc.gpsimd.memset, writes=[self.bX], ap=self.X[:], constant=0.0)
        kb.dma("sp", self.X[:, :, 1:17], prm_d["x0"], writes=[self.bX])
        col = lambda i: pv[:, i:i + 1]
        def ts(out, in0, s1, s2, o0, o1=None):
            if o1 is None: kb.op("dve", V.tensor_scalar, reads=[bpv], writes=[bpv], out=out, in0=in0, scalar1=s1, scalar2=None, op0=o0)
            else: kb.op("dve", V.tensor_scalar, reads=[bpv], writes=[bpv], out=out, in0=in0, scalar1=s1, scalar2=s2, op0=o0, op1=o1)
        def tt(out, a, b, op): kb.op("dve", V.tensor_tensor, reads=[bpv], writes=[bpv], out=out, in0=a, in1=b, op=op)
        def act(out, in_, func, **kw): kb.op("act", nc.scalar.activation, reads=[bpv], writes=[bpv], out=out, in_=in_, func=func, **kw)
        act(col(3), col(2), AF.Exp)
        tt(col(4), col(3), col(0), ALU.mult); tt(col(5), col(3), col(1), ALU.mult)
        act(col(6), col(4), AF.Exp)
        def wrap(dst, src, thrs):
            kb.op("dve", V.tensor_copy, reads=[bpv], writes=[bpv], out=dst, in_=src)
            for th in thrs:
                ts(col(7), src, th, -2 * PI, ALU.is_gt, ALU.mult)
                tt(dst, dst, col(7), ALU.add)
        wrap(col(8), col(5), [PI, 3 * PI, 5 * PI])
        act(col(9), col(8), AF.Sin)
        ts(col(17), col(8), PI / 2, None, ALU.add)
        wrap(col(8), col(17), [PI])
        act(col(10), col(8), AF.Sin)
        tt(col(11), col(6), col(10), ALU.mult); tt(col(12), col(6), col(9), ALU.mult)
        tt(col(13), col(0), col(0), ALU.mult); tt(col(7), col(1), col(1), ALU.mult); tt(col(13), col(13), col(7), ALU.add)
        kb.op("dve", V.reciprocal, reads=[bpv], writes=[bpv], out=col(13), in_=col(13))
        ts(col(14), col(11), -1.0, None, ALU.add)
        tt(col(15), col(14), col(0), ALU.mult); tt(col(7), col(12), col(1), ALU.mult); tt(col(15), col(15), col(7), ALU.add); tt(col(15), col(15), col(13), ALU.mult)
        tt(col(16), col(12), col(0), ALU.mult); tt(col(7), col(14), col(1), ALU.mult); tt(col(16), col(16), col(7), ALU.subtract); tt(col(16), col(16), col(13), ALU.mult)
        ts(col(18), col(16), -1.0, None, ALU.mult)
        kb.op("pool", nc.gpsimd.memset, writes=[self.brho], ap=self.rho[:], constant=0.0)
        kb.op("dve", V.tensor_scalar, reads=[bpv, self.brho], writes=[self.brho], out=self.rho[:], in0=self.rho[:], scalar1=col(6), scalar2=None, op0=ALU.add)
        for t0 in [0, 2048, 2112]:
            kb.op("pool", nc.gpsimd.memset, writes=[self.brho], ap=self.rho[:, t0:t0 + 1], constant=0.0)
        def table(T, bT, first_re, first_im, lead_one):
            pw = self.pw; bpw = self.bpw
            if lead_one:
                kb.op("pool", nc.gpsimd.memset, writes=[bT], ap=T[:, 0, 0:1], constant=1.0)
                kb.op("pool", nc.gpsimd.memset, writes=[bT], ap=T[:, 1, 0:1], constant=0.0)
            else:
                kb.op("dve", V.tensor_copy, reads=[bpv], writes=[bT], out=T[:, 0, 0:1], in_=first_re)
                kb.op("dve", V.tensor_copy, reads=[bpv], writes=[bT], out=T[:, 1, 0:1], in_=first_im)
            kb.op("dve", V.tensor_copy, reads=[bpv], writes=[bpw], out=pw[:, 0:1], in_=first_re)
            kb.op("dve", V.tensor_copy, reads=[bpv], writes=[bpw], out=pw[:, 1:2], in_=first_im)
            m = 1
            while m < 2048:
                kb.op("dve", V.tensor_scalar, reads=[bpw], writes=[bpw], out=pw[:, 2:3], in0=pw[:, 1:2], scalar1=-1.0, scalar2=None, op0=ALU.mult)
                kb.op("dve", V.tensor_scalar, reads=[bT, bpw], writes=[bT], out=T[:, 0, m:2 * m], in0=T[:, 0, 0:m], scalar1=pw[:, 0:1], scalar2=None, op0=ALU.mult)
                kb.op("dve", V.scalar_tensor_tensor, reads=[bT, bpw], writes=[bT], out=T[:, 0, m:2 * m], in0=T[:, 1, 0:m], scalar=pw[:, 2:3], in1=T[:, 0, m:2 * m], op0=ALU.mult, op1=ALU.add)
                kb.op("dve", V.tensor_scalar, reads=[bT, bpw], writes=[bT], out=T[:, 1, m:2 * m], in0=T[:, 0, 0:m], scalar1=pw[:, 1:2], scalar2=None, op0=ALU.mult)
                kb.op("dve", V.scalar_tensor_tensor, reads=[bT, bpw], writes=[bT], out=T[:, 1, m:2 * m], in0=T[:, 1, 0:m], scalar=pw[:, 0:1], in1=T[:, 1, m:2 * m], op0=ALU.mult, op1=ALU.add)
                kb.op("dve", V.tensor_tensor, reads=[bpw], writes=[bpw], out=pw[:, 3:4], in0=pw[:, 0:1], in1=pw[:, 1:2], op=ALU.mult)
                kb.op("dve", V.tensor_tensor, reads=[bpw], writes=[bpw], out=pw[:, 4:5], in0=pw[:, 1:2], in1=pw[:, 1:2], op=ALU.mult)
                kb.op("dve", V.scalar_tensor_tensor, reads=[bpw], writes=[bpw], out=pw[:, 0:1], in0=pw[:, 0:1], scalar=pw[:, 0:1], in1=pw[:, 4:5], op0=ALU.mult, op1=ALU.subtract)
                kb.op("dve", V.tensor_scalar, reads=[bpw], writes=[bpw], out=pw[:, 1:2], in0=pw[:, 3:4], scalar1=2.0, scalar2=None, op0=ALU.mult)
                m *= 2
        table(self.U, self.bU, col(10), col(9), True)
        table(self.L, self.bL, col(11), col(12), False)

    def segment(self, s, zg, bzg, idx, bidx, icol):
        c = self.c; kb = c.kb; nc = c.nc; V = nc.vector; G = nc.gpsimd; pv = self.pv; bpv = self.bpv
        col = lambda i: pv[:, i:i + 1]
        gather_rows(c, self.uT[:32, :], self.buT, zg, idx[:32, icol:icol + 1], bidx, bzg=bzg)
        bu = self.bu; bbu = self.bbu
        for t0 in range(0, NCOL, 512):
            w = min(512, NCOL - t0)
            (p1, b1), (p2, b2) = c.ps[0], c.ps[1]
            kb.op("pe", nc.tensor.matmul, reads=[self.buT, self.bBT], writes=[b1], out=p1[:, :w], lhsT=self.BT[:, 0, :], rhs=self.uT[:, t0:t0 + w], start=True, stop=True)
            kb.op("pe", nc.tensor.matmul, reads=[self.buT, self.bBT], writes=[b2], out=p2[:, :w], lhsT=self.BT[:, 1, :], rhs=self.uT[:, t0:t0 + w], start=True, stop=True)
            kb.op("dve", V.tensor_scalar, reads=[b1, bpv], writes=[bbu], out=bu[:, 0, t0:t0 + w], in0=p1[:, :w], scalar1=col(15), scalar2=None, op0=ALU.mult)
            kb.op("dve", V.scalar_tensor_tensor, reads=[b2, bpv, bbu], writes=[bbu], out=bu[:, 0, t0:t0 + w], in0=p2[:, :w], scalar=col(18), in1=bu[:, 0, t0:t0 + w], op0=ALU.mult, op1=ALU.add)
            kb.op("dve", V.tensor_scalar, reads=[b2, bpv], writes=[bbu], out=bu[:, 1, t0:t0 + w], in0=p2[:, :w], scalar1=col(15), scalar2=None, op0=ALU.mult)
            kb.op("dve", V.scalar_tensor_tensor, reads=[b1, bpv, bbu], writes=[bbu], out=bu[:, 1, t0:t0 + w], in0=p1[:, :w], scalar=col(16), in1=bu[:, 1, t0:t0 + w], op0=ALU.mult, op1=ALU.add)
        U = self.U; bU = self.bU; L = self.L; bL = self.bL; rr = self.rr; brr = self.brr; ww = self.ww; bww = self.bww; t1 = self.t1; bt1 = self.bt1
        regs = [(0, 2048, 0), (2048, 32, 1 + 2 * s), (2112, 32, 2 + 2 * s)]
        for (c0, ln, slot) in regs:
            sl = slice(c0, c0 + ln); ul = slice(0, ln)
            kb.op("dve", V.tensor_tensor, reads=[bU, bbu], writes=[brr], out=rr[:, 0, sl], in0=U[:, 0, ul], in1=bu[:, 0, sl], op=ALU.mult)
            kb.op("pool", G.tensor_tensor, reads=[bU, bbu], writes=[bt1], out=t1[:, sl], in0=U[:, 1, ul], in1=bu[:, 1, sl], op=ALU.mult)
            kb.op("dve", V.tensor_tensor, reads=[brr, bt1], writes=[brr], out=rr[:, 0, sl], in0=rr[:, 0, sl], in1=t1[:, sl], op=ALU.add)
            kb.op("dve", V.tensor_tensor, reads=[bU, bbu], writes=[brr], out=rr[:, 1, sl], in0=U[:, 0, ul], in1=bu[:, 1, sl], op=ALU.mult)
            kb.op("pool", G.tensor_tensor, reads=[bU, bbu], writes=[bt1], out=t1[:, sl], in0=U[:, 1, ul], in1=bu[:, 0, sl], op=ALU.mult)
            kb.op("dve", V.tensor_tensor, reads=[brr, bt1], writes=[brr], out=rr[:, 1, sl], in0=rr[:, 1, sl], in1=t1[:, sl], op=ALU.subtract)
        for k in range(2):
            kb.op("dve", V.tensor_tensor_scan, reads=[brr, self.brho], writes=[bww], out=ww[:, k, :], data0=self.rho[:], data1=rr[:, k, :], initial=0.0, op0=ALU.mult, op1=ALU.add)
        X = self.X; bX = self.bX
        for (c0, ln, slot) in regs:
            sl = slice(c0, c0 + ln); ul = slice(0, ln)
            xr = X[:, 0, slot:slot + 1]; xi = X[:, 1, slot:slot + 1]
            kb.op("dve", V.tensor_tensor, reads=[bU, bww], writes=[brr], out=rr[:, 0, sl], in0=U[:, 0, ul], in1=ww[:, 0, sl], op=ALU.mult)
            kb.op("pool", G.tensor_tensor, reads=[bU, bww], writes=[bt1], out=t1[:, sl], in0=U[:, 1, ul], in1=ww[:, 1, sl], op=ALU.mult)
            kb.op("dve", V.tensor_tensor, reads=[brr, bt1], writes=[brr], out=rr[:, 0, sl], in0=rr[:, 0, sl], in1=t1[:, sl], op=ALU.subtract)
            kb.op("dve", V.tensor_tensor, reads=[bU, bww], writes=[brr], out=rr[:, 1, sl], in0=U[:, 0, ul], in1=ww[:, 1, sl], op=ALU.mult)
            kb.op("pool", G.tensor_tensor, reads=[bU, bww], writes=[bt1], out=t1[:, sl], in0=U[:, 1, ul], in1=ww[:, 0, sl], op=ALU.mult)
            kb.op("dve", V.tensor_tensor, reads=[brr, bt1], writes=[brr], out=rr[:, 1, sl], in0=rr[:, 1, sl], in1=t1[:, sl], op=ALU.add)
            kb.op("dve", V.tensor_scalar, reads=[bX], writes=[self.bpw], out=self.pw[:, 5:6], in0=xi, scalar1=-1.0, scalar2=None, op0=ALU.mult)
            kb.op("dve", V.scalar_tensor_tensor, reads=[bL, bX, brr], writes=[brr], out=rr[:, 0, sl], in0=L[:, 0, ul], scalar=xr, in1=rr[:, 0, sl], op0=ALU.mult, op1=ALU.add)
            kb.op("dve", V.scalar_tensor_tensor, reads=[bL, self.bpw, brr], writes=[brr], out=rr[:, 0, sl], in0=L[:, 1, ul], scalar=self.pw[:, 5:6], in1=rr[:, 0, sl], op0=ALU.mult, op1=ALU.add)
            kb.op("dve", V.scalar_tensor_tensor, reads=[bL, bX, brr], writes=[brr], out=rr[:, 1, sl], in0=L[:, 0, ul], scalar=xi, in1=rr[:, 1, sl], op0=ALU.mult, op1=ALU.add)
            kb.op("dve", V.scalar_tensor_tensor, reads=[bL, bX, brr], writes=[brr], out=rr[:, 1, sl], in0=L[:, 1, ul], scalar=xr, in1=rr[:, 1, sl], op0=ALU.mult, op1=ALU.add)
            kb.op("dve", V.tensor_copy, reads=[brr], writes=[bX], out=X[:, 0, slot:slot + 1], in_=rr[:, 0, c0 + ln - 1:c0 + ln])
            kb.op("dve", V.tensor_copy, reads=[brr], writes=[bX], out=X[:, 1, slot:slot + 1], in_=rr[:, 1, c0 + ln - 1:c0 + ln])
        kb.op("pool", G.tensor_scalar, reads=[brr], writes=[bt1], out=t1[:], in0=rr[:, 1, :], scalar1=-1.0, scalar2=None, op0=ALU.mult)
        for t0 in range(0, NCOL, 512):
            w = min(512, NCOL - t0)
            p1, b1 = c.ps[2]
            kb.op("pe", nc.tensor.matmul, reads=[brr, self.bCT], writes=[b1], out=p1[:32, :w], lhsT=self.CT[:, 0, :], rhs=rr[:, 0, t0:t0 + w], start=True, stop=False)
            kb.op("pe", nc.tensor.matmul, reads=[bt1, self.bCT], writes=[b1], out=p1[:32, :w], lhsT=self.CT[:, 1, :], rhs=t1[:, t0:t0 + w], start=False, stop=True)
            kb.op("dve", V.scalar_tensor_tensor, reads=[b1, self.buT, self.bdv], writes=[self.byT], out=self.yT[:, t0:t0 + w], in0=self.uT[:, t0:t0 + w], scalar=self.dv[:, 0:1], in1=p1[:32, :w], op0=ALU.mult, op1=ALU.add)

QW = 1280
NEG = -30000.0

class SB:
    def __init__(self, c, prm_d, nseg=8):
        self.c = c; kb = c.kb; nc = c.nc; sb = c.sb; self.nseg = nseg; self.prm_d = prm_d
        NK = 2048 * nseg
        self.KT, self.bKT = sb("sbKT", [64, NK], BF16); self.V, self.bV = sb("sbV", [128, 16 * nseg, 64], BF16)
        self.Q, self.bQ = sb("sbQ", [64, 1024 * nseg], BF16)
        self.KS, self.bKS = sb("sbKS", [64, 16, 32], BF16); self.VS, self.bVS = sb("sbVS", [32, 16, 64], BF16); self.QS, self.bQS = sb("sbQS", [64, 16, 32], BF16)
        self.MB, self.bMB = sb("sbMB", [128, 8, 512], BF16); self.DM, self.bDM = sb("sbDM", [32, 32], BF16)
        self.idb, self.bidb = sb("sbidb", [128, 128], BF16); self.trin, self.btrin = sb("sbtrin", [128, 128], BF16); self.onen, self.bonen = sb("sbonen", [128, 128], BF16)
        self.idf, self.bidf = sb("sbidf", [64, 64])
        self.stg, self.bstg = sb("sbstg", [64, SEGW]); self.stq, self.bstq = sb("sbstq", [64, QW])
        self.e = [sb("sbe%d" % i, [128, 512]) for i in range(2)]
        self.L = [sb("sbL%d" % i, [128, 512], BF16) for i in range(3)]
        self.R = [sb("sbR%d" % i, [128, 512], BF16) for i in range(3)]
        self.a = [sb("sba%d" % i, [128, 512], BF16) for i in range(3)]
        self.L0, self.bL0 = sb("sbLfirst", [32, 32], BF16)
        self.zero, self.bzero = sb("sbzero", [128, 512], BF16)
        self.oT, self.boT = sb("sboT", [64, SEGW]); self.oS, self.boS = sb("sboS", [64, 16, 64])
        kb.op("pool", nc.gpsimd.memset, writes=[self.boT], ap=self.oT[:], constant=0.0)
        self.kc = [sb("sbkc%d" % i, [64, 4096], BF16) for i in range(2)]; self.vc = [sb("sbvc%d" % i, [128, 32, 64], BF16) for i in range(2)]
        kb.dma("pool", self.MB[:], prm_d["mb"], writes=[self.bMB]); kb.dma("pool", self.DM[:], prm_d["dmask"], writes=[self.bDM])
        kb.dma("pool", self.idb[:], prm_d["identb"], writes=[self.bidb]); kb.dma("pool", self.trin[:], prm_d["trin"], writes=[self.btrin])
        kb.op("pool", nc.gpsimd.memset, writes=[self.bonen], ap=self.onen[:], constant=-1.0)
        kb.op("pool", nc.gpsimd.memset, writes=[self.bzero], ap=self.zero[:], constant=0.0)
        kb.op("pool", nc.gpsimd.memset, writes=[self.boS], ap=self.oS[:], constant=0.0)

    def load_segment(self, s, zg, bzg, zq, bzq, idx, bidx, icol):
        c = self.c; kb = c.kb; nc = c.nc
        stg, bstg = self.stg, self.bstg
        gather_rows(c, stg[:, :], bstg, zg, idx[:64, icol["k"]:icol["k"] + 1], bidx, bzg=bzg)
        kb.op("act", nc.scalar.copy, reads=[bstg], writes=[self.bKT], out=self.KT[:, 2048 * s:2048 * (s + 1)], in_=stg[:, 0:2048])
        kb.op("act", nc.scalar.copy, reads=[bstg], writes=[self.bKS], out=self.KS[:, 2 * s, :], in_=stg[:, 2048:2080])
        kb.op("act", nc.scalar.copy, reads=[bstg], writes=[self.bKS], out=self.KS[:, 2 * s + 1, :], in_=stg[:, 2112:2144])
        gather_rows(c, stg[:, :], bstg, zg, idx[:64, icol["v"]:icol["v"] + 1], bidx, bzg=bzg)
        for g in range(2):
            p, pb = c.ps[4 + g]
            for b in range(8):
                blk = g * 8 + b
                kb.op("pe", nc.tensor.transpose, reads=[bstg, c.bcm], writes=[pb], out=p[:, b * 64:(b + 1) * 64], in_=stg[:, blk * 128:(blk + 1) * 128], identity=c.ident)
            kb.op("act", nc.scalar.copy, reads=[pb], writes=[self.bV], out=self.V[:, 16 * s + g * 8:16 * s + g * 8 + 8, :], in_=p[:, :].rearrange("p (b d) -> p b d", d=64))
        p, pb = c.ps[4]
        for i, c0 in enumerate([2048, 2112]):
            kb.op("pe", nc.tensor.transpose, reads=[bstg, c.bcm], writes=[pb], out=p[:32, i * 64:(i + 1) * 64], in_=stg[:, c0:c0 + 32], identity=c.ident)
        kb.op("act", nc.scalar.copy, reads=[pb], writes=[self.bVS], out=self.VS[:, 2 * s:2 * s + 2, :], in_=p[:32, 0:128].rearrange("p (b d) -> p b d", d=64))
        stq, bstq = self.stq, self.bstq
        gather_rows(c, stq[:, :], bstq, zq, idx[:64, icol["q"]:icol["q"] + 1], bidx, bzg=bzq)
        kb.op("act", nc.scalar.mul, reads=[bstq], writes=[self.bQ], out=self.Q[:, 1024 * s:1024 * (s + 1)], in_=stq[:, 0:1024], mul=0.125)
        kb.op("act", nc.scalar.mul, reads=[bstq], writes=[self.bQS], out=self.QS[:, 2 * s, :], in_=stq[:, 1024:1056], mul=0.125)
        kb.op("act", nc.scalar.mul, reads=[bstq], writes=[self.bQS], out=self.QS[:, 2 * s + 1, :], in_=stq[:, 1088:1120], mul=0.125)

    def attend(self, steps, qap, bq, N, obank, out_ap, bout):
        c = self.c; kb = c.kb; nc = c.nc
        n = len(steps)
        po, pob = c.ps[obank]
        st = {}
        racc = (self.zero, self.bzero)
        first_nk = steps[0]["nk"]
        for i in range(n + 2):
            if i < n:
                S = steps[i]; nk = S["nk"]
                z, zb = c.ps[i % 3]
                kT, bkT = S["kT"]
                kb.op("pe", nc.tensor.matmul, reads=[bkT, bq], writes=[zb], out=z[:nk, :N], lhsT=kT, rhs=qap, start=True, stop=(S["mask"] is None))
                if S["mask"] is not None:
                    m, bm = S["mask"]
                    kb.op("pe", nc.tensor.matmul, reads=[bm, self.bidb], writes=[zb], out=z[:nk, :N], lhsT=self.idb[:nk, :nk], rhs=m, start=False, stop=True)
                e, be = self.e[i % 2]; L, bL = self.L[i % 3]
                if i == 0 and nk != 128: L, bL = self.L0, self.bL0
                kb.op("act", nc.scalar.activation, reads=[zb], writes=[be], out=e[:nk, :N], in_=z[:nk, :N], func=AF.Exp)
                kb.op("act", nc.scalar.activation, reads=[be], writes=[bL], out=L[:nk, :N], in_=e[:nk, :N], func=AF.Ln, bias=1.0)
                st[i] = dict(L=(L, bL), racc=racc, nk=nk)
                if i >= 1 or nk == 128:
                    Rn, bRn = self.R[i % 3]
                    if nk == 128:
                        kb.op("pool", nc.gpsimd.tensor_tensor, reads=[bL, racc[1]], writes=[bRn], out=Rn[:, :N], in0=L[:, :N], in1=racc[0][:, :N], op=ALU.add)
                        racc = (Rn, bRn)
            if 1 <= i <= n:
                j = i - 1; S = steps[j]; nk = S["nk"]; z, zb = c.ps[j % 3]
                L, bL = st[j]["L"]; ra, bra = st[j]["racc"]
                terms = [(self.trin[:nk, :nk], self.btrin, L[:nk, :N], bL)]
                if j >= 1:
                    if first_nk != 128:
                        L0, bL0 = st[0]["L"]
                        terms.append((self.onen[:first_nk, :nk], self.bonen, L0[:first_nk, :N], bL0))
                    if j >= (2 if first_nk != 128 else 1):
                        terms.append((self.onen[:, :nk], self.bonen, ra[:, :N], bra))
                for ti, (lh, blh, rh, brh) in enumerate(terms):
                    kb.op("pe", nc.tensor.matmul, reads=[blh, brh], writes=[zb], out=z[:nk, :N], lhsT=lh, rhs=rh, start=False, stop=(ti == len(terms) - 1))
                a, ba = self.a[j % 3]
                kb.op("act", nc.scalar.activation, reads=[zb], writes=[ba], out=a[:nk, :N], in_=z[:nk, :N], func=AF.Exp)
                st[j]["a"] = (a, ba)
            if i >= 2:
                j = i - 2; S = steps[j]; nk = S["nk"]
                a, ba = st[j]["a"]; v, bv = S["v"]
                kb.op("pe", nc.tensor.matmul, reads=[ba, bv], writes=[pob], out=po[:64, :N], lhsT=v, rhs=a[:nk, :N], start=(j == 0), stop=(j == n - 1))
        kb.op("act", nc.scalar.copy, reads=[pob], writes=[bout], out=out_ap, in_=po[:64, :N])

    def prompt_tile(self, m):
        steps = []
        for kbk in range(8 * m + 7, -1, -1):
            mask = (self.MB[:, kbk - 8 * m, :], self.bMB) if kbk >= 8 * m else None
            steps.append(dict(kT=(self.KT[:, kbk * 128:(kbk + 1) * 128], self.bKT), v=(self.V[:, kbk, :], self.bV), nk=128, mask=mask))
        self.attend(steps, self.Q[:, 512 * m:512 * (m + 1)], self.bQ, 512, 3, self.oT[:, 512 * (m % 2):512 * (m % 2 + 1)], self.boT)

    def sample_seq(self, q):
        c = self.c; kb = c.kb
        kc, bkc = self.kc[q % 2]; vc, bvc = self.vc[q % 2]
        kb.dma("pool", kc[:], self.prm_d["kc"][q], writes=[bkc])
        kb.dma("pool", vc[:], self.prm_d["vc"][q].rearrange("(b p) d -> p b d", p=128), writes=[bvc])
        steps = [dict(kT=(self.KS[:, q, :], self.bKS), v=(self.VS[:, q, :], self.bVS), nk=32, mask=(self.DM[:], self.bDM))]
        for kbk in range(31, -1, -1):
            steps.append(dict(kT=(kc[:, kbk * 128:(kbk + 1) * 128], bkc), v=(vc[:, kbk, :], bvc), nk=128, mask=None))
        self.attend(steps, self.QS[:, q, :], self.bQS, 32, 7, self.oS[:, q, 0:32], self.boS)

D = 2048; DFF = 2048; NIN = 3088; DMIX = 1024
TT = 2112
RPR = 2320
OPR = 1280
TILES = [(0, 512), (512, 512), (1024, 512), (1536, 576)]

def win_chunks():
    ch = []
    for c in range(6): ch.append((128 * c, 128, 'z', 128 * c))
    ch.append((768, 8, 'z', 768)); ch.append((776, 128, 'g', 0)); ch.append((904, 128, 'g', 128))
    for c in range(6): ch.append((1032 + 128 * c, 128, 'z', 776 + 128 * c))
    ch.append((1800, 8, 'z', 1544)); ch.append((1808, 128, 'g', 256)); ch.append((1936, 128, 'g', 384))
    ch.append((2064, 128, 'z', 1552)); ch.append((2192, 128, 'z', 1680))
    ch.append((2320, 128, 'q', 0)); ch.append((2448, 128, 'q', 128))
    ch.append((2576, 128, 'z', 1808)); ch.append((2704, 128, 'z', 1936)); ch.append((2832, 128, 'z', 2064)); ch.append((2960, 128, 'z', 2192))
    return ch
WCH = win_chunks()

class Tab:
    def __init__(self):
        self.cols = {}; self.n = 0
    def add(self, key):
        self.cols[key] = self.n; self.n += 1
def make_tab():
    t = Tab()
    zi = 0
    for k, (c0, w, kind, r0) in enumerate(WCH):
        if kind == 'z':
            for b in range(9): t.add(('az', k, b))
        if kind == 'q':
            for p in range(2):
                for b in range(5): t.add(('aq', k, p, b))
    for s in range(8):
        for nm in ['gq', 'gk', 'gv', 'gg', 'mq', 'mk', 'mv', 'mg', 's5', 'sk', 'sv', 'sq', 'og', 'om', 'os', 'ob']:
            t.add((nm, s))
    for cc in range(6):
        for b in range(9): t.add(('co', cc, b))
    for cc in range(2):
        for p in range(2):
            for b in range(5): t.add(('cs', cc, p, b))
    return t
TAB = make_tab()

def tab_values(core, fused=False):
    h, r = core // 2, core % 2
    cR = core * RPR if fused else 0; cQ = core * 512 if fused else 0; SR_ = RPR if fused else 484; SQ_ = 512 if fused else 64
    cO = core * OPR if fused else 0; OJ = OPR if fused else 160; cC = core * 160 if fused else 0
    T = np.zeros((128, TAB.n), np.int32)
    P = np.arange(128)
    for k, (c0, w, kind, r0) in enumerate(WCH):
        if kind == 'z':
            for b in range(9): T[:, TAB.cols[('az', k, b)]] = (cR + r0 + P) * 9 + b
        if kind == 'q':
            for p in range(2):
                for b in range(5): T[:, TAB.cols[('aq', k, p, b)]] = (cQ + p * 256 + r0 + P) * 5 + b
    for s in range(8):
        base = s * SR_
        if fused:
            oq, ok, ov, og = h * 64, 256 + h * 64, 512 + h * 64 + r * 32, 768 + h
            mq, mk, mv, mg = 776 + h * 64, 776 + 256 + h * 64, 776 + 512 + h * 64 + r * 32, 1544 + h
            o5, osk, osv = 1552 + 32 * core, 1808 + h * 64, 2064 + h * 64; gstep = 4
            sq0 = s * 512 + r * 256 + h * 64
        else:
            oq, ok, ov, og = 0, 64, 128, 160
            mq, mk, mv, mg = 162, 226, 290, 322
            o5, osk, osv = 324, 356, 420; gstep = 1
            sq0 = s * 64
        T[:, TAB.cols[('gq', s)]] = base + oq + P; T[:, TAB.cols[('gk', s)]] = base + ok + P
        T[:, TAB.cols[('gv', s)]] = base + ov + P; T[:, TAB.cols[('gg', s)]] = base + og + gstep * P
        T[:, TAB.cols[('mq', s)]] = base + mq + P; T[:, TAB.cols[('mk', s)]] = base + mk + P
        T[:, TAB.cols[('mv', s)]] = base + mv + P; T[:, TAB.cols[('mg', s)]] = base + mg + gstep * P
        T[:, TAB.cols[('s5', s)]] = base + o5 + P
        T[:, TAB.cols[('sk', s)]] = base + osk + P; T[:, TAB.cols[('sv', s)]] = base + osv + P
        T[:, TAB.cols[('sq', s)]] = sq0 + P
        ob = cO + s * 160
        T[:, TAB.cols[('og', s)]] = ob + P; T[:, TAB.cols[('om', s)]] = ob + 32 + P; T[:, TAB.cols[('os', s)]] = ob + 64 + P; T[:, TAB.cols[('ob', s)]] = ob + 96 + P
    for cc in range(6):
        moff = [0, 0, 32, 32, 64, 64][cc]; half = cc % 2
        j = (half * 128 + P) // 32; i = P % 32
        for b in range(9): T[:, TAB.cols[('co', cc, b)]] = (j * OJ + cC + moff + i) * 9 + b
    for cc in range(2):
        head = cc * 2 + P // 64; i = P % 64
        for p in range(2):
            for b in range(5): T[:, TAB.cols[('cs', cc, p, b)]] = ((2 * head + p) * OJ + cC + 96 + i) * 9 + b
    return np.clip(T, 0, None)

class Rot:
    def __init__(self, items): self.items = items; self.i = 0
    def next(self):
        x = self.items[self.i % len(self.items)]; self.i += 1; return x

def barrier(kb):
    engs = list(kb.eng.keys())
    for e in engs:
        for e2 in engs:
            if e2 != e: kb.wait(e, (kb.cur[e2][0], kb.cur[e2][1]))
        for slot in range(kb.NDMA):
            if kb.dcnt[slot] > 0: kb.wait(e, (kb.dsem[slot], 16 * kb.dcnt[slot]))

def collective(kb, nc, kind, src, bsrc, dst, bdst):
    kb.deps("pool", [bsrc], [bdst])
    c = kb.cur["pool"]
    if c[1] >= kb.EPOCH:
        kb._newsem("pool"); c = kb.cur["pool"]
    ins = nc.gpsimd.collective_compute(kind, ALU.add, replica_groups=[list(range(8))], ins=[src], outs=[dst])
    c[1] += 1; ins.then_inc(c[0], 1)
    kb.mark((c[0], c[1]), [bsrc], [bdst]); kb.ninst += 1

def token_phase(G, lc, la, final):
    nc = G.nc; kb = G.kb; Dm = G.D; ps = G.ps
    has_c = lc is not None; has_a = la is not None
    TMAX = 576
    with ExitStack() as st:
        G.uid += 1; uid = G.uid
        def sb(name, shape, dt=F32): return st.enter_context(nc.sbuf_tensor("t%d_" % uid + name, list(shape), dt)), Buf(name)
        xT, bxT = sb("xT", [128, 16, TMAX]); hT, bhT = sb("hT", [128, 16, TMAX], BF16); aT, baT = sb("aT", [128, 16, TMAX], BF16)
        wbufA = [sb("wA%d" % i, [128, 16, 256], BF16) for i in range(2)]; wbufB = [sb("wB%d" % i, [128, 16, 256], BF16) for i in range(2)]
        sq = [sb("sq%d" % i, [128, 512], BF16) for i in range(2)]
        rstd, brstd = sb("rstd", [128, TMAX])
        sg = [sb("sg%d" % i, [128, 512]) for i in range(2)]
        zst = [sb("zst%d" % i, [128, TMAX]) for i in range(2)]
        ones, bones = sb("ones", [128, 128], BF16); ones64, bones64 = sb("ones64", [128, 128], BF16)
        epsT, beps = sb("epsT", [128, 1]); vecs, bvecs = sb("vecs", [128, 5, 16])
        zblk, bzblk = sb("zblk", [128, 256]); sblk = [sb("sblk%d" % i, [128, 256]) for i in range(2)]
        if has_c:
            oTs, boT = sb("oTs", [128, 8, TMAX]); gTs, bgT = sb("gTs", [128, 4, TMAX]); mT, bmT = sb("mT", [128, 8, TMAX], BF16)
            tmpc = [sb("tmpc%d" % i, [128, 512]) for i in range(3)]; wglu, bwglu = sb("wglu", [128, 2, 256], BF16)
            ostg, bostg = sb("ostg", [128, 256])
        if has_a:
            kvs = [sb("kvs%d" % i, [128, 512]) for i in range(2)]
        V = nc.vector
        kb.op("pool", nc.gpsimd.memset, writes=[bones], ap=ones[:], constant=1.0)
        kb.op("pool", nc.gpsimd.memset, writes=[bones64], ap=ones64[:], constant=0.0)
        kb.op("pool", nc.gpsimd.memset, writes=[bones64], ap=ones64[0:64, 0:64], constant=1.0)
        kb.op("pool", nc.gpsimd.memset, writes=[bones64], ap=ones64[64:128, 64:128], constant=1.0)
        kb.op("pool", nc.gpsimd.memset, writes=[beps], ap=epsT[:], constant=1e-6)
        for t_, b_ in sblk: kb.op("pool", nc.gpsimd.memset, writes=[b_], ap=t_[:], constant=0.0)
        if has_a:
            kb.dma("sp", vecs[:, 0, :], Dm["tvec"][G.lw(la), :, 0, :], writes=[bvecs]); kb.dma("sp", vecs[:, 1, :], Dm["tvec"][G.lw(la), :, 1, :], writes=[bvecs])
        if has_c:
            kb.dma("sp", vecs[:, 2, :], Dm["tvecc"][G.lw(lc), :, 2, :], writes=[bvecs]); kb.dma("sp", vecs[:, 4, :], Dm["tvecc"][G.lw(lc), :, 3, :], writes=[bvecs])
            kb.dma("pool", wglu[:], Dm["wglu"][G.lw(lc)].rearrange("(c p) f -> p c f", p=128), writes=[bwglu])
            if final: kb.dma("sp", vecs[:, 3, :], Dm["nf"], writes=[bvecs])
        sqr = Rot(sq); sgr = Rot(sg); zr = Rot(zst)
        itab = G.itab; bitab = G.bitab
        tcol = lambda key: itab[:, TAB.cols[key]:TAB.cols[key] + 1]

        def sumsq_rstd(src, rds, nchunk, gw, lhs_ones, bl, pst, scale, dst, bdst):
            p, pb = pst
            for c in range(nchunk):
                s, bs = sqr.next()
                kb.op("act", nc.scalar.activation, reads=rds, writes=[bs], out=s[:, :gw], in_=src(c), func=AF.Square)
                kb.op("pe", nc.tensor.matmul, reads=[bs, bl], writes=[pb], out=p[:, :gw], lhsT=lhs_ones, rhs=s[:, :gw], start=(c == 0), stop=(c == nchunk - 1))
            kb.op("act", nc.scalar.activation, reads=[pb, beps], writes=[bdst], out=dst, in_=p[:, :gw], func=AF.Sqrt, scale=scale, bias=epsT[:, 0:1])
            kb.op("dve", V.reciprocal, reads=[bdst], writes=[bdst], out=dst, in_=dst)

        def norm_to(vi, grps, dst, bdst):
            for (g0, gw) in grps:
                sumsq_rstd(lambda c: xT[:, c, g0:g0 + gw], [bxT], 16, gw, ones[:], bones, ps[0], 1.0 / D, rstd[:, g0:g0 + gw], brstd)
                for c in range(16):
                    kb.op("dve", V.scalar_tensor_tensor, reads=[bxT, brstd, bvecs], writes=[bdst], out=dst[:, c, g0:g0 + gw], in0=xT[:, c, g0:g0 + gw],
                          scalar=vecs[:, vi, c:c + 1], in1=rstd[:, g0:g0 + gw], op0=ALU.mult, op1=ALU.mult)

        def load_w(wt, wb, W, c0, cw, nk):
            kb.dma("pool", wt[:, :nk, :cw], W[:, c0:c0 + cw].rearrange("(c p) f -> p c f", p=128), writes=[wb])

        def ffn(wg_d, wu_d, wd_d, vi, grps):
            norm_to(vi, grps, hT, bhT)
            pg = Rot([ps[1], ps[2]]); pu = Rot([ps[3], ps[4]]); pd = Rot([ps[5], ps[6]])
            for b in range(DFF // 256):
                (wgt, wgb), (wut, wub) = wbufA[b % 2], wbufB[b % 2]
                load_w(wgt, wgb, wg_d, b * 256, 256, 16); load_w(wut, wub, wu_d, b * 256, 256, 16)
                for ci in range(2):
                    f = b * 2 + ci
                    for (g0, gw) in grps:
                        (p1, b1), (p2, b2) = pg.next(), pu.next()
                        for k in range(16):
                            kb.op("pe", nc.tensor.matmul, reads=[wgb, bhT], writes=[b1], out=p1[:, :gw], lhsT=wgt[:, k, ci * 128:(ci + 1) * 128], rhs=hT[:, k, g0:g0 + gw], start=(k == 0), stop=(k == 15))
                        for k in range(16):
                            kb.op("pe", nc.tensor.matmul, reads=[wub, bhT], writes=[b2], out=p2[:, :gw], lhsT=wut[:, k, ci * 128:(ci + 1) * 128], rhs=hT[:, k, g0:g0 + gw], start=(k == 0), stop=(k == 15))
                        s, bs = sgr.next()
                        kb.op("act", nc.scalar.activation, reads=[b1], writes=[bs], out=s[:, :gw], in_=p1[:, :gw], func=AF.Silu)
                        kb.op("dve", V.tensor_tensor, reads=[bs, b2], writes=[baT], out=aT[:, f, g0:g0 + gw], in0=s[:, :gw], in1=p2[:, :gw], op=ALU.mult)
            for b in range(D // 256):
                wdt, wdb = wbufA[b % 2]
                load_w(wdt, wdb, wd_d, b * 256, 256, 16)
                for ci in range(2):
                    dch = b * 2 + ci
                    for (g0, gw) in grps:
                        p1, b1 = pd.next()
                        for k in range(16):
                            kb.op("pe", nc.tensor.matmul, reads=[wdb, baT], writes=[b1], out=p1[:, :gw], lhsT=wdt[:, k, ci * 128:(ci + 1) * 128], rhs=aT[:, k, g0:g0 + gw], start=(k == 0), stop=(k == 15))
                        kb.op("dve", V.scalar_tensor_tensor, reads=[b1, bxT], writes=[bxT], out=xT[:, dch, g0:g0 + gw], in0=p1[:, :gw], scalar=0.5, in1=xT[:, dch, g0:g0 + gw], op0=ALU.mult, op1=ALU.add)

        xsrc = G.xsrc
        for ti, (t0, T) in enumerate(TILES):
            grps = [(0, 512)] + ([(512, 64)] if T > 512 else [])
            kb.dma("sp", xT[:, :, :T], xsrc[:, t0:t0 + T].rearrange("(c p) t -> p c t", p=128), reads=[G.bXR], writes=[bxT])
            if has_c:
                kb.dma("sp", gTs[:, :, :T], G.GSr[:, t0:t0 + T].rearrange("(c p) t -> p c t", p=128), reads=[G.bGS], writes=[bgT])
                OR9 = G.OR.rearrange("r (b c) -> (r b) c", c=256)
                for cc in range(6):
                    for bi in range(2):
                        gather_rows(G, oTs[:, cc, bi * 256:(bi + 1) * 256], boT, OR9, tcol(('co', cc, 2 * ti + bi)), bitab, bzg=G.bOR)
                    if T > 512:
                        gather_rows(G, ostg[:, :], bostg, OR9, tcol(('co', cc, 8)), bitab, bzg=G.bOR)
                        kb.op("pool", nc.gpsimd.tensor_copy, reads=[bostg], writes=[boT], out=oTs[:, cc, 512:544], in_=ostg[:, 0:32])
                        kb.op("pool", nc.gpsimd.tensor_copy, reads=[bostg], writes=[boT], out=oTs[:, cc, 544:576], in_=ostg[:, 64:96])
                for cc in range(2):
                    for p in range(2):
                        gather_rows(G, ostg[:, :], bostg, OR9, tcol(('cs', cc, p, ti)), bitab, bzg=G.bOR)
                        dst = oTs[:, 6 + cc, 0:512].rearrange("q (a b c) -> q a b c", a=2, b=2)[:, :, p, :]
                        kb.op("pool", nc.gpsimd.tensor_copy, reads=[bostg], writes=[boT], out=dst, in_=ostg[:, :].rearrange("q (a c) -> q a c", a=2))
                    if T > 512:
                        gather_rows(G, ostg[:, :], bostg, OR9, tcol(('cs', cc, 0, 4)), bitab, bzg=G.bOR)
                        kb.op("pool", nc.gpsimd.tensor_copy, reads=[bostg], writes=[boT], out=oTs[:, 6 + cc, 512:544], in_=ostg[:, 0:32])
                        kb.op("pool", nc.gpsimd.tensor_copy, reads=[bostg], writes=[boT], out=oTs[:, 6 + cc, 544:576], in_=ostg[:, 64:96])
                cv = lambda j: vecs[:, 4, j:j + 1]
                for (g0, gw) in grps:
                    sl = slice(g0, g0 + gw)
                    for c in range(2):
                        sumsq_rstd(lambda cc_: oTs[:, c, sl], [boT], 1, gw, ones64[:], bones64, ps[0], 1.0 / 64, rstd[:, sl], brstd)
                        t1, bt1 = tmpc[0]; t2, bt2 = tmpc[1]
                        kb.op("act", nc.scalar.activation, reads=[bgT], writes=[bt1], out=t1[:, :gw], in_=gTs[:, c, sl], func=AF.Silu)
                        kb.op("dve", V.scalar_tensor_tensor, reads=[boT, brstd, bvecs], writes=[bt2], out=t2[:, :gw], in0=oTs[:, c, sl], scalar=cv(c), in1=rstd[:, sl], op0=ALU.mult, op1=ALU.mult)
                        kb.op("dve", V.tensor_tensor, reads=[bt1, bt2], writes=[bmT], out=mT[:, c, sl], in0=t1[:, :gw], in1=t2[:, :gw], op=ALU.mult)
                    for c in range(2):
                        sumsq_rstd(lambda cc_: oTs[:, 2 + c, sl], [boT], 1, gw, ones64[:], bones64, ps[0], 1.0 / 64, rstd[:, sl], brstd)
                        t1, bt1 = tmpc[0]; t2, bt2 = tmpc[1]
                        kb.op("act", nc.scalar.activation, reads=[bgT], writes=[bt1], out=t1[:, :gw], in_=gTs[:, 2 + c, sl], func=AF.Sigmoid)
                        kb.op("dve", V.scalar_tensor_tensor, reads=[boT, brstd, bvecs], writes=[bt2], out=t2[:, :gw], in0=oTs[:, 2 + c, sl], scalar=cv(2 + c), in1=rstd[:, sl], op0=ALU.mult, op1=ALU.mult)
                        kb.op("dve", V.tensor_tensor, reads=[bt1, bt2], writes=[bmT], out=mT[:, 2 + c, sl], in0=t1[:, :gw], in1=t2[:, :gw], op=ALU.mult)
                    ych = [sg[0], sg[1]]; ybf = [sq[0], sq[1]]
                    for c in range(2):
                        y, by = ych[c]; t1, bt1 = tmpc[0]; t2, bt2 = tmpc[1]
                        kb.op("act", nc.scalar.activation, reads=[boT], writes=[bt1], out=t1[:, :gw], in_=oTs[:, 4 + c, sl], func=AF.Square)
                        kb.op("dve", V.tensor_scalar, reads=[bt1], writes=[bt1], out=t1[:, :gw], in0=t1[:, :gw], scalar1=0.044715, scalar2=1.0, op0=ALU.mult, op1=ALU.add)
                        kb.op("dve", V.tensor_tensor, reads=[bt1, boT], writes=[bt2], out=t2[:, :gw], in0=t1[:, :gw], in1=oTs[:, 4 + c, sl], op=ALU.mult)
                        kb.op("act", nc.scalar.activation, reads=[bt2], writes=[bt1], out=t1[:, :gw], in_=t2[:, :gw], func=AF.Sigmoid, scale=1.5957691216)
                        kb.op("dve", V.tensor_tensor, reads=[bt1, boT], writes=[by], out=y[:, :gw], in0=t1[:, :gw], in1=oTs[:, 4 + c, sl], op=ALU.mult)
                        yq, byq = ybf[c]
                        kb.op("act", nc.scalar.copy, reads=[by], writes=[byq], out=yq[:, :gw], in_=y[:, :gw])
                    for c in range(2):
                        p1, b1 = ps[1 + c]
                        for k in range(2):
                            kb.op("pe", nc.tensor.matmul, reads=[bwglu, ybf[k][1]], writes=[b1], out=p1[:, :gw], lhsT=wglu[:, k, c * 128:(c + 1) * 128], rhs=ybf[k][0][:, :gw], start=(k == 0), stop=(k == 1))
                        t1, bt1 = tmpc[c]
                        kb.op("act", nc.scalar.activation, reads=[b1, bvecs], writes=[bt1], out=t1[:, :gw], in_=p1[:, :gw], func=AF.Sigmoid, bias=cv(4 + c))
                        kb.op("dve", V.tensor_tensor, reads=[bt1, ych[c][1]], writes=[ych[c][1]], out=ych[c][0][:, :gw], in0=t1[:, :gw], in1=ych[c][0][:, :gw], op=ALU.mult)
                    sumsq_rstd(lambda cc_: ych[cc_][0][:, :gw], [ych[0][1], ych[1][1]], 2, gw, ones[:], bones, ps[0], 1.0 / 256, rstd[:, sl], brstd)
                    for c in range(2):
                        kb.op("dve", V.scalar_tensor_tensor, reads=[ych[c][1], brstd, bvecs], writes=[bmT], out=mT[:, 4 + c, sl], in0=ych[c][0][:, :gw], scalar=cv(6 + c), in1=rstd[:, sl], op0=ALU.mult, op1=ALU.mult)
                    sumsq_rstd(lambda cc_: oTs[:, 6 + cc_, sl], [boT], 2, gw, ones[:], bones, ps[0], 1.0 / 256, rstd[:, sl], brstd)
                    for c in range(2):
                        kb.op("dve", V.scalar_tensor_tensor, reads=[boT, brstd, bvecs], writes=[bmT], out=mT[:, 6 + c, sl], in0=oTs[:, 6 + c, sl], scalar=cv(8 + c), in1=rstd[:, sl], op0=ALU.mult, op1=ALU.mult)
                pd = Rot([ps[5], ps[6]])
                for b in range(D // 256):
                    wt, wb = wbufA[b % 2]
                    load_w(wt, wb, Dm["wout"][G.lw(lc)], b * 256, 256, 8)
                    for ci in range(2):
                        dch = b * 2 + ci
                        for (g0, gw) in grps:
                            p1, b1 = pd.next()
                            for k in range(8):
                                kb.op("pe", nc.tensor.matmul, reads=[wb, bmT], writes=[b1], out=p1[:, :gw], lhsT=wt[:, k, ci * 128:(ci + 1) * 128], rhs=mT[:, k, g0:g0 + gw], start=(k == 0), stop=(k == 7))
                            kb.op("dve", V.tensor_tensor, reads=[b1, bxT], writes=[bxT], out=xT[:, dch, g0:g0 + gw], in0=p1[:, :gw], in1=xT[:, dch, g0:g0 + gw], op=ALU.add)
                ffn(Dm["wg2"][G.lw(lc)], Dm["wu2"][G.lw(lc)], Dm["wd2"][G.lw(lc)], 2, grps)
            if has_a:
                ffn(Dm["wg1"][G.lw(la)], Dm["wu1"][G.lw(la)], Dm["wd1"][G.lw(la)], 0, grps)
                norm_to(1, grps, hT, bhT)
                pd = Rot([ps[5], ps[6]])
                ZS9 = G.ZS.rearrange("r (b c) -> (r b) c", c=256); ZQ5 = G.ZQS.rearrange("r (b c) -> (r b) c", c=256)
                win = Dm["win"][G.lw(la)]
                for k, (c0, mw, kind, r0) in enumerate(WCH):
                    wt, wb = wbufA[k % 2]
                    load_w(wt, wb, win, c0, mw, 16)
                    z, bzs = zr.next()
                    for (g0, gw) in grps:
                        p1, b1 = pd.next()
                        for kk in range(16):
                            kb.op("pe", nc.tensor.matmul, reads=[wb, bhT], writes=[b1], out=p1[:mw, :gw], lhsT=wt[:, kk, 0:mw], rhs=hT[:, kk, g0:g0 + gw], start=(kk == 0), stop=(kk == 15))
                        kb.op("act", nc.scalar.copy, reads=[b1], writes=[bzs], out=z[:mw, g0:g0 + gw], in_=p1[:mw, :gw])
                    if kind == 'g':
                        kb.dma("sp", G.GSw[r0:r0 + mw, t0:t0 + T], z[:mw, :T], reads=[bzs], writes=[G.bGS])
                    elif kind == 'z':
                        for bi in range(2):
                            scatter_rows(G, z[:mw, bi * 256:(bi + 1) * 256], bzs, ZS9, tcol(('az', k, 2 * ti + bi))[:mw], bitab, G.bZS)
                        if T > 512:
                            sbk, bsbk = sblk[k % 2]
                            kb.op("pool", nc.gpsimd.tensor_copy, reads=[bzs], writes=[bsbk], out=sbk[:mw, 0:32], in_=z[:mw, 512:544])
                            kb.op("pool", nc.gpsimd.tensor_copy, reads=[bzs], writes=[bsbk], out=sbk[:mw, 64:96], in_=z[:mw, 544:576])
                            scatter_rows(G, sbk[:mw, :], bsbk, ZS9, tcol(('az', k, 8))[:mw], bitab, G.bZS)
                        if c0 < 768 and T > 512:
                            for i3, cc0 in enumerate([509, 541, 573]):
                                kb.dma("sp", G.Dout["convT"][G.lo(la), c0:c0 + mw, 3 * i3:3 * i3 + 3], z[:mw, cc0:cc0 + 3], reads=[bzs], writes=[G.bout])
                    else:
                        for p in range(2):
                            kb.op("pool", nc.gpsimd.tensor_copy, reads=[bzs], writes=[bzblk], out=zblk[:, :].rearrange("q (a c) -> q a c", a=2),
                                  in_=z[:, 0:512].rearrange("q (a b c) -> q a b c", a=2, b=2)[:, :, p, :])
                            scatter_rows(G, zblk[:, :], bzblk, ZQ5, tcol(('aq', k, p, ti)), bitab, G.bZQS)
                        if T > 512:
                            sbk, bsbk = sblk[k % 2]
                            kb.op("pool", nc.gpsimd.tensor_copy, reads=[bzs], writes=[bsbk], out=sbk[:, 0:32], in_=z[:, 512:544])
                            kb.op("pool", nc.gpsimd.tensor_copy, reads=[bzs], writes=[bsbk], out=sbk[:, 64:96], in_=z[:, 544:576])
                            for p in range(2):
                                scatter_rows(G, sbk[:, :], bsbk, ZQ5, tcol(('aq', k, p, 4)), bitab, G.bZQS)
                (wkt, wkb), (wvt, wvb) = wbufB[0], wbufB[1]
                load_w(wkt, wkb, win, 2576, 256, 16); load_w(wvt, wvb, win, 2832, 256, 16)
                for b0 in range(0, T, 128):
                    bw = min(128, T - b0)
                    p1, b1 = pd.next()
                    for kk in range(16):
                        kb.op("pe", nc.tensor.matmul, reads=[wkb, bhT], writes=[b1], out=p1[:bw, 0:256], lhsT=hT[:, kk, b0:b0 + bw], rhs=wkt[:, kk, :], start=(kk == 0), stop=(kk == 15))
                    for kk in range(16):
                        kb.op("pe", nc.tensor.matmul, reads=[wvb, bhT], writes=[b1], out=p1[:bw, 256:512], lhsT=hT[:, kk, b0:b0 + bw], rhs=wvt[:, kk, :], start=(kk == 0), stop=(kk == 15))
                    kv, bkv = kvs[(b0 // 128) % 2]
                    kb.op("act", nc.scalar.copy, reads=[b1], writes=[bkv], out=kv[:bw, :], in_=p1[:bw, :])
                    kb.dma("sp", G.Dout["kvout"][G.lo(la), t0 + b0:t0 + b0 + bw, :], kv[:bw, :], reads=[bkv], writes=[G.bout])
            if final:
                for (g0, gw) in grps:
                    sumsq_rstd(lambda c: xT[:, c, g0:g0 + gw], [bxT], 16, gw, ones[:], bones, ps[0], 1.0 / D, rstd[:, g0:g0 + gw], brstd)
                    for c in range(16):
                        kb.op("dve", V.scalar_tensor_tensor, reads=[bxT, brstd, bvecs], writes=[bxT], out=xT[:, c, g0:g0 + gw], in0=xT[:, c, g0:g0 + gw],
                              scalar=vecs[:, 3, c:c + 1], in1=rstd[:, g0:g0 + gw], op0=ALU.mult, op1=ALU.mult)
                kb.dma("sp", G.Dout["yT"][:, t0:t0 + T].rearrange("(c p) t -> p c t", p=128), xT[:, :, :T], reads=[bxT], writes=[G.bout])
            else:
                kb.dma("sp", G.xdst[:, t0:t0 + T].rearrange("(c p) t -> p c t", p=128), xT[:, :, :T], reads=[bxT], writes=[G.bout])
        barrier(kb)
    G.first_phase = False

class GCtx:
    pass

def b_phase(G, l):
    nc = G.nc; kb = G.kb; Dm = G.D; Do = G.Dout
    tc = lambda nm, s: TAB.cols[(nm, s)]
    def new_ctx(st):
        c = Ctx(); c.nc = nc; c.st = st; c.kb = kb; c.ps = G.ps
        c.sb = lambda name, shape, dt=F32: (st.enter_context(nc.sbuf_tensor("b%d_" % G.uid + name, list(shape), dt)), Buf(name))
        G.uid += 1
        load_consts(c, Dm["cm"], Dm["rows"])
        return c
    with ExitStack() as st:
        c = new_ctx(st)
        g = GDN(c, {"convw": Dm["g_convw"][G.lw(l)], "alog": Dm["g_alog"][G.lw(l)], "dtb": Dm["g_dtb"][G.lw(l)], "convst": Dm["g_convst"][G.lw(l)], "s0": Dm["g_s0"][G.lw(l)]})
        for s in range(8):
            g.segment(s, G.ZR, G.bZR[s], G.itab, G.bitab, {"q": tc('gq', s), "k": tc('gk', s), "v": tc('gv', s), "g": tc('gg', s)})
            scatter_rows(c, g.la.oT[:32, :], g.la.boT, G.OS, G.itab[:32, tc('og', s):tc('og', s) + 1], G.bitab, G.bOS)
        kb.dma("sp", Do["gdnS"][G.lo(l)], g.la.S[:], reads=[g.la.bS], writes=[G.bout])
        barrier(kb)
    with ExitStack() as st:
        c = new_ctx(st)
        g = MLSTM(c, {"bi": Dm["m_bi"][G.lw(l)], "bf": Dm["m_bf"][G.lw(l)], "s0": Dm["m_s0"][G.lw(l)], "m0": Dm["m_m0"][G.lw(l)]})
        for s in range(8):
            g.segment(s, G.ZR, G.bZR[s], G.itab, G.bitab, {"q": tc('mq', s), "k": tc('mk', s), "v": tc('mv', s), "g": tc('mg', s)})
            scatter_rows(c, g.hT[:32, :], g.bhT, G.OS, G.itab[:32, tc('om', s):tc('om', s) + 1], G.bitab, G.bOS)
        g.finish()
        kb.dma("sp", Do["mlS"][G.lo(l)], g.Sout[:], reads=[g.bSout], writes=[G.bout])
        kb.dma("sp", Do["mlM"][G.lo(l)], g.mfin[:], reads=[g.bmfin], writes=[G.bout])
        barrier(kb)
    with ExitStack() as st:
        c = new_ctx(st)
        g = S5(c, {"vec": Dm["s_vec"][G.lw(l)], "BreT": Dm["s_BreT"][G.lw(l)], "BimT": Dm["s_BimT"][G.lw(l)], "CreT": Dm["s_CreT"][G.lw(l)], "CimT": Dm["s_CimT"][G.lw(l)], "dvec": Dm["s_dvec"][G.lw(l)], "x0": Dm["s_x0"][G.lw(l)]})
        for s in range(8):
            g.segment(s, G.ZR, G.bZR[s], G.itab, G.bitab, tc('s5', s))
            scatter_rows(c, g.yT[:32, :], g.byT, G.OS, G.itab[:32, tc('os', s):tc('os', s) + 1], G.bitab, G.bOS)
        kb.dma("sp", Do["s5X"][G.lo(l)], g.X[:], reads=[g.bX], writes=[G.bout])
        barrier(kb)
    with ExitStack() as st:
        c = new_ctx(st)
        g = SB(c, {"mb": Dm["b_mb"], "dmask": Dm["b_dmask"], "identb": Dm["b_identb"], "trin": Dm["b_trin"], "kc": Dm["b_kc"][G.lw(l)], "vc": Dm["b_vc"][G.lw(l)]}, 8)
        for s in range(8):
            g.load_segment(s, G.ZR, G.bZR[s], G.ZQR, G.bZQR, G.itab, G.bitab, {"k": tc('sk', s), "v": tc('sv', s), "q": tc('sq', s)})
            g.prompt_tile(2 * s); g.prompt_tile(2 * s + 1)
            g.sample_seq(2 * s); g.sample_seq(2 * s + 1)
            kb.op("pool", nc.gpsimd.tensor_copy, reads=[g.boS], writes=[g.boT], out=g.oT[:, 1024:1056], in_=g.oS[:, 2 * s, 0:32])
            kb.op("pool", nc.gpsimd.tensor_copy, reads=[g.boS], writes=[g.boT], out=g.oT[:, 1088:1120], in_=g.oS[:, 2 * s + 1, 0:32])
            scatter_rows(c, g.oT[:64, :], g.boT, G.OS, G.itab[:64, tc('ob', s):tc('ob', s) + 1], G.bitab, G.bOS)
        barrier(kb)


WSPEC = {"tvec": [128, 4, 16], "wg1": [D, DFF], "wu1": [D, DFF], "wd1": [DFF, D], "win": [D, NIN], "wout": [DMIX, D], "wg2": [D, DFF], "wu2": [D, DFF], "wd2": [DFF, D], "wglu": [256, 256]}
A_W = ["wg1", "wu1", "wd1", "win"]; C_W = ["wout", "wg2", "wu2", "wd2", "wglu"]
BSPEC = {"g_convw": [160, 4], "g_alog": [1, 1], "g_dtb": [1, 1], "g_convst": [160, 16, 3], "g_s0": [64, 16, 32], "m_bi": [1, 1], "m_bf": [1, 1], "m_s0": [64, 16, 33], "m_m0": [1, 16],
         "s_vec": [128, 3], "s_BreT": [32, 128], "s_BimT": [32, 128], "s_CreT": [128, 32], "s_CimT": [128, 32], "s_dvec": [32, 1], "s_x0": [128, 2, 16], "b_kc": [16, 64, 4096], "b_vc": [16, 4096, 64]}
BSHARED = {"cm": [64, 6, 64], "rows": [1, 2, NCOL], "b_mb": [128, 8, 512], "b_dmask": [32, 32], "b_identb": [128, 128], "b_trin": [128, 128]}

def build_launch(kind):
    nc = bass.Bass("TRN2", target_bir_lowering=False)
    G = GCtx(); G.nc = nc; G.uid = 0
    G.lw = lambda l: 0; G.lo = lambda l: 0
    def din(name, shape, dt=F32): return nc.dram_tensor(name, list(shape), dt, kind="ExternalInput").ap()
    def dout(name, shape, dt=F32): return nc.dram_tensor(name, list(shape), dt, kind="ExternalOutput").ap()
    Dm = {}; G.D = Dm; G.Dout = {}
    itab_d = din("itab", [128, TAB.n], I32)
    G.bXR = Buf("XR"); G.bGS = Buf("GS"); G.bZS = Buf("ZS"); G.bZQS = Buf("ZQS"); G.bOS = Buf("OS"); G.bOR = Buf("OR"); G.bout = Buf("out"); G.bZQR = Buf("ZQR")
    zero_list = []
    if kind in ("tpa", "tpca", "tpcf"):
        has_c = kind != "tpa"; has_a = kind != "tpcf"
        G.xsrc = din("xin", [D, TT])
        if has_a:
            Dm["tvec"] = din("tvec", [1, 128, 4, 16])
            for nm in A_W: Dm[nm] = din(nm, [1] + WSPEC[nm])
            G.GSw = dout("GSo", [512, TT]); G.ZS = dout("ZSo", [RPR, SEGW]); G.ZQS = dout("ZQSo", [512, QW])
            G.Dout["kvout"] = dout("kvout", [1, TT, 512]); G.Dout["convT"] = dout("convT", [1, 768, 9])
            zero_list += [(G.ZS, G.bZS, RPR, SEGW), (G.ZQS, G.bZQS, 512, QW)]
        if has_c:
            for nm in C_W: Dm[nm] = din(nm + "c", [1] + WSPEC[nm])
            Dm["tvecc"] = din("tvecc", [1, 128, 4, 16])
            G.GSr = din("GSi", [512, TT]); G.OR = din("ORi", [8 * 160, SEGW])
        if kind == "tpcf":
            Dm["nf"] = din("nf", [128, 16]); G.Dout["yT"] = dout("yT", [D, TT]); G.xdst = None
        else:
            G.xdst = dout("xout", [D, TT])
    else:
        for nm, shp in BSPEC.items(): Dm[nm] = din(nm, [1] + shp)
        for nm, shp in BSHARED.items(): Dm[nm] = din(nm, shp)
        G.ZR = din("ZRi", [8 * 484, SEGW]); G.ZQR = din("ZQRi", [8 * 64, QW]); G.bZR = [Buf("ZR")] * 8
        G.OS = dout("OSo", [8 * 160, SEGW])
        G.Dout.update({"gdnS": dout("gdnS", [1, 64, 17, 32]), "mlS": dout("mlS", [1, 64, 17, 33]), "mlM": dout("mlM", [1, 1, 17]), "s5X": dout("s5X", [1, 128, 2, 17])})
        zero_list += [(G.OS, G.bOS, 8 * 160, SEGW)]
    with ExitStack() as st:
        kb = KB(nc, st); G.kb = kb
        G.ps = [(st.enter_context(nc.psum_tensor("ps%d" % i, [128, 512], F32)), Buf("ps%d" % i)) for i in range(8)]
        G.itab = st.enter_context(nc.sbuf_tensor("itab_s", [128, TAB.n], I32)); G.bitab = Buf("itab")
        kb.dma("sp", G.itab[:], itab_d, writes=[G.bitab])
        with ExitStack() as st2:
            zt = st2.enter_context(nc.sbuf_tensor("zerot", [128, SEGW], F32)); bzt = Buf("zt")
            kb.op("pool", nc.gpsimd.memset, writes=[bzt], ap=zt[:], constant=0.0)
            for (buf, bb, rows, w) in zero_list:
                for r0 in range(0, rows, 128):
                    rr = min(128, rows - r0)
                    kb.dma("act", buf[r0:r0 + rr, :], zt[:rr, :w], reads=[bzt], writes=[bb])
            barrier(kb)
        if kind == "b":
            b_phase(G, 0)
        else:
            token_phase(G, 0 if kind != "tpa" else None, 0 if kind != "tpcf" else None, kind == "tpcf")
        kb.finish([G.bout, G.bZS, G.bZQS, G.bOS, G.bGS])
        barrier(kb)
        print(kind, "instructions", kb.ninst, "sems", kb.nsem)
    return nc

def vecT(v):
    return np.ascontiguousarray(v.reshape(-1, 128).T)

def host_inputs(inp, depth):
    L = depth
    f = lambda a: np.ascontiguousarray(np.asarray(a, dtype=np.float32))
    shared = {}
    tvec = np.zeros((L, 128, 4, 16), np.float32)
    for l in range(L):
        tvec[l, :, 0, :] = vecT(inp["ffn1_norm"][l]); tvec[l, :, 1, :] = vecT(inp["mix_norm"][l]); tvec[l, :, 2, :] = vecT(inp["ffn2_norm"][l])
        cv = np.zeros((128, 16), np.float32)
        g64 = np.tile(inp["gdn_norm"][l], 2)
        cv[:, 0] = g64; cv[:, 1] = g64
        cv[:, 2:4] = vecT(inp["ml_norm"][l]); cv[:, 4:6] = vecT(inp["s5_b_glu"][l]); cv[:, 6:8] = vecT(inp["s5_norm"][l]); cv[:, 8:10] = vecT(inp["sb_norm"][l])
        tvec[l, :, 3, :] = cv
    shared["tvec"] = tvec; shared["nf"] = vecT(inp["final_norm"])
    for nm, src in [("wg1", "ffn1_w_gate"), ("wu1", "ffn1_w_up"), ("wd1", "ffn1_w_down"), ("win", "w_in"), ("wout", "w_out"), ("wg2", "ffn2_w_gate"), ("wu2", "ffn2_w_up"),
                    ("wd2", "ffn2_w_down"), ("wglu", "s5_w_glu")]:
        shared[nm] = f(inp[src][:L])
    cm = np.zeros((64, 6, 64), np.float32)
    jj, ii = np.meshgrid(np.arange(64), np.arange(64), indexing="ij")
    cm[:, 0] = np.eye(64); cm[:, 1] = -1.0 * (ii > jj); cm[:, 2] = -1.0 * (ii < jj); cm[:, 3] = (ii >= jj); cm[:, 4] = 1.0
    rows = np.ones((1, 2, NCOL), np.float32); rows[0, 0, ::64] = 0.0; rows[0, 1, 2080:2112] = 0.0; rows[0, 1, 2144:2176] = 0.0
    shared["cm"] = cm; shared["rows"] = rows
    jj, ii = np.meshgrid(np.arange(32), np.arange(32), indexing="ij")
    shared["b_dmask"] = np.where(jj < ii, 0.0, NEG).astype(np.float32)
    J, Sx = np.meshgrid(np.arange(128), np.arange(128), indexing="ij")
    shared["b_trin"] = (-1.0 * (J >= Sx)).astype(np.float32); shared["b_identb"] = np.eye(128, dtype=np.float32)
    xp = inp["x_prompt"][0]; xs = inp["x_sample"]
    maps = []
    for core in range(8):
        h, r = core // 2, core % 2
        m = dict(shared)
        xt = np.concatenate([xp[2048 * core:2048 * (core + 1)], xs[2 * core], xs[2 * core + 1]], 0)
        m["xT0"] = np.ascontiguousarray(xt.T)
        m["itab"] = tab_values(core)
        cch = list(range(h * 64, h * 64 + 64)) + list(range(256 + h * 64, 256 + h * 64 + 64)) + list(range(512 + h * 64 + r * 32, 512 + h * 64 + r * 32 + 32))
        m["g_convw"] = f(inp["gdn_conv_w"][:L][:, :, cch].transpose(0, 2, 1))
        m["g_alog"] = f(inp["gdn_a_log"][:L, h].reshape(L, 1, 1)); m["g_dtb"] = f(inp["gdn_dt_bias"][:L, h].reshape(L, 1, 1))
        m["g_convst"] = f(inp["state_gdn_conv"][:L][:, :, :, cch].transpose(0, 3, 1, 2))
        m["g_s0"] = f(inp["state_gdn_s"][:L, :, h, :, r * 32:(r + 1) * 32].transpose(0, 2, 1, 3))
        m["m_bi"] = f(inp["ml_b_i"][:L, h].reshape(L, 1, 1)); m["m_bf"] = f(inp["ml_b_f"][:L, h].reshape(L, 1, 1))
        c0 = inp["state_mlstm_c"][:L, :, h, :, r * 32:(r + 1) * 32]; n0 = inp["state_mlstm_n"][:L, :, h, :]
        m["m_s0"] = f(np.concatenate([c0, n0[..., None]], -1).transpose(0, 2, 1, 3))
        m["m_m0"] = f(inp["state_mlstm_m"][:L, :, h].reshape(L, 1, 16))
        gs = [2 * core, 2 * core + 1]
        vec = np.zeros((L, 128, 3), np.float32); BreT = np.zeros((L, 32, 128), np.float32); BimT = np.zeros((L, 32, 128), np.float32)
        CreT = np.zeros((L, 128, 32), np.float32); CimT = np.zeros((L, 128, 32), np.float32)
        for gi, gg in enumerate(gs):
            sl = slice(gi * 64, gi * 64 + 64); cl = slice(gi * 16, gi * 16 + 16)
            vec[:, sl, 0] = inp["s5_a_re"][:L, gg]; vec[:, sl, 1] = inp["s5_a_im"][:L, gg]; vec[:, sl, 2] = inp["s5_log_step"][:L, gg][:, None]
            BreT[:, cl, sl] = inp["s5_b_re"][:L, gg].transpose(0, 2, 1); BimT[:, cl, sl] = inp["s5_b_im"][:L, gg].transpose(0, 2, 1)
            CreT[:, sl, cl] = inp["s5_c_re"][:L, gg].transpose(0, 2, 1); CimT[:, sl, cl] = inp["s5_c_im"][:L, gg].transpose(0, 2, 1)
        m["s_vec"] = vec; m["s_BreT"] = BreT; m["s_BimT"] = BimT; m["s_CreT"] = CreT; m["s_CimT"] = CimT
        m["s_dvec"] = f(inp["s5_d"][:L, 32 * core:32 * core + 32].reshape(L, 32, 1))
        m["s_x0"] = f(np.stack([inp["state_s5_re"][:L][:, :, gs, :].reshape(L, 16, 128).transpose(0, 2, 1), inp["state_s5_im"][:L][:, :, gs, :].reshape(L, 16, 128).transpose(0, 2, 1)], 2))
        p = np.arange(128)[:, None, None, None]; j = np.arange(8)[None, :, None, None]; g4 = np.arange(4)[None, None, :, None]; cc = np.arange(128)[None, None, None, :]
        m["b_mb"] = np.where((128 * j + p) < (128 * (r + 2 * g4) + cc), 0.0, NEG).astype(np.float32).reshape(128, 8, 512)
        m["b_kc"] = f(inp["cache_sb_k"][:L, :, :, h, :].transpose(0, 1, 3, 2)); m["b_vc"] = f(inp["cache_sb_v"][:L, :, :, h, :])
        maps.append(m)
    return maps

def assemble(results, depth):
    L = depth
    yp = np.zeros((1, 16384, D), np.float32); ys = np.zeros((16, 32, D), np.float32)
    pk = np.zeros((L, 1, 16384, 4, 64), np.float32); pv = np.zeros_like(pk); sk = np.zeros((L, 16, 32, 4, 64), np.float32); sv = np.zeros_like(sk)
    pgs = np.zeros((L, 1, 4, 64, 64), np.float32); sgs = np.zeros((L, 16, 4, 64, 64), np.float32)
    pgc = np.zeros((L, 1, 3, 768), np.float32); sgc = np.zeros((L, 16, 3, 768), np.float32)
    pmc = np.zeros((L, 1, 4, 64, 64), np.float32); smc = np.zeros((L, 16, 4, 64, 64), np.float32)
    pmn = np.zeros((L, 1, 4, 64), np.float32); smn = np.zeros((L, 16, 4, 64), np.float32); pmm = np.zeros((L, 1, 4), np.float32); smm = np.zeros((L, 16, 4), np.float32)
    pxr = np.zeros((L, 1, 16, 64), np.float32); pxi = np.zeros_like(pxr); sxr = np.zeros((L, 16, 16, 64), np.float32); sxi = np.zeros_like(sxr)
    for core in range(8):
        R_ = results[core]; h, r = core // 2, core % 2
        yT = R_["yT"]
        yp[0, 2048 * core:2048 * (core + 1)] = yT[:, :2048].T; ys[2 * core] = yT[:, 2048:2080].T; ys[2 * core + 1] = yT[:, 2080:2112].T
        kv = R_["kvout"]
        pk[:, 0, 2048 * core:2048 * (core + 1)] = kv[:, :2048, 0:256].reshape(L, 2048, 4, 64); pv[:, 0, 2048 * core:2048 * (core + 1)] = kv[:, :2048, 256:512].reshape(L, 2048, 4, 64)
        for q in range(2):
            sk[:, 2 * core + q] = kv[:, 2048 + 32 * q:2080 + 32 * q, 0:256].reshape(L, 32, 4, 64); sv[:, 2 * core + q] = kv[:, 2048 + 32 * q:2080 + 32 * q, 256:512].reshape(L, 32, 4, 64)
        cT = R_["convT"]
        if core == 7: pgc[:, 0] = cT[:, :, 0:3].transpose(0, 2, 1)
        sgc[:, 2 * core] = cT[:, :, 3:6].transpose(0, 2, 1); sgc[:, 2 * core + 1] = cT[:, :, 6:9].transpose(0, 2, 1)
        gS = R_["gdnS"]
        pgs[:, 0, h, :, r * 32:(r + 1) * 32] = gS[:, :, 0, :]; sgs[:, :, h, :, r * 32:(r + 1) * 32] = gS[:, :, 1:17, :].transpose(0, 2, 1, 3)
        mS = R_["mlS"]; mM = R_["mlM"]
        pmc[:, 0, h, :, r * 32:(r + 1) * 32] = mS[:, :, 0, :32]; smc[:, :, h, :, r * 32:(r + 1) * 32] = mS[:, :, 1:17, :32].transpose(0, 2, 1, 3)
        if r == 0:
            pmn[:, 0, h] = mS[:, :, 0, 32]; smn[:, :, h] = mS[:, :, 1:17, 32].transpose(0, 2, 1); pmm[:, 0, h] = mM[:, 0, 0]; smm[:, :, h] = mM[:, 0, 1:17]
        X = R_["s5X"]
        gs = [2 * core, 2 * core + 1]
        pxr[:, 0, gs] = X[:, :, 0, 0].reshape(L, 2, 64); pxi[:, 0, gs] = X[:, :, 1, 0].reshape(L, 2, 64)
        sxr[:, :, gs] = X[:, :, 0, 1:17].transpose(0, 2, 1).reshape(L, 16, 2, 64); sxi[:, :, gs] = X[:, :, 1, 1:17].transpose(0, 2, 1).reshape(L, 16, 2, 64)
    return (yp, ys, pk, pv, pgs, pgc, pmc, pmn, pmm, pxr, pxi, sk, sv, sgs, sgc, smc, smn, smm, sxr, sxi)


def e1_rows(core):
    h, r = core // 2, core % 2
    rows = list(range(h * 64, h * 64 + 64)) + list(range(256 + h * 64, 256 + h * 64 + 64)) + list(range(512 + h * 64 + r * 32, 512 + h * 64 + r * 32 + 32)) + [768 + h, 772 + h]
    rows += [776 + x for x in list(range(h * 64, h * 64 + 64)) + list(range(256 + h * 64, 256 + h * 64 + 64)) + list(range(512 + h * 64 + r * 32, 512 + h * 64 + r * 32 + 32))] + [1544 + h, 1548 + h]
    rows += list(range(1552 + 32 * core, 1552 + 32 * core + 32))
    rows += list(range(1808 + h * 64, 1808 + h * 64 + 64)) + list(range(2064 + h * 64, 2064 + h * 64 + 64))
    assert len(rows) == 484
    return np.array(rows)

_PROGS = {}
def prog(kind):
    if kind not in _PROGS: _PROGS[kind] = build_launch(kind)
    return _PROGS[kind]

def run_multi(inp):
    L = 4
    hm = host_inputs(inp, L)
    tabs = [tab_values(c, fused=False) for c in range(8)]
    cores = list(range(8))
    def launch(kind, maps):
        return run_bass_kernel_spmd(prog(kind), maps, core_ids=cores).results
    sl = lambda a, l: np.ascontiguousarray(a[l:l + 1])
    def a_inputs(c, l):
        d = {"tvec": sl(hm[c]["tvec"], l)}
        for nm in A_W: d[nm] = sl(hm[c][nm], l)
        return d
    def c_inputs(c, l):
        d = {"tvecc": sl(hm[c]["tvec"], l)}
        for nm in C_W: d[nm + "c"] = sl(hm[c][nm], l)
        return d
    res = launch("tpa", [dict(itab=tabs[c], xin=hm[c]["xT0"], **a_inputs(c, 0)) for c in cores])
    kv = [[None] * L for _ in cores]; cv = [[None] * L for _ in cores]; st = [[None] * L for _ in cores]
    final = None
    for l in range(L):
        for c in cores: kv[c][l] = res[c]["kvout"][0]; cv[c][l] = res[c]["convT"][0]
        xcur = [res[c]["xout"] for c in cores]; gs = [res[c]["GSo"] for c in cores]
        Z = [res[c]["ZSo"] for c in cores]; ZQ = [res[c]["ZQSo"] for c in cores]
        bmaps = []
        for c in cores:
            h, r = c // 2, c % 2
            rows = e1_rows(c)
            d = {"itab": tabs[c], "ZRi": np.concatenate([Z[s][rows] for s in range(8)], 0), "ZQRi": np.concatenate([ZQ[s][r * 256 + h * 64:r * 256 + h * 64 + 64] for s in range(8)], 0)}
            for nm in BSPEC: d[nm] = sl(hm[c][nm], l)
            for nm in BSHARED: d[nm] = hm[c][nm]
            bmaps.append(d)
        bres = launch("b", bmaps)
        for c in cores: st[c][l] = {k: bres[c][k][0] for k in ["gdnS", "mlS", "mlM", "s5X"]}
        OS = [bres[c]["OSo"] for c in cores]
        tmaps = []
        for c in cores:
            d = {"itab": tabs[c], "xin": xcur[c], "GSi": gs[c], "ORi": np.concatenate([OS[j][c * 160:(c + 1) * 160] for j in range(8)], 0)}
            d.update(c_inputs(c, l))
            if l < L - 1: d.update(a_inputs(c, l + 1))
            else: d["nf"] = hm[c]["nf"]
            tmaps.append(d)
        res = launch("tpca" if l < L - 1 else "tpcf", tmaps)
    results = []
    for c in cores:
        results.append({"yT": res[c]["yT"], "kvout": np.stack(kv[c]), "convT": np.stack(cv[c]), "gdnS": np.stack([st[c][l]["gdnS"] for l in range(L)]),
                        "mlS": np.stack([st[c][l]["mlS"] for l in range(L)]), "mlM": np.stack([st[c][l]["mlM"] for l in range(L)]), "s5X": np.stack([st[c][l]["s5X"] for l in range(L)])})
    return assemble(results, L)

def kernel(**inputs):
    inp = {k: np.asarray(v) for k, v in inputs.items()}
    return run_multi(inp)
```

```python
import numpy as np
from contextlib import ExitStack
import concourse.bass as bass
import concourse.mybir as mybir
from concourse.bass_utils import run_bass_kernel_spmd

F32 = mybir.dt.float32; BF16 = mybir.dt.bfloat16; I32 = mybir.dt.int32
AF = mybir.ActivationFunctionType
ALU = mybir.AluOpType
AX = mybir.AxisListType

class Buf:
    __slots__ = ("name", "w", "r")
    def __init__(self, name):
        self.name = name; self.w = None; self.r = []

class KB:
    EPOCH = 30000
    NDMA = 24
    def __init__(self, nc, stack):
        self.nc = nc; self.stack = stack
        self.eng = {"pe": nc.tensor, "act": nc.scalar, "dve": nc.vector, "pool": nc.gpsimd, "sp": nc.sync}
        self.cur = {}
        self.nsem = 0
        for e in self.eng: self._newsem(e)
        self.seen = {}
        self.dsem = [self._sem("d%d" % i) for i in range(self.NDMA)]
        self.dcnt = [0] * self.NDMA
        self.drr = 0
        self.ninst = 0
    def _sem(self, name):
        self.nsem += 1
        return self.stack.enter_context(self.nc.semaphore("%s_%d" % (name, self.nsem)))
    def _newsem(self, e):
        self.cur[e] = [self._sem("c" + e), 0]
    def wait(self, e, tok):
        if tok is None: return
        sem, val = tok
        k = (e, id(sem))
        if self.seen.get(k, 0) >= val: return
        self.seen[k] = val
        self.eng[e].wait_ge(sem, val)
    def deps(self, e, reads, writes):
        for b in reads:
            self.wait(e, b.w)
        for b in writes:
            self.wait(e, b.w)
            for t in b.r: self.wait(e, t)
    def mark(self, tok, reads, writes):
        for b in reads: b.r.append(tok)
        for b in writes:
            b.w = tok; b.r = []
    def op(self, e, fn, reads=(), writes=(), **kw):
        self.deps(e, reads, writes)
        c = self.cur[e]
        if c[1] >= self.EPOCH:
            self._newsem(e); c = self.cur[e]
        ins = fn(**kw)
        c[1] += 1
        ins.then_inc(c[0], 1)
        tok = (c[0], c[1])
        import os
        self.seen[(e, id(c[0]))] = c[1] if (e == "pe" or os.environ.get("NOSELF")) else self.seen.get((e, id(c[0])), 0)
        self.mark(tok, reads, writes)
        self.ninst += 1
        return tok
    def dma(self, q, out, in_, reads=(), writes=(), **kw):
        slot = self.drr % self.NDMA; self.drr += 1
        sem = self.dsem[slot]
        if self.dcnt[slot] > 0:
            self.wait(q, (sem, 16 * self.dcnt[slot]))
        self.deps(q, reads, writes)
        self.eng[q].dma_start(out=out, in_=in_, **kw).then_inc(sem, 16)
        self.dcnt[slot] += 1
        tok = (sem, 16 * self.dcnt[slot])
        self.mark(tok, reads, writes)
        self.ninst += 1
        return tok
    def finish(self, bufs):
        for b in bufs:
            self.wait("sp", b.w)


NCOL = 2176
REG = [(0, 2048, 0), (2048, 32, 2048), (2080, 32, 2112)]
RAWOFF = [3, 2054, 2089]
GROUPS = [(0, 8), (8, 8), (16, 8), (24, 8), (32, 2)]
SEGW = 2304
EPS = 1e-6

def bc_last(ap, n):
    return bass.AP(ap.tensor, ap.offset, [list(x) for x in ap.ap] + [[0, n]])
def bc_mid(ap, n):
    a = [list(x) for x in ap.ap]
    return bass.AP(ap.tensor, ap.offset, [a[0], [0, n]] + a[1:])

class Ctx:
    pass

def b_setup(nc, st, kb):
    c = Ctx(); c.nc = nc; c.st = st; c.kb = kb
    def sb(name, shape, dt=F32):
        return st.enter_context(nc.sbuf_tensor("s_" + name, list(shape), dt)), Buf(name)
    c.sb = sb
    c.ps = [(st.enter_context(nc.psum_tensor("bps%d" % i, [128, 512], F32)), Buf("bps%d" % i)) for i in range(8)]
    return c

def load_consts(c, cm_d, rows_d):
    kb = c.kb
    c.cm, c.bcm = c.sb("cm", [64, 6, 64])
    c.rows, c.brows = c.sb("rows", [1, 2, NCOL])
    kb.dma("sp", c.cm[:], cm_d, writes=[c.bcm])
    kb.dma("sp", c.rows[:], rows_d, writes=[c.brows])
    c.eps, c.beps = c.sb("epsb", [128, 1])
    kb.op("pool", c.nc.gpsimd.memset, writes=[c.beps], ap=c.eps[:], constant=EPS)
    c.ident = c.cm[:, 0, :]; c.maskUn = c.cm[:, 1, :]; c.maskLn = c.cm[:, 2, :]; c.maskI = c.cm[:, 3, :]; c.ones64 = c.cm[:, 4, :]

def gather_rows(c, dst_ap, bdst, zg, idx_ap, bidx, bzg=None):
    kb = c.kb; nc = c.nc
    reads = [bidx] + ([bzg] if bzg is not None else [])
    kb.deps("pool", reads, [bdst])
    slot = kb.drr % kb.NDMA; kb.drr += 1; sem = kb.dsem[slot]
    if kb.dcnt[slot] > 0: kb.wait("pool", (sem, 16 * kb.dcnt[slot]))
    nc.gpsimd.indirect_dma_start(out=dst_ap, out_offset=None, in_=zg, in_offset=bass.IndirectOffsetOnAxis(ap=idx_ap, axis=0)).then_inc(sem, 16)
    kb.dcnt[slot] += 1
    kb.mark((sem, 16 * kb.dcnt[slot]), reads, [bdst]); kb.ninst += 1

def scatter_rows(c, src_ap, bsrc, og, idx_ap, bidx, bog):
    kb = c.kb; nc = c.nc
    kb.deps("pool", [bsrc, bidx], [bog])
    slot = kb.drr % kb.NDMA; kb.drr += 1; sem = kb.dsem[slot]
    if kb.dcnt[slot] > 0: kb.wait("pool", (sem, 16 * kb.dcnt[slot]))
    nc.gpsimd.indirect_dma_start(out=og, out_offset=bass.IndirectOffsetOnAxis(ap=idx_ap, axis=0), in_=src_ap, in_offset=None).then_inc(sem, 16)
    kb.dcnt[slot] += 1
    kb.mark((sem, 16 * kb.dcnt[slot]), [bsrc, bidx], [bog]); kb.ninst += 1

def row_to_col(c, col, bcol, row_ap, brow, n, one11, bone):
    kb = c.kb; nc = c.nc
    p, pb = c.ps[6]
    for k in range(n):
        kb.op("pe", nc.tensor.matmul, reads=[brow, bone], writes=[pb], out=p[:64, k:k + 1], lhsT=row_ap[0:1, k * 64:(k + 1) * 64], rhs=one11, start=True, stop=True)
    kb.op("act", nc.scalar.copy, reads=[pb], writes=[bcol], out=col[:, :n], in_=p[:64, :n])

def bcast_rows(c, dst, bdst, row_ap, brow, ncols, func=None, ones_row=None, bones=None, psi=6):
    kb = c.kb; nc = c.nc
    t = 0
    while t < ncols:
        w = min(512, ncols - t)
        p, pb = c.ps[psi]
        kb.op("pe", nc.tensor.matmul, reads=[brow, bones], writes=[pb], out=p[:64, :w], lhsT=ones_row, rhs=row_ap[:, t:t + w], start=True, stop=True)
        if func is None:
            kb.op("act", nc.scalar.copy, reads=[pb], writes=[bdst], out=dst[:, t:t + w], in_=p[:64, :w])
        else:
            kb.op("act", nc.scalar.activation, reads=[pb], writes=[bdst], out=dst[:, t:t + w], in_=p[:64, :w], func=func)
        t += w

def drain(g):
    for _ in g: pass

def pipeline(la, seg_slots, post=None):
    gm = None
    for gi, (ci, nch) in enumerate(GROUPS):
        si = gi % 2
        if gm is None:
            drain(la.mats(ci, nch, si))
        nxt = la.mats(GROUPS[gi + 1][0], GROUPS[gi + 1][1], (gi + 1) % 2) if gi + 1 < len(GROUPS) else None
        sc = la.scan(ci, nch, seg_slots(ci, nch), si)
        for _ in sc:
            if nxt is not None:
                for k in range(3):
                    try: next(nxt)
                    except StopIteration: nxt = None; break
        if nxt is not None: drain(nxt)
        gm = True
        if post is not None: post(ci, nch)

class LinAttn:
    def __init__(self, c, name, delta, DV):
        self.c = c; self.name = name; self.delta = delta; self.DV = DV
        sb = lambda n, s, dt=F32: c.sb(name + n, s, dt)
        self.QT, self.bQT = sb("QT", [64, SEGW]); self.KT, self.bKT = sb("KT", [64, SEGW]); self.VT, self.bVT = sb("VT", [33, SEGW])
        self.grow, self.bgrow = sb("grow", [1, NCOL])
        self.g2row, self.bg2row = sb("g2row", [1, NCOL])
        self.gcol, self.bgcol = sb("gcol", [64, 34]); self.g2col, self.bg2col = sb("g2col", [64, 34])
        self.kwcol, self.bkwcol = sb("kwcol", [64, 34])
        self.rkcol, self.brkcol = sb("rkcol", [64, 34])
        self.gbc, self.bgbc = sb("gbc", [64, NCOL]); self.gam, self.bgam = sb("gam", [64, NCOL])
        if delta: self.g2bc, self.bg2bc = sb("g2bc", [64, NCOL])
        self.onesrow, self.bonesrow = sb("onesrow", [1, 64])
        c.kb.op("pool", c.nc.gpsimd.memset, writes=[self.bonesrow], ap=self.onesrow[:], constant=1.0)
        names = ["D", "DM", "N0", "P", "tmp"] if delta else ["D", "DM", "tmp"]
        self.w = {n: sb("w" + n, [64, 512]) for n in names}
        if delta:
            for n in ["Nb0", "NTb0", "Nb1", "NTb1", "Pb"]: self.w[n] = sb("w" + n, [64, 512], BF16)
        self.wset = [{n: sb("w%s%d" % (n, i), [64, 512]) for n in ["pT", "qgT", "KG"]} for i in range(2)]
        self.cur = 0
        if delta:
            self.RK = sb("RK", [64, 512]); self.wkTs = [sb("wkT%d" % i, [64, 512]) for i in range(2)]
            self.RVs = [sb("RV%d" % i, [64, 8, DV]) for i in range(2)]; self.wvs = [sb("wv%d" % i, [64, 8, DV]) for i in range(2)]; self.u = [sb("u%d" % i, [64, DV]) for i in range(2)]
        else:
            self.RVs = [sb("RV%d" % i, [64, 8, DV]) for i in range(2)]
            for i in range(2): c.kb.op("pool", c.nc.gpsimd.memset, writes=[self.RVs[i][1]], ap=self.RVs[i][0][:], constant=1.0)
        self.S, self.bS = sb("S", [64, 17, DV])
        self.oT, self.boT = sb("oT", [DV, SEGW])
        c.kb.op("pool", c.nc.gpsimd.memset, writes=[self.boT], ap=self.oT[:], constant=0.0)

    def mats(self, ci, nch, si=0):
        c = self.c; kb = c.kb; nc = c.nc; W = nch * 64; c0 = ci * 64
        v3 = lambda t: t[:, :W].rearrange("p (n i) -> p n i", i=64)
        ws = self.wset[si]
        D, bD = self.w["D"]; DM, bDM = self.w["DM"]; tmp, btmp = self.w["tmp"]
        gbc3 = v3(self.gbc[:, c0:c0 + W])
        kb.op("dve", nc.vector.tensor_tensor, reads=[self.bgbc, self.bgcol], writes=[bD], out=v3(D), in0=gbc3, in1=bc_last(self.gcol[:, ci:ci + nch], 64), op=ALU.subtract)
        kb.op("dve", nc.vector.tensor_scalar, reads=[bD], writes=[bD], out=D[:, :W], in0=D[:, :W], scalar1=0.0, scalar2=None, op0=ALU.min)
        kb.op("act", nc.scalar.activation, reads=[bD], writes=[bD], out=D[:, :W], in_=D[:, :W], func=AF.Exp)
        kb.op("dve", nc.vector.tensor_tensor, reads=[bD, c.bcm], writes=[bDM], out=v3(DM), in0=v3(D), in1=bc_mid(c.maskI, nch), op=ALU.mult)
        if not self.delta:
            kb.op("dve", nc.vector.tensor_tensor, reads=[bDM, self.bg2col], writes=[bDM], out=v3(DM), in0=v3(DM), in1=bc_last(self.g2col[:, ci:ci + nch], 64), op=ALU.mult)
        p, pb = c.ps[5]
        for n in range(nch):
            sl = slice(c0 + n * 64, c0 + n * 64 + 64)
            kb.op("pe", nc.tensor.matmul, reads=[self.bKT, self.bQT], writes=[pb], out=p[:64, n * 64:n * 64 + 64], lhsT=self.KT[:, sl], rhs=self.QT[:, sl], start=True, stop=True)
        pT, bpT = ws["pT"]
        yield
        kb.op("dve", nc.vector.tensor_tensor, reads=[pb, bDM], writes=[bpT], out=pT[:, :W], in0=p[:64, :W], in1=DM[:, :W], op=ALU.mult)
        qg, bqg = ws["qgT"]
        kb.op("pool", nc.gpsimd.tensor_tensor, reads=[self.bQT, self.bgam], writes=[bqg], out=qg[:, :W], in0=self.QT[:, c0:c0 + W], in1=self.gam[:, c0:c0 + W], op=ALU.mult)
        p2, pb2 = c.ps[2]
        for n in range(nch):
            sl = slice(c0 + n * 64, c0 + n * 64 + 64)
            kb.op("pe", nc.tensor.transpose, reads=[self.bKT, c.bcm], writes=[pb2], out=p2[:64, n * 64:n * 64 + 64], in_=self.KT[:, sl], identity=c.ident)
        KG, bKG = ws["KG"]
        yield
        kb.op("dve", nc.vector.tensor_tensor, reads=[pb2, self.bkwcol], writes=[bKG], out=v3(KG), in0=v3(p2[:64, :]), in1=bc_last(self.kwcol[:, ci:ci + nch], 64), op=ALU.mult)
        if self.delta:
            RK, bRK = self.RK
            kb.op("dve", nc.vector.tensor_tensor, reads=[pb2, self.brkcol], writes=[bRK], out=v3(RK), in0=v3(p2[:64, :]), in1=bc_last(self.rkcol[:, ci:ci + nch], 64), op=ALU.mult)
        DV = self.DV
        p3, pb3 = c.ps[3]
        nv = 32
        for n in range(nch):
            sl = slice(c0 + n * 64, c0 + n * 64 + 64)
            kb.op("pe", nc.tensor.transpose, reads=[self.bVT, c.bcm], writes=[pb3], out=p3[:64, n * 32:n * 32 + 32], in_=self.VT[:32, sl], identity=c.ident[:32, :32])
        RV, bRV = self.RVs[si]
        yield
        pv3 = p3[:64, :nch * 32].rearrange("p (n d) -> p n d", d=32)
        if self.delta:
            kb.op("dve", nc.vector.tensor_tensor, reads=[pb3, self.bg2col], writes=[bRV], out=RV[:, :nch, :], in0=pv3, in1=bc_last(self.g2col[:, ci:ci + nch], 32), op=ALU.mult)
        else:
            kb.op("act", nc.scalar.copy, reads=[pb3], writes=[bRV], out=RV[:, :nch, 0:32], in_=pv3)
        if not self.delta:
            return
        yield
        p0, pb0 = c.ps[0]
        for n in range(nch):
            sl = slice(c0 + n * 64, c0 + n * 64 + 64)
            kb.op("pe", nc.tensor.matmul, reads=[self.bKT], writes=[pb0], out=p0[:64, n * 64:n * 64 + 64], lhsT=self.KT[:, sl], rhs=self.KT[:, sl], start=True, stop=True)
        N0, bN0 = self.w["N0"]; NT0, bNT0 = self.w["NTb0"]; N1, bN1 = self.w["Nb1"]; NT1, bNT1 = self.w["NTb1"]; P, bP = self.w["P"]; Nb0, bNb0 = self.w["Nb0"]; Pb, bPb = self.w["Pb"]
        kb.op("dve", nc.vector.tensor_tensor, reads=[self.bg2bc, c.bcm], writes=[btmp], out=v3(tmp), in0=v3(self.g2bc[:, c0:c0 + W]), in1=bc_mid(c.maskUn, nch), op=ALU.mult)
        kb.op("dve", nc.vector.tensor_tensor, reads=[btmp, bD], writes=[btmp], out=tmp[:, :W], in0=tmp[:, :W], in1=D[:, :W], op=ALU.mult)
        kb.op("dve", nc.vector.tensor_tensor, reads=[pb0, btmp], writes=[bN0], out=N0[:, :W], in0=p0[:64, :W], in1=tmp[:, :W], op=ALU.mult)
        kb.op("dve", nc.vector.tensor_tensor, reads=[self.bgbc, self.bgcol], writes=[btmp], out=v3(tmp), in0=bc_last(self.gcol[:, ci:ci + nch], 64), in1=gbc3, op=ALU.subtract)
        kb.op("dve", nc.vector.tensor_scalar, reads=[btmp], writes=[btmp], out=tmp[:, :W], in0=tmp[:, :W], scalar1=0.0, scalar2=None, op0=ALU.min)
        kb.op("act", nc.scalar.activation, reads=[btmp], writes=[btmp], out=tmp[:, :W], in_=tmp[:, :W], func=AF.Exp)
        kb.op("dve", nc.vector.tensor_tensor, reads=[btmp, c.bcm], writes=[btmp], out=v3(tmp), in0=v3(tmp), in1=bc_mid(c.maskLn, nch), op=ALU.mult)
        kb.op("dve", nc.vector.tensor_tensor, reads=[btmp, self.bg2col], writes=[btmp], out=v3(tmp), in0=v3(tmp), in1=bc_last(self.g2col[:, ci:ci + nch], 64), op=ALU.mult)
        kb.op("dve", nc.vector.tensor_tensor, reads=[pb0, btmp], writes=[bNT0], out=NT0[:, :W], in0=p0[:64, :W], in1=tmp[:, :W], op=ALU.mult)
        kb.op("dve", nc.vector.tensor_tensor, reads=[bN0, c.bcm], writes=[bP], out=v3(P), in0=v3(N0), in1=bc_mid(c.ident, nch), op=ALU.add)
        kb.op("act", nc.scalar.copy, reads=[bN0], writes=[bNb0], out=Nb0[:, :W], in_=N0[:, :W])
        kb.op("act", nc.scalar.copy, reads=[bP], writes=[bPb], out=Pb[:, :W], in_=P[:, :W])
        A, bA, AT, bAT = Nb0, bNb0, NT0, bNT0
        A2, bA2, AT2, bAT2 = N1, bN1, NT1, bNT1
        pa, pab = c.ps[0]; pat, patb = c.ps[1]; pp, ppb = c.ps[4]
        for r in range(5):
            last = (r == 4)
            yield
            if not last:
                for n in range(nch):
                    s = slice(n * 64, n * 64 + 64)
                    kb.op("pe", nc.tensor.matmul, reads=[bA, bAT], writes=[pab], out=pa[:64, s], lhsT=AT[:, s], rhs=A[:, s], start=True, stop=True)
            for n in range(nch):
                s = slice(n * 64, n * 64 + 64)
                kb.op("pe", nc.tensor.matmul, reads=[bA, bAT], writes=[patb], out=pat[:64, s], lhsT=A[:, s], rhs=AT[:, s], start=True, stop=True)
            if not last:
                kb.op("act", nc.scalar.copy, reads=[pab], writes=[bA2], out=A2[:, :W], in_=pa[:64, :W])
            kb.op("dve", nc.vector.tensor_copy, reads=[patb], writes=[bAT2], out=AT2[:, :W], in_=pat[:64, :W])
            yield
            for n in range(nch):
                s = slice(n * 64, n * 64 + 64)
                kb.op("pe", nc.tensor.matmul, reads=[bAT2, bPb], writes=[ppb], out=pp[:64, s], lhsT=AT2[:, s], rhs=Pb[:, s], start=True, stop=True)
            kb.op("dve", nc.vector.tensor_tensor, reads=[ppb, bP], writes=[bP], out=P[:, :W], in0=pp[:64, :W], in1=P[:, :W], op=ALU.add)
            if not last:
                kb.op("act", nc.scalar.copy, reads=[bP], writes=[bPb], out=Pb[:, :W], in_=P[:, :W])
            A, bA, AT, bAT, A2, bA2, AT2, bAT2 = A2, bA2, AT2, bAT2, A, bA, AT, bAT
        yield
        pw, pwb = c.ps[3]; pk, pkb = c.ps[2]
        RK, bRK = self.RK
        for n in range(nch):
            s = slice(n * 64, n * 64 + 64)
            kb.op("pe", nc.tensor.matmul, reads=[bP, bRV], writes=[pwb], out=pw[:64, n * 32:n * 32 + 32], lhsT=P[:, s], rhs=RV[:, n, :], start=True, stop=True)
        for n in range(nch):
            s = slice(n * 64, n * 64 + 64)
            kb.op("pe", nc.tensor.matmul, reads=[bP, bRK], writes=[pkb], out=pk[:64, s], lhsT=RK[:, s], rhs=P[:, s], start=True, stop=True)
        wv, bwv = self.wvs[si]; wkT, bwkT = self.wkTs[si]
        yield
        kb.op("act", nc.scalar.copy, reads=[pwb], writes=[bwv], out=wv[:, :nch, :], in_=pw[:64, :nch * 32].rearrange("p (n d) -> p n d", d=32))
        kb.op("act", nc.scalar.copy, reads=[pkb], writes=[bwkT], out=wkT[:, :W], in_=pk[:64, :W])

    def scan(self, ci, nch, slots, si=0):
        c = self.c; kb = c.kb; nc = c.nc; DV = self.DV; c0 = ci * 64
        ws = self.wset[si]
        pT, bpT = ws["pT"]; qg, bqg = ws["qgT"]; KG, bKG = ws["KG"]; RV, bRV = self.RVs[si]
        po, pob = c.ps[7]; psm, psmb = c.ps[6]
        for n in range(nch):
            s = slice(n * 64, n * 64 + 64); sl = slots[n]
            S = self.S[:, sl, :]
            if self.delta:
                wv, bwv = self.wvs[si]; wkT, bwkT = self.wkTs[si]; u, bu = self.u[n % 2]
                kb.op("pe", nc.tensor.matmul, reads=[bwkT, self.bS], writes=[psmb], out=psm[:64, 0:DV], lhsT=wkT[:, s], rhs=S, start=True, stop=True)
                kb.op("pe", nc.tensor.matmul, reads=[bqg, self.bS], writes=[pob], out=po[:DV, s], lhsT=S, rhs=qg[:, s], start=True, stop=False)
                kb.op("dve", nc.vector.tensor_tensor, reads=[psmb, bwv], writes=[bu], out=u[:], in0=wv[:, n, :], in1=psm[:64, 0:DV], op=ALU.subtract)
                uu, buu = u[:], bu
            else:
                kb.op("pe", nc.tensor.matmul, reads=[bqg, self.bS], writes=[pob], out=po[:DV, s], lhsT=S, rhs=qg[:, s], start=True, stop=False)
                uu, buu = RV[:, n, :], bRV
            kb.op("pe", nc.tensor.matmul, reads=[bpT, buu], writes=[pob], out=po[:DV, s], lhsT=uu, rhs=pT[:, s], start=False, stop=True)
            kb.op("pe", nc.tensor.matmul, reads=[bKG, buu], writes=[psmb], out=psm[:64, 64:64 + DV], lhsT=KG[:, s], rhs=uu, start=True, stop=True)
            glast = self.gam[:, c0 + n * 64 + 63:c0 + n * 64 + 64]
            kb.op("dve", nc.vector.scalar_tensor_tensor, reads=[psmb, self.bS, self.bgam], writes=[self.bS], out=S, in0=S, scalar=glast, in1=psm[:64, 64:64 + DV], op0=ALU.mult, op1=ALU.add)
            yield
        kb.op("act", nc.scalar.copy, reads=[pob], writes=[self.boT], out=self.oT[:, c0:c0 + nch * 64], in_=po[:DV, :nch * 64])

def gate_cols(c, la, ci0=0):
    kb = c.kb; nc = c.nc
    row_to_col(c, la.gcol, la.bgcol, la.grow[0:1, :], la.bgrow, 34, la.onesrow[0:1, 0:1], la.bonesrow)
    row_to_col(c, la.g2col, la.bg2col, la.g2row[0:1, :], la.bg2row, 34, la.onesrow[0:1, 0:1], la.bonesrow)
    bcast_rows(c, la.gbc, la.bgbc, la.grow, la.bgrow, NCOL, ones_row=la.onesrow[:], bones=la.bonesrow)
    bcast_rows(c, la.gam, la.bgam, la.grow, la.bgrow, NCOL, func=AF.Exp, ones_row=la.onesrow[:], bones=la.bonesrow)

class GDN:
    def __init__(self, c, prm_d):
        self.c = c; kb = c.kb; nc = c.nc
        self.la = LinAttn(c, "gdn", True, 32)
        sb = c.sb
        self.R, self.bR = sb("gR", [64, 3 + SEGW]); self.H = [sb("gH%d" % i, [64, 3]) for i in range(3)]
        self.SR, self.bSR = sb("gSR", [64, 70])
        self.cw = [sb("gcw%d" % i, [64, 4]) for i in range(3)]
        self.cst = [sb("gcst%d" % i, [64, 16, 3]) for i in range(3)]
        self.prm, self.bprm = sb("gprm", [1, 4])
        self.gt, self.bgt = sb("ggt", [2, SEGW])
        self.ysq, self.bysq = sb("gysq", [64, 512]); self.rinv, self.brinv = sb("grinv", [64, 512])
        for i, (r0, nr) in enumerate([(0, 64), (64, 64), (128, 32)]):
            kb.dma("sp", self.cw[i][0][:nr, :], prm_d["convw"][r0:r0 + nr, :], writes=[self.cw[i][1]])
            kb.dma("sp", self.cst[i][0][:nr], prm_d["convst"][r0:r0 + nr], writes=[self.cst[i][1]])
        kb.dma("sp", self.prm[:, 0:1], prm_d["alog"], writes=[self.bprm]); kb.dma("sp", self.prm[:, 1:2], prm_d["dtb"], writes=[self.bprm])
        kb.op("act", nc.scalar.activation, reads=[self.bprm], writes=[self.bprm], out=self.prm[:, 2:3], in_=self.prm[:, 0:1], func=AF.Exp)
        kb.op("dve", nc.vector.tensor_scalar, reads=[self.bprm], writes=[self.bprm], out=self.prm[:, 2:3], in0=self.prm[:, 2:3], scalar1=-1.0, scalar2=None, op0=ALU.mult)
        la = self.la
        kb.dma("sp", la.S[:, 1:17, :], prm_d["s0"], writes=[la.bS])
        kb.op("pool", nc.gpsimd.memset, writes=[la.bS], ap=la.S[:, 0, :], constant=0.0)
        for t, b in [(la.QT, la.bQT), (la.KT, la.bKT), (la.VT, la.bVT), (self.gt, self.bgt)]:
            kb.op("pool", nc.gpsimd.memset, writes=[b], ap=t[:], constant=0.0)
        for t, b in self.H:
            kb.op("pool", nc.gpsimd.memset, writes=[b], ap=t[:], constant=0.0)

    def segment(self, s, zg, bzg, idx, bidx, icol):
        c = self.c; kb = c.kb; nc = c.nc; la = self.la
        raws = [(self.R, self.bR, 64, la.QT, la.bQT), (self.R, self.bR, 64, la.KT, la.bKT), (self.R, self.bR, 32, la.VT, la.bVT)]
        for i, (R, bR, nr, Y, bY) in enumerate(raws):
            gather_rows(c, R[:nr, 3:3 + SEGW], bR, zg, idx[:nr, icol["qkv"[i]]:icol["qkv"[i]] + 1], bidx, bzg=bzg)
            Hh, bHh = self.H[i]
            kb.op("pool", nc.gpsimd.tensor_copy, reads=[bHh], writes=[bR], out=R[:nr, 0:3], in_=Hh[:nr, :])
            kb.op("pool", nc.gpsimd.tensor_copy, reads=[bR], writes=[bHh], out=Hh[:nr, :], in_=R[:nr, 2048:2051])
            cst, bcst = self.cst[i]; SR, bSR = self.SR, self.bSR
            kb.op("pool", nc.gpsimd.tensor_copy, reads=[bcst], writes=[bSR], out=SR[:nr, 0:3], in_=cst[:nr, 2 * s, :])
            kb.op("pool", nc.gpsimd.tensor_copy, reads=[bR], writes=[bSR], out=SR[:nr, 3:35], in_=R[:nr, 3 + 2048:3 + 2080])
            kb.op("pool", nc.gpsimd.tensor_copy, reads=[bcst], writes=[bSR], out=SR[:nr, 35:38], in_=cst[:nr, 2 * s + 1, :])
            kb.op("pool", nc.gpsimd.tensor_copy, reads=[bR], writes=[bSR], out=SR[:nr, 38:70], in_=R[:nr, 3 + 2112:3 + 2144])
            cw, bcw = self.cw[i]
            for (X, bX, r0, ln, d0) in [(R, bR, 3, 2048, 0), (SR, bSR, 3, 32, 2048), (SR, bSR, 38, 32, 2112)]:
                kb.op("dve", nc.vector.tensor_scalar, reads=[bX, bcw], writes=[bY], out=Y[:nr, d0:d0 + ln], in0=X[:nr, r0:r0 + ln], scalar1=cw[:nr, 3:4], scalar2=None, op0=ALU.mult)
                for t in range(3):
                    kb.op("dve", nc.vector.scalar_tensor_tensor, reads=[bX, bcw, bY], writes=[bY], out=Y[:nr, d0:d0 + ln], in0=X[:nr, r0 - 3 + t:r0 - 3 + t + ln],
                          scalar=cw[:nr, t:t + 1], in1=Y[:nr, d0:d0 + ln], op0=ALU.mult, op1=ALU.add)
            kb.op("act", nc.scalar.activation, reads=[bY], writes=[bY], out=Y[:nr, :], in_=Y[:nr, :], func=AF.Silu)
            if i < 2:
                for t0 in range(0, NCOL, 512):
                    w = min(512, NCOL - t0)
                    kb.op("act", nc.scalar.activation, reads=[bY], writes=[self.bysq], out=self.ysq[:, :w], in_=Y[:, t0:t0 + w], func=AF.Square)
                    p, pb = c.ps[6]
                    kb.op("pe", nc.tensor.matmul, reads=[self.bysq, c.bcm], writes=[pb], out=p[:64, :w], lhsT=c.ones64, rhs=self.ysq[:, :w], start=True, stop=True)
                    kb.op("act", nc.scalar.activation, reads=[pb, c.beps], writes=[self.brinv], out=self.rinv[:, :w], in_=p[:64, :w], func=AF.Sqrt, bias=c.eps[:64, 0:1])
                    kb.op("dve", nc.vector.reciprocal, reads=[self.brinv], writes=[self.brinv], out=self.rinv[:, :w], in_=self.rinv[:, :w])
                    kb.op("dve", nc.vector.scalar_tensor_tensor, reads=[bY, self.brinv], writes=[bY], out=Y[:, t0:t0 + w], in0=Y[:, t0:t0 + w], scalar=(0.125 if i == 0 else 1.0),
                          in1=self.rinv[:, :w], op0=ALU.mult, op1=ALU.mult)
        gather_rows(c, self.gt[:2, :], self.bgt, zg, idx[:2, icol["g"]:icol["g"] + 1], bidx, bzg=bzg)
        kb.dma("sp", la.g2row[:], self.gt[1:2, :NCOL], reads=[self.bgt], writes=[la.bg2row])
        valid = c.rows[:, 1, :]
        kb.op("act", nc.scalar.activation, reads=[self.bgt, self.bprm], writes=[la.bgrow], out=la.grow[:], in_=self.gt[0:1, :NCOL], func=AF.Exp, bias=self.prm[:, 1:2])
        kb.op("act", nc.scalar.activation, reads=[la.bgrow], writes=[la.bgrow], out=la.grow[:], in_=la.grow[:], func=AF.Ln, bias=1.0)
        kb.op("dve", nc.vector.scalar_tensor_tensor, reads=[la.bgrow, self.bprm, c.brows], writes=[la.bgrow], out=la.grow[:], in0=la.grow[:], scalar=self.prm[:, 2:3], in1=valid, op0=ALU.mult, op1=ALU.mult)
        kb.op("dve", nc.vector.tensor_tensor_scan, reads=[la.bgrow, c.brows], writes=[la.bgrow], out=la.grow[:], data0=c.rows[:, 0, :], data1=la.grow[:], initial=0.0, op0=ALU.mult, op1=ALU.add)
        kb.op("act", nc.scalar.activation, reads=[la.bg2row], writes=[la.bg2row], out=la.g2row[:], in_=la.g2row[:], func=AF.Sigmoid)
        kb.op("dve", nc.vector.tensor_tensor, reads=[la.bg2row, c.brows], writes=[la.bg2row], out=la.g2row[:], in0=la.g2row[:], in1=valid, op=ALU.mult)
        gate_cols(c, la)
        bcast_rows(c, la.g2bc, la.bg2bc, la.g2row, la.bg2row, NCOL, ones_row=la.onesrow[:], bones=la.bonesrow)
        glast = la.gbc[:, 63:NCOL:64]
        kb.op("dve", nc.vector.tensor_tensor, reads=[la.bgbc, la.bgcol], writes=[la.bkwcol], out=la.kwcol[:], in0=glast, in1=la.gcol[:], op=ALU.subtract)
        kb.op("act", nc.scalar.activation, reads=[la.bkwcol], writes=[la.bkwcol], out=la.kwcol[:], in_=la.kwcol[:], func=AF.Exp)
        kb.op("act", nc.scalar.activation, reads=[la.bgcol], writes=[la.brkcol], out=la.rkcol[:], in_=la.gcol[:], func=AF.Exp)
        kb.op("dve", nc.vector.tensor_tensor, reads=[la.brkcol, la.bg2col], writes=[la.brkcol], out=la.rkcol[:], in0=la.rkcol[:], in1=la.g2col[:], op=ALU.mult)
        pipeline(la, lambda ci, nch: [0] * nch if ci < 32 else [1 + 2 * s, 2 + 2 * s])


class MLSTM:
    def __init__(self, c, prm_d):
        self.c = c; kb = c.kb; nc = c.nc
        self.la = la = LinAttn(c, "ml", False, 33)
        sb = c.sb
        self.prm, self.bprm = sb("mprm", [1, 4])
        self.gt, self.bgt = sb("mgt", [2, SEGW])
        self.padb, self.bpadb = sb("mpadb", [1, NCOL])
        self.mrow, self.bmrow = sb("mmrow", [1, NCOL]); self.mfin, self.bmfin = sb("mmfin", [1, 17]); self.em, self.bem = sb("mem", [1, 17])
        self.embc, self.bembc = sb("membc", [64, 17]); self.Sout, self.bSout = sb("mSout", [64, 17, 33])
        self.lf, self.blf = sb("mlf", [1, NCOL])
        self.on33, self.bon33 = sb("mon33", [33, 32]); self.rrow, self.brrow = sb("mrrow", [33, 512]); self.hT, self.bhT = sb("mhT", [32, SEGW])
        kb.op("pool", nc.gpsimd.memset, writes=[self.bhT], ap=self.hT[:], constant=0.0)
        kb.op("pool", nc.gpsimd.memset, writes=[self.bon33], ap=self.on33[:], constant=1.0)
        kb.dma("sp", self.prm[:, 0:1], prm_d["bi"], writes=[self.bprm]); kb.dma("sp", self.prm[:, 1:2], prm_d["bf"], writes=[self.bprm])
        kb.op("dve", nc.vector.tensor_scalar, reads=[self.bprm], writes=[self.bprm], out=self.prm[:, 2:3], in0=self.prm[:, 1:2], scalar1=-1.0, scalar2=None, op0=ALU.mult)
        kb.op("dve", nc.vector.tensor_scalar, reads=[c.brows], writes=[self.bpadb], out=self.padb[:], in0=c.rows[:, 1, :], scalar1=1.0, scalar2=30000.0, op0=ALU.subtract, op1=ALU.mult)
        kb.dma("sp", la.S[:, 1:17, :], prm_d["s0"], writes=[la.bS])
        kb.op("pool", nc.gpsimd.memset, writes=[la.bS], ap=la.S[:, 0, :], constant=0.0)
        kb.op("pool", nc.gpsimd.memset, writes=[self.bmfin], ap=self.mfin[:], constant=0.0)
        kb.dma("sp", self.mfin[:, 1:17], prm_d["m0"], writes=[self.bmfin])
        kb.op("act", nc.scalar.activation, reads=[self.bmfin], writes=[self.bem], out=self.em[:], in_=self.mfin[:], func=AF.Exp)
        p, pb = c.ps[6]
        kb.op("pe", nc.tensor.matmul, reads=[self.bem, la.bonesrow], writes=[pb], out=p[:64, :17], lhsT=la.onesrow[:], rhs=self.em[:], start=True, stop=True)
        kb.op("act", nc.scalar.copy, reads=[pb], writes=[self.bembc], out=self.embc[:], in_=p[:64, :17])
        kb.op("dve", nc.vector.tensor_tensor, reads=[la.bS, self.bembc], writes=[la.bS], out=la.S[:, 1:17, :], in0=la.S[:, 1:17, :], in1=bc_last(self.embc[:, 1:17], 33), op=ALU.mult)
        kb.op("pool", nc.gpsimd.memset, writes=[la.bVT], ap=la.VT[:], constant=1.0)

    def segment(self, s, zg, bzg, idx, bidx, icol):
        c = self.c; kb = c.kb; nc = c.nc; la = self.la
        gather_rows(c, la.QT[:, :], la.bQT, zg, idx[:64, icol["q"]:icol["q"] + 1], bidx, bzg=bzg)
        gather_rows(c, la.KT[:, :], la.bKT, zg, idx[:64, icol["k"]:icol["k"] + 1], bidx, bzg=bzg)
        gather_rows(c, la.VT[:32, :], la.bVT, zg, idx[:32, icol["v"]:icol["v"] + 1], bidx, bzg=bzg)
        gather_rows(c, self.gt[:2, :], self.bgt, zg, idx[:2, icol["g"]:icol["g"] + 1], bidx, bzg=bzg)
        kb.op("act", nc.scalar.mul, reads=[la.bKT], writes=[la.bKT], out=la.KT[:, :NCOL], in_=la.KT[:, :NCOL], mul=0.125)
        kb.dma("sp", self.lf[:], self.gt[1:2, :NCOL], reads=[self.bgt], writes=[self.blf])
        valid = c.rows[:, 1, :]
        kb.op("dve", nc.vector.scalar_tensor_tensor, reads=[self.bgt, self.bprm, c.brows], writes=[la.bg2row], out=la.g2row[:], in0=self.gt[0:1, :NCOL], scalar=self.prm[:, 0:1], in1=valid, op0=ALU.add, op1=ALU.mult)
        kb.op("dve", nc.vector.tensor_tensor, reads=[la.bg2row, self.bpadb], writes=[la.bg2row], out=la.g2row[:], in0=la.g2row[:], in1=self.padb[:], op=ALU.add)
        kb.op("act", nc.scalar.activation, reads=[self.blf, self.bprm], writes=[self.blf], out=self.lf[:], in_=self.lf[:], func=AF.Exp, scale=-1.0, bias=self.prm[:, 2:3])
        kb.op("act", nc.scalar.activation, reads=[self.blf], writes=[self.blf], out=self.lf[:], in_=self.lf[:], func=AF.Ln, bias=1.0)
        kb.op("dve", nc.vector.scalar_tensor_tensor, reads=[self.blf, c.brows], writes=[self.blf], out=self.lf[:], in0=self.lf[:], scalar=-1.0, in1=valid, op0=ALU.mult, op1=ALU.mult)
        for (c0, ln, slot) in [(0, 2048, 0), (2048, 32, 1 + 2 * s), (2112, 32, 2 + 2 * s)]:
            kb.op("dve", nc.vector.tensor_tensor_scan, reads=[self.blf, la.bg2row, self.bmfin], writes=[self.bmrow], out=self.mrow[:, c0:c0 + ln], data0=self.lf[:, c0:c0 + ln], data1=la.g2row[:, c0:c0 + ln],
                  initial=self.mfin[:, slot:slot + 1], op0=ALU.add, op1=ALU.max)
            kb.op("dve", nc.vector.tensor_copy, reads=[self.bmrow], writes=[self.bmfin], out=self.mfin[:, slot:slot + 1], in_=self.mrow[:, c0 + ln - 1:c0 + ln])
        kb.op("dve", nc.vector.tensor_tensor_scan, reads=[self.blf, c.brows], writes=[la.bgrow], out=la.grow[:], data0=c.rows[:, 0, :], data1=self.lf[:], initial=0.0, op0=ALU.mult, op1=ALU.add)
        gate_cols(c, la)
        kb.op("act", nc.scalar.activation, reads=[la.bg2col], writes=[la.bg2col], out=la.g2col[:], in_=la.g2col[:], func=AF.Exp)
        glast = la.gbc[:, 63:NCOL:64]
        kb.op("dve", nc.vector.tensor_tensor, reads=[la.bgbc, la.bgcol], writes=[la.bkwcol], out=la.kwcol[:], in0=glast, in1=la.gcol[:], op=ALU.subtract)
        kb.op("act", nc.scalar.activation, reads=[la.bkwcol], writes=[la.bkwcol], out=la.kwcol[:], in_=la.kwcol[:], func=AF.Exp)
        kb.op("dve", nc.vector.tensor_tensor, reads=[la.bkwcol, la.bg2col], writes=[la.bkwcol], out=la.kwcol[:], in0=la.kwcol[:], in1=la.g2col[:], op=ALU.mult)
        def post(ci, nch):
            c0 = ci * 64; W = nch * 64
            kb.op("dve", nc.vector.scalar_tensor_tensor, reads=[la.boT], writes=[self.brrow], out=self.rrow[32:33, :W], in0=la.oT[32:33, c0:c0 + W], scalar=-1.0, in1=la.oT[32:33, c0:c0 + W], op0=ALU.mult, op1=ALU.max)
            kb.op("dve", nc.vector.tensor_scalar, reads=[self.brrow], writes=[self.brrow], out=self.rrow[32:33, :W], in0=self.rrow[32:33, :W], scalar1=1.0, scalar2=None, op0=ALU.max)
            kb.op("dve", nc.vector.reciprocal, reads=[self.brrow], writes=[self.brrow], out=self.rrow[32:33, :W], in_=self.rrow[32:33, :W])
            p, pb = c.ps[6]
            kb.op("pe", nc.tensor.matmul, reads=[self.brrow, self.bon33], writes=[pb], out=p[:32, :W], lhsT=self.on33[32:33, :], rhs=self.rrow[32:33, :W], start=True, stop=True)
            kb.op("dve", nc.vector.tensor_tensor, reads=[pb, la.boT], writes=[self.bhT], out=self.hT[:, c0:c0 + W], in0=la.oT[0:32, c0:c0 + W], in1=p[:32, :W], op=ALU.mult)
        pipeline(la, lambda ci, nch: [0] * nch if ci < 32 else [1 + 2 * s, 2 + 2 * s], post)

    def finish(self):
        c = self.c; kb = c.kb; nc = c.nc; la = self.la
        kb.op("act", nc.scalar.activation, reads=[self.bmfin], writes=[self.bem], out=self.em[:], in_=self.mfin[:], func=AF.Exp, scale=-1.0)
        p, pb = c.ps[6]
        kb.op("pe", nc.tensor.matmul, reads=[self.bem, la.bonesrow], writes=[pb], out=p[:64, :17], lhsT=la.onesrow[:], rhs=self.em[:], start=True, stop=True)
        kb.op("act", nc.scalar.copy, reads=[pb], writes=[self.bembc], out=self.embc[:], in_=p[:64, :17])
        kb.op("dve", nc.vector.tensor_tensor, reads=[la.bS, self.bembc], writes=[self.bSout], out=self.Sout[:], in0=la.S[:], in1=bc_last(self.embc[:], 33), op=ALU.mult)

PI = 3.141592653589793

class S5:
    def __init__(self, c, prm_d):
        self.c = c; kb = c.kb; nc = c.nc; sb = c.sb
        V = nc.vector
        self.pv, self.bpv = sb("s5pv", [128, 32])
        self.BT, self.bBT = sb("s5BT", [32, 2, 128]); self.CT, self.bCT = sb("s5CT", [128, 2, 32]); self.dv, self.bdv = sb("s5dv", [32, 1])
        self.X, self.bX = sb("s5X", [128, 2, 17])
        self.U, self.bU = sb("s5U", [128, 2, 2048]); self.L, self.bL = sb("s5L", [128, 2, 2048])
        self.rho, self.brho = sb("s5rho", [128, NCOL])
        self.uT, self.buT = sb("s5uT", [32, SEGW])
        self.bu, self.bbu = sb("s5bu", [128, 2, NCOL]); self.rr, self.brr = sb("s5rr", [128, 2, NCOL]); self.ww, self.bww = sb("s5ww", [128, 2, NCOL])
        self.t1, self.bt1 = sb("s5t1", [128, NCOL]); self.yT, self.byT = sb("s5yT", [32, SEGW])
        kb.op("pool", nc.gpsimd.memset, writes=[self.byT], ap=self.yT[:], constant=0.0)
        self.pw, self.bpw = sb("s5pw", [128, 8])
        pv = self.pv; bpv = self.bpv
        kb.dma("sp", pv[:, 0:3], prm_d["vec"], writes=[bpv])
        kb.dma("sp", self.BT[:, 0, :], prm_d["BreT"], writes=[self.bBT]); kb.dma("sp", self.BT[:, 1, :], prm_d["BimT"], writes=[self.bBT])
        kb.dma("sp", self.CT[:, 0, :], prm_d["CreT"], writes=[self.bCT]); kb.dma("sp", self.CT[:, 1, :], prm_d["CimT"], writes=[self.bCT])
        kb.dma("sp", self.dv[:], prm_d["dvec"], writes=[self.bdv])
        kb.op("pool", nc.gpsimd.memset, writes=[self.bX], ap=self.X[:], constant=0.0)
        kb.dma("sp", self.X[:, :, 1:17], prm_d["x0"], writes=[self.bX])
        col = lambda i: pv[:, i:i + 1]
        def ts(out, in0, s1, s2, o0, o1=None):
            if o1 is None: kb.op("dve", V.tensor_scalar, reads=[bpv], writes=[bpv], out=out, in0=in0, scalar1=s1, scalar2=None, op0=o0)
            else: kb.op("dve", V.tensor_scalar, reads=[bpv], writes=[bpv], out=out, in0=in0, scalar1=s1, scalar2=s2, op0=o0, op1=o1)
        def tt(out, a, b, op): kb.op("dve", V.tensor_tensor, reads=[bpv], writes=[bpv], out=out, in0=a, in1=b, op=op)
        def act(out, in_, func, **kw): kb.op("act", nc.scalar.activation, reads=[bpv], writes=[bpv], out=out, in_=in_, func=func, **kw)
        act(col(3), col(2), AF.Exp)
        tt(col(4), col(3), col(0), ALU.mult); tt(col(5), col(3), col(1), ALU.mult)
        act(col(6), col(4), AF.Exp)
        def wrap(dst, src, thrs):
            kb.op("dve", V.tensor_copy, reads=[bpv], writes=[bpv], out=dst, in_=src)
            for th in thrs:
                ts(col(7), src, th, -2 * PI, ALU.is_gt, ALU.mult)
                tt(dst, dst, col(7), ALU.add)
        wrap(col(8), col(5), [PI, 3 * PI, 5 * PI])
        act(col(9), col(8), AF.Sin)
        ts(col(17), col(8), PI / 2, None, ALU.add)
        wrap(col(8), col(17), [PI])
        act(col(10), col(8), AF.Sin)
        tt(col(11), col(6), col(10), ALU.mult); tt(col(12), col(6), col(9), ALU.mult)
        tt(col(13), col(0), col(0), ALU.mult); tt(col(7), col(1), col(1), ALU.mult); tt(col(13), col(13), col(7), ALU.add)
        kb.op("dve", V.reciprocal, reads=[bpv], writes=[bpv], out=col(13), in_=col(13))
        ts(col(14), col(11), -1.0, None, ALU.add)
        tt(col(15), col(14), col(0), ALU.mult); tt(col(7), col(12), col(1), ALU.mult); tt(col(15), col(15), col(7), ALU.add); tt(col(15), col(15), col(13), ALU.mult)
        tt(col(16), col(12), col(0), ALU.mult); tt(col(7), col(14), col(1), ALU.mult); tt(col(16), col(16), col(7), ALU.subtract); tt(col(16), col(16), col(13), ALU.mult)
        ts(col(18), col(16), -1.0, None, ALU.mult)
        kb.op("pool", nc.gpsimd.memset, writes=[self.brho], ap=self.rho[:], constant=0.0)
        kb.op("dve", V.tensor_scalar, reads=[bpv, self.brho], writes=[self.brho], out=self.rho[:], in0=self.rho[:], scalar1=col(6), scalar2=None, op0=ALU.add)
        for t0 in [0, 2048, 2112]:
            kb.op("pool", nc.gpsimd.memset, writes=[self.brho], ap=self.rho[:, t0:t0 + 1], constant=0.0)
        def table(T, bT, first_re, first_im, lead_one):
            pw = self.pw; bpw = self.bpw
            if lead_one:
                kb.op("pool", nc.gpsimd.memset, writes=[bT], ap=T[:, 0, 0:1], constant=1.0)
                kb.op("pool", nc.gpsimd.memset, writes=[bT], ap=T[:, 1, 0:1], constant=0.0)
            else:
                kb.op("dve", V.tensor_copy, reads=[bpv], writes=[bT], out=T[:, 0, 0:1], in_=first_re)
                kb.op("dve", V.tensor_copy, reads=[bpv], writes=[bT], out=T[:, 1, 0:1], in_=first_im)
            kb.op("dve", V.tensor_copy, reads=[bpv], writes=[bpw], out=pw[:, 0:1], in_=first_re)
            kb.op("dve", V.tensor_copy, reads=[bpv], writes=[bpw], out=pw[:, 1:2], in_=first_im)
            m = 1
            while m < 2048:
                kb.op("dve", V.tensor_scalar, reads=[bpw], writes=[bpw], out=pw[:, 2:3], in0=pw[:, 1:2], scalar1=-1.0, scalar2=None, op0=ALU.mult)
                kb.op("dve", V.tensor_scalar, reads=[bT, bpw], writes=[bT], out=T[:, 0, m:2 * m], in0=T[:, 0, 0:m], scalar1=pw[:, 0:1], scalar2=None, op0=ALU.mult)
                kb.op("dve", V.scalar_tensor_tensor, reads=[bT, bpw], writes=[bT], out=T[:, 0, m:2 * m], in0=T[:, 1, 0:m], scalar=pw[:, 2:3], in1=T[:, 0, m:2 * m], op0=ALU.mult, op1=ALU.add)
                kb.op("dve", V.tensor_scalar, reads=[bT, bpw], writes=[bT], out=T[:, 1, m:2 * m], in0=T[:, 0, 0:m], scalar1=pw[:, 1:2], scalar2=None, op0=ALU.mult)
                kb.op("dve", V.scalar_tensor_tensor, reads=[bT, bpw], writes=[bT], out=T[:, 1, m:2 * m], in0=T[:, 1, 0:m], scalar=pw[:, 0:1], in1=T[:, 1, m:2 * m], op0=ALU.mult, op1=ALU.add)
                kb.op("dve", V.tensor_tensor, reads=[bpw], writes=[bpw], out=pw[:, 3:4], in0=pw[:, 0:1], in1=pw[:, 1:2], op=ALU.mult)
                kb.op("dve", V.tensor_tensor, reads=[bpw], writes=[bpw], out=pw[:, 4:5], in0=pw[:, 1:2], in1=pw[:, 1:2], op=ALU.mult)
                kb.op("dve", V.scalar_tensor_tensor, reads=[bpw], writes=[bpw], out=pw[:, 0:1], in0=pw[:, 0:1], scalar=pw[:, 0:1], in1=pw[:, 4:5], op0=ALU.mult, op1=ALU.subtract)
                kb.op("dve", V.tensor_scalar, reads=[bpw], writes=[bpw], out=pw[:, 1:2], in0=pw[:, 3:4], scalar1=2.0, scalar2=None, op0=ALU.mult)
                m *= 2
        table(self.U, self.bU, col(10), col(9), True)
        table(self.L, self.bL, col(11), col(12), False)

    def segment(self, s, zg, bzg, idx, bidx, icol):
        c = self.c; kb = c.kb; nc = c.nc; V = nc.vector; G = nc.gpsimd; pv = self.pv; bpv = self.bpv
        col = lambda i: pv[:, i:i + 1]
        gather_rows(c, self.uT[:32, :], self.buT, zg, idx[:32, icol:icol + 1], bidx, bzg=bzg)
        bu = self.bu; bbu = self.bbu
        for t0 in range(0, NCOL, 512):
            w = min(512, NCOL - t0)
            (p1, b1), (p2, b2) = c.ps[0], c.ps[1]
            kb.op("pe", nc.tensor.matmul, reads=[self.buT, self.bBT], writes=[b1], out=p1[:, :w], lhsT=self.BT[:, 0, :], rhs=self.uT[:, t0:t0 + w], start=True, stop=True)
            kb.op("pe", nc.tensor.matmul, reads=[self.buT, self.bBT], writes=[b2], out=p2[:, :w], lhsT=self.BT[:, 1, :], rhs=self.uT[:, t0:t0 + w], start=True, stop=True)
            kb.op("dve", V.tensor_scalar, reads=[b1, bpv], writes=[bbu], out=bu[:, 0, t0:t0 + w], in0=p1[:, :w], scalar1=col(15), scalar2=None, op0=ALU.mult)
            kb.op("dve", V.scalar_tensor_tensor, reads=[b2, bpv, bbu], writes=[bbu], out=bu[:, 0, t0:t0 + w], in0=p2[:, :w], scalar=col(18), in1=bu[:, 0, t0:t0 + w], op0=ALU.mult, op1=ALU.add)
            kb.op("dve", V.tensor_scalar, reads=[b2, bpv], writes=[bbu], out=bu[:, 1, t0:t0 + w], in0=p2[:, :w], scalar1=col(15), scalar2=None, op0=ALU.mult)
            kb.op("dve", V.scalar_tensor_tensor, reads=[b1, bpv, bbu], writes=[bbu], out=bu[:, 1, t0:t0 + w], in0=p1[:, :w], scalar=col(16), in1=bu[:, 1, t0:t0 + w], op0=ALU.mult, op1=ALU.add)
        U = self.U; bU = self.bU; L = self.L; bL = self.bL; rr = self.rr; brr = self.brr; ww = self.ww; bww = self.bww; t1 = self.t1; bt1 = self.bt1
        regs = [(0, 2048, 0), (2048, 32, 1 + 2 * s), (2112, 32, 2 + 2 * s)]
        for (c0, ln, slot) in regs:
            sl = slice(c0, c0 + ln); ul = slice(0, ln)
            kb.op("dve", V.tensor_tensor, reads=[bU, bbu], writes=[brr], out=rr[:, 0, sl], in0=U[:, 0, ul], in1=bu[:, 0, sl], op=ALU.mult)
            kb.op("pool", G.tensor_tensor, reads=[bU, bbu], writes=[bt1], out=t1[:, sl], in0=U[:, 1, ul], in1=bu[:, 1, sl], op=ALU.mult)
            kb.op("dve", V.tensor_tensor, reads=[brr, bt1], writes=[brr], out=rr[:, 0, sl], in0=rr[:, 0, sl], in1=t1[:, sl], op=ALU.add)
            kb.op("dve", V.tensor_tensor, reads=[bU, bbu], writes=[brr], out=rr[:, 1, sl], in0=U[:, 0, ul], in1=bu[:, 1, sl], op=ALU.mult)
            kb.op("pool", G.tensor_tensor, reads=[bU, bbu], writes=[bt1], out=t1[:, sl], in0=U[:, 1, ul], in1=bu[:, 0, sl], op=ALU.mult)
            kb.op("dve", V.tensor_tensor, reads=[brr, bt1], writes=[brr], out=rr[:, 1, sl], in0=rr[:, 1, sl], in1=t1[:, sl], op=ALU.subtract)
        for k in range(2):
            kb.op("dve", V.tensor_tensor_scan, reads=[brr, self.brho], writes=[bww], out=ww[:, k, :], data0=self.rho[:], data1=rr[:, k, :], initial=0.0, op0=ALU.mult, op1=ALU.add)
        X = self.X; bX = self.bX
        for (c0, ln, slot) in regs:
            sl = slice(c0, c0 + ln); ul = slice(0, ln)
            xr = X[:, 0, slot:slot + 1]; xi = X[:, 1, slot:slot + 1]
            kb.op("dve", V.tensor_tensor, reads=[bU, bww], writes=[brr], out=rr[:, 0, sl], in0=U[:, 0, ul], in1=ww[:, 0, sl], op=ALU.mult)
            kb.op("pool", G.tensor_tensor, reads=[bU, bww], writes=[bt1], out=t1[:, sl], in0=U[:, 1, ul], in1=ww[:, 1, sl], op=ALU.mult)
            kb.op("dve", V.tensor_tensor, reads=[brr, bt1], writes=[brr], out=rr[:, 0, sl], in0=rr[:, 0, sl], in1=t1[:, sl], op=ALU.subtract)
            kb.op("dve", V.tensor_tensor, reads=[bU, bww], writes=[brr], out=rr[:, 1, sl], in0=U[:, 0, ul], in1=ww[:, 1, sl], op=ALU.mult)
            kb.op("pool", G.tensor_tensor, reads=[bU, bww], writes=[bt1], out=t1[:, sl], in0=U[:, 1, ul], in1=ww[:, 0, sl], op=ALU.mult)
            kb.op("dve", V.tensor_tensor, reads=[brr, bt1], writes=[brr], out=rr[:, 1, sl], in0=rr[:, 1, sl], in1=t1[:, sl], op=ALU.add)
            kb.op("dve", V.tensor_scalar, reads=[bX], writes=[self.bpw], out=self.pw[:, 5:6], in0=xi, scalar1=-1.0, scalar2=None, op0=ALU.mult)
            kb.op("dve", V.scalar_tensor_tensor, reads=[bL, bX, brr], writes=[brr], out=rr[:, 0, sl], in0=L[:, 0, ul], scalar=xr, in1=rr[:, 0, sl], op0=ALU.mult, op1=ALU.add)
            kb.op("dve", V.scalar_tensor_tensor, reads=[bL, self.bpw, brr], writes=[brr], out=rr[:, 0, sl], in0=L[:, 1, ul], scalar=self.pw[:, 5:6], in1=rr[:, 0, sl], op0=ALU.mult, op1=ALU.add)
            kb.op("dve", V.scalar_tensor_tensor, reads=[bL, bX, brr], writes=[brr], out=rr[:, 1, sl], in0=L[:, 0, ul], scalar=xi, in1=rr[:, 1, sl], op0=ALU.mult, op1=ALU.add)
            kb.op("dve", V.scalar_tensor_tensor, reads=[bL, bX, brr], writes=[brr], out=rr[:, 1, sl], in0=L[:, 1, ul], scalar=xr, in1=rr[:, 1, sl], op0=ALU.mult, op1=ALU.add)
            kb.op("dve", V.tensor_copy, reads=[brr], writes=[bX], out=X[:, 0, slot:slot + 1], in_=rr[:, 0, c0 + ln - 1:c0 + ln])
            kb.op("dve", V.tensor_copy, reads=[brr], writes=[bX], out=X[:, 1, slot:slot + 1], in_=rr[:, 1, c0 + ln - 1:c0 + ln])
        kb.op("pool", G.tensor_scalar, reads=[brr], writes=[bt1], out=t1[:], in0=rr[:, 1, :], scalar1=-1.0, scalar2=None, op0=ALU.mult)
        for t0 in range(0, NCOL, 512):
            w = min(512, NCOL - t0)
            p1, b1 = c.ps[2]
            kb.op("pe", nc.tensor.matmul, reads=[brr, self.bCT], writes=[b1], out=p1[:32, :w], lhsT=self.CT[:, 0, :], rhs=rr[:, 0, t0:t0 + w], start=True, stop=False)
            kb.op("pe", nc.tensor.matmul, reads=[bt1, self.bCT], writes=[b1], out=p1[:32, :w], lhsT=self.CT[:, 1, :], rhs=t1[:, t0:t0 + w], start=False, stop=True)
            kb.op("dve", V.scalar_tensor_tensor, reads=[b1, self.buT, self.bdv], writes=[self.byT], out=self.yT[:, t0:t0 + w], in0=self.uT[:, t0:t0 + w], scalar=self.dv[:, 0:1], in1=p1[:32, :w], op0=ALU.mult, op1=ALU.add)

QW = 1280
NEG = -30000.0

class SB:
    def __init__(self, c, prm_d, nseg=8):
        self.c = c; kb = c.kb; nc = c.nc; sb = c.sb; self.nseg = nseg; self.prm_d = prm_d
        NK = 2048 * nseg
        self.KT, self.bKT = sb("sbKT", [64, NK], BF16); self.V, self.bV = sb("sbV", [128, 16 * nseg, 64], BF16)
        self.Q, self.bQ = sb("sbQ", [64, 1024 * nseg], BF16)
        self.KS, self.bKS = sb("sbKS", [64, 16, 32], BF16); self.VS, self.bVS = sb("sbVS", [32, 16, 64], BF16); self.QS, self.bQS = sb("sbQS", [64, 16, 32], BF16)
        self.MB, self.bMB = sb("sbMB", [128, 8, 512], BF16); self.DM, self.bDM = sb("sbDM", [32, 32], BF16)
        self.idb, self.bidb = sb("sbidb", [128, 128], BF16); self.trin, self.btrin = sb("sbtrin", [128, 128], BF16); self.onen, self.bonen = sb("sbonen", [128, 128], BF16)
        self.idf, self.bidf = sb("sbidf", [64, 64])
        self.stg, self.bstg = sb("sbstg", [64, SEGW]); self.stq, self.bstq = sb("sbstq", [64, QW])
        self.e = [sb("sbe%d" % i, [128, 512]) for i in range(2)]
        self.L = [sb("sbL%d" % i, [128, 512], BF16) for i in range(3)]
        self.R = [sb("sbR%d" % i, [128, 512], BF16) for i in range(3)]
        self.a = [sb("sba%d" % i, [128, 512], BF16) for i in range(3)]
        self.L0, self.bL0 = sb("sbLfirst", [32, 32], BF16)
        self.zero, self.bzero = sb("sbzero", [128, 512], BF16)
        self.oT, self.boT = sb("sboT", [64, SEGW]); self.oS, self.boS = sb("sboS", [64, 16, 64])
        kb.op("pool", nc.gpsimd.memset, writes=[self.boT], ap=self.oT[:], constant=0.0)
        self.kc = [sb("sbkc%d" % i, [64, 4096], BF16) for i in range(2)]; self.vc = [sb("sbvc%d" % i, [128, 32, 64], BF16) for i in range(2)]
        kb.dma("pool", self.MB[:], prm_d["mb"], writes=[self.bMB]); kb.dma("pool", self.DM[:], prm_d["dmask"], writes=[self.bDM])
        kb.dma("pool", self.idb[:], prm_d["identb"], writes=[self.bidb]); kb.dma("pool", self.trin[:], prm_d["trin"], writes=[self.btrin])
        kb.op("pool", nc.gpsimd.memset, writes=[self.bonen], ap=self.onen[:], constant=-1.0)
        kb.op("pool", nc.gpsimd.memset, writes=[self.bzero], ap=self.zero[:], constant=0.0)
        kb.op("pool", nc.gpsimd.memset, writes=[self.boS], ap=self.oS[:], constant=0.0)

    def load_segment(self, s, zg, bzg, zq, bzq, idx, bidx, icol):
        c = self.c; kb = c.kb; nc = c.nc
        stg, bstg = self.stg, self.bstg
        gather_rows(c, stg[:, :], bstg, zg, idx[:64, icol["k"]:icol["k"] + 1], bidx, bzg=bzg)
        kb.op("act", nc.scalar.copy, reads=[bstg], writes=[self.bKT], out=self.KT[:, 2048 * s:2048 * (s + 1)], in_=stg[:, 0:2048])
        kb.op("act", nc.scalar.copy, reads=[bstg], writes=[self.bKS], out=self.KS[:, 2 * s, :], in_=stg[:, 2048:2080])
        kb.op("act", nc.scalar.copy, reads=[bstg], writes=[self.bKS], out=self.KS[:, 2 * s + 1, :], in_=stg[:, 2112:2144])
        gather_rows(c, stg[:, :], bstg, zg, idx[:64, icol["v"]:icol["v"] + 1], bidx, bzg=bzg)
        for g in range(2):
            p, pb = c.ps[4 + g]
            for b in range(8):
                blk = g * 8 + b
                kb.op("pe", nc.tensor.transpose, reads=[bstg, c.bcm], writes=[pb], out=p[:, b * 64:(b + 1) * 64], in_=stg[:, blk * 128:(blk + 1) * 128], identity=c.ident)
            kb.op("act", nc.scalar.copy, reads=[pb], writes=[self.bV], out=self.V[:, 16 * s + g * 8:16 * s + g * 8 + 8, :], in_=p[:, :].rearrange("p (b d) -> p b d", d=64))
        p, pb = c.ps[4]
        for i, c0 in enumerate([2048, 2112]):
            kb.op("pe", nc.tensor.transpose, reads=[bstg, c.bcm], writes=[pb], out=p[:32, i * 64:(i + 1) * 64], in_=stg[:, c0:c0 + 32], identity=c.ident)
        kb.op("act", nc.scalar.copy, reads=[pb], writes=[self.bVS], out=self.VS[:, 2 * s:2 * s + 2, :], in_=p[:32, 0:128].rearrange("p (b d) -> p b d", d=64))
        stq, bstq = self.stq, self.bstq
        gather_rows(c, stq[:, :], bstq, zq, idx[:64, icol["q"]:icol["q"] + 1], bidx, bzg=bzq)
        kb.op("act", nc.scalar.mul, reads=[bstq], writes=[self.bQ], out=self.Q[:, 1024 * s:1024 * (s + 1)], in_=stq[:, 0:1024], mul=0.125)
        kb.op("act", nc.scalar.mul, reads=[bstq], writes=[self.bQS], out=self.QS[:, 2 * s, :], in_=stq[:, 1024:1056], mul=0.125)
        kb.op("act", nc.scalar.mul, reads=[bstq], writes=[self.bQS], out=self.QS[:, 2 * s + 1, :], in_=stq[:, 1088:1120], mul=0.125)

    def attend(self, steps, qap, bq, N, obank, out_ap, bout):
        c = self.c; kb = c.kb; nc = c.nc
        n = len(steps)
        po, pob = c.ps[obank]
        st = {}
        racc = (self.zero, self.bzero)
        first_nk = steps[0]["nk"]
        for i in range(n + 2):
            if i < n:
                S = steps[i]; nk = S["nk"]
                z, zb = c.ps[i % 3]
                kT, bkT = S["kT"]
                kb.op("pe", nc.tensor.matmul, reads=[bkT, bq], writes=[zb], out=z[:nk, :N], lhsT=kT, rhs=qap, start=True, stop=(S["mask"] is None))
                if S["mask"] is not None:
                    m, bm = S["mask"]
                    kb.op("pe", nc.tensor.matmul, reads=[bm, self.bidb], writes=[zb], out=z[:nk, :N], lhsT=self.idb[:nk, :nk], rhs=m, start=False, stop=True)
                e, be = self.e[i % 2]; L, bL = self.L[i % 3]
                if i == 0 and nk != 128: L, bL = self.L0, self.bL0
                kb.op("act", nc.scalar.activation, reads=[zb], writes=[be], out=e[:nk, :N], in_=z[:nk, :N], func=AF.Exp)
                kb.op("act", nc.scalar.activation, reads=[be], writes=[bL], out=L[:nk, :N], in_=e[:nk, :N], func=AF.Ln, bias=1.0)
                st[i] = dict(L=(L, bL), racc=racc, nk=nk)
                if i >= 1 or nk == 128:
                    Rn, bRn = self.R[i % 3]
                    if nk == 128:
                        kb.op("pool", nc.gpsimd.tensor_tensor, reads=[bL, racc[1]], writes=[bRn], out=Rn[:, :N], in0=L[:, :N], in1=racc[0][:, :N], op=ALU.add)
                        racc = (Rn, bRn)
            if 1 <= i <= n:
                j = i - 1; S = steps[j]; nk = S["nk"]; z, zb = c.ps[j % 3]
                L, bL = st[j]["L"]; ra, bra = st[j]["racc"]
                terms = [(self.trin[:nk, :nk], self.btrin, L[:nk, :N], bL)]
                if j >= 1:
                    if first_nk != 128:
                        L0, bL0 = st[0]["L"]
                        terms.append((self.onen[:first_nk, :nk], self.bonen, L0[:first_nk, :N], bL0))
                    if j >= (2 if first_nk != 128 else 1):
                        terms.append((self.onen[:, :nk], self.bonen, ra[:, :N], bra))
                for ti, (lh, blh, rh, brh) in enumerate(terms):
                    kb.op("pe", nc.tensor.matmul, reads=[blh, brh], writes=[zb], out=z[:nk, :N], lhsT=lh, rhs=rh, start=False, stop=(ti == len(terms) - 1))
                a, ba = self.a[j % 3]
                kb.op("act", nc.scalar.activation, reads=[zb], writes=[ba], out=a[:nk, :N], in_=z[:nk, :N], func=AF.Exp)
                st[j]["a"] = (a, ba)
            if i >= 2:
                j = i - 2; S = steps[j]; nk = S["nk"]
                a, ba = st[j]["a"]; v, bv = S["v"]
                kb.op("pe", nc.tensor.matmul, reads=[ba, bv], writes=[pob], out=po[:64, :N], lhsT=v, rhs=a[:nk, :N], start=(j == 0), stop=(j == n - 1))
        kb.op("act", nc.scalar.copy, reads=[pob], writes=[bout], out=out_ap, in_=po[:64, :N])

    def prompt_tile(self, m):
        steps = []
        for kbk in range(8 * m + 7, -1, -1):
            mask = (self.MB[:, kbk - 8 * m, :], self.bMB) if kbk >= 8 * m else None
            steps.append(dict(kT=(self.KT[:, kbk * 128:(kbk + 1) * 128], self.bKT), v=(self.V[:, kbk, :], self.bV), nk=128, mask=mask))
        self.attend(steps, self.Q[:, 512 * m:512 * (m + 1)], self.bQ, 512, 3, self.oT[:, 512 * (m % 2):512 * (m % 2 + 1)], self.boT)

    def sample_seq(self, q):
        c = self.c; kb = c.kb
        kc, bkc = self.kc[q % 2]; vc, bvc = self.vc[q % 2]
        kb.dma("pool", kc[:], self.prm_d["kc"][q], writes=[bkc])
        kb.dma("pool", vc[:], self.prm_d["vc"][q].rearrange("(b p) d -> p b d", p=128), writes=[bvc])
        steps = [dict(kT=(self.KS[:, q, :], self.bKS), v=(self.VS[:, q, :], self.bVS), nk=32, mask=(self.DM[:], self.bDM))]
        for kbk in range(31, -1, -1):
            steps.append(dict(kT=(kc[:, kbk * 128:(kbk + 1) * 128], bkc), v=(vc[:, kbk, :], bvc), nk=128, mask=None))
        self.attend(steps, self.QS[:, q, :], self.bQS, 32, 7, self.oS[:, q, 0:32], self.boS)

D = 2048; DFF = 2048; NIN = 3088; DMIX = 1024
TT = 2112
RPR = 2320
OPR = 1280
TILES = [(0, 512), (512, 512), (1024, 512), (1536, 576)]

def win_chunks():
    ch = []
    for c in range(6): ch.append((128 * c, 128, 'z', 128 * c))
    ch.append((768, 8, 'z', 768)); ch.append((776, 128, 'g', 0)); ch.append((904, 128, 'g', 128))
    for c in range(6): ch.append((1032 + 128 * c, 128, 'z', 776 + 128 * c))
    ch.append((1800, 8, 'z', 1544)); ch.append((1808, 128, 'g', 256)); ch.append((1936, 128, 'g', 384))
    ch.append((2064, 128, 'z', 1552)); ch.append((2192, 128, 'z', 1680))
    ch.append((2320, 128, 'q', 0)); ch.append((2448, 128, 'q', 128))
    ch.append((2576, 128, 'z', 1808)); ch.append((2704, 128, 'z', 1936)); ch.append((2832, 128, 'z', 2064)); ch.append((2960, 128, 'z', 2192))
    return ch
WCH = win_chunks()

class Tab:
    def __init__(self):
        self.cols = {}; self.n = 0
    def add(self, key):
        self.cols[key] = self.n; self.n += 1
def make_tab():
    t = Tab()
    zi = 0
    for k, (c0, w, kind, r0) in enumerate(WCH):
        if kind == 'z':
            for b in range(9): t.add(('az', k, b))
        if kind == 'q':
            for p in range(2):
                for b in range(5): t.add(('aq', k, p, b))
    for s in range(8):
        for nm in ['gq', 'gk', 'gv', 'gg', 'mq', 'mk', 'mv', 'mg', 's5', 'sk', 'sv', 'sq', 'og', 'om', 'os', 'ob']:
            t.add((nm, s))
    for cc in range(6):
        for b in range(9): t.add(('co', cc, b))
    for cc in range(2):
        for p in range(2):
            for b in range(5): t.add(('cs', cc, p, b))
    return t
TAB = make_tab()

def tab_values(core, fused=False):
    h, r = core // 2, core % 2
    cR = core * RPR if fused else 0; cQ = core * 512 if fused else 0; SR_ = RPR if fused else 484; SQ_ = 512 if fused else 64
    cO = core * OPR if fused else 0; OJ = OPR if fused else 160; cC = core * 160 if fused else 0
    T = np.zeros((128, TAB.n), np.int32)
    P = np.arange(128)
    for k, (c0, w, kind, r0) in enumerate(WCH):
        if kind == 'z':
            for b in range(9): T[:, TAB.cols[('az', k, b)]] = (cR + r0 + P) * 9 + b
        if kind == 'q':
            for p in range(2):
                for b in range(5): T[:, TAB.cols[('aq', k, p, b)]] = (cQ + p * 256 + r0 + P) * 5 + b
    for s in range(8):
        base = s * SR_
        if fused:
            oq, ok, ov, og = h * 64, 256 + h * 64, 512 + h * 64 + r * 32, 768 + h
            mq, mk, mv, mg = 776 + h * 64, 776 + 256 + h * 64, 776 + 512 + h * 64 + r * 32, 1544 + h
            o5, osk, osv = 1552 + 32 * core, 1808 + h * 64, 2064 + h * 64; gstep = 4
            sq0 = s * 512 + r * 256 + h * 64
        else:
            oq, ok, ov, og = 0, 64, 128, 160
            mq, mk, mv, mg = 162, 226, 290, 322
            o5, osk, osv = 324, 356, 420; gstep = 1
            sq0 = s * 64
        T[:, TAB.cols[('gq', s)]] = base + oq + P; T[:, TAB.cols[('gk', s)]] = base + ok + P
        T[:, TAB.cols[('gv', s)]] = base + ov + P; T[:, TAB.cols[('gg', s)]] = base + og + gstep * P
        T[:, TAB.cols[('mq', s)]] = base + mq + P; T[:, TAB.cols[('mk', s)]] = base + mk + P
        T[:, TAB.cols[('mv', s)]] = base + mv + P; T[:, TAB.cols[('mg', s)]] = base + mg + gstep * P
        T[:, TAB.cols[('s5', s)]] = base + o5 + P
        T[:, TAB.cols[('sk', s)]] = base + osk + P; T[:, TAB.cols[('sv', s)]] = base + osv + P
        T[:, TAB.cols[('sq', s)]] = sq0 + P
        ob = cO + s * 160
        T[:, TAB.cols[('og', s)]] = ob + P; T[:, TAB.cols[('om', s)]] = ob + 32 + P; T[:, TAB.cols[('os', s)]] = ob + 64 + P; T[:, TAB.cols[('ob', s)]] = ob + 96 + P
    for cc in range(6):
        moff = [0, 0, 32, 32, 64, 64][cc]; half = cc % 2
        j = (half * 128 + P) // 32; i = P % 32
        for b in range(9): T[:, TAB.cols[('co', cc, b)]] = (j * OJ + cC + moff + i) * 9 + b
    for cc in range(2):
        head = cc * 2 + P // 64; i = P % 64
        for p in range(2):
            for b in range(5): T[:, TAB.cols[('cs', cc, p, b)]] = ((2 * head + p) * OJ + cC + 96 + i) * 9 + b
    return np.clip(T, 0, None)

class Rot:
    def __init__(self, items): self.items = items; self.i = 0
    def next(self):
        x = self.items[self.i % len(self.items)]; self.i += 1; return x

def barrier(kb):
    engs = list(kb.eng.keys())
    for e in engs:
        for e2 in engs:
            if e2 != e: kb.wait(e, (kb.cur[e2][0], kb.cur[e2][1]))
        for slot in range(kb.NDMA):
            if kb.dcnt[slot] > 0: kb.wait(e, (kb.dsem[slot], 16 * kb.dcnt[slot]))

def collective(kb, nc, kind, src, bsrc, dst, bdst):
    kb.deps("pool", [bsrc], [bdst])
    c = kb.cur["pool"]
    if c[1] >= kb.EPOCH:
        kb._newsem("pool"); c = kb.cur["pool"]
    ins = nc.gpsimd.collective_compute(kind, ALU.add, replica_groups=[list(range(8))], ins=[src], outs=[dst])
    c[1] += 1; ins.then_inc(c[0], 1)
    kb.mark((c[0], c[1]), [bsrc], [bdst]); kb.ninst += 1

def token_phase(G, lc, la, final):
    nc = G.nc; kb = G.kb; Dm = G.D; ps = G.ps
    has_c = lc is not None; has_a = la is not None
    TMAX = 576
    with ExitStack() as st:
        G.uid += 1; uid = G.uid
        def sb(name, shape, dt=F32): return st.enter_context(nc.sbuf_tensor("t%d_" % uid + name, list(shape), dt)), Buf(name)
        xT, bxT = sb("xT", [128, 16, TMAX]); hT, bhT = sb("hT", [128, 16, TMAX], BF16); aT, baT = sb("aT", [128, 16, TMAX], BF16)
        wbufA = [sb("wA%d" % i, [128, 16, 256], BF16) for i in range(2)]; wbufB = [sb("wB%d" % i, [128, 16, 256], BF16) for i in range(2)]
        sq = [sb("sq%d" % i, [128, 512], BF16) for i in range(2)]
        rstd, brstd = sb("rstd", [128, TMAX])
        sg = [sb("sg%d" % i, [128, 512]) for i in range(2)]
        zst = [sb("zst%d" % i, [128, TMAX]) for i in range(2)]
        ones, bones = sb("ones", [128, 128], BF16); ones64, bones64 = sb("ones64", [128, 128], BF16)
        epsT, beps = sb("epsT", [128, 1]); vecs, bvecs = sb("vecs", [128, 5, 16])
        zblk, bzblk = sb("zblk", [128, 256]); sblk = [sb("sblk%d" % i, [128, 256]) for i in range(2)]
        if has_c:
            oTs, boT = sb("oTs", [128, 8, TMAX]); gTs, bgT = sb("gTs", [128, 4, TMAX]); mT, bmT = sb("mT", [128, 8, TMAX], BF16)
            tmpc = [sb("tmpc%d" % i, [128, 512]) for i in range(3)]; wglu, bwglu = sb("wglu", [128, 2, 256], BF16)
            ostg, bostg = sb("ostg", [128, 256])
        if has_a:
            kvs = [sb("kvs%d" % i, [128, 512]) for i in range(2)]
        V = nc.vector
        kb.op("pool", nc.gpsimd.memset, writes=[bones], ap=ones[:], constant=1.0)
        kb.op("pool", nc.gpsimd.memset, writes=[bones64], ap=ones64[:], constant=0.0)
        kb.op("pool", nc.gpsimd.memset, writes=[bones64], ap=ones64[0:64, 0:64], constant=1.0)
        kb.op("pool", nc.gpsimd.memset, writes=[bones64], ap=ones64[64:128, 64:128], constant=1.0)
        kb.op("pool", nc.gpsimd.memset, writes=[beps], ap=epsT[:], constant=1e-6)
        for t_, b_ in sblk: kb.op("pool", nc.gpsimd.memset, writes=[b_], ap=t_[:], constant=0.0)
        if has_a:
            kb.dma("sp", vecs[:, 0, :], Dm["tvec"][G.lw(la), :, 0, :], writes=[bvecs]); kb.dma("sp", vecs[:, 1, :], Dm["tvec"][G.lw(la), :, 1, :], writes=[bvecs])
        if has_c:
            kb.dma("sp", vecs[:, 2, :], Dm["tvecc"][G.lw(lc), :, 2, :], writes=[bvecs]); kb.dma("sp", vecs[:, 4, :], Dm["tvecc"][G.lw(lc), :, 3, :], writes=[bvecs])
            kb.dma("pool", wglu[:], Dm["wglu"][G.lw(lc)].rearrange("(c p) f -> p c f", p=128), writes=[bwglu])
            if final: kb.dma("sp", vecs[:, 3, :], Dm["nf"], writes=[bvecs])
        sqr = Rot(sq); sgr = Rot(sg); zr = Rot(zst)
        itab = G.itab; bitab = G.bitab
        tcol = lambda key: itab[:, TAB.cols[key]:TAB.cols[key] + 1]

        def sumsq_rstd(src, rds, nchunk, gw, lhs_ones, bl, pst, scale, dst, bdst):
            p, pb = pst
            for c in range(nchunk):
                s, bs = sqr.next()
                kb.op("act", nc.scalar.activation, reads=rds, writes=[bs], out=s[:, :gw], in_=src(c), func=AF.Square)
                kb.op("pe", nc.tensor.matmul, reads=[bs, bl], writes=[pb], out=p[:, :gw], lhsT=lhs_ones, rhs=s[:, :gw], start=(c == 0), stop=(c == nchunk - 1))
            kb.op("act", nc.scalar.activation, reads=[pb, beps], writes=[bdst], out=dst, in_=p[:, :gw], func=AF.Sqrt, scale=scale, bias=epsT[:, 0:1])
            kb.op("dve", V.reciprocal, reads=[bdst], writes=[bdst], out=dst, in_=dst)

        def norm_to(vi, grps, dst, bdst):
            for (g0, gw) in grps:
                sumsq_rstd(lambda c: xT[:, c, g0:g0 + gw], [bxT], 16, gw, ones[:], bones, ps[0], 1.0 / D, rstd[:, g0:g0 + gw], brstd)
                for c in range(16):
                    kb.op("dve", V.scalar_tensor_tensor, reads=[bxT, brstd, bvecs], writes=[bdst], out=dst[:, c, g0:g0 + gw], in0=xT[:, c, g0:g0 + gw],
                          scalar=vecs[:, vi, c:c + 1], in1=rstd[:, g0:g0 + gw], op0=ALU.mult, op1=ALU.mult)

        cur = {"ti": 0}
        def load_w(wt, wb, W, c0, cw, nk, key=None):
            if key is None:
                kb.dma("pool", wt[:, :nk, :cw], W[:, c0:c0 + cw].rearrange("(c p) f -> p c f", p=128), writes=[wb]); return
            if key not in G.wscr:
                G.uid += 1
                G.wscr[key] = (nc.dram_tensor("wscr%d_%s" % (G.uid, key), [W.shape[0], W.shape[1]], BF16, kind="Internal").ap(), {})
            scr, bufs = G.wscr[key]
            bk = bufs.setdefault(c0, Buf("scr"))
            if cur["ti"] == 0:
                kb.dma("pool", wt[:, :nk, :cw], W[:, c0:c0 + cw].rearrange("(c p) f -> p c f", p=128), writes=[wb])
                kb.dma("act", scr[:, c0:c0 + cw].rearrange("(c p) f -> p c f", p=128), wt[:, :nk, :cw], reads=[wb], writes=[bk])
            else:
                kb.dma("sp", wt[:, :nk, :cw], scr[:, c0:c0 + cw].rearrange("(c p) f -> p c f", p=128), reads=[bk], writes=[wb])

        def ffn(wg_d, wu_d, wd_d, vi, grps, kp):
            norm_to(vi, grps, hT, bhT)
            pg = Rot([ps[1], ps[2]]); pu = Rot([ps[3], ps[4]]); pd = Rot([ps[5], ps[6]])
            for b in range(DFF // 256):
                (wgt, wgb), (wut, wub) = wbufA[b % 2], wbufB[b % 2]
                load_w(wgt, wgb, wg_d, b * 256, 256, 16, kp + 'g'); load_w(wut, wub, wu_d, b * 256, 256, 16, kp + 'u')
                for ci in range(2):
                    f = b * 2 + ci
                    for (g0, gw) in grps:
                        (p1, b1), (p2, b2) = pg.next(), pu.next()
                        for k in range(16):
                            kb.op("pe", nc.tensor.matmul, reads=[wgb, bhT], writes=[b1], out=p1[:, :gw], lhsT=wgt[:, k, ci * 128:(ci + 1) * 128], rhs=hT[:, k, g0:g0 + gw], start=(k == 0), stop=(k == 15))
                        for k in range(16):
                            kb.op("pe", nc.tensor.matmul, reads=[wub, bhT], writes=[b2], out=p2[:, :gw], lhsT=wut[:, k, ci * 128:(ci + 1) * 128], rhs=hT[:, k, g0:g0 + gw], start=(k == 0), stop=(k == 15))
                        s, bs = sgr.next()
                        kb.op("act", nc.scalar.activation, reads=[b1], writes=[bs], out=s[:, :gw], in_=p1[:, :gw], func=AF.Silu)
                        kb.op("dve", V.tensor_tensor, reads=[bs, b2], writes=[baT], out=aT[:, f, g0:g0 + gw], in0=s[:, :gw], in1=p2[:, :gw], op=ALU.mult)
            for b in range(D // 256):
                wdt, wdb = wbufA[b % 2]
                load_w(wdt, wdb, wd_d, b * 256, 256, 16, kp + 'd')
                for ci in range(2):
                    dch = b * 2 + ci
                    for (g0, gw) in grps:
                        p1, b1 = pd.next()
                        for k in range(16):
                            kb.op("pe", nc.tensor.matmul, reads=[wdb, baT], writes=[b1], out=p1[:, :gw], lhsT=wdt[:, k, ci * 128:(ci + 1) * 128], rhs=aT[:, k, g0:g0 + gw], start=(k == 0), stop=(k == 15))
                        kb.op("dve", V.scalar_tensor_tensor, reads=[b1, bxT], writes=[bxT], out=xT[:, dch, g0:g0 + gw], in0=p1[:, :gw], scalar=0.5, in1=xT[:, dch, g0:g0 + gw], op0=ALU.mult, op1=ALU.add)

        xsrc = G.xsrc
        for ti, (t0, T) in enumerate(TILES):
            cur["ti"] = ti
            grps = [(0, 512)] + ([(512, 64)] if T > 512 else [])
            kb.dma("sp", xT[:, :, :T], xsrc[:, t0:t0 + T].rearrange("(c p) t -> p c t", p=128), reads=[G.bXR], writes=[bxT])
            if has_c:
                kb.dma("sp", gTs[:, :, :T], G.GSr[:, t0:t0 + T].rearrange("(c p) t -> p c t", p=128), reads=[G.bGS], writes=[bgT])
                OR9 = G.OR.rearrange("r (b c) -> (r b) c", c=256)
                for cc in range(6):
                    for bi in range(2):
                        gather_rows(G, oTs[:, cc, bi * 256:(bi + 1) * 256], boT, OR9, tcol(('co', cc, 2 * ti + bi)), bitab, bzg=G.bOR)
                    if T > 512:
                        gather_rows(G, ostg[:, :], bostg, OR9, tcol(('co', cc, 8)), bitab, bzg=G.bOR)
                        kb.op("pool", nc.gpsimd.tensor_copy, reads=[bostg], writes=[boT], out=oTs[:, cc, 512:544], in_=ostg[:, 0:32])
                        kb.op("pool", nc.gpsimd.tensor_copy, reads=[bostg], writes=[boT], out=oTs[:, cc, 544:576], in_=ostg[:, 64:96])
                for cc in range(2):
                    for p in range(2):
                        gather_rows(G, ostg[:, :], bostg, OR9, tcol(('cs', cc, p, ti)), bitab, bzg=G.bOR)
                        dst = oTs[:, 6 + cc, 0:512].rearrange("q (a b c) -> q a b c", a=2, b=2)[:, :, p, :]
                        kb.op("pool", nc.gpsimd.tensor_copy, reads=[bostg], writes=[boT], out=dst, in_=ostg[:, :].rearrange("q (a c) -> q a c", a=2))
                    if T > 512:
                        gather_rows(G, ostg[:, :], bostg, OR9, tcol(('cs', cc, 0, 4)), bitab, bzg=G.bOR)
                        kb.op("pool", nc.gpsimd.tensor_copy, reads=[bostg], writes=[boT], out=oTs[:, 6 + cc, 512:544], in_=ostg[:, 0:32])
                        kb.op("pool", nc.gpsimd.tensor_copy, reads=[bostg], writes=[boT], out=oTs[:, 6 + cc, 544:576], in_=ostg[:, 64:96])
                cv = lambda j: vecs[:, 4, j:j + 1]
                for (g0, gw) in grps:
                    sl = slice(g0, g0 + gw)
                    for c in range(2):
                        sumsq_rstd(lambda cc_: oTs[:, c, sl], [boT], 1, gw, ones64[:], bones64, ps[0], 1.0 / 64, rstd[:, sl], brstd)
                        t1, bt1 = tmpc[0]; t2, bt2 = tmpc[1]
                        kb.op("act", nc.scalar.activation, reads=[bgT], writes=[bt1], out=t1[:, :gw], in_=gTs[:, c, sl], func=AF.Silu)
                        kb.op("dve", V.scalar_tensor_tensor, reads=[boT, brstd, bvecs], writes=[bt2], out=t2[:, :gw], in0=oTs[:, c, sl], scalar=cv(c), in1=rstd[:, sl], op0=ALU.mult, op1=ALU.mult)
                        kb.op("dve", V.tensor_tensor, reads=[bt1, bt2], writes=[bmT], out=mT[:, c, sl], in0=t1[:, :gw], in1=t2[:, :gw], op=ALU.mult)
                    for c in range(2):
                        sumsq_rstd(lambda cc_: oTs[:, 2 + c, sl], [boT], 1, gw, ones64[:], bones64, ps[0], 1.0 / 64, rstd[:, sl], brstd)
                        t1, bt1 = tmpc[0]; t2, bt2 = tmpc[1]
                        kb.op("act", nc.scalar.activation, reads=[bgT], writes=[bt1], out=t1[:, :gw], in_=gTs[:, 2 + c, sl], func=AF.Sigmoid)
                        kb.op("dve", V.scalar_tensor_tensor, reads=[boT, brstd, bvecs], writes=[bt2], out=t2[:, :gw], in0=oTs[:, 2 + c, sl], scalar=cv(2 + c), in1=rstd[:, sl], op0=ALU.mult, op1=ALU.mult)
                        kb.op("dve", V.tensor_tensor, reads=[bt1, bt2], writes=[bmT], out=mT[:, 2 + c, sl], in0=t1[:, :gw], in1=t2[:, :gw], op=ALU.mult)
                    ych = [sg[0], sg[1]]; ybf = [sq[0], sq[1]]
                    for c in range(2):
                        y, by = ych[c]; t1, bt1 = tmpc[0]; t2, bt2 = tmpc[1]
                        kb.op("act", nc.scalar.activation, reads=[boT], writes=[bt1], out=t1[:, :gw], in_=oTs[:, 4 + c, sl], func=AF.Square)
                        kb.op("dve", V.tensor_scalar, reads=[bt1], writes=[bt1], out=t1[:, :gw], in0=t1[:, :gw], scalar1=0.044715, scalar2=1.0, op0=ALU.mult, op1=ALU.add)
                        kb.op("dve", V.tensor_tensor, reads=[bt1, boT], writes=[bt2], out=t2[:, :gw], in0=t1[:, :gw], in1=oTs[:, 4 + c, sl], op=ALU.mult)
                        kb.op("act", nc.scalar.activation, reads=[bt2], writes=[bt1], out=t1[:, :gw], in_=t2[:, :gw], func=AF.Sigmoid, scale=1.5957691216)
                        kb.op("dve", V.tensor_tensor, reads=[bt1, boT], writes=[by], out=y[:, :gw], in0=t1[:, :gw], in1=oTs[:, 4 + c, sl], op=ALU.mult)
                        yq, byq = ybf[c]
                        kb.op("act", nc.scalar.copy, reads=[by], writes=[byq], out=yq[:, :gw], in_=y[:, :gw])
                    for c in range(2):
                        p1, b1 = ps[1 + c]
                        for k in range(2):
                            kb.op("pe", nc.tensor.matmul, reads=[bwglu, ybf[k][1]], writes=[b1], out=p1[:, :gw], lhsT=wglu[:, k, c * 128:(c + 1) * 128], rhs=ybf[k][0][:, :gw], start=(k == 0), stop=(k == 1))
                        t1, bt1 = tmpc[c]
                        kb.op("act", nc.scalar.activation, reads=[b1, bvecs], writes=[bt1], out=t1[:, :gw], in_=p1[:, :gw], func=AF.Sigmoid, bias=cv(4 + c))
                        kb.op("dve", V.tensor_tensor, reads=[bt1, ych[c][1]], writes=[ych[c][1]], out=ych[c][0][:, :gw], in0=t1[:, :gw], in1=ych[c][0][:, :gw], op=ALU.mult)
                    sumsq_rstd(lambda cc_: ych[cc_][0][:, :gw], [ych[0][1], ych[1][1]], 2, gw, ones[:], bones, ps[0], 1.0 / 256, rstd[:, sl], brstd)
                    for c in range(2):
                        kb.op("dve", V.scalar_tensor_tensor, reads=[ych[c][1], brstd, bvecs], writes=[bmT], out=mT[:, 4 + c, sl], in0=ych[c][0][:, :gw], scalar=cv(6 + c), in1=rstd[:, sl], op0=ALU.mult, op1=ALU.mult)
                    sumsq_rstd(lambda cc_: oTs[:, 6 + cc_, sl], [boT], 2, gw, ones[:], bones, ps[0], 1.0 / 256, rstd[:, sl], brstd)
                    for c in range(2):
                        kb.op("dve", V.scalar_tensor_tensor, reads=[boT, brstd, bvecs], writes=[bmT], out=mT[:, 6 + c, sl], in0=oTs[:, 6 + c, sl], scalar=cv(8 + c), in1=rstd[:, sl], op0=ALU.mult, op1=ALU.mult)
                pd = Rot([ps[5], ps[6]])
                for b in range(D // 256):
                    wt, wb = wbufA[b % 2]
                    load_w(wt, wb, Dm["wout"][G.lw(lc)], b * 256, 256, 8, "wout")
                    for ci in range(2):
                        dch = b * 2 + ci
                        for (g0, gw) in grps:
                            p1, b1 = pd.next()
                            for k in range(8):
                                kb.op("pe", nc.tensor.matmul, reads=[wb, bmT], writes=[b1], out=p1[:, :gw], lhsT=wt[:, k, ci * 128:(ci + 1) * 128], rhs=mT[:, k, g0:g0 + gw], start=(k == 0), stop=(k == 7))
                            kb.op("dve", V.tensor_tensor, reads=[b1, bxT], writes=[bxT], out=xT[:, dch, g0:g0 + gw], in0=p1[:, :gw], in1=xT[:, dch, g0:g0 + gw], op=ALU.add)
                ffn(Dm["wg2"][G.lw(lc)], Dm["wu2"][G.lw(lc)], Dm["wd2"][G.lw(lc)], 2, grps, "f2")
            if has_a:
                ffn(Dm["wg1"][G.lw(la)], Dm["wu1"][G.lw(la)], Dm["wd1"][G.lw(la)], 0, grps, "f1")
                norm_to(1, grps, hT, bhT)
                pd = Rot([ps[5], ps[6]])
                ZS9 = G.ZS.rearrange("r (b c) -> (r b) c", c=256); ZQ5 = G.ZQS.rearrange("r (b c) -> (r b) c", c=256)
                win = Dm["win"][G.lw(la)]
                for k, (c0, mw, kind, r0) in enumerate(WCH):
                    wt, wb = wbufA[k % 2]
                    load_w(wt, wb, win, c0, mw, 16, 'win')
                    z, bzs = zr.next()
                    for (g0, gw) in grps:
                        p1, b1 = pd.next()
                        for kk in range(16):
                            kb.op("pe", nc.tensor.matmul, reads=[wb, bhT], writes=[b1], out=p1[:mw, :gw], lhsT=wt[:, kk, 0:mw], rhs=hT[:, kk, g0:g0 + gw], start=(kk == 0), stop=(kk == 15))
                        kb.op("act", nc.scalar.copy, reads=[b1], writes=[bzs], out=z[:mw, g0:g0 + gw], in_=p1[:mw, :gw])
                    if kind == 'g':
                        kb.dma("sp", G.GSw[r0:r0 + mw, t0:t0 + T], z[:mw, :T], reads=[bzs], writes=[G.bGS])
                    elif kind == 'z':
                        for bi in range(2):
                            scatter_rows(G, z[:mw, bi * 256:(bi + 1) * 256], bzs, ZS9, tcol(('az', k, 2 * ti + bi))[:mw], bitab, G.bZS)
                        if T > 512:
                            sbk, bsbk = sblk[k % 2]
                            kb.op("pool", nc.gpsimd.tensor_copy, reads=[bzs], writes=[bsbk], out=sbk[:mw, 0:32], in_=z[:mw, 512:544])
                            kb.op("pool", nc.gpsimd.tensor_copy, reads=[bzs], writes=[bsbk], out=sbk[:mw, 64:96], in_=z[:mw, 544:576])
                            scatter_rows(G, sbk[:mw, :], bsbk, ZS9, tcol(('az', k, 8))[:mw], bitab, G.bZS)
                        if c0 < 768 and T > 512:
                            for i3, cc0 in enumerate([509, 541, 573]):
                                kb.dma("sp", G.Dout["convT"][G.lo(la), c0:c0 + mw, 3 * i3:3 * i3 + 3], z[:mw, cc0:cc0 + 3], reads=[bzs], writes=[G.bout])
                    else:
                        for p in range(2):
                            kb.op("pool", nc.gpsimd.tensor_copy, reads=[bzs], writes=[bzblk], out=zblk[:, :].rearrange("q (a c) -> q a c", a=2),
                                  in_=z[:, 0:512].rearrange("q (a b c) -> q a b c", a=2, b=2)[:, :, p, :])
                            scatter_rows(G, zblk[:, :], bzblk, ZQ5, tcol(('aq', k, p, ti)), bitab, G.bZQS)
                        if T > 512:
                            sbk, bsbk = sblk[k % 2]
                            kb.op("pool", nc.gpsimd.tensor_copy, reads=[bzs], writes=[bsbk], out=sbk[:, 0:32], in_=z[:, 512:544])
                            kb.op("pool", nc.gpsimd.tensor_copy, reads=[bzs], writes=[bsbk], out=sbk[:, 64:96], in_=z[:, 544:576])
                            for p in range(2):
                                scatter_rows(G, sbk[:, :], bsbk, ZQ5, tcol(('aq', k, p, 4)), bitab, G.bZQS)
                (wkt, wkb), (wvt, wvb) = wbufB[0], wbufB[1]
                load_w(wkt, wkb, win, 2576, 256, 16, 'wink'); load_w(wvt, wvb, win, 2832, 256, 16, 'winv')
                for b0 in range(0, T, 128):
                    bw = min(128, T - b0)
                    p1, b1 = pd.next()
                    for kk in range(16):
                        kb.op("pe", nc.tensor.matmul, reads=[wkb, bhT], writes=[b1], out=p1[:bw, 0:256], lhsT=hT[:, kk, b0:b0 + bw], rhs=wkt[:, kk, :], start=(kk == 0), stop=(kk == 15))
                    for kk in range(16):
                        kb.op("pe", nc.tensor.matmul, reads=[wvb, bhT], writes=[b1], out=p1[:bw, 256:512], lhsT=hT[:, kk, b0:b0 + bw], rhs=wvt[:, kk, :], start=(kk == 0), stop=(kk == 15))
                    kv, bkv = kvs[(b0 // 128) % 2]
                    kb.op("act", nc.scalar.copy, reads=[b1], writes=[bkv], out=kv[:bw, :], in_=p1[:bw, :])
                    kb.dma("sp", G.Dout["kvout"][G.lo(la), t0 + b0:t0 + b0 + bw, :], kv[:bw, :], reads=[bkv], writes=[G.bout])
            if final:
                for (g0, gw) in grps:
                    sumsq_rstd(lambda c: xT[:, c, g0:g0 + gw], [bxT], 16, gw, ones[:], bones, ps[0], 1.0 / D, rstd[:, g0:g0 + gw], brstd)
                    for c in range(16):
                        kb.op("dve", V.scalar_tensor_tensor, reads=[bxT, brstd, bvecs], writes=[bxT], out=xT[:, c, g0:g0 + gw], in0=xT[:, c, g0:g0 + gw],
                              scalar=vecs[:, 3, c:c + 1], in1=rstd[:, g0:g0 + gw], op0=ALU.mult, op1=ALU.mult)
                kb.dma("sp", G.Dout["yT"][:, t0:t0 + T].rearrange("(c p) t -> p c t", p=128), xT[:, :, :T], reads=[bxT], writes=[G.bout])
            else:
                kb.dma("sp", G.xdst[:, t0:t0 + T].rearrange("(c p) t -> p c t", p=128), xT[:, :, :T], reads=[bxT], writes=[G.bout])
        barrier(kb)
    G.first_phase = False

class GCtx:
    pass

def b_phase(G, l):
    nc = G.nc; kb = G.kb; Dm = G.D; Do = G.Dout
    tc = lambda nm, s: TAB.cols[(nm, s)]
    def new_ctx(st):
        c = Ctx(); c.nc = nc; c.st = st; c.kb = kb; c.ps = G.ps
        c.sb = lambda name, shape, dt=F32: (st.enter_context(nc.sbuf_tensor("b%d_" % G.uid + name, list(shape), dt)), Buf(name))
        G.uid += 1
        load_consts(c, Dm["cm"], Dm["rows"])
        return c
    with ExitStack() as st:
        c = new_ctx(st)
        g = GDN(c, {"convw": Dm["g_convw"][G.lw(l)], "alog": Dm["g_alog"][G.lw(l)], "dtb": Dm["g_dtb"][G.lw(l)], "convst": Dm["g_convst"][G.lw(l)], "s0": Dm["g_s0"][G.lw(l)]})
        for s in range(8):
            g.segment(s, G.ZR, G.bZR[s], G.itab, G.bitab, {"q": tc('gq', s), "k": tc('gk', s), "v": tc('gv', s), "g": tc('gg', s)})
            scatter_rows(c, g.la.oT[:32, :], g.la.boT, G.OS, G.itab[:32, tc('og', s):tc('og', s) + 1], G.bitab, G.bOS)
        kb.dma("sp", Do["gdnS"][G.lo(l)], g.la.S[:], reads=[g.la.bS], writes=[G.bout])
        barrier(kb)
    with ExitStack() as st:
        c = new_ctx(st)
        g = MLSTM(c, {"bi": Dm["m_bi"][G.lw(l)], "bf": Dm["m_bf"][G.lw(l)], "s0": Dm["m_s0"][G.lw(l)], "m0": Dm["m_m0"][G.lw(l)]})
        for s in range(8):
            g.segment(s, G.ZR, G.bZR[s], G.itab, G.bitab, {"q": tc('mq', s), "k": tc('mk', s), "v": tc('mv', s), "g": tc('mg', s)})
            scatter_rows(c, g.hT[:32, :], g.bhT, G.OS, G.itab[:32, tc('om', s):tc('om', s) + 1], G.bitab, G.bOS)
        g.finish()
        kb.dma("sp", Do["mlS"][G.lo(l)], g.Sout[:], reads=[g.bSout], writes=[G.bout])
        kb.dma("sp", Do["mlM"][G.lo(l)], g.mfin[:], reads=[g.bmfin], writes=[G.bout])
        barrier(kb)
    with ExitStack() as st:
        c = new_ctx(st)
        g = S5(c, {"vec": Dm["s_vec"][G.lw(l)], "BreT": Dm["s_BreT"][G.lw(l)], "BimT": Dm["s_BimT"][G.lw(l)], "CreT": Dm["s_CreT"][G.lw(l)], "CimT": Dm["s_CimT"][G.lw(l)], "dvec": Dm["s_dvec"][G.lw(l)], "x0": Dm["s_x0"][G.lw(l)]})
        for s in range(8):
            g.segment(s, G.ZR, G.bZR[s], G.itab, G.bitab, tc('s5', s))
            scatter_rows(c, g.yT[:32, :], g.byT, G.OS, G.itab[:32, tc('os', s):tc('os', s) + 1], G.bitab, G.bOS)
        kb.dma("sp", Do["s5X"][G.lo(l)], g.X[:], reads=[g.bX], writes=[G.bout])
        barrier(kb)
    with ExitStack() as st:
        c = new_ctx(st)
        g = SB(c, {"mb": Dm["b_mb"], "dmask": Dm["b_dmask"], "identb": Dm["b_identb"], "trin": Dm["b_trin"], "kc": Dm["b_kc"][G.lw(l)], "vc": Dm["b_vc"][G.lw(l)]}, 8)
        for s in range(8):
            g.load_segment(s, G.ZR, G.bZR[s], G.ZQR, G.bZQR, G.itab, G.bitab, {"k": tc('sk', s), "v": tc('sv', s), "q": tc('sq', s)})
            g.prompt_tile(2 * s); g.prompt_tile(2 * s + 1)
            g.sample_seq(2 * s); g.sample_seq(2 * s + 1)
            kb.op("pool", nc.gpsimd.tensor_copy, reads=[g.boS], writes=[g.boT], out=g.oT[:, 1024:1056], in_=g.oS[:, 2 * s, 0:32])
            kb.op("pool", nc.gpsimd.tensor_copy, reads=[g.boS], writes=[g.boT], out=g.oT[:, 1088:1120], in_=g.oS[:, 2 * s + 1, 0:32])
            scatter_rows(c, g.oT[:64, :], g.boT, G.OS, G.itab[:64, tc('ob', s):tc('ob', s) + 1], G.bitab, G.bOS)
        barrier(kb)


WSPEC = {"tvec": [128, 4, 16], "wg1": [D, DFF], "wu1": [D, DFF], "wd1": [DFF, D], "win": [D, NIN], "wout": [DMIX, D], "wg2": [D, DFF], "wu2": [D, DFF], "wd2": [DFF, D], "wglu": [256, 256]}
A_W = ["wg1", "wu1", "wd1", "win"]; C_W = ["wout", "wg2", "wu2", "wd2", "wglu"]
BSPEC = {"g_convw": [160, 4], "g_alog": [1, 1], "g_dtb": [1, 1], "g_convst": [160, 16, 3], "g_s0": [64, 16, 32], "m_bi": [1, 1], "m_bf": [1, 1], "m_s0": [64, 16, 33], "m_m0": [1, 16],
         "s_vec": [128, 3], "s_BreT": [32, 128], "s_BimT": [32, 128], "s_CreT": [128, 32], "s_CimT": [128, 32], "s_dvec": [32, 1], "s_x0": [128, 2, 16], "b_kc": [16, 64, 4096], "b_vc": [16, 4096, 64]}
BSHARED = {"cm": [64, 6, 64], "rows": [1, 2, NCOL], "b_mb": [128, 8, 512], "b_dmask": [32, 32], "b_identb": [128, 128], "b_trin": [128, 128]}

def build_launch(kind):
    nc = bass.Bass("TRN2", target_bir_lowering=False)
    G = GCtx(); G.nc = nc; G.uid = 0; G.wscr = {}
    G.lw = lambda l: 0; G.lo = lambda l: 0
    def din(name, shape, dt=F32): return nc.dram_tensor(name, list(shape), dt, kind="ExternalInput").ap()
    def dout(name, shape, dt=F32): return nc.dram_tensor(name, list(shape), dt, kind="ExternalOutput").ap()
    Dm = {}; G.D = Dm; G.Dout = {}
    itab_d = din("itab", [128, TAB.n], I32)
    G.bXR = Buf("XR"); G.bGS = Buf("GS"); G.bZS = Buf("ZS"); G.bZQS = Buf("ZQS"); G.bOS = Buf("OS"); G.bOR = Buf("OR"); G.bout = Buf("out"); G.bZQR = Buf("ZQR")
    zero_list = []
    if kind in ("tpa", "tpca", "tpcf"):
        has_c = kind != "tpa"; has_a = kind != "tpcf"
        G.xsrc = din("xin", [D, TT])
        if has_a:
            Dm["tvec"] = din("tvec", [1, 128, 4, 16])
            for nm in A_W: Dm[nm] = din(nm, [1] + WSPEC[nm])
            G.GSw = dout("GSo", [512, TT]); G.ZS = dout("ZSo", [RPR, SEGW]); G.ZQS = dout("ZQSo", [512, QW])
            G.Dout["kvout"] = dout("kvout", [1, TT, 512]); G.Dout["convT"] = dout("convT", [1, 768, 9])
            zero_list += [(G.ZS, G.bZS, RPR, SEGW), (G.ZQS, G.bZQS, 512, QW)]
        if has_c:
            for nm in C_W: Dm[nm] = din(nm + "c", [1] + WSPEC[nm])
            Dm["tvecc"] = din("tvecc", [1, 128, 4, 16])
            G.GSr = din("GSi", [512, TT]); G.OR = din("ORi", [8 * 160, SEGW])
        if kind == "tpcf":
            Dm["nf"] = din("nf", [128, 16]); G.Dout["yT"] = dout("yT", [D, TT]); G.xdst = None
        else:
            G.xdst = dout("xout", [D, TT])
    else:
        for nm, shp in BSPEC.items(): Dm[nm] = din(nm, [1] + shp)
        for nm, shp in BSHARED.items(): Dm[nm] = din(nm, shp)
        G.ZR = din("ZRi", [8 * 484, SEGW]); G.ZQR = din("ZQRi", [8 * 64, QW]); G.bZR = [Buf("ZR")] * 8
        G.OS = dout("OSo", [8 * 160, SEGW])
        G.Dout.update({"gdnS": dout("gdnS", [1, 64, 17, 32]), "mlS": dout("mlS", [1, 64, 17, 33]), "mlM": dout("mlM", [1, 1, 17]), "s5X": dout("s5X", [1, 128, 2, 17])})
        zero_list += [(G.OS, G.bOS, 8 * 160, SEGW)]
    with ExitStack() as st:
        kb = KB(nc, st); G.kb = kb
        G.ps = [(st.enter_context(nc.psum_tensor("ps%d" % i, [128, 512], F32)), Buf("ps%d" % i)) for i in range(8)]
        G.itab = st.enter_context(nc.sbuf_tensor("itab_s", [128, TAB.n], I32)); G.bitab = Buf("itab")
        kb.dma("sp", G.itab[:], itab_d, writes=[G.bitab])
        with ExitStack() as st2:
            zt = st2.enter_context(nc.sbuf_tensor("zerot", [128, SEGW], F32)); bzt = Buf("zt")
            kb.op("pool", nc.gpsimd.memset, writes=[bzt], ap=zt[:], constant=0.0)
            for (buf, bb, rows, w) in zero_list:
                for r0 in range(0, rows, 128):
                    rr = min(128, rows - r0)
                    kb.dma("act", buf[r0:r0 + rr, :], zt[:rr, :w], reads=[bzt], writes=[bb])
            barrier(kb)
        if kind == "b":
            b_phase(G, 0)
        else:
            token_phase(G, 0 if kind != "tpa" else None, 0 if kind != "tpcf" else None, kind == "tpcf")
        kb.finish([G.bout, G.bZS, G.bZQS, G.bOS, G.bGS])
        barrier(kb)
        print(kind, "instructions", kb.ninst, "sems", kb.nsem)
    return nc

def vecT(v):
    return np.ascontiguousarray(v.reshape(-1, 128).T)

def host_inputs(inp, depth):
    L = depth
    f = lambda a: np.ascontiguousarray(np.asarray(a, dtype=np.float32))
    shared = {}
    tvec = np.zeros((L, 128, 4, 16), np.float32)
    for l in range(L):
        tvec[l, :, 0, :] = vecT(inp["ffn1_norm"][l]); tvec[l, :, 1, :] = vecT(inp["mix_norm"][l]); tvec[l, :, 2, :] = vecT(inp["ffn2_norm"][l])
        cv = np.zeros((128, 16), np.float32)
        g64 = np.tile(inp["gdn_norm"][l], 2)
        cv[:, 0] = g64; cv[:, 1] = g64
        cv[:, 2:4] = vecT(inp["ml_norm"][l]); cv[:, 4:6] = vecT(inp["s5_b_glu"][l]); cv[:, 6:8] = vecT(inp["s5_norm"][l]); cv[:, 8:10] = vecT(inp["sb_norm"][l])
        tvec[l, :, 3, :] = cv
    shared["tvec"] = tvec; shared["nf"] = vecT(inp["final_norm"])
    for nm, src in [("wg1", "ffn1_w_gate"), ("wu1", "ffn1_w_up"), ("wd1", "ffn1_w_down"), ("win", "w_in"), ("wout", "w_out"), ("wg2", "ffn2_w_gate"), ("wu2", "ffn2_w_up"),
                    ("wd2", "ffn2_w_down"), ("wglu", "s5_w_glu")]:
        shared[nm] = f(inp[src][:L])
    cm = np.zeros((64, 6, 64), np.float32)
    jj, ii = np.meshgrid(np.arange(64), np.arange(64), indexing="ij")
    cm[:, 0] = np.eye(64); cm[:, 1] = -1.0 * (ii > jj); cm[:, 2] = -1.0 * (ii < jj); cm[:, 3] = (ii >= jj); cm[:, 4] = 1.0
    rows = np.ones((1, 2, NCOL), np.float32); rows[0, 0, ::64] = 0.0; rows[0, 1, 2080:2112] = 0.0; rows[0, 1, 2144:2176] = 0.0
    shared["cm"] = cm; shared["rows"] = rows
    jj, ii = np.meshgrid(np.arange(32), np.arange(32), indexing="ij")
    shared["b_dmask"] = np.where(jj < ii, 0.0, NEG).astype(np.float32)
    J, Sx = np.meshgrid(np.arange(128), np.arange(128), indexing="ij")
    shared["b_trin"] = (-1.0 * (J >= Sx)).astype(np.float32); shared["b_identb"] = np.eye(128, dtype=np.float32)
    xp = inp["x_prompt"][0]; xs = inp["x_sample"]
    maps = []
    for core in range(8):
        h, r = core // 2, core % 2
        m = dict(shared)
        xt = np.concatenate([xp[2048 * core:2048 * (core + 1)], xs[2 * core], xs[2 * core + 1]], 0)
        m["xT0"] = np.ascontiguousarray(xt.T)
        m["itab"] = tab_values(core)
        cch = list(range(h * 64, h * 64 + 64)) + list(range(256 + h * 64, 256 + h * 64 + 64)) + list(range(512 + h * 64 + r * 32, 512 + h * 64 + r * 32 + 32))
        m["g_convw"] = f(inp["gdn_conv_w"][:L][:, :, cch].transpose(0, 2, 1))
        m["g_alog"] = f(inp["gdn_a_log"][:L, h].reshape(L, 1, 1)); m["g_dtb"] = f(inp["gdn_dt_bias"][:L, h].reshape(L, 1, 1))
        m["g_convst"] = f(inp["state_gdn_conv"][:L][:, :, :, cch].transpose(0, 3, 1, 2))
        m["g_s0"] = f(inp["state_gdn_s"][:L, :, h, :, r * 32:(r + 1) * 32].transpose(0, 2, 1, 3))
        m["m_bi"] = f(inp["ml_b_i"][:L, h].reshape(L, 1, 1)); m["m_bf"] = f(inp["ml_b_f"][:L, h].reshape(L, 1, 1))
        c0 = inp["state_mlstm_c"][:L, :, h, :, r * 32:(r + 1) * 32]; n0 = inp["state_mlstm_n"][:L, :, h, :]
        m["m_s0"] = f(np.concatenate([c0, n0[..., None]], -1).transpose(0, 2, 1, 3))
        m["m_m0"] = f(inp["state_mlstm_m"][:L, :, h].reshape(L, 1, 16))
        gs = [2 * core, 2 * core + 1]
        vec = np.zeros((L, 128, 3), np.float32); BreT = np.zeros((L, 32, 128), np.float32); BimT = np.zeros((L, 32, 128), np.float32)
        CreT = np.zeros((L, 128, 32), np.float32); CimT = np.zeros((L, 128, 32), np.float32)
        for gi, gg in enumerate(gs):
            sl = slice(gi * 64, gi * 64 + 64); cl = slice(gi * 16, gi * 16 + 16)
            vec[:, sl, 0] = inp["s5_a_re"][:L, gg]; vec[:, sl, 1] = inp["s5_a_im"][:L, gg]; vec[:, sl, 2] = inp["s5_log_step"][:L, gg][:, None]
            BreT[:, cl, sl] = inp["s5_b_re"][:L, gg].transpose(0, 2, 1); BimT[:, cl, sl] = inp["s5_b_im"][:L, gg].transpose(0, 2, 1)
            CreT[:, sl, cl] = inp["s5_c_re"][:L, gg].transpose(0, 2, 1); CimT[:, sl, cl] = inp["s5_c_im"][:L, gg].transpose(0, 2, 1)
        m["s_vec"] = vec; m["s_BreT"] = BreT; m["s_BimT"] = BimT; m["s_CreT"] = CreT; m["s_CimT"] = CimT
        m["s_dvec"] = f(inp["s5_d"][:L, 32 * core:32 * core + 32].reshape(L, 32, 1))
        m["s_x0"] = f(np.stack([inp["state_s5_re"][:L][:, :, gs, :].reshape(L, 16, 128).transpose(0, 2, 1), inp["state_s5_im"][:L][:, :, gs, :].reshape(L, 16, 128).transpose(0, 2, 1)], 2))
        p = np.arange(128)[:, None, None, None]; j = np.arange(8)[None, :, None, None]; g4 = np.arange(4)[None, None, :, None]; cc = np.arange(128)[None, None, None, :]
        m["b_mb"] = np.where((128 * j + p) < (128 * (r + 2 * g4) + cc), 0.0, NEG).astype(np.float32).reshape(128, 8, 512)
        m["b_kc"] = f(inp["cache_sb_k"][:L, :, :, h, :].transpose(0, 1, 3, 2)); m["b_vc"] = f(inp["cache_sb_v"][:L, :, :, h, :])
        maps.append(m)
    return maps

def assemble(results, depth):
    L = depth
    yp = np.zeros((1, 16384, D), np.float32); ys = np.zeros((16, 32, D), np.float32)
    pk = np.zeros((L, 1, 16384, 4, 64), np.float32); pv = np.zeros_like(pk); sk = np.zeros((L, 16, 32, 4, 64), np.float32); sv = np.zeros_like(sk)
    pgs = np.zeros((L, 1, 4, 64, 64), np.float32); sgs = np.zeros((L, 16, 4, 64, 64), np.float32)
    pgc = np.zeros((L, 1, 3, 768), np.float32); sgc = np.zeros((L, 16, 3, 768), np.float32)
    pmc = np.zeros((L, 1, 4, 64, 64), np.float32); smc = np.zeros((L, 16, 4, 64, 64), np.float32)
    pmn = np.zeros((L, 1, 4, 64), np.float32); smn = np.zeros((L, 16, 4, 64), np.float32); pmm = np.zeros((L, 1, 4), np.float32); smm = np.zeros((L, 16, 4), np.float32)
    pxr = np.zeros((L, 1, 16, 64), np.float32); pxi = np.zeros_like(pxr); sxr = np.zeros((L, 16, 16, 64), np.float32); sxi = np.zeros_like(sxr)
    for core in range(8):
        R_ = results[core]; h, r = core // 2, core % 2
        yT = R_["yT"]
        yp[0, 2048 * core:2048 * (core + 1)] = yT[:, :2048].T; ys[2 * core] = yT[:, 2048:2080].T; ys[2 * core + 1] = yT[:, 2080:2112].T
        kv = R_["kvout"]
        pk[:, 0, 2048 * core:2048 * (core + 1)] = kv[:, :2048, 0:256].reshape(L, 2048, 4, 64); pv[:, 0, 2048 * core:2048 * (core + 1)] = kv[:, :2048, 256:512].reshape(L, 2048, 4, 64)
        for q in range(2):
            sk[:, 2 * core + q] = kv[:, 2048 + 32 * q:2080 + 32 * q, 0:256].reshape(L, 32, 4, 64); sv[:, 2 * core + q] = kv[:, 2048 + 32 * q:2080 + 32 * q, 256:512].reshape(L, 32, 4, 64)
        cT = R_["convT"]
        if core == 7: pgc[:, 0] = cT[:, :, 0:3].transpose(0, 2, 1)
        sgc[:, 2 * core] = cT[:, :, 3:6].transpose(0, 2, 1); sgc[:, 2 * core + 1] = cT[:, :, 6:9].transpose(0, 2, 1)
        gS = R_["gdnS"]
        pgs[:, 0, h, :, r * 32:(r + 1) * 32] = gS[:, :, 0, :]; sgs[:, :, h, :, r * 32:(r + 1) * 32] = gS[:, :, 1:17, :].transpose(0, 2, 1, 3)
        mS = R_["mlS"]; mM = R_["mlM"]
        pmc[:, 0, h, :, r * 32:(r + 1) * 32] = mS[:, :, 0, :32]; smc[:, :, h, :, r * 32:(r + 1) * 32] = mS[:, :, 1:17, :32].transpose(0, 2, 1, 3)
        if r == 0:
            pmn[:, 0, h] = mS[:, :, 0, 32]; smn[:, :, h] = mS[:, :, 1:17, 32].transpose(0, 2, 1); pmm[:, 0, h] = mM[:, 0, 0]; smm[:, :, h] = mM[:, 0, 1:17]
        X = R_["s5X"]
        gs = [2 * core, 2 * core + 1]
        pxr[:, 0, gs] = X[:, :, 0, 0].reshape(L, 2, 64); pxi[:, 0, gs] = X[:, :, 1, 0].reshape(L, 2, 64)
        sxr[:, :, gs] = X[:, :, 0, 1:17].transpose(0, 2, 1).reshape(L, 16, 2, 64); sxi[:, :, gs] = X[:, :, 1, 1:17].transpose(0, 2, 1).reshape(L, 16, 2, 64)
    return (yp, ys, pk, pv, pgs, pgc, pmc, pmn, pmm, pxr, pxi, sk, sv, sgs, sgc, smc, smn, smm, sxr, sxi)


def e1_rows(core):
    h, r = core // 2, core % 2
    rows = list(range(h * 64, h * 64 + 64)) + list(range(256 + h * 64, 256 + h * 64 + 64)) + list(range(512 + h * 64 + r * 32, 512 + h * 64 + r * 32 + 32)) + [768 + h, 772 + h]
    rows += [776 + x for x in list(range(h * 64, h * 64 + 64)) + list(range(256 + h * 64, 256 + h * 64 + 64)) + list(range(512 + h * 64 + r * 32, 512 + h * 64 + r * 32 + 32))] + [1544 + h, 1548 + h]
    rows += list(range(1552 + 32 * core, 1552 + 32 * core + 32))
    rows += list(range(1808 + h * 64, 1808 + h * 64 + 64)) + list(range(2064 + h * 64, 2064 + h * 64 + 64))
    assert len(rows) == 484
    return np.array(rows)

_PROGS = {}
def prog(kind):
    if kind not in _PROGS: _PROGS[kind] = build_launch(kind)
    return _PROGS[kind]

def run_multi(inp):
    L = 4
    hm = host_inputs(inp, L)
    tabs = [tab_values(c, fused=False) for c in range(8)]
    cores = list(range(8))
    def launch(kind, maps):
        return run_bass_kernel_spmd(prog(kind), maps, core_ids=cores).results
    sl = lambda a, l: np.ascontiguousarray(a[l:l + 1])
    def a_inputs(c, l):
        d = {"tvec": sl(hm[c]["tvec"], l)}
        for nm in A_W: d[nm] = sl(hm[c][nm], l)
        return d
    def c_inputs(c, l):
        d = {"tvecc": sl(hm[c]["tvec"], l)}
        for nm in C_W: d[nm + "c"] = sl(hm[c][nm], l)
        return d
    res = launch("tpa", [dict(itab=tabs[c], xin=hm[c]["xT0"], **a_inputs(c, 0)) for c in cores])
    kv = [[None] * L for _ in cores]; cv = [[None] * L for _ in cores]; st = [[None] * L for _ in cores]
    final = None
    for l in range(L):
        for c in cores: kv[c][l] = res[c]["kvout"][0]; cv[c][l] = res[c]["convT"][0]
        xcur = [res[c]["xout"] for c in cores]; gs = [res[c]["GSo"] for c in cores]
        Z = [res[c]["ZSo"] for c in cores]; ZQ = [res[c]["ZQSo"] for c in cores]
        bmaps = []
        for c in cores:
            h, r = c // 2, c % 2
            rows = e1_rows(c)
            d = {"itab": tabs[c], "ZRi": np.concatenate([Z[s][rows] for s in range(8)], 0), "ZQRi": np.concatenate([ZQ[s][r * 256 + h * 64:r * 256 + h * 64 + 64] for s in range(8)], 0)}
            for nm in BSPEC: d[nm] = sl(hm[c][nm], l)
            for nm in BSHARED: d[nm] = hm[c][nm]
            bmaps.append(d)
        bres = launch("b", bmaps)
        for c in cores: st[c][l] = {k: bres[c][k][0] for k in ["gdnS", "mlS", "mlM", "s5X"]}
        OS = [bres[c]["OSo"] for c in cores]
        tmaps = []
        for c in cores:
            d = {"itab": tabs[c], "xin": xcur[c], "GSi": gs[c], "ORi": np.concatenate([OS[j][c * 160:(c + 1) * 160] for j in range(8)], 0)}
            d.update(c_inputs(c, l))
            if l < L - 1: d.update(a_inputs(c, l + 1))
            else: d["nf"] = hm[c]["nf"]
            tmaps.append(d)
        res = launch("tpca" if l < L - 1 else "tpcf", tmaps)
    results = []
    for c in cores:
        results.append({"yT": res[c]["yT"], "kvout": np.stack(kv[c]), "convT": np.stack(cv[c]), "gdnS": np.stack([st[c][l]["gdnS"] for l in range(L)]),
                        "mlS": np.stack([st[c][l]["mlS"] for l in range(L)]), "mlM": np.stack([st[c][l]["mlM"] for l in range(L)]), "s5X": np.stack([st[c][l]["s5X"] for l in range(L)])})
    return assemble(results, L)

def kernel(**inputs):
    inp = {k: np.asarray(v) for k, v in inputs.items()}
    return run_multi(inp)
```

```python
import numpy as np
from contextlib import ExitStack
import concourse.bass as bass
import concourse.mybir as mybir
from concourse.bass_utils import run_bass_kernel_spmd

F32 = mybir.dt.float32; BF16 = mybir.dt.bfloat16; I32 = mybir.dt.int32
AF = mybir.ActivationFunctionType
ALU = mybir.AluOpType
AX = mybir.AxisListType

class Buf:
    __slots__ = ("name", "w", "r")
    def __init__(self, name):
        self.name = name; self.w = None; self.r = []

class KB:
    EPOCH = 30000
    NDMA = 24
    def __init__(self, nc, stack):
        self.nc = nc; self.stack = stack
        self.eng = {"pe": nc.tensor, "act": nc.scalar, "dve": nc.vector, "pool": nc.gpsimd, "sp": nc.sync}
        self.cur = {}
        self.nsem = 0
        for e in self.eng: self._newsem(e)
        self.seen = {}
        self.dsem = [self._sem("d%d" % i) for i in range(self.NDMA)]
        self.dcnt = [0] * self.NDMA
        self.drr = 0
        self.ninst = 0
    def _sem(self, name):
        self.nsem += 1
        return self.stack.enter_context(self.nc.semaphore("%s_%d" % (name, self.nsem)))
    def _newsem(self, e):
        self.cur[e] = [self._sem("c" + e), 0]
    def wait(self, e, tok):
        if tok is None: return
        sem, val = tok
        k = (e, id(sem))
        if self.seen.get(k, 0) >= val: return
        self.seen[k] = val
        self.eng[e].wait_ge(sem, val)
    def deps(self, e, reads, writes):
        for b in reads:
            self.wait(e, b.w)
        for b in writes:
            self.wait(e, b.w)
            for t in b.r: self.wait(e, t)
    def mark(self, tok, reads, writes):
        for b in reads: b.r.append(tok)
        for b in writes:
            b.w = tok; b.r = []
    def op(self, e, fn, reads=(), writes=(), **kw):
        self.deps(e, reads, writes)
        c = self.cur[e]
        if c[1] >= self.EPOCH:
            self._newsem(e); c = self.cur[e]
        ins = fn(**kw)
        c[1] += 1
        ins.then_inc(c[0], 1)
        tok = (c[0], c[1])
        import os
        self.seen[(e, id(c[0]))] = c[1] if (e == "pe" or os.environ.get("NOSELF")) else self.seen.get((e, id(c[0])), 0)
        self.mark(tok, reads, writes)
        self.ninst += 1
        return tok
    def dma(self, q, out, in_, reads=(), writes=(), **kw):
        slot = self.drr % self.NDMA; self.drr += 1
        sem = self.dsem[slot]
        if self.dcnt[slot] > 0:
            self.wait(q, (sem, 16 * self.dcnt[slot]))
        self.deps(q, reads, writes)
        self.eng[q].dma_start(out=out, in_=in_, **kw).then_inc(sem, 16)
        self.dcnt[slot] += 1
        tok = (sem, 16 * self.dcnt[slot])
        self.mark(tok, reads, writes)
        self.ninst += 1
        return tok
    def finish(self, bufs):
        for b in bufs:
            self.wait("sp", b.w)


NCOL = 2176
REG = [(0, 2048, 0), (2048, 32, 2048), (2080, 32, 2112)]
RAWOFF = [3, 2054, 2089]
GROUPS = [(0, 8), (8, 8), (16, 8), (24, 8), (32, 2)]
SEGW = 2304
EPS = 1e-6

def bc_last(ap, n):
    return bass.AP(ap.tensor, ap.offset, [list(x) for x in ap.ap] + [[0, n]])
def bc_mid(ap, n):
    a = [list(x) for x in ap.ap]
    return bass.AP(ap.tensor, ap.offset, [a[0], [0, n]] + a[1:])

class Ctx:
    pass

def b_setup(nc, st, kb):
    c = Ctx(); c.nc = nc; c.st = st; c.kb = kb
    def sb(name, shape, dt=F32):
        return st.enter_context(nc.sbuf_tensor("s_" + name, list(shape), dt)), Buf(name)
    c.sb = sb
    c.ps = [(st.enter_context(nc.psum_tensor("bps%d" % i, [128, 512], F32)), Buf("bps%d" % i)) for i in range(8)]
    return c

def load_consts(c, cm_d, rows_d):
    kb = c.kb
    c.cm, c.bcm = c.sb("cm", [64, 6, 64])
    c.rows, c.brows = c.sb("rows", [1, 2, NCOL])
    kb.dma("sp", c.cm[:], cm_d, writes=[c.bcm])
    kb.dma("sp", c.rows[:], rows_d, writes=[c.brows])
    c.eps, c.beps = c.sb("epsb", [128, 1])
    kb.op("pool", c.nc.gpsimd.memset, writes=[c.beps], ap=c.eps[:], constant=EPS)
    c.ident = c.cm[:, 0, :]; c.maskUn = c.cm[:, 1, :]; c.maskLn = c.cm[:, 2, :]; c.maskI = c.cm[:, 3, :]; c.ones64 = c.cm[:, 4, :]

def gather_rows(c, dst_ap, bdst, zg, idx_ap, bidx, bzg=None):
    kb = c.kb; nc = c.nc
    reads = [bidx] + ([bzg] if bzg is not None else [])
    kb.deps("pool", reads, [bdst])
    slot = kb.drr % kb.NDMA; kb.drr += 1; sem = kb.dsem[slot]
    if kb.dcnt[slot] > 0: kb.wait("pool", (sem, 16 * kb.dcnt[slot]))
    nc.gpsimd.indirect_dma_start(out=dst_ap, out_offset=None, in_=zg, in_offset=bass.IndirectOffsetOnAxis(ap=idx_ap, axis=0)).then_inc(sem, 16)
    kb.dcnt[slot] += 1
    kb.mark((sem, 16 * kb.dcnt[slot]), reads, [bdst]); kb.ninst += 1

def scatter_rows(c, src_ap, bsrc, og, idx_ap, bidx, bog):
    kb = c.kb; nc = c.nc
    kb.deps("pool", [bsrc, bidx], [bog])
    slot = kb.drr % kb.NDMA; kb.drr += 1; sem = kb.dsem[slot]
    if kb.dcnt[slot] > 0: kb.wait("pool", (sem, 16 * kb.dcnt[slot]))
    nc.gpsimd.indirect_dma_start(out=og, out_offset=bass.IndirectOffsetOnAxis(ap=idx_ap, axis=0), in_=src_ap, in_offset=None).then_inc(sem, 16)
    kb.dcnt[slot] += 1
    kb.mark((sem, 16 * kb.dcnt[slot]), [bsrc, bidx], [bog]); kb.ninst += 1

def row_to_col(c, col, bcol, row_ap, brow, n, one11, bone):
    kb = c.kb; nc = c.nc
    p, pb = c.ps[6]
    for k in range(n):
        kb.op("pe", nc.tensor.matmul, reads=[brow, bone], writes=[pb], out=p[:64, k:k + 1], lhsT=row_ap[0:1, k * 64:(k + 1) * 64], rhs=one11, start=True, stop=True)
    kb.op("act", nc.scalar.copy, reads=[pb], writes=[bcol], out=col[:, :n], in_=p[:64, :n])

def bcast_rows(c, dst, bdst, row_ap, brow, ncols, func=None, ones_row=None, bones=None, psi=6):
    kb = c.kb; nc = c.nc
    t = 0
    while t < ncols:
        w = min(512, ncols - t)
        p, pb = c.ps[psi]
        kb.op("pe", nc.tensor.matmul, reads=[brow, bones], writes=[pb], out=p[:64, :w], lhsT=ones_row, rhs=row_ap[:, t:t + w], start=True, stop=True)
        if func is None:
            kb.op("act", nc.scalar.copy, reads=[pb], writes=[bdst], out=dst[:, t:t + w], in_=p[:64, :w])
        else:
            kb.op("act", nc.scalar.activation, reads=[pb], writes=[bdst], out=dst[:, t:t + w], in_=p[:64, :w], func=func)
        t += w

def drain(g):
    for _ in g: pass

def pipeline(la, seg_slots, post=None):
    gm = None
    for gi, (ci, nch) in enumerate(GROUPS):
        si = gi % 2
        if gm is None:
            drain(la.mats(ci, nch, si))
        nxt = la.mats(GROUPS[gi + 1][0], GROUPS[gi + 1][1], (gi + 1) % 2) if gi + 1 < len(GROUPS) else None
        sc = la.scan(ci, nch, seg_slots(ci, nch), si)
        for _ in sc:
            if nxt is not None:
                for k in range(3):
                    try: next(nxt)
                    except StopIteration: nxt = None; break
        if nxt is not None: drain(nxt)
        gm = True
        if post is not None: post(ci, nch)

class LinAttn:
    def __init__(self, c, name, delta, DV):
        self.c = c; self.name = name; self.delta = delta; self.DV = DV
        sb = lambda n, s, dt=F32: c.sb(name + n, s, dt)
        self.QT, self.bQT = sb("QT", [64, SEGW]); self.KT, self.bKT = sb("KT", [64, SEGW]); self.VT, self.bVT = sb("VT", [33, SEGW])
        self.grow, self.bgrow = sb("grow", [1, NCOL])
        self.g2row, self.bg2row = sb("g2row", [1, NCOL])
        self.gcol, self.bgcol = sb("gcol", [64, 34]); self.g2col, self.bg2col = sb("g2col", [64, 34])
        self.kwcol, self.bkwcol = sb("kwcol", [64, 34])
        self.rkcol, self.brkcol = sb("rkcol", [64, 34])
        self.gbc, self.bgbc = sb("gbc", [64, NCOL]); self.gam, self.bgam = sb("gam", [64, NCOL])
        if delta: self.g2bc, self.bg2bc = sb("g2bc", [64, NCOL])
        self.onesrow, self.bonesrow = sb("onesrow", [1, 64])
        c.kb.op("pool", c.nc.gpsimd.memset, writes=[self.bonesrow], ap=self.onesrow[:], constant=1.0)
        names = ["D", "DM", "N0", "P", "tmp"] if delta else ["D", "DM", "tmp"]
        self.w = {n: sb("w" + n, [64, 512]) for n in names}
        if delta:
            for n in ["Nb0", "NTb0", "Nb1", "NTb1", "Pb"]: self.w[n] = sb("w" + n, [64, 512], BF16)
        self.wset = [{n: sb("w%s%d" % (n, i), [64, 512]) for n in ["pT", "qgT", "KG"]} for i in range(2)]
        self.cur = 0
        if delta:
            self.RK = sb("RK", [64, 512]); self.wkTs = [sb("wkT%d" % i, [64, 512]) for i in range(2)]
            self.RVs = [sb("RV%d" % i, [64, 8, DV]) for i in range(2)]; self.wvs = [sb("wv%d" % i, [64, 8, DV]) for i in range(2)]; self.u = [sb("u%d" % i, [64, DV]) for i in range(2)]
        else:
            self.RVs = [sb("RV%d" % i, [64, 8, DV]) for i in range(2)]
            for i in range(2): c.kb.op("pool", c.nc.gpsimd.memset, writes=[self.RVs[i][1]], ap=self.RVs[i][0][:], constant=1.0)
        self.S, self.bS = sb("S", [64, 17, DV])
        self.oT, self.boT = sb("oT", [DV, SEGW])
        c.kb.op("pool", c.nc.gpsimd.memset, writes=[self.boT], ap=self.oT[:], constant=0.0)

    def mats(self, ci, nch, si=0):
        c = self.c; kb = c.kb; nc = c.nc; W = nch * 64; c0 = ci * 64
        v3 = lambda t: t[:, :W].rearrange("p (n i) -> p n i", i=64)
        ws = self.wset[si]
        D, bD = self.w["D"]; DM, bDM = self.w["DM"]; tmp, btmp = self.w["tmp"]
        gbc3 = v3(self.gbc[:, c0:c0 + W])
        kb.op("dve", nc.vector.tensor_tensor, reads=[self.bgbc, self.bgcol], writes=[bD], out=v3(D), in0=gbc3, in1=bc_last(self.gcol[:, ci:ci + nch], 64), op=ALU.subtract)
        kb.op("dve", nc.vector.tensor_scalar, reads=[bD], writes=[bD], out=D[:, :W], in0=D[:, :W], scalar1=0.0, scalar2=None, op0=ALU.min)
        kb.op("act", nc.scalar.activation, reads=[bD], writes=[bD], out=D[:, :W], in_=D[:, :W], func=AF.Exp)
        kb.op("dve", nc.vector.tensor_tensor, reads=[bD, c.bcm], writes=[bDM], out=v3(DM), in0=v3(D), in1=bc_mid(c.maskI, nch), op=ALU.mult)
        if not self.delta:
            kb.op("dve", nc.vector.tensor_tensor, reads=[bDM, self.bg2col], writes=[bDM], out=v3(DM), in0=v3(DM), in1=bc_last(self.g2col[:, ci:ci + nch], 64), op=ALU.mult)
        p, pb = c.ps[5]
        for n in range(nch):
            sl = slice(c0 + n * 64, c0 + n * 64 + 64)
            kb.op("pe", nc.tensor.matmul, reads=[self.bKT, self.bQT], writes=[pb], out=p[:64, n * 64:n * 64 + 64], lhsT=self.KT[:, sl], rhs=self.QT[:, sl], start=True, stop=True)
        pT, bpT = ws["pT"]
        yield
        kb.op("dve", nc.vector.tensor_tensor, reads=[pb, bDM], writes=[bpT], out=pT[:, :W], in0=p[:64, :W], in1=DM[:, :W], op=ALU.mult)
        qg, bqg = ws["qgT"]
        kb.op("pool", nc.gpsimd.tensor_tensor, reads=[self.bQT, self.bgam], writes=[bqg], out=qg[:, :W], in0=self.QT[:, c0:c0 + W], in1=self.gam[:, c0:c0 + W], op=ALU.mult)
        p2, pb2 = c.ps[2]
        for n in range(nch):
            sl = slice(c0 + n * 64, c0 + n * 64 + 64)
            kb.op("pe", nc.tensor.transpose, reads=[self.bKT, c.bcm], writes=[pb2], out=p2[:64, n * 64:n * 64 + 64], in_=self.KT[:, sl], identity=c.ident)
        KG, bKG = ws["KG"]
        yield
        kb.op("dve", nc.vector.tensor_tensor, reads=[pb2, self.bkwcol], writes=[bKG], out=v3(KG), in0=v3(p2[:64, :]), in1=bc_last(self.kwcol[:, ci:ci + nch], 64), op=ALU.mult)
        if self.delta:
            RK, bRK = self.RK
            kb.op("dve", nc.vector.tensor_tensor, reads=[pb2, self.brkcol], writes=[bRK], out=v3(RK), in0=v3(p2[:64, :]), in1=bc_last(self.rkcol[:, ci:ci + nch], 64), op=ALU.mult)
        DV = self.DV
        p3, pb3 = c.ps[3]
        nv = 32
        for n in range(nch):
            sl = slice(c0 + n * 64, c0 + n * 64 + 64)
            kb.op("pe", nc.tensor.transpose, reads=[self.bVT, c.bcm], writes=[pb3], out=p3[:64, n * 32:n * 32 + 32], in_=self.VT[:32, sl], identity=c.ident[:32, :32])
        RV, bRV = self.RVs[si]
        yield
        pv3 = p3[:64, :nch * 32].rearrange("p (n d) -> p n d", d=32)
        if self.delta:
            kb.op("dve", nc.vector.tensor_tensor, reads=[pb3, self.bg2col], writes=[bRV], out=RV[:, :nch, :], in0=pv3, in1=bc_last(self.g2col[:, ci:ci + nch], 32), op=ALU.mult)
        else:
            kb.op("act", nc.scalar.copy, reads=[pb3], writes=[bRV], out=RV[:, :nch, 0:32], in_=pv3)
        if not self.delta:
            return
        yield
        p0, pb0 = c.ps[0]
        for n in range(nch):
            sl = slice(c0 + n * 64, c0 + n * 64 + 64)
            kb.op("pe", nc.tensor.matmul, reads=[self.bKT], writes=[pb0], out=p0[:64, n * 64:n * 64 + 64], lhsT=self.KT[:, sl], rhs=self.KT[:, sl], start=True, stop=True)
        N0, bN0 = self.w["N0"]; NT0, bNT0 = self.w["NTb0"]; N1, bN1 = self.w["Nb1"]; NT1, bNT1 = self.w["NTb1"]; P, bP = self.w["P"]; Nb0, bNb0 = self.w["Nb0"]; Pb, bPb = self.w["Pb"]
        kb.op("dve", nc.vector.tensor_tensor, reads=[self.bg2bc, c.bcm], writes=[btmp], out=v3(tmp), in0=v3(self.g2bc[:, c0:c0 + W]), in1=bc_mid(c.maskUn, nch), op=ALU.mult)
        kb.op("dve", nc.vector.tensor_tensor, reads=[btmp, bD], writes=[btmp], out=tmp[:, :W], in0=tmp[:, :W], in1=D[:, :W], op=ALU.mult)
        kb.op("dve", nc.vector.tensor_tensor, reads=[pb0, btmp], writes=[bN0], out=N0[:, :W], in0=p0[:64, :W], in1=tmp[:, :W], op=ALU.mult)
        kb.op("dve", nc.vector.tensor_tensor, reads=[self.bgbc, self.bgcol], writes=[btmp], out=v3(tmp), in0=bc_last(self.gcol[:, ci:ci + nch], 64), in1=gbc3, op=ALU.subtract)
        kb.op("dve", nc.vector.tensor_scalar, reads=[btmp], writes=[btmp], out=tmp[:, :W], in0=tmp[:, :W], scalar1=0.0, scalar2=None, op0=ALU.min)
        kb.op("act", nc.scalar.activation, reads=[btmp], writes=[btmp], out=tmp[:, :W], in_=tmp[:, :W], func=AF.Exp)
        kb.op("dve", nc.vector.tensor_tensor, reads=[btmp, c.bcm], writes=[btmp], out=v3(tmp), in0=v3(tmp), in1=bc_mid(c.maskLn, nch), op=ALU.mult)
        kb.op("dve", nc.vector.tensor_tensor, reads=[btmp, self.bg2col], writes=[btmp], out=v3(tmp), in0=v3(tmp), in1=bc_last(self.g2col[:, ci:ci + nch], 64), op=ALU.mult)
        kb.op("dve", nc.vector.tensor_tensor, reads=[pb0, btmp], writes=[bNT0], out=NT0[:, :W], in0=p0[:64, :W], in1=tmp[:, :W], op=ALU.mult)
        kb.op("dve", nc.vector.tensor_tensor, reads=[bN0, c.bcm], writes=[bP], out=v3(P), in0=v3(N0), in1=bc_mid(c.ident, nch), op=ALU.add)
        kb.op("act", nc.scalar.copy, reads=[bN0], writes=[bNb0], out=Nb0[:, :W], in_=N0[:, :W])
        kb.op("act", nc.scalar.copy, reads=[bP], writes=[bPb], out=Pb[:, :W], in_=P[:, :W])
        A, bA, AT, bAT = Nb0, bNb0, NT0, bNT0
        A2, bA2, AT2, bAT2 = N1, bN1, NT1, bNT1
        pa, pab = c.ps[0]; pat, patb = c.ps[1]; pp, ppb = c.ps[4]
        for r in range(5):
            last = (r == 4)
            yield
            if not last:
                for n in range(nch):
                    s = slice(n * 64, n * 64 + 64)
                    kb.op("pe", nc.tensor.matmul, reads=[bA, bAT], writes=[pab], out=pa[:64, s], lhsT=AT[:, s], rhs=A[:, s], start=True, stop=True)
            for n in range(nch):
                s = slice(n * 64, n * 64 + 64)
                kb.op("pe", nc.tensor.matmul, reads=[bA, bAT], writes=[patb], out=pat[:64, s], lhsT=A[:, s], rhs=AT[:, s], start=True, stop=True)
            if not last:
                kb.op("act", nc.scalar.copy, reads=[pab], writes=[bA2], out=A2[:, :W], in_=pa[:64, :W])
            kb.op("dve", nc.vector.tensor_copy, reads=[patb], writes=[bAT2], out=AT2[:, :W], in_=pat[:64, :W])
            yield
            for n in range(nch):
                s = slice(n * 64, n * 64 + 64)
                kb.op("pe", nc.tensor.matmul, reads=[bAT2, bPb], writes=[ppb], out=pp[:64, s], lhsT=AT2[:, s], rhs=Pb[:, s], start=True, stop=True)
            kb.op("dve", nc.vector.tensor_tensor, reads=[ppb, bP], writes=[bP], out=P[:, :W], in0=pp[:64, :W], in1=P[:, :W], op=ALU.add)
            if not last:
                kb.op("act", nc.scalar.copy, reads=[bP], writes=[bPb], out=Pb[:, :W], in_=P[:, :W])
            A, bA, AT, bAT, A2, bA2, AT2, bAT2 = A2, bA2, AT2, bAT2, A, bA, AT, bAT
        yield
        pw, pwb = c.ps[3]; pk, pkb = c.ps[2]
        RK, bRK = self.RK
        for n in range(nch):
            s = slice(n * 64, n * 64 + 64)
            kb.op("pe", nc.tensor.matmul, reads=[bP, bRV], writes=[pwb], out=pw[:64, n * 32:n * 32 + 32], lhsT=P[:, s], rhs=RV[:, n, :], start=True, stop=True)
        for n in range(nch):
            s = slice(n * 64, n * 64 + 64)
            kb.op("pe", nc.tensor.matmul, reads=[bP, bRK], writes=[pkb], out=pk[:64, s], lhsT=RK[:, s], rhs=P[:, s], start=True, stop=True)
        wv, bwv = self.wvs[si]; wkT, bwkT = self.wkTs[si]
        yield
        kb.op("act", nc.scalar.copy, reads=[pwb], writes=[bwv], out=wv[:, :nch, :], in_=pw[:64, :nch * 32].rearrange("p (n d) -> p n d", d=32))
        kb.op("act", nc.scalar.copy, reads=[pkb], writes=[bwkT], out=wkT[:, :W], in_=pk[:64, :W])

    def scan(self, ci, nch, slots, si=0):
        c = self.c; kb = c.kb; nc = c.nc; DV = self.DV; c0 = ci * 64
        ws = self.wset[si]
        pT, bpT = ws["pT"]; qg, bqg = ws["qgT"]; KG, bKG = ws["KG"]; RV, bRV = self.RVs[si]
        po, pob = c.ps[7]; psm, psmb = c.ps[6]
        for n in range(nch):
            s = slice(n * 64, n * 64 + 64); sl = slots[n]
            S = self.S[:, sl, :]
            if self.delta:
                wv, bwv = self.wvs[si]; wkT, bwkT = self.wkTs[si]; u, bu = self.u[n % 2]
                kb.op("pe", nc.tensor.matmul, reads=[bwkT, self.bS], writes=[psmb], out=psm[:64, 0:DV], lhsT=wkT[:, s], rhs=S, start=True, stop=True)
                kb.op("pe", nc.tensor.matmul, reads=[bqg, self.bS], writes=[pob], out=po[:DV, s], lhsT=S, rhs=qg[:, s], start=True, stop=False)
                kb.op("dve", nc.vector.tensor_tensor, reads=[psmb, bwv], writes=[bu], out=u[:], in0=wv[:, n, :], in1=psm[:64, 0:DV], op=ALU.subtract)
                uu, buu = u[:], bu
            else:
                kb.op("pe", nc.tensor.matmul, reads=[bqg, self.bS], writes=[pob], out=po[:DV, s], lhsT=S, rhs=qg[:, s], start=True, stop=False)
                uu, buu = RV[:, n, :], bRV
            kb.op("pe", nc.tensor.matmul, reads=[bpT, buu], writes=[pob], out=po[:DV, s], lhsT=uu, rhs=pT[:, s], start=False, stop=True)
            kb.op("pe", nc.tensor.matmul, reads=[bKG, buu], writes=[psmb], out=psm[:64, 64:64 + DV], lhsT=KG[:, s], rhs=uu, start=True, stop=True)
            glast = self.gam[:, c0 + n * 64 + 63:c0 + n * 64 + 64]
            kb.op("dve", nc.vector.scalar_tensor_tensor, reads=[psmb, self.bS, self.bgam], writes=[self.bS], out=S, in0=S, scalar=glast, in1=psm[:64, 64:64 + DV], op0=ALU.mult, op1=ALU.add)
            yield
        kb.op("act", nc.scalar.copy, reads=[pob], writes=[self.boT], out=self.oT[:, c0:c0 + nch * 64], in_=po[:DV, :nch * 64])

def gate_cols(c, la, ci0=0):
    kb = c.kb; nc = c.nc
    row_to_col(c, la.gcol, la.bgcol, la.grow[0:1, :], la.bgrow, 34, la.onesrow[0:1, 0:1], la.bonesrow)
    row_to_col(c, la.g2col, la.bg2col, la.g2row[0:1, :], la.bg2row, 34, la.onesrow[0:1, 0:1], la.bonesrow)
    bcast_rows(c, la.gbc, la.bgbc, la.grow, la.bgrow, NCOL, ones_row=la.onesrow[:], bones=la.bonesrow)
    bcast_rows(c, la.gam, la.bgam, la.grow, la.bgrow, NCOL, func=AF.Exp, ones_row=la.onesrow[:], bones=la.bonesrow)

class GDN:
    def __init__(self, c, prm_d):
        self.c = c; kb = c.kb; nc = c.nc
        self.la = LinAttn(c, "gdn", True, 32)
        sb = c.sb
        self.R, self.bR = sb("gR", [64, 3 + SEGW]); self.H = [sb("gH%d" % i, [64, 3]) for i in range(3)]
        self.SR, self.bSR = sb("gSR", [64, 70])
        self.cw = [sb("gcw%d" % i, [64, 4]) for i in range(3)]
        self.cst = [sb("gcst%d" % i, [64, 16, 3]) for i in range(3)]
        self.prm, self.bprm = sb("gprm", [1, 4])
        self.gt, self.bgt = sb("ggt", [2, SEGW])
        self.ysq, self.bysq = sb("gysq", [64, 512]); self.rinv, self.brinv = sb("grinv", [64, 512])
        for i, (r0, nr) in enumerate([(0, 64), (64, 64), (128, 32)]):
            kb.dma("sp", self.cw[i][0][:nr, :], prm_d["convw"][r0:r0 + nr, :], writes=[self.cw[i][1]])
            kb.dma("sp", self.cst[i][0][:nr], prm_d["convst"][r0:r0 + nr], writes=[self.cst[i][1]])
        kb.dma("sp", self.prm[:, 0:1], prm_d["alog"], writes=[self.bprm]); kb.dma("sp", self.prm[:, 1:2], prm_d["dtb"], writes=[self.bprm])
        kb.op("act", nc.scalar.activation, reads=[self.bprm], writes=[self.bprm], out=self.prm[:, 2:3], in_=self.prm[:, 0:1], func=AF.Exp)
        kb.op("dve", nc.vector.tensor_scalar, reads=[self.bprm], writes=[self.bprm], out=self.prm[:, 2:3], in0=self.prm[:, 2:3], scalar1=-1.0, scalar2=None, op0=ALU.mult)
        la = self.la
        kb.dma("sp", la.S[:, 1:17, :], prm_d["s0"], writes=[la.bS])
        kb.op("pool", nc.gpsimd.memset, writes=[la.bS], ap=la.S[:, 0, :], constant=0.0)
        for t, b in [(la.QT, la.bQT), (la.KT, la.bKT), (la.VT, la.bVT), (self.gt, self.bgt)]:
            kb.op("pool", nc.gpsimd.memset, writes=[b], ap=t[:], constant=0.0)
        for t, b in self.H:
            kb.op("pool", nc.gpsimd.memset, writes=[b], ap=t[:], constant=0.0)

    def segment(self, s, zg, bzg, idx, bidx, icol):
        c = self.c; kb = c.kb; nc = c.nc; la = self.la
        raws = [(self.R, self.bR, 64, la.QT, la.bQT), (self.R, self.bR, 64, la.KT, la.bKT), (self.R, self.bR, 32, la.VT, la.bVT)]
        for i, (R, bR, nr, Y, bY) in enumerate(raws):
            gather_rows(c, R[:nr, 3:3 + SEGW], bR, zg, idx[:nr, icol["qkv"[i]]:icol["qkv"[i]] + 1], bidx, bzg=bzg)
            Hh, bHh = self.H[i]
            kb.op("pool", nc.gpsimd.tensor_copy, reads=[bHh], writes=[bR], out=R[:nr, 0:3], in_=Hh[:nr, :])
            kb.op("pool", nc.gpsimd.tensor_copy, reads=[bR], writes=[bHh], out=Hh[:nr, :], in_=R[:nr, 2048:2051])
            cst, bcst = self.cst[i]; SR, bSR = self.SR, self.bSR
            kb.op("pool", nc.gpsimd.tensor_copy, reads=[bcst], writes=[bSR], out=SR[:nr, 0:3], in_=cst[:nr, 2 * s, :])
            kb.op("pool", nc.gpsimd.tensor_copy, reads=[bR], writes=[bSR], out=SR[:nr, 3:35], in_=R[:nr, 3 + 2048:3 + 2080])
            kb.op("pool", nc.gpsimd.tensor_copy, reads=[bcst], writes=[bSR], out=SR[:nr, 35:38], in_=cst[:nr, 2 * s + 1, :])
            kb.op("pool", nc.gpsimd.tensor_copy, reads=[bR], writes=[bSR], out=SR[:nr, 38:70], in_=R[:nr, 3 + 2112:3 + 2144])
            cw, bcw = self.cw[i]
            for (X, bX, r0, ln, d0) in [(R, bR, 3, 2048, 0), (SR, bSR, 3, 32, 2048), (SR, bSR, 38, 32, 2112)]:
                kb.op("dve", nc.vector.tensor_scalar, reads=[bX, bcw], writes=[bY], out=Y[:nr, d0:d0 + ln], in0=X[:nr, r0:r0 + ln], scalar1=cw[:nr, 3:4], scalar2=None, op0=ALU.mult)
                for t in range(3):
                    kb.op("dve", nc.vector.scalar_tensor_tensor, reads=[bX, bcw, bY], writes=[bY], out=Y[:nr, d0:d0 + ln], in0=X[:nr, r0 - 3 + t:r0 - 3 + t + ln],
                          scalar=cw[:nr, t:t + 1], in1=Y[:nr, d0:d0 + ln], op0=ALU.mult, op1=ALU.add)
            kb.op("act", nc.scalar.activation, reads=[bY], writes=[bY], out=Y[:nr, :], in_=Y[:nr, :], func=AF.Silu)
            if i < 2:
                for t0 in range(0, NCOL, 512):
                    w = min(512, NCOL - t0)
                    kb.op("act", nc.scalar.activation, reads=[bY], writes=[self.bysq], out=self.ysq[:, :w], in_=Y[:, t0:t0 + w], func=AF.Square)
                    p, pb = c.ps[6]
                    kb.op("pe", nc.tensor.matmul, reads=[self.bysq, c.bcm], writes=[pb], out=p[:64, :w], lhsT=c.ones64, rhs=self.ysq[:, :w], start=True, stop=True)
                    kb.op("act", nc.scalar.activation, reads=[pb, c.beps], writes=[self.brinv], out=self.rinv[:, :w], in_=p[:64, :w], func=AF.Sqrt, bias=c.eps[:64, 0:1])
                    kb.op("dve", nc.vector.reciprocal, reads=[self.brinv], writes=[self.brinv], out=self.rinv[:, :w], in_=self.rinv[:, :w])
                    kb.op("dve", nc.vector.scalar_tensor_tensor, reads=[bY, self.brinv], writes=[bY], out=Y[:, t0:t0 + w], in0=Y[:, t0:t0 + w], scalar=(0.125 if i == 0 else 1.0),
                          in1=self.rinv[:, :w], op0=ALU.mult, op1=ALU.mult)
        gather_rows(c, self.gt[:2, :], self.bgt, zg, idx[:2, icol["g"]:icol["g"] + 1], bidx, bzg=bzg)
        kb.dma("sp", la.g2row[:], self.gt[1:2, :NCOL], reads=[self.bgt], writes=[la.bg2row])
        valid = c.rows[:, 1, :]
        kb.op("act", nc.scalar.activation, reads=[self.bgt, self.bprm], writes=[la.bgrow], out=la.grow[:], in_=self.gt[0:1, :NCOL], func=AF.Exp, bias=self.prm[:, 1:2])
        kb.op("act", nc.scalar.activation, reads=[la.bgrow], writes=[la.bgrow], out=la.grow[:], in_=la.grow[:], func=AF.Ln, bias=1.0)
        kb.op("dve", nc.vector.scalar_tensor_tensor, reads=[la.bgrow, self.bprm, c.brows], writes=[la.bgrow], out=la.grow[:], in0=la.grow[:], scalar=self.prm[:, 2:3], in1=valid, op0=ALU.mult, op1=ALU.mult)
        kb.op("dve", nc.vector.tensor_tensor_scan, reads=[la.bgrow, c.brows], writes=[la.bgrow], out=la.grow[:], data0=c.rows[:, 0, :], data1=la.grow[:], initial=0.0, op0=ALU.mult, op1=ALU.add)
        kb.op("act", nc.scalar.activation, reads=[la.bg2row], writes=[la.bg2row], out=la.g2row[:], in_=la.g2row[:], func=AF.Sigmoid)
        kb.op("dve", nc.vector.tensor_tensor, reads=[la.bg2row, c.brows], writes=[la.bg2row], out=la.g2row[:], in0=la.g2row[:], in1=valid, op=ALU.mult)
        gate_cols(c, la)
        bcast_rows(c, la.g2bc, la.bg2bc, la.g2row, la.bg2row, NCOL, ones_row=la.onesrow[:], bones=la.bonesrow)
        glast = la.gbc[:, 63:NCOL:64]
        kb.op("dve", nc.vector.tensor_tensor, reads=[la.bgbc, la.bgcol], writes=[la.bkwcol], out=la.kwcol[:], in0=glast, in1=la.gcol[:], op=ALU.subtract)
        kb.op("act", nc.scalar.activation, reads=[la.bkwcol], writes=[la.bkwcol], out=la.kwcol[:], in_=la.kwcol[:], func=AF.Exp)
        kb.op("act", nc.scalar.activation, reads=[la.bgcol], writes=[la.brkcol], out=la.rkcol[:], in_=la.gcol[:], func=AF.Exp)
        kb.op("dve", nc.vector.tensor_tensor, reads=[la.brkcol, la.bg2col], writes=[la.brkcol], out=la.rkcol[:], in0=la.rkcol[:], in1=la.g2col[:], op=ALU.mult)
        pipeline(la, lambda ci, nch: [0] * nch if ci < 32 else [1 + 2 * s, 2 + 2 * s])


class MLSTM:
    def __init__(self, c, prm_d):
        self.c = c; kb = c.kb; nc = c.nc
        self.la = la = LinAttn(c, "ml", False, 33)
        sb = c.sb
        self.prm, self.bprm = sb("mprm", [1, 4])
        self.gt, self.bgt = sb("mgt", [2, SEGW])
        self.padb, self.bpadb = sb("mpadb", [1, NCOL])
        self.mrow, self.bmrow = sb("mmrow", [1, NCOL]); self.mfin, self.bmfin = sb("mmfin", [1, 17]); self.em, self.bem = sb("mem", [1, 17])
        self.embc, self.bembc = sb("membc", [64, 17]); self.Sout, self.bSout = sb("mSout", [64, 17, 33])
        self.lf, self.blf = sb("mlf", [1, NCOL])
        self.on33, self.bon33 = sb("mon33", [33, 32]); self.rrow, self.brrow = sb("mrrow", [33, 512]); self.hT, self.bhT = sb("mhT", [32, SEGW])
        kb.op("pool", nc.gpsimd.memset, writes=[self.bhT], ap=self.hT[:], constant=0.0)
        kb.op("pool", nc.gpsimd.memset, writes=[self.bon33], ap=self.on33[:], constant=1.0)
        kb.dma("sp", self.prm[:, 0:1], prm_d["bi"], writes=[self.bprm]); kb.dma("sp", self.prm[:, 1:2], prm_d["bf"], writes=[self.bprm])
        kb.op("dve", nc.vector.tensor_scalar, reads=[self.bprm], writes=[self.bprm], out=self.prm[:, 2:3], in0=self.prm[:, 1:2], scalar1=-1.0, scalar2=None, op0=ALU.mult)
        kb.op("dve", nc.vector.tensor_scalar, reads=[c.brows], writes=[self.bpadb], out=self.padb[:], in0=c.rows[:, 1, :], scalar1=1.0, scalar2=30000.0, op0=ALU.subtract, op1=ALU.mult)
        kb.dma("sp", la.S[:, 1:17, :], prm_d["s0"], writes=[la.bS])
        kb.op("pool", nc.gpsimd.memset, writes=[la.bS], ap=la.S[:, 0, :], constant=0.0)
        kb.op("pool", nc.gpsimd.memset, writes=[self.bmfin], ap=self.mfin[:], constant=0.0)
        kb.dma("sp", self.mfin[:, 1:17], prm_d["m0"], writes=[self.bmfin])
        kb.op("act", nc.scalar.activation, reads=[self.bmfin], writes=[self.bem], out=self.em[:], in_=self.mfin[:], func=AF.Exp)
        p, pb = c.ps[6]
        kb.op("pe", nc.tensor.matmul, reads=[self.bem, la.bonesrow], writes=[pb], out=p[:64, :17], lhsT=la.onesrow[:], rhs=self.em[:], start=True, stop=True)
        kb.op("act", nc.scalar.copy, reads=[pb], writes=[self.bembc], out=self.embc[:], in_=p[:64, :17])
        kb.op("dve", nc.vector.tensor_tensor, reads=[la.bS, self.bembc], writes=[la.bS], out=la.S[:, 1:17, :], in0=la.S[:, 1:17, :], in1=bc_last(self.embc[:, 1:17], 33), op=ALU.mult)
        kb.op("pool", nc.gpsimd.memset, writes=[la.bVT], ap=la.VT[:], constant=1.0)

    def segment(self, s, zg, bzg, idx, bidx, icol):
        c = self.c; kb = c.kb; nc = c.nc; la = self.la
        gather_rows(c, la.QT[:, :], la.bQT, zg, idx[:64, icol["q"]:icol["q"] + 1], bidx, bzg=bzg)
        gather_rows(c, la.KT[:, :], la.bKT, zg, idx[:64, icol["k"]:icol["k"] + 1], bidx, bzg=bzg)
        gather_rows(c, la.VT[:32, :], la.bVT, zg, idx[:32, icol["v"]:icol["v"] + 1], bidx, bzg=bzg)
        gather_rows(c, self.gt[:2, :], self.bgt, zg, idx[:2, icol["g"]:icol["g"] + 1], bidx, bzg=bzg)
        kb.op("act", nc.scalar.mul, reads=[la.bKT], writes=[la.bKT], out=la.KT[:, :NCOL], in_=la.KT[:, :NCOL], mul=0.125)
        kb.dma("sp", self.lf[:], self.gt[1:2, :NCOL], reads=[self.bgt], writes=[self.blf])
        valid = c.rows[:, 1, :]
        kb.op("dve", nc.vector.scalar_tensor_tensor, reads=[self.bgt, self.bprm, c.brows], writes=[la.bg2row], out=la.g2row[:], in0=self.gt[0:1, :NCOL], scalar=self.prm[:, 0:1], in1=valid, op0=ALU.add, op1=ALU.mult)
        kb.op("dve", nc.vector.tensor_tensor, reads=[la.bg2row, self.bpadb], writes=[la.bg2row], out=la.g2row[:], in0=la.g2row[:], in1=self.padb[:], op=ALU.add)
        kb.op("act", nc.scalar.activation, reads=[self.blf, self.bprm], writes=[self.blf], out=self.lf[:], in_=self.lf[:], func=AF.Exp, scale=-1.0, bias=self.prm[:, 2:3])
        kb.op("act", nc.scalar.activation, reads=[self.blf], writes=[self.blf], out=self.lf[:], in_=self.lf[:], func=AF.Ln, bias=1.0)
        kb.op("dve", nc.vector.scalar_tensor_tensor, reads=[self.blf, c.brows], writes=[self.blf], out=self.lf[:], in0=self.lf[:], scalar=-1.0, in1=valid, op0=ALU.mult, op1=ALU.mult)
        for (c0, ln, slot) in [(0, 2048, 0), (2048, 32, 1 + 2 * s), (2112, 32, 2 + 2 * s)]:
            kb.op("dve", nc.vector.tensor_tensor_scan, reads=[self.blf, la.bg2row, self.bmfin], writes=[self.bmrow], out=self.mrow[:, c0:c0 + ln], data0=self.lf[:, c0:c0 + ln], data1=la.g2row[:, c0:c0 + ln],
                  initial=self.mfin[:, slot:slot + 1], op0=ALU.add, op1=ALU.max)
            kb.op("dve", nc.vector.tensor_copy, reads=[self.bmrow], writes=[self.bmfin], out=self.mfin[:, slot:slot + 1], in_=self.mrow[:, c0 + ln - 1:c0 + ln])
        kb.op("dve", nc.vector.tensor_tensor_scan, reads=[self.blf, c.brows], writes=[la.bgrow], out=la.grow[:], data0=c.rows[:, 0, :], data1=self.lf[:], initial=0.0, op0=ALU.mult, op1=ALU.add)
        gate_cols(c, la)
        kb.op("act", nc.scalar.activation, reads=[la.bg2col], writes=[la.bg2col], out=la.g2col[:], in_=la.g2col[:], func=AF.Exp)
        glast = la.gbc[:, 63:NCOL:64]
        kb.op("dve", nc.vector.tensor_tensor, reads=[la.bgbc, la.bgcol], writes=[la.bkwcol], out=la.kwcol[:], in0=glast, in1=la.gcol[:], op=ALU.subtract)
        kb.op("act", nc.scalar.activation, reads=[la.bkwcol], writes=[la.bkwcol], out=la.kwcol[:], in_=la.kwcol[:], func=AF.Exp)
        kb.op("dve", nc.vector.tensor_tensor, reads=[la.bkwcol, la.bg2col], writes=[la.bkwcol], out=la.kwcol[:], in0=la.kwcol[:], in1=la.g2col[:], op=ALU.mult)
        def post(ci, nch):
            c0 = ci * 64; W = nch * 64
            kb.op("dve", nc.vector.scalar_tensor_tensor, reads=[la.boT], writes=[self.brrow], out=self.rrow[32:33, :W], in0=la.oT[32:33, c0:c0 + W], scalar=-1.0, in1=la.oT[32:33, c0:c0 + W], op0=ALU.mult, op1=ALU.max)
            kb.op("dve", nc.vector.tensor_scalar, reads=[self.brrow], writes=[self.brrow], out=self.rrow[32:33, :W], in0=self.rrow[32:33, :W], scalar1=1.0, scalar2=None, op0=ALU.max)
            kb.op("dve", nc.vector.reciprocal, reads=[self.brrow], writes=[self.brrow], out=self.rrow[32:33, :W], in_=self.rrow[32:33, :W])
            p, pb = c.ps[6]
            kb.op("pe", nc.tensor.matmul, reads=[self.brrow, self.bon33], writes=[pb], out=p[:32, :W], lhsT=self.on33[32:33, :], rhs=self.rrow[32:33, :W], start=True, stop=True)
            kb.op("dve", nc.vector.tensor_tensor, reads=[pb, la.boT], writes=[self.bhT], out=self.hT[:, c0:c0 + W], in0=la.oT[0:32, c0:c0 + W], in1=p[:32, :W], op=ALU.mult)
        pipeline(la, lambda ci, nch: [0] * nch if ci < 32 else [1 + 2 * s, 2 + 2 * s], post)

    def finish(self):
        c = self.c; kb = c.kb; nc = c.nc; la = self.la
        kb.op("act", nc.scalar.activation, reads=[self.bmfin], writes=[self.bem], out=self.em[:], in_=self.mfin[:], func=AF.Exp, scale=-1.0)
        p, pb = c.ps[6]
        kb.op("pe", nc.tensor.matmul, reads=[self.bem, la.bonesrow], writes=[pb], out=p[:64, :17], lhsT=la.onesrow[:], rhs=self.em[:], start=True, stop=True)
        kb.op("act", nc.scalar.copy, reads=[pb], writes=[self.bembc], out=self.embc[:], in_=p[:64, :17])
        kb.op("dve", nc.vector.tensor_tensor, reads=[la.bS, self.bembc], writes=[self.bSout], out=self.Sout[:], in0=la.S[:], in1=bc_last(self.embc[:], 33), op=ALU.mult)

PI = 3.141592653589793

class S5:
    def __init__(self, c, prm_d):
        self.c = c; kb = c.kb; nc = c.nc; sb = c.sb
        V = nc.vector
        self.pv, self.bpv = sb("s5pv", [128, 32])
        self.BT, self.bBT = sb("s5BT", [32, 2, 128]); self.CT, self.bCT = sb("s5CT", [128, 2, 32]); self.dv, self.bdv = sb("s5dv", [32, 1])
        self.X, self.bX = sb("s5X", [128, 2, 17])
        self.U, self.bU = sb("s5U", [128, 2, 2048]); self.L, self.bL = sb("s5L", [128, 2, 2048])
        self.rho, self.brho = sb("s5rho", [128, NCOL])
        self.uT, self.buT = sb("s5uT", [32, SEGW])
        self.bu, self.bbu = sb("s5bu", [128, 2, NCOL]); self.rr, self.brr = sb("s5rr", [128, 2, NCOL]); self.ww, self.bww = sb("s5ww", [128, 2, NCOL])
        self.t1, self.bt1 = sb("s5t1", [128, NCOL]); self.yT, self.byT = sb("s5yT", [32, SEGW])
        kb.op("pool", nc.gpsimd.memset, writes=[self.byT], ap=self.yT[:], constant=0.0)
        self.pw, self.bpw = sb("s5pw", [128, 8])
        pv = self.pv; bpv = self.bpv
        kb.dma("sp", pv[:, 0:3], prm_d["vec"], writes=[bpv])
        kb.dma("sp", self.BT[:, 0, :], prm_d["BreT"], writes=[self.bBT]); kb.dma("sp", self.BT[:, 1, :], prm_d["BimT"], writes=[self.bBT])
        kb.dma("sp", self.CT[:, 0, :], prm_d["CreT"], writes=[self.bCT]); kb.dma("sp", self.CT[:, 1, :], prm_d["CimT"], writes=[self.bCT])
        kb.dma("sp", self.dv[:], prm_d["dvec"], writes=[self.bdv])
        kb.op("pool", nc.gpsimd.memset, writes=[self.bX], ap=self.X[:], constant=0.0)
        kb.dma("sp", self.X[:, :, 1:17], prm_d["x0"], writes=[self.bX])
        col = lambda i: pv[:, i:i + 1]
        def ts(out, in0, s1, s2, o0, o1=None):
            if o1 is None: kb.op("dve", V.tensor_scalar, reads=[bpv], writes=[bpv], out=out, in0=in0, scalar1=s1, scalar2=None, op0=o0)
            else: kb.op("dve", V.tensor_scalar, reads=[bpv], writes=[bpv], out=out, in0=in0, scalar1=s1, scalar2=s2, op0=o0, op1=o1)
        def tt(out, a, b, op): kb.op("dve", V.tensor_tensor, reads=[bpv], writes=[bpv], out=out, in0=a, in1=b, op=op)
        def act(out, in_, func, **kw): kb.op("act", nc.scalar.activation, reads=[bpv], writes=[bpv], out=out, in_=in_, func=func, **kw)
        act(col(3), col(2), AF.Exp)
        tt(col(4), col(3), col(0), ALU.mult); tt(col(5), col(3), col(1), ALU.mult)
        act(col(6), col(4), AF.Exp)
        def wrap(dst, src, thrs):
            kb.op("dve", V.tensor_copy, reads=[bpv], writes=[bpv], out=dst, in_=src)
            for th in thrs:
                ts(col(7), src, th, -2 * PI, ALU.is_gt, ALU.mult)
                tt(dst, dst, col(7), ALU.add)
        wrap(col(8), col(5), [PI, 3 * PI, 5 * PI])
        act(col(9), col(8), AF.Sin)
        ts(col(17), col(8), PI / 2, None, ALU.add)
        wrap(col(8), col(17), [PI])
        act(col(10), col(8), AF.Sin)
        tt(col(11), col(6), col(10), ALU.mult); tt(col(12), col(6), col(9), ALU.mult)
        tt(col(13), col(0), col(0), ALU.mult); tt(col(7), col(1), col(1), ALU.mult); tt(col(13), col(13), col(7), ALU.add)
        kb.op("dve", V.reciprocal, reads=[bpv], writes=[bpv], out=col(13), in_=col(13))
        ts(col(14), col(11), -1.0, None, ALU.add)
        tt(col(15), col(14), col(0), ALU.mult); tt(col(7), col(12), col(1), ALU.mult); tt(col(15), col(15), col(7), ALU.add); tt(col(15), col(15), col(13), ALU.mult)
        tt(col(16), col(12), col(0), ALU.mult); tt(col(7), col(14), col(1), ALU.mult); tt(col(16), col(16), col(7), ALU.subtract); tt(col(16), col(16), col(13), ALU.mult)
        ts(col(18), col(16), -1.0, None, ALU.mult)
        kb.op("pool", nc.gpsimd.memset, writes=[self.brho], ap=self.rho[:], constant=0.0)
        kb.op("dve", V.tensor_scalar, reads=[bpv, self.brho], writes=[self.brho], out=self.rho[:], in0=self.rho[:], scalar1=col(6), scalar2=None, op0=ALU.add)
        for t0 in [0, 2048, 2112]:
            kb.op("pool", nc.gpsimd.memset, writes=[self.brho], ap=self.rho[:, t0:t0 + 1], constant=0.0)
        def table(T, bT, first_re, first_im, lead_one):
            pw = self.pw; bpw = self.bpw
            if lead_one:
                kb.op("pool", nc.gpsimd.memset, writes=[bT], ap=T[:, 0, 0:1], constant=1.0)
                kb.op("pool", nc.gpsimd.memset, writes=[bT], ap=T[:, 1, 0:1], constant=0.0)
            else:
                kb.op("dve", V.tensor_copy, reads=[bpv], writes=[bT], out=T[:, 0, 0:1], in_=first_re)
                kb.op("dve", V.tensor_copy, reads=[bpv], writes=[bT], out=T[:, 1, 0:1], in_=first_im)
            kb.op("dve", V.tensor_copy, reads=[bpv], writes=[bpw], out=pw[:, 0:1], in_=first_re)
            kb.op("dve", V.tensor_copy, reads=[bpv], writes=[bpw], out=pw[:, 1:2], in_=first_im)
            m = 1
            while m < 2048:
                kb.op("dve", V.tensor_scalar, reads=[bpw], writes=[bpw], out=pw[:, 2:3], in0=pw[:, 1:2], scalar1=-1.0, scalar2=None, op0=ALU.mult)
                kb.op("dve", V.tensor_scalar, reads=[bT, bpw], writes=[bT], out=T[:, 0, m:2 * m], in0=T[:, 0, 0:m], scalar1=pw[:, 0:1], scalar2=None, op0=ALU.mult)
                kb.op("dve", V.scalar_tensor_tensor, reads=[bT, bpw], writes=[bT], out=T[:, 0, m:2 * m], in0=T[:, 1, 0:m], scalar=pw[:, 2:3], in1=T[:, 0, m:2 * m], op0=ALU.mult, op1=ALU.add)
                kb.op("dve", V.tensor_scalar, reads=[bT, bpw], writes=[bT], out=T[:, 1, m:2 * m], in0=T[:, 0, 0:m], scalar1=pw[:, 1:2], scalar2=None, op0=ALU.mult)
                kb.op("dve", V.scalar_tensor_tensor, reads=[bT, bpw], writes=[bT], out=T[:, 1, m:2 * m], in0=T[:, 1, 0:m], scalar=pw[:, 0:1], in1=T[:, 1, m:2 * m], op0=ALU.mult, op1=ALU.add)
                kb.op("dve", V.tensor_tensor, reads=[bpw], writes=[bpw], out=pw[:, 3:4], in0=pw[:, 0:1], in1=pw[:, 1:2], op=ALU.mult)
                kb.op("dve", V.tensor_tensor, reads=[bpw], writes=[bpw], out=pw[:, 4:5], in0=pw[:, 1:2], in1=pw[:, 1:2], op=ALU.mult)
                kb.op("dve", V.scalar_tensor_tensor, reads=[bpw], writes=[bpw], out=pw[:, 0:1], in0=pw[:, 0:1], scalar=pw[:, 0:1], in1=pw[:, 4:5], op0=ALU.mult, op1=ALU.subtract)
                kb.op("dve", V.tensor_scalar, reads=[bpw], writes=[bpw], out=pw[:, 1:2], in0=pw[:, 3:4], scalar1=2.0, scalar2=None, op0=ALU.mult)
                m *= 2
        table(self.U, self.bU, col(10), col(9), True)
        table(self.L, self.bL, col(11), col(12), False)

    def segment(self, s, zg, bzg, idx, bidx, icol):
        c = self.c; kb = c.kb; nc = c.nc; V = nc.vector; G = nc.gpsimd; pv = self.pv; bpv = self.bpv
        col = lambda i: pv[:, i:i + 1]
        gather_rows(c, self.uT[:32, :], self.buT, zg, idx[:32, icol:icol + 1], bidx, bzg=bzg)
        bu = self.bu; bbu = self.bbu
        for t0 in range(0, NCOL, 512):
            w = min(512, NCOL - t0)
            (p1, b1), (p2, b2) = c.ps[0], c.ps[1]
            kb.op("pe", nc.tensor.matmul, reads=[self.buT, self.bBT], writes=[b1], out=p1[:, :w], lhsT=self.BT[:, 0, :], rhs=self.uT[:, t0:t0 + w], start=True, stop=True)
            kb.op("pe", nc.tensor.matmul, reads=[self.buT, self.bBT], writes=[b2], out=p2[:, :w], lhsT=self.BT[:, 1, :], rhs=self.uT[:, t0:t0 + w], start=True, stop=True)
            kb.op("dve", V.tensor_scalar, reads=[b1, bpv], writes=[bbu], out=bu[:, 0, t0:t0 + w], in0=p1[:, :w], scalar1=col(15), scalar2=None, op0=ALU.mult)
            kb.op("dve", V.scalar_tensor_tensor, reads=[b2, bpv, bbu], writes=[bbu], out=bu[:, 0, t0:t0 + w], in0=p2[:, :w], scalar=col(18), in1=bu[:, 0, t0:t0 + w], op0=ALU.mult, op1=ALU.add)
            kb.op("dve", V.tensor_scalar, reads=[b2, bpv], writes=[bbu], out=bu[:, 1, t0:t0 + w], in0=p2[:, :w], scalar1=col(15), scalar2=None, op0=ALU.mult)
            kb.op("dve", V.scalar_tensor_tensor, reads=[b1, bpv, bbu], writes=[bbu], out=bu[:, 1, t0:t0 + w], in0=p1[:, :w], scalar=col(16), in1=bu[:, 1, t0:t0 + w], op0=ALU.mult, op1=ALU.add)
        U = self.U; bU = self.bU; L = self.L; bL = self.bL; rr = self.rr; brr = self.brr; ww = self.ww; bww = self.bww; t1 = self.t1; bt1 = self.bt1
        regs = [(0, 2048, 0), (2048, 32, 1 + 2 * s), (2112, 32, 2 + 2 * s)]
        for (c0, ln, slot) in regs:
            sl = slice(c0, c0 + ln); ul = slice(0, ln)
            kb.op("dve", V.tensor_tensor, reads=[bU, bbu], writes=[brr], out=rr[:, 0, sl], in0=U[:, 0, ul], in1=bu[:, 0, sl], op=ALU.mult)
            kb.op("pool", G.tensor_tensor, reads=[bU, bbu], writes=[bt1], out=t1[:, sl], in0=U[:, 1, ul], in1=bu[:, 1, sl], op=ALU.mult)
            kb.op("dve", V.tensor_tensor, reads=[brr, bt1], writes=[brr], out=rr[:, 0, sl], in0=rr[:, 0, sl], in1=t1[:, sl], op=ALU.add)
            kb.op("dve", V.tensor_tensor, reads=[bU, bbu], writes=[brr], out=rr[:, 1, sl], in0=U[:, 0, ul], in1=bu[:, 1, sl], op=ALU.mult)
            kb.op("pool", G.tensor_tensor, reads=[bU, bbu], writes=[bt1], out=t1[:, sl], in0=U[:, 1, ul], in1=bu[:, 0, sl], op=ALU.mult)
            kb.op("dve", V.tensor_tensor, reads=[brr, bt1], writes=[brr], out=rr[:, 1, sl], in0=rr[:, 1, sl], in1=t1[:, sl], op=ALU.subtract)
        for k in range(2):
            kb.op("dve", V.tensor_tensor_scan, reads=[brr, self.brho], writes=[bww], out=ww[:, k, :], data0=self.rho[:], data1=rr[:, k, :], initial=0.0, op0=ALU.mult, op1=ALU.add)
        X = self.X; bX = self.bX
        for (c0, ln, slot) in regs:
            sl = slice(c0, c0 + ln); ul = slice(0, ln)
            xr = X[:, 0, slot:slot + 1]; xi = X[:, 1, slot:slot + 1]
            kb.op("dve", V.tensor_tensor, reads=[bU, bww], writes=[brr], out=rr[:, 0, sl], in0=U[:, 0, ul], in1=ww[:, 0, sl], op=ALU.mult)
            kb.op("pool", G.tensor_tensor, reads=[bU, bww], writes=[bt1], out=t1[:, sl], in0=U[:, 1, ul], in1=ww[:, 1, sl], op=ALU.mult)
            kb.op("dve", V.tensor_tensor, reads=[brr, bt1], writes=[brr], out=rr[:, 0, sl], in0=rr[:, 0, sl], in1=t1[:, sl], op=ALU.subtract)
            kb.op("dve", V.tensor_tensor, reads=[bU, bww], writes=[brr], out=rr[:, 1, sl], in0=U[:, 0, ul], in1=ww[:, 1, sl], op=ALU.mult)
            kb.op("pool", G.tensor_tensor, reads=[bU, bww], writes=[bt1], out=t1[:, sl], in0=U[:, 1, ul], in1=ww[:, 0, sl], op=ALU.mult)
            kb.op("dve", V.tensor_tensor, reads=[brr, bt1], writes=[brr], out=rr[:, 1, sl], in0=rr[:, 1, sl], in1=t1[:, sl], op=ALU.add)
            kb.op("dve", V.tensor_scalar, reads=[bX], writes=[self.bpw], out=self.pw[:, 5:6], in0=xi, scalar1=-1.0, scalar2=None, op0=ALU.mult)
            kb.op("dve", V.scalar_tensor_tensor, reads=[bL, bX, brr], writes=[brr], out=rr[:, 0, sl], in0=L[:, 0, ul], scalar=xr, in1=rr[:, 0, sl], op0=ALU.mult, op1=ALU.add)
            kb.op("dve", V.scalar_tensor_tensor, reads=[bL, self.bpw, brr], writes=[brr], out=rr[:, 0, sl], in0=L[:, 1, ul], scalar=self.pw[:, 5:6], in1=rr[:, 0, sl], op0=ALU.mult, op1=ALU.add)
            kb.op("dve", V.scalar_tensor_tensor, reads=[bL, bX, brr], writes=[brr], out=rr[:, 1, sl], in0=L[:, 0, ul], scalar=xi, in1=rr[:, 1, sl], op0=ALU.mult, op1=ALU.add)
            kb.op("dve", V.scalar_tensor_tensor, reads=[bL, bX, brr], writes=[brr], out=rr[:, 1, sl], in0=L[:, 1, ul], scalar=xr, in1=rr[:, 1, sl], op0=ALU.mult, op1=ALU.add)
            kb.op("dve", V.tensor_copy, reads=[brr], writes=[bX], out=X[:, 0, slot:slot + 1], in_=rr[:, 0, c0 + ln - 1:c0 + ln])
            kb.op("dve", V.tensor_copy, reads=[brr], writes=[bX], out=X[:, 1, slot:slot + 1], in_=rr[:, 1, c0 + ln - 1:c0 + ln])
        kb.op("pool", G.tensor_scalar, reads=[brr], writes=[bt1], out=t1[:], in0=rr[:, 1, :], scalar1=-1.0, scalar2=None, op0=ALU.mult)
        for t0 in range(0, NCOL, 512):
            w = min(512, NCOL - t0)
            p1, b1 = c.ps[2]
            kb.op("pe", nc.tensor.matmul, reads=[brr, self.bCT], writes=[b1], out=p1[:32, :w], lhsT=self.CT[:, 0, :], rhs=rr[:, 0, t0:t0 + w], start=True, stop=False)
            kb.op("pe", nc.tensor.matmul, reads=[bt1, self.bCT], writes=[b1], out=p1[:32, :w], lhsT=self.CT[:, 1, :], rhs=t1[:, t0:t0 + w], start=False, stop=True)
            kb.op("dve", V.scalar_tensor_tensor, reads=[b1, self.buT, self.bdv], writes=[self.byT], out=self.yT[:, t0:t0 + w], in0=self.uT[:, t0:t0 + w], scalar=self.dv[:, 0:1], in1=p1[:32, :w], op0=ALU.mult, op1=ALU.add)

QW = 1280
NEG = -30000.0

class SB:
    def __init__(self, c, prm_d, nseg=8):
        self.c = c; kb = c.kb; nc = c.nc; sb = c.sb; self.nseg = nseg; self.prm_d = prm_d
        NK = 2048 * nseg
        self.KT, self.bKT = sb("sbKT", [64, NK], BF16); self.V, self.bV = sb("sbV", [128, 16 * nseg, 64], BF16)
        self.Q, self.bQ = sb("sbQ", [64, 1024 * nseg], BF16)
        self.KS, self.bKS = sb("sbKS", [64, 16, 32], BF16); self.VS, self.bVS = sb("sbVS", [32, 16, 64], BF16); self.QS, self.bQS = sb("sbQS", [64, 16, 32], BF16)
        self.MB, self.bMB = sb("sbMB", [128, 8, 512], BF16); self.DM, self.bDM = sb("sbDM", [32, 32], BF16)
        self.idb, self.bidb = sb("sbidb", [128, 128], BF16); self.trin, self.btrin = sb("sbtrin", [128, 128], BF16); self.onen, self.bonen = sb("sbonen", [128, 128], BF16)
        self.idf, self.bidf = sb("sbidf", [64, 64])
        self.stg, self.bstg = sb("sbstg", [64, SEGW]); self.stq, self.bstq = sb("sbstq", [64, QW])
        self.e = [sb("sbe%d" % i, [128, 512]) for i in range(2)]
        self.L = [sb("sbL%d" % i, [128, 512], BF16) for i in range(3)]
        self.R = [sb("sbR%d" % i, [128, 512], BF16) for i in range(3)]
        self.a = [sb("sba%d" % i, [128, 512], BF16) for i in range(3)]
        self.L0, self.bL0 = sb("sbLfirst", [32, 32], BF16)
        self.zero, self.bzero = sb("sbzero", [128, 512], BF16)
        self.oT, self.boT = sb("sboT", [64, SEGW]); self.oS, self.boS = sb("sboS", [64, 16, 64])
        kb.op("pool", nc.gpsimd.memset, writes=[self.boT], ap=self.oT[:], constant=0.0)
        self.kc = [sb("sbkc%d" % i, [64, 4096], BF16) for i in range(2)]; self.vc = [sb("sbvc%d" % i, [128, 32, 64], BF16) for i in range(2)]
        kb.dma("pool", self.MB[:], prm_d["mb"], writes=[self.bMB]); kb.dma("pool", self.DM[:], prm_d["dmask"], writes=[self.bDM])
        kb.dma("pool", self.idb[:], prm_d["identb"], writes=[self.bidb]); kb.dma("pool", self.trin[:], prm_d["trin"], writes=[self.btrin])
        kb.op("pool", nc.gpsimd.memset, writes=[self.bonen], ap=self.onen[:], constant=-1.0)
        kb.op("pool", nc.gpsimd.memset, writes=[self.bzero], ap=self.zero[:], constant=0.0)
        kb.op("pool", nc.gpsimd.memset, writes=[self.boS], ap=self.oS[:], constant=0.0)

    def load_segment(self, s, zg, bzg, zq, bzq, idx, bidx, icol):
        c = self.c; kb = c.kb; nc = c.nc
        stg, bstg = self.stg, self.bstg
        gather_rows(c, stg[:, :], bstg, zg, idx[:64, icol["k"]:icol["k"] + 1], bidx, bzg=bzg)
        kb.op("act", nc.scalar.copy, reads=[bstg], writes=[self.bKT], out=self.KT[:, 2048 * s:2048 * (s + 1)], in_=stg[:, 0:2048])
        kb.op("act", nc.scalar.copy, reads=[bstg], writes=[self.bKS], out=self.KS[:, 2 * s, :], in_=stg[:, 2048:2080])
        kb.op("act", nc.scalar.copy, reads=[bstg], writes=[self.bKS], out=self.KS[:, 2 * s + 1, :], in_=stg[:, 2112:2144])
        gather_rows(c, stg[:, :], bstg, zg, idx[:64, icol["v"]:icol["v"] + 1], bidx, bzg=bzg)
        for g in range(2):
            p, pb = c.ps[4 + g]
            for b in range(8):
                blk = g * 8 + b
                kb.op("pe", nc.tensor.transpose, reads=[bstg, c.bcm], writes=[pb], out=p[:, b * 64:(b + 1) * 64], in_=stg[:, blk * 128:(blk + 1) * 128], identity=c.ident)
            kb.op("act", nc.scalar.copy, reads=[pb], writes=[self.bV], out=self.V[:, 16 * s + g * 8:16 * s + g * 8 + 8, :], in_=p[:, :].rearrange("p (b d) -> p b d", d=64))
        p, pb = c.ps[4]
        for i, c0 in enumerate([2048, 2112]):
            kb.op("pe", nc.tensor.transpose, reads=[bstg, c.bcm], writes=[pb], out=p[:32, i * 64:(i + 1) * 64], in_=stg[:, c0:c0 + 32], identity=c.ident)
        kb.op("act", nc.scalar.copy, reads=[pb], writes=[self.bVS], out=self.VS[:, 2 * s:2 * s + 2, :], in_=p[:32, 0:128].rearrange("p (b d) -> p b d", d=64))
        stq, bstq = self.stq, self.bstq
        gather_rows(c, stq[:, :], bstq, zq, idx[:64, icol["q"]:icol["q"] + 1], bidx, bzg=bzq)
        kb.op("act", nc.scalar.mul, reads=[bstq], writes=[self.bQ], out=self.Q[:, 1024 * s:1024 * (s + 1)], in_=stq[:, 0:1024], mul=0.125)
        kb.op("act", nc.scalar.mul, reads=[bstq], writes=[self.bQS], out=self.QS[:, 2 * s, :], in_=stq[:, 1024:1056], mul=0.125)
        kb.op("act", nc.scalar.mul, reads=[bstq], writes=[self.bQS], out=self.QS[:, 2 * s + 1, :], in_=stq[:, 1088:1120], mul=0.125)

    def attend(self, steps, qap, bq, N, obank, out_ap, bout):
        c = self.c; kb = c.kb; nc = c.nc
        n = len(steps)
        po, pob = c.ps[obank]
        st = {}
        racc = (self.zero, self.bzero)
        first_nk = steps[0]["nk"]
        for i in range(n + 2):
            if i < n:
                S = steps[i]; nk = S["nk"]
                z, zb = c.ps[i % 3]
                kT, bkT = S["kT"]
                kb.op("pe", nc.tensor.matmul, reads=[bkT, bq], writes=[zb], out=z[:nk, :N], lhsT=kT, rhs=qap, start=True, stop=(S["mask"] is None))
                if S["mask"] is not None:
                    m, bm = S["mask"]
                    kb.op("pe", nc.tensor.matmul, reads=[bm, self.bidb], writes=[zb], out=z[:nk, :N], lhsT=self.idb[:nk, :nk], rhs=m, start=False, stop=True)
                e, be = self.e[i % 2]; L, bL = self.L[i % 3]
                if i == 0 and nk != 128: L, bL = self.L0, self.bL0
                kb.op("act", nc.scalar.activation, reads=[zb], writes=[be], out=e[:nk, :N], in_=z[:nk, :N], func=AF.Exp)
                kb.op("act", nc.scalar.activation, reads=[be], writes=[bL], out=L[:nk, :N], in_=e[:nk, :N], func=AF.Ln, bias=1.0)
                st[i] = dict(L=(L, bL), racc=racc, nk=nk)
                if i >= 1 or nk == 128:
                    Rn, bRn = self.R[i % 3]
                    if nk == 128:
                        kb.op("pool", nc.gpsimd.tensor_tensor, reads=[bL, racc[1]], writes=[bRn], out=Rn[:, :N], in0=L[:, :N], in1=racc[0][:, :N], op=ALU.add)
                        racc = (Rn, bRn)
            if 1 <= i <= n:
                j = i - 1; S = steps[j]; nk = S["nk"]; z, zb = c.ps[j % 3]
                L, bL = st[j]["L"]; ra, bra = st[j]["racc"]
                terms = [(self.trin[:nk, :nk], self.btrin, L[:nk, :N], bL)]
                if j >= 1:
                    if first_nk != 128:
                        L0, bL0 = st[0]["L"]
                        terms.append((self.onen[:first_nk, :nk], self.bonen, L0[:first_nk, :N], bL0))
                    if j >= (2 if first_nk != 128 else 1):
                        terms.append((self.onen[:, :nk], self.bonen, ra[:, :N], bra))
                for ti, (lh, blh, rh, brh) in enumerate(terms):
                    kb.op("pe", nc.tensor.matmul, reads=[blh, brh], writes=[zb], out=z[:nk, :N], lhsT=lh, rhs=rh, start=False, stop=(ti == len(terms) - 1))
                a, ba = self.a[j % 3]
                kb.op("act", nc.scalar.activation, reads=[zb], writes=[ba], out=a[:nk, :N], in_=z[:nk, :N], func=AF.Exp)
                st[j]["a"] = (a, ba)
            if i >= 2:
                j = i - 2; S = steps[j]; nk = S["nk"]
                a, ba = st[j]["a"]; v, bv = S["v"]
                kb.op("pe", nc.tensor.matmul, reads=[ba, bv], writes=[pob], out=po[:64, :N], lhsT=v, rhs=a[:nk, :N], start=(j == 0), stop=(j == n - 1))
        kb.op("act", nc.scalar.copy, reads=[pob], writes=[bout], out=out_ap, in_=po[:64, :N])

    def prompt_tile(self, m):
        steps = []
        for kbk in range(8 * m + 7, -1, -1):
            mask = (self.MB[:, kbk - 8 * m, :], self.bMB) if kbk >= 8 * m else None
            steps.append(dict(kT=(self.KT[:, kbk * 128:(kbk + 1) * 128], self.bKT), v=(self.V[:, kbk, :], self.bV), nk=128, mask=mask))
        self.attend(steps, self.Q[:, 512 * m:512 * (m + 1)], self.bQ, 512, 3, self.oT[:, 512 * (m % 2):512 * (m % 2 + 1)], self.boT)

    def sample_seq(self, q):
        c = self.c; kb = c.kb
        kc, bkc = self.kc[q % 2]; vc, bvc = self.vc[q % 2]
        kb.dma("pool", kc[:], self.prm_d["kc"][q], writes=[bkc])
        kb.dma("pool", vc[:], self.prm_d["vc"][q].rearrange("(b p) d -> p b d", p=128), writes=[bvc])
        steps = [dict(kT=(self.KS[:, q, :], self.bKS), v=(self.VS[:, q, :], self.bVS), nk=32, mask=(self.DM[:], self.bDM))]
        for kbk in range(31, -1, -1):
            steps.append(dict(kT=(kc[:, kbk * 128:(kbk + 1) * 128], bkc), v=(vc[:, kbk, :], bvc), nk=128, mask=None))
        self.attend(steps, self.QS[:, q, :], self.bQS, 32, 7, self.oS[:, q, 0:32], self.boS)

import os as _os
D = 2048; DFF = 2048; NIN = 3088; DMIX = 1024
TT = 2112
RPR = 2320
OPR = 1280
TILES = [(0, 512), (512, 512), (1024, 512), (1536, 576)]

def win_chunks():
    ch = []
    for c in range(6): ch.append((128 * c, 128, 'z', 128 * c))
    ch.append((768, 8, 'z', 768)); ch.append((776, 128, 'g', 0)); ch.append((904, 128, 'g', 128))
    for c in range(6): ch.append((1032 + 128 * c, 128, 'z', 776 + 128 * c))
    ch.append((1800, 8, 'z', 1544)); ch.append((1808, 128, 'g', 256)); ch.append((1936, 128, 'g', 384))
    ch.append((2064, 128, 'z', 1552)); ch.append((2192, 128, 'z', 1680))
    ch.append((2320, 128, 'q', 0)); ch.append((2448, 128, 'q', 128))
    ch.append((2576, 128, 'z', 1808)); ch.append((2704, 128, 'z', 1936)); ch.append((2832, 128, 'z', 2064)); ch.append((2960, 128, 'z', 2192))
    return ch
WCH = win_chunks()

class Tab:
    def __init__(self):
        self.cols = {}; self.n = 0
    def add(self, key):
        self.cols[key] = self.n; self.n += 1
def make_tab():
    t = Tab()
    zi = 0
    for k, (c0, w, kind, r0) in enumerate(WCH):
        if kind == 'z':
            for b in range(9): t.add(('az', k, b))
        if kind == 'q':
            for p in range(2):
                for b in range(5): t.add(('aq', k, p, b))
    for s in range(8):
        for nm in ['gq', 'gk', 'gv', 'gg', 'mq', 'mk', 'mv', 'mg', 's5', 'sk', 'sv', 'sq', 'og', 'om', 'os', 'ob']:
            t.add((nm, s))
    for cc in range(6):
        for b in range(9): t.add(('co', cc, b))
    for cc in range(2):
        for p in range(2):
            for b in range(5): t.add(('cs', cc, p, b))
    return t
TAB = make_tab()

def tab_values(core, fused=False):
    h, r = core // 2, core % 2
    cR = core * RPR if fused else 0; cQ = core * 512 if fused else 0; SR_ = RPR if fused else 484; SQ_ = 512 if fused else 64
    cO = core * OPR if fused else 0; OJ = OPR if fused else 160; cC = core * 160 if fused else 0
    T = np.zeros((128, TAB.n), np.int32)
    P = np.arange(128)
    for k, (c0, w, kind, r0) in enumerate(WCH):
        if kind == 'z':
            for b in range(9): T[:, TAB.cols[('az', k, b)]] = (cR + r0 + P) * 9 + b
        if kind == 'q':
            for p in range(2):
                for b in range(5): T[:, TAB.cols[('aq', k, p, b)]] = (cQ + p * 256 + r0 + P) * 5 + b
    for s in range(8):
        base = s * SR_
        if fused:
            oq, ok, ov, og = h * 64, 256 + h * 64, 512 + h * 64 + r * 32, 768 + h
            mq, mk, mv, mg = 776 + h * 64, 776 + 256 + h * 64, 776 + 512 + h * 64 + r * 32, 1544 + h
            o5, osk, osv = 1552 + 32 * core, 1808 + h * 64, 2064 + h * 64; gstep = 4
            sq0 = s * 512 + r * 256 + h * 64
        else:
            oq, ok, ov, og = 0, 64, 128, 160
            mq, mk, mv, mg = 162, 226, 290, 322
            o5, osk, osv = 324, 356, 420; gstep = 1
            sq0 = s * 64
        T[:, TAB.cols[('gq', s)]] = base + oq + P; T[:, TAB.cols[('gk', s)]] = base + ok + P
        T[:, TAB.cols[('gv', s)]] = base + ov + P; T[:, TAB.cols[('gg', s)]] = base + og + gstep * P
        T[:, TAB.cols[('mq', s)]] = base + mq + P; T[:, TAB.cols[('mk', s)]] = base + mk + P
        T[:, TAB.cols[('mv', s)]] = base + mv + P; T[:, TAB.cols[('mg', s)]] = base + mg + gstep * P
        T[:, TAB.cols[('s5', s)]] = base + o5 + P
        T[:, TAB.cols[('sk', s)]] = base + osk + P; T[:, TAB.cols[('sv', s)]] = base + osv + P
        T[:, TAB.cols[('sq', s)]] = sq0 + P
        ob = cO + s * 160
        T[:, TAB.cols[('og', s)]] = ob + P; T[:, TAB.cols[('om', s)]] = ob + 32 + P; T[:, TAB.cols[('os', s)]] = ob + 64 + P; T[:, TAB.cols[('ob', s)]] = ob + 96 + P
    for cc in range(6):
        moff = [0, 0, 32, 32, 64, 64][cc]; half = cc % 2
        j = (half * 128 + P) // 32; i = P % 32
        for b in range(9): T[:, TAB.cols[('co', cc, b)]] = (j * OJ + cC + moff + i) * 9 + b
    for cc in range(2):
        head = cc * 2 + P // 64; i = P % 64
        for p in range(2):
            for b in range(5): T[:, TAB.cols[('cs', cc, p, b)]] = ((2 * head + p) * OJ + cC + 96 + i) * 9 + b
    return np.clip(T, 0, None)

class Rot:
    def __init__(self, items): self.items = items; self.i = 0
    def next(self):
        x = self.items[self.i % len(self.items)]; self.i += 1; return x

def barrier(kb):
    engs = list(kb.eng.keys())
    for e in engs:
        for e2 in engs:
            if e2 != e: kb.wait(e, (kb.cur[e2][0], kb.cur[e2][1]))
        for slot in range(kb.NDMA):
            if kb.dcnt[slot] > 0: kb.wait(e, (kb.dsem[slot], 16 * kb.dcnt[slot]))

def collective(kb, nc, kind, src, bsrc, dst, bdst):
    kb.deps("pool", [bsrc], [bdst])
    c = kb.cur["pool"]
    if c[1] >= kb.EPOCH:
        kb._newsem("pool"); c = kb.cur["pool"]
    ins = nc.gpsimd.collective_compute(kind, ALU.add, replica_groups=[list(range(8))], ins=[src], outs=[dst])
    c[1] += 1; ins.then_inc(c[0], 1)
    kb.mark((c[0], c[1]), [bsrc], [bdst]); kb.ninst += 1

def token_phase(G, lc, la, final):
    nc = G.nc; kb = G.kb; Dm = G.D; ps = G.ps
    has_c = lc is not None; has_a = la is not None
    TMAX = 576
    with ExitStack() as st:
        G.uid += 1; uid = G.uid
        def sb(name, shape, dt=F32): return st.enter_context(nc.sbuf_tensor("t%d_" % uid + name, list(shape), dt)), Buf(name)
        xT, bxT = sb("xT", [128, 16, TMAX]); hT, bhT = sb("hT", [128, 16, TMAX], BF16); aT, baT = sb("aT", [128, 16, TMAX], BF16)
        wbufA = [sb("wA%d" % i, [128, 16, 256], BF16) for i in range(2)]; wbufB = [sb("wB%d" % i, [128, 16, 256], BF16) for i in range(2)]
        sq = [sb("sq%d" % i, [128, 512], BF16) for i in range(2)]
        rstd, brstd = sb("rstd", [128, TMAX])
        sg = [sb("sg%d" % i, [128, 512]) for i in range(2)]
        zst = [sb("zst%d" % i, [128, TMAX]) for i in range(2)]
        ones, bones = sb("ones", [128, 128], BF16); ones64, bones64 = sb("ones64", [128, 128], BF16)
        epsT, beps = sb("epsT", [128, 1]); vecs, bvecs = sb("vecs", [128, 5, 16])
        zblk, bzblk = sb("zblk", [128, 256]); sblk = [sb("sblk%d" % i, [128, 256]) for i in range(2)]
        if has_c:
            oTs, boT = sb("oTs", [128, 8, TMAX]); gTs, bgT = sb("gTs", [128, 4, TMAX]); mT, bmT = sb("mT", [128, 8, TMAX], BF16)
            tmpc = [sb("tmpc%d" % i, [128, 512]) for i in range(3)]; wglu, bwglu = sb("wglu", [128, 2, 256], BF16)
            ostg, bostg = sb("ostg", [128, 256])
        if has_a:
            kvs = [sb("kvs%d" % i, [128, 512]) for i in range(2)]
        V = nc.vector
        kb.op("pool", nc.gpsimd.memset, writes=[bones], ap=ones[:], constant=1.0)
        kb.op("pool", nc.gpsimd.memset, writes=[bones64], ap=ones64[:], constant=0.0)
        kb.op("pool", nc.gpsimd.memset, writes=[bones64], ap=ones64[0:64, 0:64], constant=1.0)
        kb.op("pool", nc.gpsimd.memset, writes=[bones64], ap=ones64[64:128, 64:128], constant=1.0)
        kb.op("pool", nc.gpsimd.memset, writes=[beps], ap=epsT[:], constant=1e-6)
        for t_, b_ in sblk: kb.op("pool", nc.gpsimd.memset, writes=[b_], ap=t_[:], constant=0.0)
        if has_a:
            kb.dma("sp", vecs[:, 0, :], Dm["tvec"][G.lw(la), :, 0, :], writes=[bvecs]); kb.dma("sp", vecs[:, 1, :], Dm["tvec"][G.lw(la), :, 1, :], writes=[bvecs])
        if has_c:
            kb.dma("sp", vecs[:, 2, :], Dm["tvecc"][G.lw(lc), :, 2, :], writes=[bvecs]); kb.dma("sp", vecs[:, 4, :], Dm["tvecc"][G.lw(lc), :, 3, :], writes=[bvecs])
            kb.dma("pool", wglu[:], Dm["wglu"][G.lw(lc)].rearrange("(c p) f -> p c f", p=128), writes=[bwglu])
            if final: kb.dma("sp", vecs[:, 3, :], Dm["nf"], writes=[bvecs])
        sqr = Rot(sq); sgr = Rot(sg); zr = Rot(zst)
        itab = G.itab; bitab = G.bitab
        tcol = lambda key: itab[:, TAB.cols[key]:TAB.cols[key] + 1]

        def sumsq_rstd(src, rds, nchunk, gw, lhs_ones, bl, pst, scale, dst, bdst):
            p, pb = pst
            for c in range(nchunk):
                s, bs = sqr.next()
                kb.op("act", nc.scalar.activation, reads=rds, writes=[bs], out=s[:, :gw], in_=src(c), func=AF.Square)
                kb.op("pe", nc.tensor.matmul, reads=[bs, bl], writes=[pb], out=p[:, :gw], lhsT=lhs_ones, rhs=s[:, :gw], start=(c == 0), stop=(c == nchunk - 1))
            kb.op("act", nc.scalar.activation, reads=[pb, beps], writes=[bdst], out=dst, in_=p[:, :gw], func=AF.Sqrt, scale=scale, bias=epsT[:, 0:1])
            kb.op("dve", V.reciprocal, reads=[bdst], writes=[bdst], out=dst, in_=dst)

        def norm_to(vi, grps, dst, bdst):
            for (g0, gw) in grps:
                sumsq_rstd(lambda c: xT[:, c, g0:g0 + gw], [bxT], 16, gw, ones[:], bones, ps[0], 1.0 / D, rstd[:, g0:g0 + gw], brstd)
                for c in range(16):
                    kb.op("dve", V.scalar_tensor_tensor, reads=[bxT, brstd, bvecs], writes=[bdst], out=dst[:, c, g0:g0 + gw], in0=xT[:, c, g0:g0 + gw],
                          scalar=vecs[:, vi, c:c + 1], in1=rstd[:, g0:g0 + gw], op0=ALU.mult, op1=ALU.mult)

        cur = {"ti": 0}
        def load_w(wt, wb, W, c0, cw, nk, key=None):
            if key is None:
                kb.dma("pool", wt[:, :nk, :cw], W[:, c0:c0 + cw].rearrange("(c p) f -> p c f", p=128), writes=[wb]); return
            if key not in G.wscr:
                G.uid += 1
                G.wscr[key] = (nc.dram_tensor("wscr%d_%s" % (G.uid, key), [W.shape[0], W.shape[1]], BF16, kind="Internal").ap(), {})
            scr, bufs = G.wscr[key]
            bk = bufs.setdefault(c0, Buf("scr"))
            if cur["ti"] == 0:
                kb.dma("pool", wt[:, :nk, :cw], W[:, c0:c0 + cw].rearrange("(c p) f -> p c f", p=128), writes=[wb])
                kb.dma("act", scr[:, c0:c0 + cw].rearrange("(c p) f -> p c f", p=128), wt[:, :nk, :cw], reads=[wb], writes=[bk])
            else:
                kb.dma("sp", wt[:, :nk, :cw], scr[:, c0:c0 + cw].rearrange("(c p) f -> p c f", p=128), reads=[bk], writes=[wb])

        def ffn(wg_d, wu_d, wd_d, vi, grps, kp):
            norm_to(vi, grps, hT, bhT)
            pg = Rot([ps[1], ps[2]]); pu = Rot([ps[3], ps[4]]); pd = Rot([ps[5], ps[6]])
            for b in range(DFF // 256):
                (wgt, wgb), (wut, wub) = wbufA[b % 2], wbufB[b % 2]
                load_w(wgt, wgb, wg_d, b * 256, 256, 16, kp + 'g'); load_w(wut, wub, wu_d, b * 256, 256, 16, kp + 'u')
                for ci in range(2):
                    f = b * 2 + ci
                    for (g0, gw) in grps:
                        (p1, b1), (p2, b2) = pg.next(), pu.next()
                        for k in range(16):
                            kb.op("pe", nc.tensor.matmul, reads=[wgb, bhT], writes=[b1], out=p1[:, :gw], lhsT=wgt[:, k, ci * 128:(ci + 1) * 128], rhs=hT[:, k, g0:g0 + gw], start=(k == 0), stop=(k == 15))
                        for k in range(16):
                            kb.op("pe", nc.tensor.matmul, reads=[wub, bhT], writes=[b2], out=p2[:, :gw], lhsT=wut[:, k, ci * 128:(ci + 1) * 128], rhs=hT[:, k, g0:g0 + gw], start=(k == 0), stop=(k == 15))
                        s, bs = sgr.next()
                        kb.op("act", nc.scalar.activation, reads=[b1], writes=[bs], out=s[:, :gw], in_=p1[:, :gw], func=AF.Silu)
                        kb.op("dve", V.tensor_tensor, reads=[bs, b2], writes=[baT], out=aT[:, f, g0:g0 + gw], in0=s[:, :gw], in1=p2[:, :gw], op=ALU.mult)
            for b in range(D // 256):
                wdt, wdb = (wbufA + wbufB)[b % 4]
                load_w(wdt, wdb, wd_d, b * 256, 256, 16, kp + 'd')
                for ci in range(2):
                    dch = b * 2 + ci
                    for (g0, gw) in grps:
                        p1, b1 = pd.next()
                        for k in range(16):
                            kb.op("pe", nc.tensor.matmul, reads=[wdb, baT], writes=[b1], out=p1[:, :gw], lhsT=wdt[:, k, ci * 128:(ci + 1) * 128], rhs=aT[:, k, g0:g0 + gw], start=(k == 0), stop=(k == 15))
                        kb.op("dve", V.scalar_tensor_tensor, reads=[b1, bxT], writes=[bxT], out=xT[:, dch, g0:g0 + gw], in0=p1[:, :gw], scalar=0.5, in1=xT[:, dch, g0:g0 + gw], op0=ALU.mult, op1=ALU.add)

        def fetch_mix(ti, t0, T):
            kb.dma("sp", gTs[:, :, :T], G.GSr[:, t0:t0 + T].rearrange("(c p) t -> p c t", p=128), reads=[G.bGS], writes=[bgT])
            OR9 = G.OR.rearrange("r (b c) -> (r b) c", c=256)
            for cc in range(6):
                for bi in range(2):
                    gather_rows(G, oTs[:, cc, bi * 256:(bi + 1) * 256], boT, OR9, tcol(('co', cc, 2 * ti + bi)), bitab, bzg=G.bOR)
                if T > 512:
                    gather_rows(G, ostg[:, :], bostg, OR9, tcol(('co', cc, 8)), bitab, bzg=G.bOR)
                    kb.op("pool", nc.gpsimd.tensor_copy, reads=[bostg], writes=[boT], out=oTs[:, cc, 512:544], in_=ostg[:, 0:32])
                    kb.op("pool", nc.gpsimd.tensor_copy, reads=[bostg], writes=[boT], out=oTs[:, cc, 544:576], in_=ostg[:, 64:96])
            for cc in range(2):
                for p in range(2):
                    gather_rows(G, ostg[:, :], bostg, OR9, tcol(('cs', cc, p, ti)), bitab, bzg=G.bOR)
                    dst = oTs[:, 6 + cc, 0:512].rearrange("q (a b c) -> q a b c", a=2, b=2)[:, :, p, :]
                    kb.op("pool", nc.gpsimd.tensor_copy, reads=[bostg], writes=[boT], out=dst, in_=ostg[:, :].rearrange("q (a c) -> q a c", a=2))
                if T > 512:
                    gather_rows(G, ostg[:, :], bostg, OR9, tcol(('cs', cc, 0, 4)), bitab, bzg=G.bOR)
                    kb.op("pool", nc.gpsimd.tensor_copy, reads=[bostg], writes=[boT], out=oTs[:, 6 + cc, 512:544], in_=ostg[:, 0:32])
                    kb.op("pool", nc.gpsimd.tensor_copy, reads=[bostg], writes=[boT], out=oTs[:, 6 + cc, 544:576], in_=ostg[:, 64:96])

        xsrc = G.xsrc
        for ti, (t0, T) in enumerate(TILES):
            cur["ti"] = ti
            grps = [(0, 512)] + ([(512, 64)] if T > 512 else [])
            kb.dma("sp", xT[:, :, :T], xsrc[:, t0:t0 + T].rearrange("(c p) t -> p c t", p=128), reads=[G.bXR], writes=[bxT])
            if has_c:
                if ti == 0: fetch_mix(ti, t0, T)
                cv = lambda j: vecs[:, 4, j:j + 1]
                for (g0, gw) in grps:
                    sl = slice(g0, g0 + gw)
                    for c in range(2):
                        sumsq_rstd(lambda cc_: oTs[:, c, sl], [boT], 1, gw, ones64[:], bones64, ps[0], 1.0 / 64, rstd[:, sl], brstd)
                        t1, bt1 = tmpc[0]; t2, bt2 = tmpc[1]
                        kb.op("act", nc.scalar.activation, reads=[bgT], writes=[bt1], out=t1[:, :gw], in_=gTs[:, c, sl], func=AF.Silu)
                        kb.op("dve", V.scalar_tensor_tensor, reads=[boT, brstd, bvecs], writes=[bt2], out=t2[:, :gw], in0=oTs[:, c, sl], scalar=cv(c), in1=rstd[:, sl], op0=ALU.mult, op1=ALU.mult)
                        kb.op("dve", V.tensor_tensor, reads=[bt1, bt2], writes=[bmT], out=mT[:, c, sl], in0=t1[:, :gw], in1=t2[:, :gw], op=ALU.mult)
                    for c in range(2):
                        sumsq_rstd(lambda cc_: oTs[:, 2 + c, sl], [boT], 1, gw, ones64[:], bones64, ps[0], 1.0 / 64, rstd[:, sl], brstd)
                        t1, bt1 = tmpc[0]; t2, bt2 = tmpc[1]
                        kb.op("act", nc.scalar.activation, reads=[bgT], writes=[bt1], out=t1[:, :gw], in_=gTs[:, 2 + c, sl], func=AF.Sigmoid)
                        kb.op("dve", V.scalar_tensor_tensor, reads=[boT, brstd, bvecs], writes=[bt2], out=t2[:, :gw], in0=oTs[:, 2 + c, sl], scalar=cv(2 + c), in1=rstd[:, sl], op0=ALU.mult, op1=ALU.mult)
                        kb.op("dve", V.tensor_tensor, reads=[bt1, bt2], writes=[bmT], out=mT[:, 2 + c, sl], in0=t1[:, :gw], in1=t2[:, :gw], op=ALU.mult)
                    ych = [sg[0], sg[1]]; ybf = [sq[0], sq[1]]
                    for c in range(2):
                        y, by = ych[c]; t1, bt1 = tmpc[0]; t2, bt2 = tmpc[1]
                        kb.op("act", nc.scalar.activation, reads=[boT], writes=[bt1], out=t1[:, :gw], in_=oTs[:, 4 + c, sl], func=AF.Square)
                        kb.op("dve", V.tensor_scalar, reads=[bt1], writes=[bt1], out=t1[:, :gw], in0=t1[:, :gw], scalar1=0.044715, scalar2=1.0, op0=ALU.mult, op1=ALU.add)
                        kb.op("dve", V.tensor_tensor, reads=[bt1, boT], writes=[bt2], out=t2[:, :gw], in0=t1[:, :gw], in1=oTs[:, 4 + c, sl], op=ALU.mult)
                        kb.op("act", nc.scalar.activation, reads=[bt2], writes=[bt1], out=t1[:, :gw], in_=t2[:, :gw], func=AF.Sigmoid, scale=1.5957691216)
                        kb.op("dve", V.tensor_tensor, reads=[bt1, boT], writes=[by], out=y[:, :gw], in0=t1[:, :gw], in1=oTs[:, 4 + c, sl], op=ALU.mult)
                        yq, byq = ybf[c]
                        kb.op("act", nc.scalar.copy, reads=[by], writes=[byq], out=yq[:, :gw], in_=y[:, :gw])
                    for c in range(2):
                        p1, b1 = ps[1 + c]
                        for k in range(2):
                            kb.op("pe", nc.tensor.matmul, reads=[bwglu, ybf[k][1]], writes=[b1], out=p1[:, :gw], lhsT=wglu[:, k, c * 128:(c + 1) * 128], rhs=ybf[k][0][:, :gw], start=(k == 0), stop=(k == 1))
                        t1, bt1 = tmpc[c]
                        kb.op("act", nc.scalar.activation, reads=[b1, bvecs], writes=[bt1], out=t1[:, :gw], in_=p1[:, :gw], func=AF.Sigmoid, bias=cv(4 + c))
                        kb.op("dve", V.tensor_tensor, reads=[bt1, ych[c][1]], writes=[ych[c][1]], out=ych[c][0][:, :gw], in0=t1[:, :gw], in1=ych[c][0][:, :gw], op=ALU.mult)
                    sumsq_rstd(lambda cc_: ych[cc_][0][:, :gw], [ych[0][1], ych[1][1]], 2, gw, ones[:], bones, ps[0], 1.0 / 256, rstd[:, sl], brstd)
                    for c in range(2):
                        kb.op("dve", V.scalar_tensor_tensor, reads=[ych[c][1], brstd, bvecs], writes=[bmT], out=mT[:, 4 + c, sl], in0=ych[c][0][:, :gw], scalar=cv(6 + c), in1=rstd[:, sl], op0=ALU.mult, op1=ALU.mult)
                    sumsq_rstd(lambda cc_: oTs[:, 6 + cc_, sl], [boT], 2, gw, ones[:], bones, ps[0], 1.0 / 256, rstd[:, sl], brstd)
                    for c in range(2):
                        kb.op("dve", V.scalar_tensor_tensor, reads=[boT, brstd, bvecs], writes=[bmT], out=mT[:, 6 + c, sl], in0=oTs[:, 6 + c, sl], scalar=cv(8 + c), in1=rstd[:, sl], op0=ALU.mult, op1=ALU.mult)
                if ti + 1 < len(TILES): fetch_mix(ti + 1, TILES[ti + 1][0], TILES[ti + 1][1])
                pd = Rot([ps[5], ps[6]])
                for b in range(D // 256):
                    wt, wb = (wbufA + wbufB)[b % 4]
                    load_w(wt, wb, Dm["wout"][G.lw(lc)], b * 256, 256, 8, "wout")
                    for ci in range(2):
                        dch = b * 2 + ci
                        for (g0, gw) in grps:
                            p1, b1 = pd.next()
                            for k in range(8):
                                kb.op("pe", nc.tensor.matmul, reads=[wb, bmT], writes=[b1], out=p1[:, :gw], lhsT=wt[:, k, ci * 128:(ci + 1) * 128], rhs=mT[:, k, g0:g0 + gw], start=(k == 0), stop=(k == 7))
                            kb.op("dve", V.tensor_tensor, reads=[b1, bxT], writes=[bxT], out=xT[:, dch, g0:g0 + gw], in0=p1[:, :gw], in1=xT[:, dch, g0:g0 + gw], op=ALU.add)
                ffn(Dm["wg2"][G.lw(lc)], Dm["wu2"][G.lw(lc)], Dm["wd2"][G.lw(lc)], 2, grps, "f2")
            if has_a:
                ffn(Dm["wg1"][G.lw(la)], Dm["wu1"][G.lw(la)], Dm["wd1"][G.lw(la)], 0, grps, "f1")
                norm_to(1, grps, hT, bhT)
                pd = Rot([ps[5], ps[6]])
                ZS9 = G.ZS.rearrange("r (b c) -> (r b) c", c=256); ZQ5 = G.ZQS.rearrange("r (b c) -> (r b) c", c=256)
                win = Dm["win"][G.lw(la)]
                for k, (c0, mw, kind, r0) in enumerate(WCH):
                    wt, wb = (wbufA + wbufB)[k % 4]
                    load_w(wt, wb, win, c0, mw, 16, 'win')
                    z, bzs = zr.next()
                    for (g0, gw) in grps:
                        p1, b1 = pd.next()
                        for kk in range(16):
                            kb.op("pe", nc.tensor.matmul, reads=[wb, bhT], writes=[b1], out=p1[:mw, :gw], lhsT=wt[:, kk, 0:mw], rhs=hT[:, kk, g0:g0 + gw], start=(kk == 0), stop=(kk == 15))
                        kb.op("act", nc.scalar.copy, reads=[b1], writes=[bzs], out=z[:mw, g0:g0 + gw], in_=p1[:mw, :gw])
                    if kind == 'g':
                        kb.dma("sp", G.GSw[r0:r0 + mw, t0:t0 + T], z[:mw, :T], reads=[bzs], writes=[G.bGS])
                    elif kind == 'z':
                        for bi in range(2):
                            scatter_rows(G, z[:mw, bi * 256:(bi + 1) * 256], bzs, ZS9, tcol(('az', k, 2 * ti + bi))[:mw], bitab, G.bZS)
                        if T > 512:
                            sbk, bsbk = sblk[k % 2]
                            kb.op("pool", nc.gpsimd.tensor_copy, reads=[bzs], writes=[bsbk], out=sbk[:mw, 0:32], in_=z[:mw, 512:544])
                            kb.op("pool", nc.gpsimd.tensor_copy, reads=[bzs], writes=[bsbk], out=sbk[:mw, 64:96], in_=z[:mw, 544:576])
                            scatter_rows(G, sbk[:mw, :], bsbk, ZS9, tcol(('az', k, 8))[:mw], bitab, G.bZS)
                        if c0 < 768 and T > 512:
                            for i3, cc0 in enumerate([509, 541, 573]):
                                kb.dma("sp", G.Dout["convT"][G.lo(la), c0:c0 + mw, 3 * i3:3 * i3 + 3], z[:mw, cc0:cc0 + 3], reads=[bzs], writes=[G.bout])
                    else:
                        for p in range(2):
                            kb.op("pool", nc.gpsimd.tensor_copy, reads=[bzs], writes=[bzblk], out=zblk[:, :].rearrange("q (a c) -> q a c", a=2),
                                  in_=z[:, 0:512].rearrange("q (a b c) -> q a b c", a=2, b=2)[:, :, p, :])
                            scatter_rows(G, zblk[:, :], bzblk, ZQ5, tcol(('aq', k, p, ti)), bitab, G.bZQS)
                        if T > 512:
                            sbk, bsbk = sblk[k % 2]
                            kb.op("pool", nc.gpsimd.tensor_copy, reads=[bzs], writes=[bsbk], out=sbk[:, 0:32], in_=z[:, 512:544])
                            kb.op("pool", nc.gpsimd.tensor_copy, reads=[bzs], writes=[bsbk], out=sbk[:, 64:96], in_=z[:, 544:576])
                            for p in range(2):
                                scatter_rows(G, sbk[:, :], bsbk, ZQ5, tcol(('aq', k, p, 4)), bitab, G.bZQS)
                (wkt, wkb), (wvt, wvb) = wbufB[0], wbufB[1]
                load_w(wkt, wkb, win, 2576, 256, 16, 'wink'); load_w(wvt, wvb, win, 2832, 256, 16, 'winv')
                for b0 in range(0, T, 128):
                    bw = min(128, T - b0)
                    p1, b1 = pd.next()
                    for kk in range(16):
                        kb.op("pe", nc.tensor.matmul, reads=[wkb, bhT], writes=[b1], out=p1[:bw, 0:256], lhsT=hT[:, kk, b0:b0 + bw], rhs=wkt[:, kk, :], start=(kk == 0), stop=(kk == 15))
                    for kk in range(16):
                        kb.op("pe", nc.tensor.matmul, reads=[wvb, bhT], writes=[b1], out=p1[:bw, 256:512], lhsT=hT[:, kk, b0:b0 + bw], rhs=wvt[:, kk, :], start=(kk == 0), stop=(kk == 15))
                    kv, bkv = kvs[(b0 // 128) % 2]
                    kb.op("act", nc.scalar.copy, reads=[b1], writes=[bkv], out=kv[:bw, :], in_=p1[:bw, :])
                    kb.dma("sp", G.Dout["kvout"][G.lo(la), t0 + b0:t0 + b0 + bw, :], kv[:bw, :], reads=[bkv], writes=[G.bout])
            if final:
                for (g0, gw) in grps:
                    sumsq_rstd(lambda c: xT[:, c, g0:g0 + gw], [bxT], 16, gw, ones[:], bones, ps[0], 1.0 / D, rstd[:, g0:g0 + gw], brstd)
                    for c in range(16):
                        kb.op("dve", V.scalar_tensor_tensor, reads=[bxT, brstd, bvecs], writes=[bxT], out=xT[:, c, g0:g0 + gw], in0=xT[:, c, g0:g0 + gw],
                              scalar=vecs[:, 3, c:c + 1], in1=rstd[:, g0:g0 + gw], op0=ALU.mult, op1=ALU.mult)
                kb.dma("sp", G.Dout["yT"][:, t0:t0 + T].rearrange("(c p) t -> p c t", p=128), xT[:, :, :T], reads=[bxT], writes=[G.bout])
            else:
                kb.dma("sp", G.xdst[:, t0:t0 + T].rearrange("(c p) t -> p c t", p=128), xT[:, :, :T], reads=[bxT], writes=[G.bout])
        barrier(kb)
    G.first_phase = False

class GCtx:
    pass

def b_phase(G, l):
    nc = G.nc; kb = G.kb; Dm = G.D; Do = G.Dout
    tc = lambda nm, s: TAB.cols[(nm, s)]
    def new_ctx(st):
        c = Ctx(); c.nc = nc; c.st = st; c.kb = kb; c.ps = G.ps
        c.sb = lambda name, shape, dt=F32: (st.enter_context(nc.sbuf_tensor("b%d_" % G.uid + name, list(shape), dt)), Buf(name))
        G.uid += 1
        load_consts(c, Dm["cm"], Dm["rows"])
        return c
    with ExitStack() as st:
        c = new_ctx(st)
        g = GDN(c, {"convw": Dm["g_convw"][G.lw(l)], "alog": Dm["g_alog"][G.lw(l)], "dtb": Dm["g_dtb"][G.lw(l)], "convst": Dm["g_convst"][G.lw(l)], "s0": Dm["g_s0"][G.lw(l)]})
        for s in range(8):
            g.segment(s, G.ZR, G.bZR[s], G.itab, G.bitab, {"q": tc('gq', s), "k": tc('gk', s), "v": tc('gv', s), "g": tc('gg', s)})
            scatter_rows(c, g.la.oT[:32, :], g.la.boT, G.OS, G.itab[:32, tc('og', s):tc('og', s) + 1], G.bitab, G.bOS)
        kb.dma("sp", Do["gdnS"][G.lo(l)], g.la.S[:], reads=[g.la.bS], writes=[G.bout])
        barrier(kb)
    with ExitStack() as st:
        c = new_ctx(st)
        g = MLSTM(c, {"bi": Dm["m_bi"][G.lw(l)], "bf": Dm["m_bf"][G.lw(l)], "s0": Dm["m_s0"][G.lw(l)], "m0": Dm["m_m0"][G.lw(l)]})
        for s in range(8):
            g.segment(s, G.ZR, G.bZR[s], G.itab, G.bitab, {"q": tc('mq', s), "k": tc('mk', s), "v": tc('mv', s), "g": tc('mg', s)})
            scatter_rows(c, g.hT[:32, :], g.bhT, G.OS, G.itab[:32, tc('om', s):tc('om', s) + 1], G.bitab, G.bOS)
        g.finish()
        kb.dma("sp", Do["mlS"][G.lo(l)], g.Sout[:], reads=[g.bSout], writes=[G.bout])
        kb.dma("sp", Do["mlM"][G.lo(l)], g.mfin[:], reads=[g.bmfin], writes=[G.bout])
        barrier(kb)
    with ExitStack() as st:
        c = new_ctx(st)
        g = S5(c, {"vec": Dm["s_vec"][G.lw(l)], "BreT": Dm["s_BreT"][G.lw(l)], "BimT": Dm["s_BimT"][G.lw(l)], "CreT": Dm["s_CreT"][G.lw(l)], "CimT": Dm["s_CimT"][G.lw(l)], "dvec": Dm["s_dvec"][G.lw(l)], "x0": Dm["s_x0"][G.lw(l)]})
        for s in range(8):
            g.segment(s, G.ZR, G.bZR[s], G.itab, G.bitab, tc('s5', s))
            scatter_rows(c, g.yT[:32, :], g.byT, G.OS, G.itab[:32, tc('os', s):tc('os', s) + 1], G.bitab, G.bOS)
        kb.dma("sp", Do["s5X"][G.lo(l)], g.X[:], reads=[g.bX], writes=[G.bout])
        barrier(kb)
    with ExitStack() as st:
        c = new_ctx(st)
        g = SB(c, {"mb": Dm["b_mb"], "dmask": Dm["b_dmask"], "identb": Dm["b_identb"], "trin": Dm["b_trin"], "kc": Dm["b_kc"][G.lw(l)], "vc": Dm["b_vc"][G.lw(l)]}, 8)
        for s in range(8):
            g.load_segment(s, G.ZR, G.bZR[s], G.ZQR, G.bZQR, G.itab, G.bitab, {"k": tc('sk', s), "v": tc('sv', s), "q": tc('sq', s)})
            g.prompt_tile(2 * s); g.prompt_tile(2 * s + 1)
            g.sample_seq(2 * s); g.sample_seq(2 * s + 1)
            kb.op("pool", nc.gpsimd.tensor_copy, reads=[g.boS], writes=[g.boT], out=g.oT[:, 1024:1056], in_=g.oS[:, 2 * s, 0:32])
            kb.op("pool", nc.gpsimd.tensor_copy, reads=[g.boS], writes=[g.boT], out=g.oT[:, 1088:1120], in_=g.oS[:, 2 * s + 1, 0:32])
            scatter_rows(c, g.oT[:64, :], g.boT, G.OS, G.itab[:64, tc('ob', s):tc('ob', s) + 1], G.bitab, G.bOS)
        barrier(kb)


WSPEC = {"tvec": [128, 4, 16], "wg1": [D, DFF], "wu1": [D, DFF], "wd1": [DFF, D], "win": [D, NIN], "wout": [DMIX, D], "wg2": [D, DFF], "wu2": [D, DFF], "wd2": [DFF, D], "wglu": [256, 256]}
A_W = ["wg1", "wu1", "wd1", "win"]; C_W = ["wout", "wg2", "wu2", "wd2", "wglu"]
BSPEC = {"g_convw": [160, 4], "g_alog": [1, 1], "g_dtb": [1, 1], "g_convst": [160, 16, 3], "g_s0": [64, 16, 32], "m_bi": [1, 1], "m_bf": [1, 1], "m_s0": [64, 16, 33], "m_m0": [1, 16],
         "s_vec": [128, 3], "s_BreT": [32, 128], "s_BimT": [32, 128], "s_CreT": [128, 32], "s_CimT": [128, 32], "s_dvec": [32, 1], "s_x0": [128, 2, 16], "b_kc": [16, 64, 4096], "b_vc": [16, 4096, 64]}
BSHARED = {"cm": [64, 6, 64], "rows": [1, 2, NCOL], "b_mb": [128, 8, 512], "b_dmask": [32, 32], "b_identb": [128, 128], "b_trin": [128, 128]}

def build_launch(kind):
    nc = bass.Bass("TRN2", target_bir_lowering=False)
    G = GCtx(); G.nc = nc; G.uid = 0; G.wscr = {}
    G.lw = lambda l: 0; G.lo = lambda l: 0
    def din(name, shape, dt=F32): return nc.dram_tensor(name, list(shape), dt, kind="ExternalInput").ap()
    def dout(name, shape, dt=F32): return nc.dram_tensor(name, list(shape), dt, kind="ExternalOutput").ap()
    Dm = {}; G.D = Dm; G.Dout = {}
    itab_d = din("itab", [128, TAB.n], I32)
    G.bXR = Buf("XR"); G.bGS = Buf("GS"); G.bZS = Buf("ZS"); G.bZQS = Buf("ZQS"); G.bOS = Buf("OS"); G.bOR = Buf("OR"); G.bout = Buf("out"); G.bZQR = Buf("ZQR")
    zero_list = []
    if kind in ("tpa", "tpca", "tpcf"):
        has_c = kind != "tpa"; has_a = kind != "tpcf"
        G.xsrc = din("xin", [D, TT])
        if has_a:
            Dm["tvec"] = din("tvec", [1, 128, 4, 16])
            for nm in A_W: Dm[nm] = din(nm, [1] + WSPEC[nm])
            G.GSw = dout("GSo", [512, TT]); G.ZS = dout("ZSo", [RPR, SEGW]); G.ZQS = dout("ZQSo", [512, QW])
            G.Dout["kvout"] = dout("kvout", [1, TT, 512]); G.Dout["convT"] = dout("convT", [1, 768, 9])
            zero_list += [(G.ZS, G.bZS, RPR, SEGW), (G.ZQS, G.bZQS, 512, QW)]
        if has_c:
            for nm in C_W: Dm[nm] = din(nm + "c", [1] + WSPEC[nm])
            Dm["tvecc"] = din("tvecc", [1, 128, 4, 16])
            G.GSr = din("GSi", [512, TT]); G.OR = din("ORi", [8 * 160, SEGW])
        if kind == "tpcf":
            Dm["nf"] = din("nf", [128, 16]); G.Dout["yT"] = dout("yT", [D, TT]); G.xdst = None
        else:
            G.xdst = dout("xout", [D, TT])
    else:
        for nm, shp in BSPEC.items(): Dm[nm] = din(nm, [1] + shp)
        for nm, shp in BSHARED.items(): Dm[nm] = din(nm, shp)
        G.ZR = din("ZRi", [8 * 484, SEGW]); G.ZQR = din("ZQRi", [8 * 64, QW]); G.bZR = [Buf("ZR")] * 8
        G.OS = dout("OSo", [8 * 160, SEGW])
        G.Dout.update({"gdnS": dout("gdnS", [1, 64, 17, 32]), "mlS": dout("mlS", [1, 64, 17, 33]), "mlM": dout("mlM", [1, 1, 17]), "s5X": dout("s5X", [1, 128, 2, 17])})
        zero_list += [(G.OS, G.bOS, 8 * 160, SEGW)]
    with ExitStack() as st:
        kb = KB(nc, st); G.kb = kb
        G.ps = [(st.enter_context(nc.psum_tensor("ps%d" % i, [128, 512], F32)), Buf("ps%d" % i)) for i in range(8)]
        G.itab = st.enter_context(nc.sbuf_tensor("itab_s", [128, TAB.n], I32)); G.bitab = Buf("itab")
        kb.dma("sp", G.itab[:], itab_d, writes=[G.bitab])
        with ExitStack() as st2:
            zt = st2.enter_context(nc.sbuf_tensor("zerot", [128, SEGW], F32)); bzt = Buf("zt")
            kb.op("pool", nc.gpsimd.memset, writes=[bzt], ap=zt[:], constant=0.0)
            for (buf, bb, rows, w) in zero_list:
                for r0 in range(0, rows, 128):
                    rr = min(128, rows - r0)
                    kb.dma("act", buf[r0:r0 + rr, :], zt[:rr, :w], reads=[bzt], writes=[bb])
            barrier(kb)
        if kind == "b":
            b_phase(G, 0)
        else:
            token_phase(G, 0 if kind != "tpa" else None, 0 if kind != "tpcf" else None, kind == "tpcf")
        kb.finish([G.bout, G.bZS, G.bZQS, G.bOS, G.bGS])
        barrier(kb)
        print(kind, "instructions", kb.ninst, "sems", kb.nsem)
    return nc

def vecT(v):
    return np.ascontiguousarray(v.reshape(-1, 128).T)

def host_inputs(inp, depth):
    L = depth
    f = lambda a: np.ascontiguousarray(np.asarray(a, dtype=np.float32))
    shared = {}
    tvec = np.zeros((L, 128, 4, 16), np.float32)
    for l in range(L):
        tvec[l, :, 0, :] = vecT(inp["ffn1_norm"][l]); tvec[l, :, 1, :] = vecT(inp["mix_norm"][l]); tvec[l, :, 2, :] = vecT(inp["ffn2_norm"][l])
        cv = np.zeros((128, 16), np.float32)
        g64 = np.tile(inp["gdn_norm"][l], 2)
        cv[:, 0] = g64; cv[:, 1] = g64
        cv[:, 2:4] = vecT(inp["ml_norm"][l]); cv[:, 4:6] = vecT(inp["s5_b_glu"][l]); cv[:, 6:8] = vecT(inp["s5_norm"][l]); cv[:, 8:10] = vecT(inp["sb_norm"][l])
        tvec[l, :, 3, :] = cv
    shared["tvec"] = tvec; shared["nf"] = vecT(inp["final_norm"])
    for nm, src in [("wg1", "ffn1_w_gate"), ("wu1", "ffn1_w_up"), ("wd1", "ffn1_w_down"), ("win", "w_in"), ("wout", "w_out"), ("wg2", "ffn2_w_gate"), ("wu2", "ffn2_w_up"),
                    ("wd2", "ffn2_w_down"), ("wglu", "s5_w_glu")]:
        shared[nm] = f(inp[src][:L])
    cm = np.zeros((64, 6, 64), np.float32)
    jj, ii = np.meshgrid(np.arange(64), np.arange(64), indexing="ij")
    cm[:, 0] = np.eye(64); cm[:, 1] = -1.0 * (ii > jj); cm[:, 2] = -1.0 * (ii < jj); cm[:, 3] = (ii >= jj); cm[:, 4] = 1.0
    rows = np.ones((1, 2, NCOL), np.float32); rows[0, 0, ::64] = 0.0; rows[0, 1, 2080:2112] = 0.0; rows[0, 1, 2144:2176] = 0.0
    shared["cm"] = cm; shared["rows"] = rows
    jj, ii = np.meshgrid(np.arange(32), np.arange(32), indexing="ij")
    shared["b_dmask"] = np.where(jj < ii, 0.0, NEG).astype(np.float32)
    J, Sx = np.meshgrid(np.arange(128), np.arange(128), indexing="ij")
    shared["b_trin"] = (-1.0 * (J >= Sx)).astype(np.float32); shared["b_identb"] = np.eye(128, dtype=np.float32)
    xp = inp["x_prompt"][0]; xs = inp["x_sample"]
    maps = []
    for core in range(8):
        h, r = core // 2, core % 2
        m = dict(shared)
        xt = np.concatenate([xp[2048 * core:2048 * (core + 1)], xs[2 * core], xs[2 * core + 1]], 0)
        m["xT0"] = np.ascontiguousarray(xt.T)
        m["itab"] = tab_values(core)
        cch = list(range(h * 64, h * 64 + 64)) + list(range(256 + h * 64, 256 + h * 64 + 64)) + list(range(512 + h * 64 + r * 32, 512 + h * 64 + r * 32 + 32))
        m["g_convw"] = f(inp["gdn_conv_w"][:L][:, :, cch].transpose(0, 2, 1))
        m["g_alog"] = f(inp["gdn_a_log"][:L, h].reshape(L, 1, 1)); m["g_dtb"] = f(inp["gdn_dt_bias"][:L, h].reshape(L, 1, 1))
        m["g_convst"] = f(inp["state_gdn_conv"][:L][:, :, :, cch].transpose(0, 3, 1, 2))
        m["g_s0"] = f(inp["state_gdn_s"][:L, :, h, :, r * 32:(r + 1) * 32].transpose(0, 2, 1, 3))
        m["m_bi"] = f(inp["ml_b_i"][:L, h].reshape(L, 1, 1)); m["m_bf"] = f(inp["ml_b_f"][:L, h].reshape(L, 1, 1))
        c0 = inp["state_mlstm_c"][:L, :, h, :, r * 32:(r + 1) * 32]; n0 = inp["state_mlstm_n"][:L, :, h, :]
        m["m_s0"] = f(np.concatenate([c0, n0[..., None]], -1).transpose(0, 2, 1, 3))
        m["m_m0"] = f(inp["state_mlstm_m"][:L, :, h].reshape(L, 1, 16))
        gs = [2 * core, 2 * core + 1]
        vec = np.zeros((L, 128, 3), np.float32); BreT = np.zeros((L, 32, 128), np.float32); BimT = np.zeros((L, 32, 128), np.float32)
        CreT = np.zeros((L, 128, 32), np.float32); CimT = np.zeros((L, 128, 32), np.float32)
        for gi, gg in enumerate(gs):
            sl = slice(gi * 64, gi * 64 + 64); cl = slice(gi * 16, gi * 16 + 16)
            vec[:, sl, 0] = inp["s5_a_re"][:L, gg]; vec[:, sl, 1] = inp["s5_a_im"][:L, gg]; vec[:, sl, 2] = inp["s5_log_step"][:L, gg][:, None]
            BreT[:, cl, sl] = inp["s5_b_re"][:L, gg].transpose(0, 2, 1); BimT[:, cl, sl] = inp["s5_b_im"][:L, gg].transpose(0, 2, 1)
            CreT[:, sl, cl] = inp["s5_c_re"][:L, gg].transpose(0, 2, 1); CimT[:, sl, cl] = inp["s5_c_im"][:L, gg].transpose(0, 2, 1)
        m["s_vec"] = vec; m["s_BreT"] = BreT; m["s_BimT"] = BimT; m["s_CreT"] = CreT; m["s_CimT"] = CimT
        m["s_dvec"] = f(inp["s5_d"][:L, 32 * core:32 * core + 32].reshape(L, 32, 1))
        m["s_x0"] = f(np.stack([inp["state_s5_re"][:L][:, :, gs, :].reshape(L, 16, 128).transpose(0, 2, 1), inp["state_s5_im"][:L][:, :, gs, :].reshape(L, 16, 128).transpose(0, 2, 1)], 2))
        p = np.arange(128)[:, None, None, None]; j = np.arange(8)[None, :, None, None]; g4 = np.arange(4)[None, None, :, None]; cc = np.arange(128)[None, None, None, :]
        m["b_mb"] = np.where((128 * j + p) < (128 * (r + 2 * g4) + cc), 0.0, NEG).astype(np.float32).reshape(128, 8, 512)
        m["b_kc"] = f(inp["cache_sb_k"][:L, :, :, h, :].transpose(0, 1, 3, 2)); m["b_vc"] = f(inp["cache_sb_v"][:L, :, :, h, :])
        maps.append(m)
    return maps

def assemble(results, depth):
    L = depth
    yp = np.zeros((1, 16384, D), np.float32); ys = np.zeros((16, 32, D), np.float32)
    pk = np.zeros((L, 1, 16384, 4, 64), np.float32); pv = np.zeros_like(pk); sk = np.zeros((L, 16, 32, 4, 64), np.float32); sv = np.zeros_like(sk)
    pgs = np.zeros((L, 1, 4, 64, 64), np.float32); sgs = np.zeros((L, 16, 4, 64, 64), np.float32)
    pgc = np.zeros((L, 1, 3, 768), np.float32); sgc = np.zeros((L, 16, 3, 768), np.float32)
    pmc = np.zeros((L, 1, 4, 64, 64), np.float32); smc = np.zeros((L, 16, 4, 64, 64), np.float32)
    pmn = np.zeros((L, 1, 4, 64), np.float32); smn = np.zeros((L, 16, 4, 64), np.float32); pmm = np.zeros((L, 1, 4), np.float32); smm = np.zeros((L, 16, 4), np.float32)
    pxr = np.zeros((L, 1, 16, 64), np.float32); pxi = np.zeros_like(pxr); sxr = np.zeros((L, 16, 16, 64), np.float32); sxi = np.zeros_like(sxr)
    for core in range(8):
        R_ = results[core]; h, r = core // 2, core % 2
        yT = R_["yT"]
        yp[0, 2048 * core:2048 * (core + 1)] = yT[:, :2048].T; ys[2 * core] = yT[:, 2048:2080].T; ys[2 * core + 1] = yT[:, 2080:2112].T
        kv = R_["kvout"]
        pk[:, 0, 2048 * core:2048 * (core + 1)] = kv[:, :2048, 0:256].reshape(L, 2048, 4, 64); pv[:, 0, 2048 * core:2048 * (core + 1)] = kv[:, :2048, 256:512].reshape(L, 2048, 4, 64)
        for q in range(2):
            sk[:, 2 * core + q] = kv[:, 2048 + 32 * q:2080 + 32 * q, 0:256].reshape(L, 32, 4, 64); sv[:, 2 * core + q] = kv[:, 2048 + 32 * q:2080 + 32 * q, 256:512].reshape(L, 32, 4, 64)
        cT = R_["convT"]
        if core == 7: pgc[:, 0] = cT[:, :, 0:3].transpose(0, 2, 1)
        sgc[:, 2 * core] = cT[:, :, 3:6].transpose(0, 2, 1); sgc[:, 2 * core + 1] = cT[:, :, 6:9].transpose(0, 2, 1)
        gS = R_["gdnS"]
        pgs[:, 0, h, :, r * 32:(r + 1) * 32] = gS[:, :, 0, :]; sgs[:, :, h, :, r * 32:(r + 1) * 32] = gS[:, :, 1:17, :].transpose(0, 2, 1, 3)
        mS = R_["mlS"]; mM = R_["mlM"]
        pmc[:, 0, h, :, r * 32:(r + 1) * 32] = mS[:, :, 0, :32]; smc[:, :, h, :, r * 32:(r + 1) * 32] = mS[:, :, 1:17, :32].transpose(0, 2, 1, 3)
        if r == 0:
            pmn[:, 0, h] = mS[:, :, 0, 32]; smn[:, :, h] = mS[:, :, 1:17, 32].transpose(0, 2, 1); pmm[:, 0, h] = mM[:, 0, 0]; smm[:, :, h] = mM[:, 0, 1:17]
        X = R_["s5X"]
        gs = [2 * core, 2 * core + 1]
        pxr[:, 0, gs] = X[:, :, 0, 0].reshape(L, 2, 64); pxi[:, 0, gs] = X[:, :, 1, 0].reshape(L, 2, 64)
        sxr[:, :, gs] = X[:, :, 0, 1:17].transpose(0, 2, 1).reshape(L, 16, 2, 64); sxi[:, :, gs] = X[:, :, 1, 1:17].transpose(0, 2, 1).reshape(L, 16, 2, 64)
    return (yp, ys, pk, pv, pgs, pgc, pmc, pmn, pmm, pxr, pxi, sk, sv, sgs, sgc, smc, smn, smm, sxr, sxi)


def e1_rows(core):
    h, r = core // 2, core % 2
    rows = list(range(h * 64, h * 64 + 64)) + list(range(256 + h * 64, 256 + h * 64 + 64)) + list(range(512 + h * 64 + r * 32, 512 + h * 64 + r * 32 + 32)) + [768 + h, 772 + h]
    rows += [776 + x for x in list(range(h * 64, h * 64 + 64)) + list(range(256 + h * 64, 256 + h * 64 + 64)) + list(range(512 + h * 64 + r * 32, 512 + h * 64 + r * 32 + 32))] + [1544 + h, 1548 + h]
    rows += list(range(1552 + 32 * core, 1552 + 32 * core + 32))
    rows += list(range(1808 + h * 64, 1808 + h * 64 + 64)) + list(range(2064 + h * 64, 2064 + h * 64 + 64))
    assert len(rows) == 484
    return np.array(rows)

_PROGS = {}
def prog(kind):
    if kind not in _PROGS: _PROGS[kind] = build_launch(kind)
    return _PROGS[kind]

def run_multi(inp):
    L = 4
    hm = host_inputs(inp, L)
    tabs = [tab_values(c, fused=False) for c in range(8)]
    cores = list(range(8))
    def launch(kind, maps):
        return run_bass_kernel_spmd(prog(kind), maps, core_ids=cores).results
    sl = lambda a, l: np.ascontiguousarray(a[l:l + 1])
    def a_inputs(c, l):
        d = {"tvec": sl(hm[c]["tvec"], l)}
        for nm in A_W: d[nm] = sl(hm[c][nm], l)
        return d
    def c_inputs(c, l):
        d = {"tvecc": sl(hm[c]["tvec"], l)}
        for nm in C_W: d[nm + "c"] = sl(hm[c][nm], l)
        return d
    res = launch("tpa", [dict(itab=tabs[c], xin=hm[c]["xT0"], **a_inputs(c, 0)) for c in cores])
    kv = [[None] * L for _ in cores]; cv = [[None] * L for _ in cores]; st = [[None] * L for _ in cores]
    final = None
    for l in range(L):
        for c in cores: kv[c][l] = res[c]["kvout"][0]; cv[c][l] = res[c]["convT"][0]
        xcur = [res[c]["xout"] for c in cores]; gs = [res[c]["GSo"] for c in cores]
        Z = [res[c]["ZSo"] for c in cores]; ZQ = [res[c]["ZQSo"] for c in cores]
        bmaps = []
        for c in cores:
            h, r = c // 2, c % 2
            rows = e1_rows(c)
            d = {"itab": tabs[c], "ZRi": np.concatenate([Z[s][rows] for s in range(8)], 0), "ZQRi": np.concatenate([ZQ[s][r * 256 + h * 64:r * 256 + h * 64 + 64] for s in range(8)], 0)}
            for nm in BSPEC: d[nm] = sl(hm[c][nm], l)
            for nm in BSHARED: d[nm] = hm[c][nm]
            bmaps.append(d)
        bres = launch("b", bmaps)
        for c in cores: st[c][l] = {k: bres[c][k][0] for k in ["gdnS", "mlS", "mlM", "s5X"]}
        OS = [bres[c]["OSo"] for c in cores]
        tmaps = []
        for c in cores:
            d = {"itab": tabs[c], "xin": xcur[c], "GSi": gs[c], "ORi": np.concatenate([OS[j][c * 160:(c + 1) * 160] for j in range(8)], 0)}
            d.update(c_inputs(c, l))
            if l < L - 1: d.update(a_inputs(c, l + 1))
            else: d["nf"] = hm[c]["nf"]
            tmaps.append(d)
        res = launch("tpca" if l < L - 1 else "tpcf", tmaps)
    results = []
    for c in cores:
        results.append({"yT": res[c]["yT"], "kvout": np.stack(kv[c]), "convT": np.stack(cv[c]), "gdnS": np.stack([st[c][l]["gdnS"] for l in range(L)]),
                        "mlS": np.stack([st[c][l]["mlS"] for l in range(L)]), "mlM": np.stack([st[c][l]["mlM"] for l in range(L)]), "s5X": np.stack([st[c][l]["s5X"] for l in range(L)])})
    return assemble(results, L)

def kernel(**inputs):
    inp = {k: np.asarray(v) for k, v in inputs.items()}
    return run_multi(inp)
```

```python
import numpy as np
from contextlib import ExitStack
import concourse.bass as bass
import concourse.mybir as mybir
from concourse.bass_utils import run_bass_kernel_spmd

F32 = mybir.dt.float32; BF16 = mybir.dt.bfloat16; I32 = mybir.dt.int32
AF = mybir.ActivationFunctionType
ALU = mybir.AluOpType
AX = mybir.AxisListType

class Buf:
    __slots__ = ("name", "w", "r")
    def __init__(self, name):
        self.name = name; self.w = None; self.r = []

class KB:
    EPOCH = 30000
    NDMA = 24
    def __init__(self, nc, stack):
        self.nc = nc; self.stack = stack
        self.eng = {"pe": nc.tensor, "act": nc.scalar, "dve": nc.vector, "pool": nc.gpsimd, "sp": nc.sync}
        self.cur = {}
        self.nsem = 0
        for e in self.eng: self._newsem(e)
        self.seen = {}
        self.dsem = [self._sem("d%d" % i) for i in range(self.NDMA)]
        self.dcnt = [0] * self.NDMA
        self.drr = 0
        self.ninst = 0
    def _sem(self, name):
        self.nsem += 1
        return self.stack.enter_context(self.nc.semaphore("%s_%d" % (name, self.nsem)))
    def _newsem(self, e):
        self.cur[e] = [self._sem("c" + e), 0]
    def wait(self, e, tok):
        if tok is None: return
        sem, val = tok
        k = (e, id(sem))
        if self.seen.get(k, 0) >= val: return
        self.seen[k] = val
        self.eng[e].wait_ge(sem, val)
    def deps(self, e, reads, writes):
        for b in reads:
            self.wait(e, b.w)
        for b in writes:
            self.wait(e, b.w)
            for t in b.r: self.wait(e, t)
    def mark(self, tok, reads, writes):
        for b in reads: b.r.append(tok)
        for b in writes:
            b.w = tok; b.r = []
    def op(self, e, fn, reads=(), writes=(), **kw):
        self.deps(e, reads, writes)
        c = self.cur[e]
        if c[1] >= self.EPOCH:
            self._newsem(e); c = self.cur[e]
        ins = fn(**kw)
        c[1] += 1
        ins.then_inc(c[0], 1)
        tok = (c[0], c[1])
        import os
        self.seen[(e, id(c[0]))] = c[1] if (e == "pe" or os.environ.get("NOSELF")) else self.seen.get((e, id(c[0])), 0)
        self.mark(tok, reads, writes)
        self.ninst += 1
        return tok
    def dma(self, q, out, in_, reads=(), writes=(), **kw):
        slot = self.drr % self.NDMA; self.drr += 1
        sem = self.dsem[slot]
        if self.dcnt[slot] > 0:
            self.wait(q, (sem, 16 * self.dcnt[slot]))
        self.deps(q, reads, writes)
        self.eng[q].dma_start(out=out, in_=in_, **kw).then_inc(sem, 16)
        self.dcnt[slot] += 1
        tok = (sem, 16 * self.dcnt[slot])
        self.mark(tok, reads, writes)
        self.ninst += 1
        return tok
    def finish(self, bufs):
        for b in bufs:
            self.wait("sp", b.w)


NCOL = 2176
REG = [(0, 2048, 0), (2048, 32, 2048), (2080, 32, 2112)]
RAWOFF = [3, 2054, 2089]
GROUPS = [(0, 8), (8, 8), (16, 8), (24, 8), (32, 2)]
SEGW = 2304
EPS = 1e-6

def bc_last(ap, n):
    return bass.AP(ap.tensor, ap.offset, [list(x) for x in ap.ap] + [[0, n]])
def bc_mid(ap, n):
    a = [list(x) for x in ap.ap]
    return bass.AP(ap.tensor, ap.offset, [a[0], [0, n]] + a[1:])

class Ctx:
    pass

def b_setup(nc, st, kb):
    c = Ctx(); c.nc = nc; c.st = st; c.kb = kb
    def sb(name, shape, dt=F32):
        return st.enter_context(nc.sbuf_tensor("s_" + name, list(shape), dt)), Buf(name)
    c.sb = sb
    c.ps = [(st.enter_context(nc.psum_tensor("bps%d" % i, [128, 512], F32)), Buf("bps%d" % i)) for i in range(8)]
    return c

def load_consts(c, cm_d, rows_d):
    kb = c.kb
    c.cm, c.bcm = c.sb("cm", [64, 6, 64])
    c.rows, c.brows = c.sb("rows", [1, 2, NCOL])
    kb.dma("sp", c.cm[:], cm_d, writes=[c.bcm])
    kb.dma("sp", c.rows[:], rows_d, writes=[c.brows])
    c.eps, c.beps = c.sb("epsb", [128, 1])
    kb.op("pool", c.nc.gpsimd.memset, writes=[c.beps], ap=c.eps[:], constant=EPS)
    c.ident = c.cm[:, 0, :]; c.maskUn = c.cm[:, 1, :]; c.maskLn = c.cm[:, 2, :]; c.maskI = c.cm[:, 3, :]; c.ones64 = c.cm[:, 4, :]

def gather_rows(c, dst_ap, bdst, zg, idx_ap, bidx, bzg=None):
    kb = c.kb; nc = c.nc
    reads = [bidx] + ([bzg] if bzg is not None else [])
    kb.deps("pool", reads, [bdst])
    slot = kb.drr % kb.NDMA; kb.drr += 1; sem = kb.dsem[slot]
    if kb.dcnt[slot] > 0: kb.wait("pool", (sem, 16 * kb.dcnt[slot]))
    nc.gpsimd.indirect_dma_start(out=dst_ap, out_offset=None, in_=zg, in_offset=bass.IndirectOffsetOnAxis(ap=idx_ap, axis=0)).then_inc(sem, 16)
    kb.dcnt[slot] += 1
    kb.mark((sem, 16 * kb.dcnt[slot]), reads, [bdst]); kb.ninst += 1

def scatter_rows(c, src_ap, bsrc, og, idx_ap, bidx, bog):
    kb = c.kb; nc = c.nc
    kb.deps("pool", [bsrc, bidx], [bog])
    slot = kb.drr % kb.NDMA; kb.drr += 1; sem = kb.dsem[slot]
    if kb.dcnt[slot] > 0: kb.wait("pool", (sem, 16 * kb.dcnt[slot]))
    nc.gpsimd.indirect_dma_start(out=og, out_offset=bass.IndirectOffsetOnAxis(ap=idx_ap, axis=0), in_=src_ap, in_offset=None).then_inc(sem, 16)
    kb.dcnt[slot] += 1
    kb.mark((sem, 16 * kb.dcnt[slot]), [bsrc, bidx], [bog]); kb.ninst += 1

def row_to_col(c, col, bcol, row_ap, brow, n, one11, bone):
    kb = c.kb; nc = c.nc
    p, pb = c.ps[6]
    for k in range(n):
        kb.op("pe", nc.tensor.matmul, reads=[brow, bone], writes=[pb], out=p[:64, k:k + 1], lhsT=row_ap[0:1, k * 64:(k + 1) * 64], rhs=one11, start=True, stop=True)
    kb.op("act", nc.scalar.copy, reads=[pb], writes=[bcol], out=col[:, :n], in_=p[:64, :n])

def bcast_rows(c, dst, bdst, row_ap, brow, ncols, func=None, ones_row=None, bones=None, psi=6):
    kb = c.kb; nc = c.nc
    t = 0
    while t < ncols:
        w = min(512, ncols - t)
        p, pb = c.ps[psi]
        kb.op("pe", nc.tensor.matmul, reads=[brow, bones], writes=[pb], out=p[:64, :w], lhsT=ones_row, rhs=row_ap[:, t:t + w], start=True, stop=True)
        if func is None:
            kb.op("act", nc.scalar.copy, reads=[pb], writes=[bdst], out=dst[:, t:t + w], in_=p[:64, :w])
        else:
            kb.op("act", nc.scalar.activation, reads=[pb], writes=[bdst], out=dst[:, t:t + w], in_=p[:64, :w], func=func)
        t += w

def drain(g):
    for _ in g: pass

def pipeline(la, seg_slots, post=None):
    gm = None
    for gi, (ci, nch) in enumerate(GROUPS):
        si = gi % 2
        if gm is None:
            drain(la.mats(ci, nch, si))
        nxt = la.mats(GROUPS[gi + 1][0], GROUPS[gi + 1][1], (gi + 1) % 2) if gi + 1 < len(GROUPS) else None
        sc = la.scan(ci, nch, seg_slots(ci, nch), si)
        for _ in sc:
            if nxt is not None:
                for k in range(3):
                    try: next(nxt)
                    except StopIteration: nxt = None; break
        if nxt is not None: drain(nxt)
        gm = True
        if post is not None: post(ci, nch)

class LinAttn:
    def __init__(self, c, name, delta, DV):
        self.c = c; self.name = name; self.delta = delta; self.DV = DV
        sb = lambda n, s, dt=F32: c.sb(name + n, s, dt)
        self.QT, self.bQT = sb("QT", [64, SEGW]); self.KT, self.bKT = sb("KT", [64, SEGW]); self.VT, self.bVT = sb("VT", [33, SEGW])
        self.grow, self.bgrow = sb("grow", [1, NCOL])
        self.g2row, self.bg2row = sb("g2row", [1, NCOL])
        self.gcol, self.bgcol = sb("gcol", [64, 34]); self.g2col, self.bg2col = sb("g2col", [64, 34])
        self.kwcol, self.bkwcol = sb("kwcol", [64, 34])
        self.rkcol, self.brkcol = sb("rkcol", [64, 34])
        self.gbc, self.bgbc = sb("gbc", [64, NCOL]); self.gam, self.bgam = sb("gam", [64, NCOL])
        if delta: self.g2bc, self.bg2bc = sb("g2bc", [64, NCOL])
        self.onesrow, self.bonesrow = sb("onesrow", [1, 64])
        c.kb.op("pool", c.nc.gpsimd.memset, writes=[self.bonesrow], ap=self.onesrow[:], constant=1.0)
        names = ["D", "DM", "N0", "P", "tmp"] if delta else ["D", "DM", "tmp"]
        self.w = {n: sb("w" + n, [64, 512]) for n in names}
        if delta:
            for n in ["Nb0", "NTb0", "Nb1", "NTb1", "Pb"]: self.w[n] = sb("w" + n, [64, 512], BF16)
        self.wset = [{n: sb("w%s%d" % (n, i), [64, 512]) for n in ["pT", "qgT", "KG"]} for i in range(2)]
        self.cur = 0
        if delta:
            self.RK = sb("RK", [64, 512]); self.wkTs = [sb("wkT%d" % i, [64, 512]) for i in range(2)]
            self.RVs = [sb("RV%d" % i, [64, 8, DV]) for i in range(2)]; self.wvs = [sb("wv%d" % i, [64, 8, DV]) for i in range(2)]; self.u = [sb("u%d" % i, [64, DV]) for i in range(2)]
        else:
            self.RVs = [sb("RV%d" % i, [64, 8, DV]) for i in range(2)]
            for i in range(2): c.kb.op("pool", c.nc.gpsimd.memset, writes=[self.RVs[i][1]], ap=self.RVs[i][0][:], constant=1.0)
        self.S, self.bS = sb("S", [64, 17, DV])
        self.oT, self.boT = sb("oT", [DV, SEGW])
        c.kb.op("pool", c.nc.gpsimd.memset, writes=[self.boT], ap=self.oT[:], constant=0.0)

    def mats(self, ci, nch, si=0):
        c = self.c; kb = c.kb; nc = c.nc; W = nch * 64; c0 = ci * 64
        v3 = lambda t: t[:, :W].rearrange("p (n i) -> p n i", i=64)
        ws = self.wset[si]
        D, bD = self.w["D"]; DM, bDM = self.w["DM"]; tmp, btmp = self.w["tmp"]
        gbc3 = v3(self.gbc[:, c0:c0 + W])
        kb.op("dve", nc.vector.tensor_tensor, reads=[self.bgbc, self.bgcol], writes=[bD], out=v3(D), in0=gbc3, in1=bc_last(self.gcol[:, ci:ci + nch], 64), op=ALU.subtract)
        kb.op("dve", nc.vector.tensor_scalar, reads=[bD], writes=[bD], out=D[:, :W], in0=D[:, :W], scalar1=0.0, scalar2=None, op0=ALU.min)
        kb.op("act", nc.scalar.activation, reads=[bD], writes=[bD], out=D[:, :W], in_=D[:, :W], func=AF.Exp)
        kb.op("dve", nc.vector.tensor_tensor, reads=[bD, c.bcm], writes=[bDM], out=v3(DM), in0=v3(D), in1=bc_mid(c.maskI, nch), op=ALU.mult)
        if not self.delta:
            kb.op("dve", nc.vector.tensor_tensor, reads=[bDM, self.bg2col], writes=[bDM], out=v3(DM), in0=v3(DM), in1=bc_last(self.g2col[:, ci:ci + nch], 64), op=ALU.mult)
        p, pb = c.ps[5]
        for n in range(nch):
            sl = slice(c0 + n * 64, c0 + n * 64 + 64)
            kb.op("pe", nc.tensor.matmul, reads=[self.bKT, self.bQT], writes=[pb], out=p[:64, n * 64:n * 64 + 64], lhsT=self.KT[:, sl], rhs=self.QT[:, sl], start=True, stop=True)
        pT, bpT = ws["pT"]
        yield
        kb.op("dve", nc.vector.tensor_tensor, reads=[pb, bDM], writes=[bpT], out=pT[:, :W], in0=p[:64, :W], in1=DM[:, :W], op=ALU.mult)
        qg, bqg = ws["qgT"]
        kb.op("pool", nc.gpsimd.tensor_tensor, reads=[self.bQT, self.bgam], writes=[bqg], out=qg[:, :W], in0=self.QT[:, c0:c0 + W], in1=self.gam[:, c0:c0 + W], op=ALU.mult)
        p2, pb2 = c.ps[2]
        for n in range(nch):
            sl = slice(c0 + n * 64, c0 + n * 64 + 64)
            kb.op("pe", nc.tensor.transpose, reads=[self.bKT, c.bcm], writes=[pb2], out=p2[:64, n * 64:n * 64 + 64], in_=self.KT[:, sl], identity=c.ident)
        KG, bKG = ws["KG"]
        yield
        kb.op("dve", nc.vector.tensor_tensor, reads=[pb2, self.bkwcol], writes=[bKG], out=v3(KG), in0=v3(p2[:64, :]), in1=bc_last(self.kwcol[:, ci:ci + nch], 64), op=ALU.mult)
        if self.delta:
            RK, bRK = self.RK
            kb.op("dve", nc.vector.tensor_tensor, reads=[pb2, self.brkcol], writes=[bRK], out=v3(RK), in0=v3(p2[:64, :]), in1=bc_last(self.rkcol[:, ci:ci + nch], 64), op=ALU.mult)
        DV = self.DV
        p3, pb3 = c.ps[3]
        nv = 32
        for n in range(nch):
            sl = slice(c0 + n * 64, c0 + n * 64 + 64)
            kb.op("pe", nc.tensor.transpose, reads=[self.bVT, c.bcm], writes=[pb3], out=p3[:64, n * 32:n * 32 + 32], in_=self.VT[:32, sl], identity=c.ident[:32, :32])
        RV, bRV = self.RVs[si]
        yield
        pv3 = p3[:64, :nch * 32].rearrange("p (n d) -> p n d", d=32)
        if self.delta:
            kb.op("dve", nc.vector.tensor_tensor, reads=[pb3, self.bg2col], writes=[bRV], out=RV[:, :nch, :], in0=pv3, in1=bc_last(self.g2col[:, ci:ci + nch], 32), op=ALU.mult)
        else:
            kb.op("act", nc.scalar.copy, reads=[pb3], writes=[bRV], out=RV[:, :nch, 0:32], in_=pv3)
        if not self.delta:
            return
        yield
        p0, pb0 = c.ps[0]
        for n in range(nch):
            sl = slice(c0 + n * 64, c0 + n * 64 + 64)
            kb.op("pe", nc.tensor.matmul, reads=[self.bKT], writes=[pb0], out=p0[:64, n * 64:n * 64 + 64], lhsT=self.KT[:, sl], rhs=self.KT[:, sl], start=True, stop=True)
        N0, bN0 = self.w["N0"]; NT0, bNT0 = self.w["NTb0"]; N1, bN1 = self.w["Nb1"]; NT1, bNT1 = self.w["NTb1"]; P, bP = self.w["P"]; Nb0, bNb0 = self.w["Nb0"]; Pb, bPb = self.w["Pb"]
        kb.op("dve", nc.vector.tensor_tensor, reads=[self.bg2bc, c.bcm], writes=[btmp], out=v3(tmp), in0=v3(self.g2bc[:, c0:c0 + W]), in1=bc_mid(c.maskUn, nch), op=ALU.mult)
        kb.op("dve", nc.vector.tensor_tensor, reads=[btmp, bD], writes=[btmp], out=tmp[:, :W], in0=tmp[:, :W], in1=D[:, :W], op=ALU.mult)
        kb.op("dve", nc.vector.tensor_tensor, reads=[pb0, btmp], writes=[bN0], out=N0[:, :W], in0=p0[:64, :W], in1=tmp[:, :W], op=ALU.mult)
        kb.op("dve", nc.vector.tensor_tensor, reads=[self.bgbc, self.bgcol], writes=[btmp], out=v3(tmp), in0=bc_last(self.gcol[:, ci:ci + nch], 64), in1=gbc3, op=ALU.subtract)
        kb.op("dve", nc.vector.tensor_scalar, reads=[btmp], writes=[btmp], out=tmp[:, :W], in0=tmp[:, :W], scalar1=0.0, scalar2=None, op0=ALU.min)
        kb.op("act", nc.scalar.activation, reads=[btmp], writes=[btmp], out=tmp[:, :W], in_=tmp[:, :W], func=AF.Exp)
        kb.op("dve", nc.vector.tensor_tensor, reads=[btmp, c.bcm], writes=[btmp], out=v3(tmp), in0=v3(tmp), in1=bc_mid(c.maskLn, nch), op=ALU.mult)
        kb.op("dve", nc.vector.tensor_tensor, reads=[btmp, self.bg2col], writes=[btmp], out=v3(tmp), in0=v3(tmp), in1=bc_last(self.g2col[:, ci:ci + nch], 64), op=ALU.mult)
        kb.op("dve", nc.vector.tensor_tensor, reads=[pb0, btmp], writes=[bNT0], out=NT0[:, :W], in0=p0[:64, :W], in1=tmp[:, :W], op=ALU.mult)
        kb.op("dve", nc.vector.tensor_tensor, reads=[bN0, c.bcm], writes=[bP], out=v3(P), in0=v3(N0), in1=bc_mid(c.ident, nch), op=ALU.add)
        kb.op("act", nc.scalar.copy, reads=[bN0], writes=[bNb0], out=Nb0[:, :W], in_=N0[:, :W])
        kb.op("act", nc.scalar.copy, reads=[bP], writes=[bPb], out=Pb[:, :W], in_=P[:, :W])
        A, bA, AT, bAT = Nb0, bNb0, NT0, bNT0
        A2, bA2, AT2, bAT2 = N1, bN1, NT1, bNT1
        pa, pab = c.ps[0]; pat, patb = c.ps[1]; pp, ppb = c.ps[4]
        for r in range(5):
            last = (r == 4)
            yield
            if not last:
                for n in range(nch):
                    s = slice(n * 64, n * 64 + 64)
                    kb.op("pe", nc.tensor.matmul, reads=[bA, bAT], writes=[pab], out=pa[:64, s], lhsT=AT[:, s], rhs=A[:, s], start=True, stop=True)
            for n in range(nch):
                s = slice(n * 64, n * 64 + 64)
                kb.op("pe", nc.tensor.matmul, reads=[bA, bAT], writes=[patb], out=pat[:64, s], lhsT=A[:, s], rhs=AT[:, s], start=True, stop=True)
            if not last:
                kb.op("act", nc.scalar.copy, reads=[pab], writes=[bA2], out=A2[:, :W], in_=pa[:64, :W])
            kb.op("dve", nc.vector.tensor_copy, reads=[patb], writes=[bAT2], out=AT2[:, :W], in_=pat[:64, :W])
            yield
            for n in range(nch):
                s = slice(n * 64, n * 64 + 64)
                kb.op("pe", nc.tensor.matmul, reads=[bAT2, bPb], writes=[ppb], out=pp[:64, s], lhsT=AT2[:, s], rhs=Pb[:, s], start=True, stop=True)
            kb.op("dve", nc.vector.tensor_tensor, reads=[ppb, bP], writes=[bP], out=P[:, :W], in0=pp[:64, :W], in1=P[:, :W], op=ALU.add)
            if not last:
                kb.op("act", nc.scalar.copy, reads=[bP], writes=[bPb], out=Pb[:, :W], in_=P[:, :W])
            A, bA, AT, bAT, A2, bA2, AT2, bAT2 = A2, bA2, AT2, bAT2, A, bA, AT, bAT
        yield
        pw, pwb = c.ps[3]; pk, pkb = c.ps[2]
        RK, bRK = self.RK
        for n in range(nch):
            s = slice(n * 64, n * 64 + 64)
            kb.op("pe", nc.tensor.matmul, reads=[bP, bRV], writes=[pwb], out=pw[:64, n * 32:n * 32 + 32], lhsT=P[:, s], rhs=RV[:, n, :], start=True, stop=True)
        for n in range(nch):
            s = slice(n * 64, n * 64 + 64)
            kb.op("pe", nc.tensor.matmul, reads=[bP, bRK], writes=[pkb], out=pk[:64, s], lhsT=RK[:, s], rhs=P[:, s], start=True, stop=True)
        wv, bwv = self.wvs[si]; wkT, bwkT = self.wkTs[si]
        yield
        kb.op("act", nc.scalar.copy, reads=[pwb], writes=[bwv], out=wv[:, :nch, :], in_=pw[:64, :nch * 32].rearrange("p (n d) -> p n d", d=32))
        kb.op("act", nc.scalar.copy, reads=[pkb], writes=[bwkT], out=wkT[:, :W], in_=pk[:64, :W])

    def scan(self, ci, nch, slots, si=0):
        c = self.c; kb = c.kb; nc = c.nc; DV = self.DV; c0 = ci * 64
        ws = self.wset[si]
        pT, bpT = ws["pT"]; qg, bqg = ws["qgT"]; KG, bKG = ws["KG"]; RV, bRV = self.RVs[si]
        po, pob = c.ps[7]; psm, psmb = c.ps[6]
        for n in range(nch):
            s = slice(n * 64, n * 64 + 64); sl = slots[n]
            S = self.S[:, sl, :]
            if self.delta:
                wv, bwv = self.wvs[si]; wkT, bwkT = self.wkTs[si]; u, bu = self.u[n % 2]
                kb.op("pe", nc.tensor.matmul, reads=[bwkT, self.bS], writes=[psmb], out=psm[:64, 0:DV], lhsT=wkT[:, s], rhs=S, start=True, stop=True)
                kb.op("pe", nc.tensor.matmul, reads=[bqg, self.bS], writes=[pob], out=po[:DV, s], lhsT=S, rhs=qg[:, s], start=True, stop=False)
                kb.op("dve", nc.vector.tensor_tensor, reads=[psmb, bwv], writes=[bu], out=u[:], in0=wv[:, n, :], in1=psm[:64, 0:DV], op=ALU.subtract)
                uu, buu = u[:], bu
            else:
                kb.op("pe", nc.tensor.matmul, reads=[bqg, self.bS], writes=[pob], out=po[:DV, s], lhsT=S, rhs=qg[:, s], start=True, stop=False)
                uu, buu = RV[:, n, :], bRV
            kb.op("pe", nc.tensor.matmul, reads=[bpT, buu], writes=[pob], out=po[:DV, s], lhsT=uu, rhs=pT[:, s], start=False, stop=True)
            kb.op("pe", nc.tensor.matmul, reads=[bKG, buu], writes=[psmb], out=psm[:64, 64:64 + DV], lhsT=KG[:, s], rhs=uu, start=True, stop=True)
            glast = self.gam[:, c0 + n * 64 + 63:c0 + n * 64 + 64]
            kb.op("dve", nc.vector.scalar_tensor_tensor, reads=[psmb, self.bS, self.bgam], writes=[self.bS], out=S, in0=S, scalar=glast, in1=psm[:64, 64:64 + DV], op0=ALU.mult, op1=ALU.add)
            yield
        kb.op("act", nc.scalar.copy, reads=[pob], writes=[self.boT], out=self.oT[:, c0:c0 + nch * 64], in_=po[:DV, :nch * 64])

def gate_cols(c, la, ci0=0):
    kb = c.kb; nc = c.nc
    row_to_col(c, la.gcol, la.bgcol, la.grow[0:1, :], la.bgrow, 34, la.onesrow[0:1, 0:1], la.bonesrow)
    row_to_col(c, la.g2col, la.bg2col, la.g2row[0:1, :], la.bg2row, 34, la.onesrow[0:1, 0:1], la.bonesrow)
    bcast_rows(c, la.gbc, la.bgbc, la.grow, la.bgrow, NCOL, ones_row=la.onesrow[:], bones=la.bonesrow)
    bcast_rows(c, la.gam, la.bgam, la.grow, la.bgrow, NCOL, func=AF.Exp, ones_row=la.onesrow[:], bones=la.bonesrow)

class GDN:
    def __init__(self, c, prm_d):
        self.c = c; kb = c.kb; nc = c.nc
        self.la = LinAttn(c, "gdn", True, 32)
        sb = c.sb
        self.R, self.bR = sb("gR", [64, 3 + SEGW]); self.H = [sb("gH%d" % i, [64, 3]) for i in range(3)]
        self.SR, self.bSR = sb("gSR", [64, 70])
        self.cw = [sb("gcw%d" % i, [64, 4]) for i in range(3)]
        self.cst = [sb("gcst%d" % i, [64, 16, 3]) for i in range(3)]
        self.prm, self.bprm = sb("gprm", [1, 4])
        self.gt, self.bgt = sb("ggt", [2, SEGW])
        self.ysq, self.bysq = sb("gysq", [64, 512]); self.rinv, self.brinv = sb("grinv", [64, 512])
        for i, (r0, nr) in enumerate([(0, 64), (64, 64), (128, 32)]):
            kb.dma("sp", self.cw[i][0][:nr, :], prm_d["convw"][r0:r0 + nr, :], writes=[self.cw[i][1]])
            kb.dma("sp", self.cst[i][0][:nr], prm_d["convst"][r0:r0 + nr], writes=[self.cst[i][1]])
        kb.dma("sp", self.prm[:, 0:1], prm_d["alog"], writes=[self.bprm]); kb.dma("sp", self.prm[:, 1:2], prm_d["dtb"], writes=[self.bprm])
        kb.op("act", nc.scalar.activation, reads=[self.bprm], writes=[self.bprm], out=self.prm[:, 2:3], in_=self.prm[:, 0:1], func=AF.Exp)
        kb.op("dve", nc.vector.tensor_scalar, reads=[self.bprm], writes=[self.bprm], out=self.prm[:, 2:3], in0=self.prm[:, 2:3], scalar1=-1.0, scalar2=None, op0=ALU.mult)
        la = self.la
        kb.dma("sp", la.S[:, 1:17, :], prm_d["s0"], writes=[la.bS])
        kb.op("pool", nc.gpsimd.memset, writes=[la.bS], ap=la.S[:, 0, :], constant=0.0)
        for t, b in [(la.QT, la.bQT), (la.KT, la.bKT), (la.VT, la.bVT), (self.gt, self.bgt)]:
            kb.op("pool", nc.gpsimd.memset, writes=[b], ap=t[:], constant=0.0)
        for t, b in self.H:
            kb.op("pool", nc.gpsimd.memset, writes=[b], ap=t[:], constant=0.0)

    def segment(self, s, zg, bzg, idx, bidx, icol):
        c = self.c; kb = c.kb; nc = c.nc; la = self.la
        raws = [(self.R, self.bR, 64, la.QT, la.bQT), (self.R, self.bR, 64, la.KT, la.bKT), (self.R, self.bR, 32, la.VT, la.bVT)]
        for i, (R, bR, nr, Y, bY) in enumerate(raws):
            gather_rows(c, R[:nr, 3:3 + SEGW], bR, zg, idx[:nr, icol["qkv"[i]]:icol["qkv"[i]] + 1], bidx, bzg=bzg)
            Hh, bHh = self.H[i]
            kb.op("pool", nc.gpsimd.tensor_copy, reads=[bHh], writes=[bR], out=R[:nr, 0:3], in_=Hh[:nr, :])
            kb.op("pool", nc.gpsimd.tensor_copy, reads=[bR], writes=[bHh], out=Hh[:nr, :], in_=R[:nr, 2048:2051])
            cst, bcst = self.cst[i]; SR, bSR = self.SR, self.bSR
            kb.op("pool", nc.gpsimd.tensor_copy, reads=[bcst], writes=[bSR], out=SR[:nr, 0:3], in_=cst[:nr, 2 * s, :])
            kb.op("pool", nc.gpsimd.tensor_copy, reads=[bR], writes=[bSR], out=SR[:nr, 3:35], in_=R[:nr, 3 + 2048:3 + 2080])
            kb.op("pool", nc.gpsimd.tensor_copy, reads=[bcst], writes=[bSR], out=SR[:nr, 35:38], in_=cst[:nr, 2 * s + 1, :])
            kb.op("pool", nc.gpsimd.tensor_copy, reads=[bR], writes=[bSR], out=SR[:nr, 38:70], in_=R[:nr, 3 + 2112:3 + 2144])
            cw, bcw = self.cw[i]
            for (X, bX, r0, ln, d0) in [(R, bR, 3, 2048, 0), (SR, bSR, 3, 32, 2048), (SR, bSR, 38, 32, 2112)]:
                kb.op("dve", nc.vector.tensor_scalar, reads=[bX, bcw], writes=[bY], out=Y[:nr, d0:d0 + ln], in0=X[:nr, r0:r0 + ln], scalar1=cw[:nr, 3:4], scalar2=None, op0=ALU.mult)
                for t in range(3):
                    kb.op("dve", nc.vector.scalar_tensor_tensor, reads=[bX, bcw, bY], writes=[bY], out=Y[:nr, d0:d0 + ln], in0=X[:nr, r0 - 3 + t:r0 - 3 + t + ln],
                          scalar=cw[:nr, t:t + 1], in1=Y[:nr, d0:d0 + ln], op0=ALU.mult, op1=ALU.add)
            kb.op("act", nc.scalar.activation, reads=[bY], writes=[bY], out=Y[:nr, :], in_=Y[:nr, :], func=AF.Silu)
            if i < 2:
                for t0 in range(0, NCOL, 512):
                    w = min(512, NCOL - t0)
                    kb.op("act", nc.scalar.activation, reads=[bY], writes=[self.bysq], out=self.ysq[:, :w], in_=Y[:, t0:t0 + w], func=AF.Square)
                    p, pb = c.ps[6]
                    kb.op("pe", nc.tensor.matmul, reads=[self.bysq, c.bcm], writes=[pb], out=p[:64, :w], lhsT=c.ones64, rhs=self.ysq[:, :w], start=True, stop=True)
                    kb.op("act", nc.scalar.activation, reads=[pb, c.beps], writes=[self.brinv], out=self.rinv[:, :w], in_=p[:64, :w], func=AF.Sqrt, bias=c.eps[:64, 0:1])
                    kb.op("dve", nc.vector.reciprocal, reads=[self.brinv], writes=[self.brinv], out=self.rinv[:, :w], in_=self.rinv[:, :w])
                    kb.op("dve", nc.vector.scalar_tensor_tensor, reads=[bY, self.brinv], writes=[bY], out=Y[:, t0:t0 + w], in0=Y[:, t0:t0 + w], scalar=(0.125 if i == 0 else 1.0),
                          in1=self.rinv[:, :w], op0=ALU.mult, op1=ALU.mult)
        gather_rows(c, self.gt[:2, :], self.bgt, zg, idx[:2, icol["g"]:icol["g"] + 1], bidx, bzg=bzg)
        kb.dma("sp", la.g2row[:], self.gt[1:2, :NCOL], reads=[self.bgt], writes=[la.bg2row])
        valid = c.rows[:, 1, :]
        kb.op("act", nc.scalar.activation, reads=[self.bgt, self.bprm], writes=[la.bgrow], out=la.grow[:], in_=self.gt[0:1, :NCOL], func=AF.Exp, bias=self.prm[:, 1:2])
        kb.op("act", nc.scalar.activation, reads=[la.bgrow], writes=[la.bgrow], out=la.grow[:], in_=la.grow[:], func=AF.Ln, bias=1.0)
        kb.op("dve", nc.vector.scalar_tensor_tensor, reads=[la.bgrow, self.bprm, c.brows], writes=[la.bgrow], out=la.grow[:], in0=la.grow[:], scalar=self.prm[:, 2:3], in1=valid, op0=ALU.mult, op1=ALU.mult)
        kb.op("dve", nc.vector.tensor_tensor_scan, reads=[la.bgrow, c.brows], writes=[la.bgrow], out=la.grow[:], data0=c.rows[:, 0, :], data1=la.grow[:], initial=0.0, op0=ALU.mult, op1=ALU.add)
        kb.op("act", nc.scalar.activation, reads=[la.bg2row], writes=[la.bg2row], out=la.g2row[:], in_=la.g2row[:], func=AF.Sigmoid)
        kb.op("dve", nc.vector.tensor_tensor, reads=[la.bg2row, c.brows], writes=[la.bg2row], out=la.g2row[:], in0=la.g2row[:], in1=valid, op=ALU.mult)
        gate_cols(c, la)
        bcast_rows(c, la.g2bc, la.bg2bc, la.g2row, la.bg2row, NCOL, ones_row=la.onesrow[:], bones=la.bonesrow)
        glast = la.gbc[:, 63:NCOL:64]
        kb.op("dve", nc.vector.tensor_tensor, reads=[la.bgbc, la.bgcol], writes=[la.bkwcol], out=la.kwcol[:], in0=glast, in1=la.gcol[:], op=ALU.subtract)
        kb.op("act", nc.scalar.activation, reads=[la.bkwcol], writes=[la.bkwcol], out=la.kwcol[:], in_=la.kwcol[:], func=AF.Exp)
        kb.op("act", nc.scalar.activation, reads=[la.bgcol], writes=[la.brkcol], out=la.rkcol[:], in_=la.gcol[:], func=AF.Exp)
        kb.op("dve", nc.vector.tensor_tensor, reads=[la.brkcol, la.bg2col], writes=[la.brkcol], out=la.rkcol[:], in0=la.rkcol[:], in1=la.g2col[:], op=ALU.mult)
        pipeline(la, lambda ci, nch: [0] * nch if ci < 32 else [1 + 2 * s, 2 + 2 * s])


class MLSTM:
    def __init__(self, c, prm_d):
        self.c = c; kb = c.kb; nc = c.nc
        self.la = la = LinAttn(c, "ml", False, 33)
        sb = c.sb
        self.prm, self.bprm = sb("mprm", [1, 4])
        self.gt, self.bgt = sb("mgt", [2, SEGW])
        self.padb, self.bpadb = sb("mpadb", [1, NCOL])
        self.mrow, self.bmrow = sb("mmrow", [1, NCOL]); self.mfin, self.bmfin = sb("mmfin", [1, 17]); self.em, self.bem = sb("mem", [1, 17])
        self.embc, self.bembc = sb("membc", [64, 17]); self.Sout, self.bSout = sb("mSout", [64, 17, 33])
        self.lf, self.blf = sb("mlf", [1, NCOL])
        self.on33, self.bon33 = sb("mon33", [33, 32]); self.rrow, self.brrow = sb("mrrow", [33, 512]); self.hT, self.bhT = sb("mhT", [32, SEGW])
        kb.op("pool", nc.gpsimd.memset, writes=[self.bhT], ap=self.hT[:], constant=0.0)
        kb.op("pool", nc.gpsimd.memset, writes=[self.bon33], ap=self.on33[:], constant=1.0)
        kb.dma("sp", self.prm[:, 0:1], prm_d["bi"], writes=[self.bprm]); kb.dma("sp", self.prm[:, 1:2], prm_d["bf"], writes=[self.bprm])
        kb.op("dve", nc.vector.tensor_scalar, reads=[self.bprm], writes=[self.bprm], out=self.prm[:, 2:3], in0=self.prm[:, 1:2], scalar1=-1.0, scalar2=None, op0=ALU.mult)
        kb.op("dve", nc.vector.tensor_scalar, reads=[c.brows], writes=[self.bpadb], out=self.padb[:], in0=c.rows[:, 1, :], scalar1=1.0, scalar2=30000.0, op0=ALU.subtract, op1=ALU.mult)
        kb.dma("sp", la.S[:, 1:17, :], prm_d["s0"], writes=[la.bS])
        kb.op("pool", nc.gpsimd.memset, writes=[la.bS], ap=la.S[:, 0, :], constant=0.0)
        kb.op("pool", nc.gpsimd.memset, writes=[self.bmfin], ap=self.mfin[:], constant=0.0)
        kb.dma("sp", self.mfin[:, 1:17], prm_d["m0"], writes=[self.bmfin])
        kb.op("act", nc.scalar.activation, reads=[self.bmfin], writes=[self.bem], out=self.em[:], in_=self.mfin[:], func=AF.Exp)
        p, pb = c.ps[6]
        kb.op("pe", nc.tensor.matmul, reads=[self.bem, la.bonesrow], writes=[pb], out=p[:64, :17], lhsT=la.onesrow[:], rhs=self.em[:], start=True, stop=True)
        kb.op("act", nc.scalar.copy, reads=[pb], writes=[self.bembc], out=self.embc[:], in_=p[:64, :17])
        kb.op("dve", nc.vector.tensor_tensor, reads=[la.bS, self.bembc], writes=[la.bS], out=la.S[:, 1:17, :], in0=la.S[:, 1:17, :], in1=bc_last(self.embc[:, 1:17], 33), op=ALU.mult)
        kb.op("pool", nc.gpsimd.memset, writes=[la.bVT], ap=la.VT[:], constant=1.0)

    def segment(self, s, zg, bzg, idx, bidx, icol):
        c = self.c; kb = c.kb; nc = c.nc; la = self.la
        gather_rows(c, la.QT[:, :], la.bQT, zg, idx[:64, icol["q"]:icol["q"] + 1], bidx, bzg=bzg)
        gather_rows(c, la.KT[:, :], la.bKT, zg, idx[:64, icol["k"]:icol["k"] + 1], bidx, bzg=bzg)
        gather_rows(c, la.VT[:32, :], la.bVT, zg, idx[:32, icol["v"]:icol["v"] + 1], bidx, bzg=bzg)
        gather_rows(c, self.gt[:2, :], self.bgt, zg, idx[:2, icol["g"]:icol["g"] + 1], bidx, bzg=bzg)
        kb.op("act", nc.scalar.mul, reads=[la.bKT], writes=[la.bKT], out=la.KT[:, :NCOL], in_=la.KT[:, :NCOL], mul=0.125)
        kb.dma("sp", self.lf[:], self.gt[1:2, :NCOL], reads=[self.bgt], writes=[self.blf])
        valid = c.rows[:, 1, :]
        kb.op("dve", nc.vector.scalar_tensor_tensor, reads=[self.bgt, self.bprm, c.brows], writes=[la.bg2row], out=la.g2row[:], in0=self.gt[0:1, :NCOL], scalar=self.prm[:, 0:1], in1=valid, op0=ALU.add, op1=ALU.mult)
        kb.op("dve", nc.vector.tensor_tensor, reads=[la.bg2row, self.bpadb], writes=[la.bg2row], out=la.g2row[:], in0=la.g2row[:], in1=self.padb[:], op=ALU.add)
        kb.op("act", nc.scalar.activation, reads=[self.blf, self.bprm], writes=[self.blf], out=self.lf[:], in_=self.lf[:], func=AF.Exp, scale=-1.0, bias=self.prm[:, 2:3])
        kb.op("act", nc.scalar.activation, reads=[self.blf], writes=[self.blf], out=self.lf[:], in_=self.lf[:], func=AF.Ln, bias=1.0)
        kb.op("dve", nc.vector.scalar_tensor_tensor, reads=[self.blf, c.brows], writes=[self.blf], out=self.lf[:], in0=self.lf[:], scalar=-1.0, in1=valid, op0=ALU.mult, op1=ALU.mult)
        for (c0, ln, slot) in [(0, 2048, 0), (2048, 32, 1 + 2 * s), (2112, 32, 2 + 2 * s)]:
            kb.op("dve", nc.vector.tensor_tensor_scan, reads=[self.blf, la.bg2row, self.bmfin], writes=[self.bmrow], out=self.mrow[:, c0:c0 + ln], data0=self.lf[:, c0:c0 + ln], data1=la.g2row[:, c0:c0 + ln],
                  initial=self.mfin[:, slot:slot + 1], op0=ALU.add, op1=ALU.max)
            kb.op("dve", nc.vector.tensor_copy, reads=[self.bmrow], writes=[self.bmfin], out=self.mfin[:, slot:slot + 1], in_=self.mrow[:, c0 + ln - 1:c0 + ln])
        kb.op("dve", nc.vector.tensor_tensor_scan, reads=[self.blf, c.brows], writes=[la.bgrow], out=la.grow[:], data0=c.rows[:, 0, :], data1=self.lf[:], initial=0.0, op0=ALU.mult, op1=ALU.add)
        gate_cols(c, la)
        kb.op("act", nc.scalar.activation, reads=[la.bg2col], writes=[la.bg2col], out=la.g2col[:], in_=la.g2col[:], func=AF.Exp)
        glast = la.gbc[:, 63:NCOL:64]
        kb.op("dve", nc.vector.tensor_tensor, reads=[la.bgbc, la.bgcol], writes=[la.bkwcol], out=la.kwcol[:], in0=glast, in1=la.gcol[:], op=ALU.subtract)
        kb.op("act", nc.scalar.activation, reads=[la.bkwcol], writes=[la.bkwcol], out=la.kwcol[:], in_=la.kwcol[:], func=AF.Exp)
        kb.op("dve", nc.vector.tensor_tensor, reads=[la.bkwcol, la.bg2col], writes=[la.bkwcol], out=la.kwcol[:], in0=la.kwcol[:], in1=la.g2col[:], op=ALU.mult)
        def post(ci, nch):
            c0 = ci * 64; W = nch * 64
            kb.op("dve", nc.vector.scalar_tensor_tensor, reads=[la.boT], writes=[self.brrow], out=self.rrow[32:33, :W], in0=la.oT[32:33, c0:c0 + W], scalar=-1.0, in1=la.oT[32:33, c0:c0 + W], op0=ALU.mult, op1=ALU.max)
            kb.op("dve", nc.vector.tensor_scalar, reads=[self.brrow], writes=[self.brrow], out=self.rrow[32:33, :W], in0=self.rrow[32:33, :W], scalar1=1.0, scalar2=None, op0=ALU.max)
            kb.op("dve", nc.vector.reciprocal, reads=[self.brrow], writes=[self.brrow], out=self.rrow[32:33, :W], in_=self.rrow[32:33, :W])
            p, pb = c.ps[6]
            kb.op("pe", nc.tensor.matmul, reads=[self.brrow, self.bon33], writes=[pb], out=p[:32, :W], lhsT=self.on33[32:33, :], rhs=self.rrow[32:33, :W], start=True, stop=True)
            kb.op("dve", nc.vector.tensor_tensor, reads=[pb, la.boT], writes=[self.bhT], out=self.hT[:, c0:c0 + W], in0=la.oT[0:32, c0:c0 + W], in1=p[:32, :W], op=ALU.mult)
        pipeline(la, lambda ci, nch: [0] * nch if ci < 32 else [1 + 2 * s, 2 + 2 * s], post)

    def finish(self):
        c = self.c; kb = c.kb; nc = c.nc; la = self.la
        kb.op("act", nc.scalar.activation, reads=[self.bmfin], writes=[self.bem], out=self.em[:], in_=self.mfin[:], func=AF.Exp, scale=-1.0)
        p, pb = c.ps[6]
        kb.op("pe", nc.tensor.matmul, reads=[self.bem, la.bonesrow], writes=[pb], out=p[:64, :17], lhsT=la.onesrow[:], rhs=self.em[:], start=True, stop=True)
        kb.op("act", nc.scalar.copy, reads=[pb], writes=[self.bembc], out=self.embc[:], in_=p[:64, :17])
        kb.op("dve", nc.vector.tensor_tensor, reads=[la.bS, self.bembc], writes=[self.bSout], out=self.Sout[:], in0=la.S[:], in1=bc_last(self.embc[:], 33), op=ALU.mult)

PI = 3.141592653589793

class S5:
    def __init__(self, c, prm_d):
        self.c = c; kb = c.kb; nc = c.nc; sb = c.sb
        V = nc.vector
        self.pv, self.bpv = sb("s5pv", [128, 32])
        self.BT, self.bBT = sb("s5BT", [32, 2, 128]); self.CT, self.bCT = sb("s5CT", [128, 2, 32]); self.dv, self.bdv = sb("s5dv", [32, 1])
        self.X, self.bX = sb("s5X", [128, 2, 17])
        self.U, self.bU = sb("s5U", [128, 2, 2048]); self.L, self.bL = sb("s5L", [128, 2, 2048])
        self.rho, self.brho = sb("s5rho", [128, NCOL])
        self.uT, self.buT = sb("s5uT", [32, SEGW])
        self.bu, self.bbu = sb("s5bu", [128, 2, NCOL]); self.rr, self.brr = sb("s5rr", [128, 2, NCOL]); self.ww, self.bww = sb("s5ww", [128, 2, NCOL])
        self.t1, self.bt1 = sb("s5t1", [128, NCOL]); self.yT, self.byT = sb("s5yT", [32, SEGW])
        kb.op("pool", nc.gpsimd.memset, writes=[self.byT], ap=self.yT[:], constant=0.0)
        self.pw, self.bpw = sb("s5pw", [128, 8])
        pv = self.pv; bpv = self.bpv
        kb.dma("sp", pv[:, 0:3], prm_d["vec"], writes=[bpv])
        kb.dma("sp", self.BT[:, 0, :], prm_d["BreT"], writes=[self.bBT]); kb.dma("sp", self.BT[:, 1, :], prm_d["BimT"], writes=[self.bBT])
        kb.dma("sp", self.CT[:, 0, :], prm_d["CreT"], writes=[self.bCT]); kb.dma("sp", self.CT[:, 1, :], prm_d["CimT"], writes=[self.bCT])
        kb.dma("sp", self.dv[:], prm_d["dvec"], writes=[self.bdv])
        kb.op("pool", nc.gpsimd.memset, writes=[self.bX], ap=self.X[:], constant=0.0)
        kb.dma("sp", self.X[:, :, 1:17], prm_d["x0"], writes=[self.bX])
        col = lambda i: pv[:, i:i + 1]
        def ts(out, in0, s1, s2, o0, o1=None):
            if o1 is None: kb.op("dve", V.tensor_scalar, reads=[bpv], writes=[bpv], out=out, in0=in0, scalar1=s1, scalar2=None, op0=o0)
            else: kb.op("dve", V.tensor_scalar, reads=[bpv], writes=[bpv], out=out, in0=in0, scalar1=s1, scalar2=s2, op0=o0, op1=o1)
        def tt(out, a, b, op): kb.op("dve", V.tensor_tensor, reads=[bpv], writes=[bpv], out=out, in0=a, in1=b, op=op)
        def act(out, in_, func, **kw): kb.op("act", nc.scalar.activation, reads=[bpv], writes=[bpv], out=out, in_=in_, func=func, **kw)
        act(col(3), col(2), AF.Exp)
        tt(col(4), col(3), col(0), ALU.mult); tt(col(5), col(3), col(1), ALU.mult)
        act(col(6), col(4), AF.Exp)
        def wrap(dst, src, thrs):
            kb.op("dve", V.tensor_copy, reads=[bpv], writes=[bpv], out=dst, in_=src)
            for th in thrs:
                ts(col(7), src, th, -2 * PI, ALU.is_gt, ALU.mult)
                tt(dst, dst, col(7), ALU.add)
        wrap(col(8), col(5), [PI, 3 * PI, 5 * PI])
        act(col(9), col(8), AF.Sin)
        ts(col(17), col(8), PI / 2, None, ALU.add)
        wrap(col(8), col(17), [PI])
        act(col(10), col(8), AF.Sin)
        tt(col(11), col(6), col(10), ALU.mult); tt(col(12), col(6), col(9), ALU.mult)
        tt(col(13), col(0), col(0), ALU.mult); tt(col(7), col(1), col(1), ALU.mult); tt(col(13), col(13), col(7), ALU.add)
        kb.op("dve", V.reciprocal, reads=[bpv], writes=[bpv], out=col(13), in_=col(13))
        ts(col(14), col(11), -1.0, None, ALU.add)
        tt(col(15), col(14), col(0), ALU.mult); tt(col(7), col(12), col(1), ALU.mult); tt(col(15), col(15), col(7), ALU.add); tt(col(15), col(15), col(13), ALU.mult)
        tt(col(16), col(12), col(0), ALU.mult); tt(col(7), col(14), col(1), ALU.mult); tt(col(16), col(16), col(7), ALU.subtract); tt(col(16), col(16), col(13), ALU.mult)
        ts(col(18), col(16), -1.0, None, ALU.mult)
        kb.op("pool", nc.gpsimd.memset, writes=[self.brho], ap=self.rho[:], constant=0.0)
        kb.op("dve", V.tensor_scalar, reads=[bpv, self.brho], writes=[self.brho], out=self.rho[:], in0=self.rho[:], scalar1=col(6), scalar2=None, op0=ALU.add)
        for t0 in [0, 2048, 2112]:
            kb.op("pool", nc.gpsimd.memset, writes=[self.brho], ap=self.rho[:, t0:t0 + 1], constant=0.0)
        def table(T, bT, first_re, first_im, lead_one):
            pw = self.pw; bpw = self.bpw
            if lead_one:
                kb.op("pool", nc.gpsimd.memset, writes=[bT], ap=T[:, 0, 0:1], constant=1.0)
                kb.op("pool", nc.gpsimd.memset, writes=[bT], ap=T[:, 1, 0:1], constant=0.0)
            else:
                kb.op("dve", V.tensor_copy, reads=[bpv], writes=[bT], out=T[:, 0, 0:1], in_=first_re)
                kb.op("dve", V.tensor_copy, reads=[bpv], writes=[bT], out=T[:, 1, 0:1], in_=first_im)
            kb.op("dve", V.tensor_copy, reads=[bpv], writes=[bpw], out=pw[:, 0:1], in_=first_re)
            kb.op("dve", V.tensor_copy, reads=[bpv], writes=[bpw], out=pw[:, 1:2], in_=first_im)
            m = 1
            while m < 2048:
                kb.op("dve", V.tensor_scalar, reads=[bpw], writes=[bpw], out=pw[:, 2:3], in0=pw[:, 1:2], scalar1=-1.0, scalar2=None, op0=ALU.mult)
                kb.op("dve", V.tensor_scalar, reads=[bT, bpw], writes=[bT], out=T[:, 0, m:2 * m], in0=T[:, 0, 0:m], scalar1=pw[:, 0:1], scalar2=None, op0=ALU.mult)
                kb.op("dve", V.scalar_tensor_tensor, reads=[bT, bpw], writes=[bT], out=T[:, 0, m:2 * m], in0=T[:, 1, 0:m], scalar=pw[:, 2:3], in1=T[:, 0, m:2 * m], op0=ALU.mult, op1=ALU.add)
                kb.op("dve", V.tensor_scalar, reads=[bT, bpw], writes=[bT], out=T[:, 1, m:2 * m], in0=T[:, 0, 0:m], scalar1=pw[:, 1:2], scalar2=None, op0=ALU.mult)
                kb.op("dve", V.scalar_tensor_tensor, reads=[bT, bpw], writes=[bT], out=T[:, 1, m:2 * m], in0=T[:, 1, 0:m], scalar=pw[:, 0:1], in1=T[:, 1, m:2 * m], op0=ALU.mult, op1=ALU.add)
                kb.op("dve", V.tensor_tensor, reads=[bpw], writes=[bpw], out=pw[:, 3:4], in0=pw[:, 0:1], in1=pw[:, 1:2], op=ALU.mult)
                kb.op("dve", V.tensor_tensor, reads=[bpw], writes=[bpw], out=pw[:, 4:5], in0=pw[:, 1:2], in1=pw[:, 1:2], op=ALU.mult)
                kb.op("dve", V.scalar_tensor_tensor, reads=[bpw], writes=[bpw], out=pw[:, 0:1], in0=pw[:, 0:1], scalar=pw[:, 0:1], in1=pw[:, 4:5], op0=ALU.mult, op1=ALU.subtract)
                kb.op("dve", V.tensor_scalar, reads=[bpw], writes=[bpw], out=pw[:, 1:2], in0=pw[:, 3:4], scalar1=2.0, scalar2=None, op0=ALU.mult)
                m *= 2
        table(self.U, self.bU, col(10), col(9), True)
        table(self.L, self.bL, col(11), col(12), False)

    def segment(self, s, zg, bzg, idx, bidx, icol):
        c = self.c; kb = c.kb; nc = c.nc; V = nc.vector; G = nc.gpsimd; pv = self.pv; bpv = self.bpv
        col = lambda i: pv[:, i:i + 1]
        gather_rows(c, self.uT[:32, :], self.buT, zg, idx[:32, icol:icol + 1], bidx, bzg=bzg)
        bu = self.bu; bbu = self.bbu
        for t0 in range(0, NCOL, 512):
            w = min(512, NCOL - t0)
            (p1, b1), (p2, b2) = c.ps[0], c.ps[1]
            kb.op("pe", nc.tensor.matmul, reads=[self.buT, self.bBT], writes=[b1], out=p1[:, :w], lhsT=self.BT[:, 0, :], rhs=self.uT[:, t0:t0 + w], start=True, stop=True)
            kb.op("pe", nc.tensor.matmul, reads=[self.buT, self.bBT], writes=[b2], out=p2[:, :w], lhsT=self.BT[:, 1, :], rhs=self.uT[:, t0:t0 + w], start=True, stop=True)
            kb.op("dve", V.tensor_scalar, reads=[b1, bpv], writes=[bbu], out=bu[:, 0, t0:t0 + w], in0=p1[:, :w], scalar1=col(15), scalar2=None, op0=ALU.mult)
            kb.op("dve", V.scalar_tensor_tensor, reads=[b2, bpv, bbu], writes=[bbu], out=bu[:, 0, t0:t0 + w], in0=p2[:, :w], scalar=col(18), in1=bu[:, 0, t0:t0 + w], op0=ALU.mult, op1=ALU.add)
            kb.op("dve", V.tensor_scalar, reads=[b2, bpv], writes=[bbu], out=bu[:, 1, t0:t0 + w], in0=p2[:, :w], scalar1=col(15), scalar2=None, op0=ALU.mult)
            kb.op("dve", V.scalar_tensor_tensor, reads=[b1, bpv, bbu], writes=[bbu], out=bu[:, 1, t0:t0 + w], in0=p1[:, :w], scalar=col(16), in1=bu[:, 1, t0:t0 + w], op0=ALU.mult, op1=ALU.add)
        U = self.U; bU = self.bU; L = self.L; bL = self.bL; rr = self.rr; brr = self.brr; ww = self.ww; bww = self.bww; t1 = self.t1; bt1 = self.bt1
        regs = [(0, 2048, 0), (2048, 32, 1 + 2 * s), (2112, 32, 2 + 2 * s)]
        for (c0, ln, slot) in regs:
            sl = slice(c0, c0 + ln); ul = slice(0, ln)
            kb.op("dve", V.tensor_tensor, reads=[bU, bbu], writes=[brr], out=rr[:, 0, sl], in0=U[:, 0, ul], in1=bu[:, 0, sl], op=ALU.mult)
            kb.op("pool", G.tensor_tensor, reads=[bU, bbu], writes=[bt1], out=t1[:, sl], in0=U[:, 1, ul], in1=bu[:, 1, sl], op=ALU.mult)
            kb.op("dve", V.tensor_tensor, reads=[brr, bt1], writes=[brr], out=rr[:, 0, sl], in0=rr[:, 0, sl], in1=t1[:, sl], op=ALU.add)
            kb.op("dve", V.tensor_tensor, reads=[bU, bbu], writes=[brr], out=rr[:, 1, sl], in0=U[:, 0, ul], in1=bu[:, 1, sl], op=ALU.mult)
            kb.op("pool", G.tensor_tensor, reads=[bU, bbu], writes=[bt1], out=t1[:, sl], in0=U[:, 1, ul], in1=bu[:, 0, sl], op=ALU.mult)
            kb.op("dve", V.tensor_tensor, reads=[brr, bt1], writes=[brr], out=rr[:, 1, sl], in0=rr[:, 1, sl], in1=t1[:, sl], op=ALU.subtract)
        for k in range(2):
            kb.op("dve", V.tensor_tensor_scan, reads=[brr, self.brho], writes=[bww], out=ww[:, k, :], data0=self.rho[:], data1=rr[:, k, :], initial=0.0, op0=ALU.mult, op1=ALU.add)
        X = self.X; bX = self.bX
        for (c0, ln, slot) in regs:
            sl = slice(c0, c0 + ln); ul = slice(0, ln)
            xr = X[:, 0, slot:slot + 1]; xi = X[:, 1, slot:slot + 1]
            kb.op("dve", V.tensor_tensor, reads=[bU, bww], writes=[brr], out=rr[:, 0, sl], in0=U[:, 0, ul], in1=ww[:, 0, sl], op=ALU.mult)
            kb.op("pool", G.tensor_tensor, reads=[bU, bww], writes=[bt1], out=t1[:, sl], in0=U[:, 1, ul], in1=ww[:, 1, sl], op=ALU.mult)
            kb.op("dve", V.tensor_tensor, reads=[brr, bt1], writes=[brr], out=rr[:, 0, sl], in0=rr[:, 0, sl], in1=t1[:, sl], op=ALU.subtract)
            kb.op("dve", V.tensor_tensor, reads=[bU, bww], writes=[brr], out=rr[:, 1, sl], in0=U[:, 0, ul], in1=ww[:, 1, sl], op=ALU.mult)
            kb.op("pool", G.tensor_tensor, reads=[bU, bww], writes=[bt1], out=t1[:, sl], in0=U[:, 1, ul], in1=ww[:, 0, sl], op=ALU.mult)
            kb.op("dve", V.tensor_tensor, reads=[brr, bt1], writes=[brr], out=rr[:, 1, sl], in0=rr[:, 1, sl], in1=t1[:, sl], op=ALU.add)
            kb.op("dve", V.tensor_scalar, reads=[bX], writes=[self.bpw], out=self.pw[:, 5:6], in0=xi, scalar1=-1.0, scalar2=None, op0=ALU.mult)
            kb.op("dve", V.scalar_tensor_tensor, reads=[bL, bX, brr], writes=[brr], out=rr[:, 0, sl], in0=L[:, 0, ul], scalar=xr, in1=rr[:, 0, sl], op0=ALU.mult, op1=ALU.add)
            kb.op("dve", V.scalar_tensor_tensor, reads=[bL, self.bpw, brr], writes=[brr], out=rr[:, 0, sl], in0=L[:, 1, ul], scalar=self.pw[:, 5:6], in1=rr[:, 0, sl], op0=ALU.mult, op1=ALU.add)
            kb.op("dve", V.scalar_tensor_tensor, reads=[bL, bX, brr], writes=[brr], out=rr[:, 1, sl], in0=L[:, 0, ul], scalar=xi, in1=rr[:, 1, sl], op0=ALU.mult, op1=ALU.add)
            kb.op("dve", V.scalar_tensor_tensor, reads=[bL, bX, brr], writes=[brr], out=rr[:, 1, sl], in0=L[:, 1, ul], scalar=xr, in1=rr[:, 1, sl], op0=ALU.mult, op1=ALU.add)
            kb.op("dve", V.tensor_copy, reads=[brr], writes=[bX], out=X[:, 0, slot:slot + 1], in_=rr[:, 0, c0 + ln - 1:c0 + ln])
            kb.op("dve", V.tensor_copy, reads=[brr], writes=[bX], out=X[:, 1, slot:slot + 1], in_=rr[:, 1, c0 + ln - 1:c0 + ln])
        kb.op("pool", G.tensor_scalar, reads=[brr], writes=[bt1], out=t1[:], in0=rr[:, 1, :], scalar1=-1.0, scalar2=None, op0=ALU.mult)
        for t0 in range(0, NCOL, 512):
            w = min(512, NCOL - t0)
            p1, b1 = c.ps[2]
            kb.op("pe", nc.tensor.matmul, reads=[brr, self.bCT], writes=[b1], out=p1[:32, :w], lhsT=self.CT[:, 0, :], rhs=rr[:, 0, t0:t0 + w], start=True, stop=False)
            kb.op("pe", nc.tensor.matmul, reads=[bt1, self.bCT], writes=[b1], out=p1[:32, :w], lhsT=self.CT[:, 1, :], rhs=t1[:, t0:t0 + w], start=False, stop=True)
            kb.op("dve", V.scalar_tensor_tensor, reads=[b1, self.buT, self.bdv], writes=[self.byT], out=self.yT[:, t0:t0 + w], in0=self.uT[:, t0:t0 + w], scalar=self.dv[:, 0:1], in1=p1[:32, :w], op0=ALU.mult, op1=ALU.add)

QW = 1280
NEG = -30000.0

class SB:
    def __init__(self, c, prm_d, nseg=8):
        self.c = c; kb = c.kb; nc = c.nc; sb = c.sb; self.nseg = nseg; self.prm_d = prm_d
        NK = 2048 * nseg
        self.KT, self.bKT = sb("sbKT", [64, NK], BF16); self.V, self.bV = sb("sbV", [128, 16 * nseg, 64], BF16)
        self.Q, self.bQ = sb("sbQ", [64, 1024 * nseg], BF16)
        self.KS, self.bKS = sb("sbKS", [64, 16, 32], BF16); self.VS, self.bVS = sb("sbVS", [32, 16, 64], BF16); self.QS, self.bQS = sb("sbQS", [64, 16, 32], BF16)
        self.MB, self.bMB = sb("sbMB", [128, 8, 512], BF16); self.DM, self.bDM = sb("sbDM", [32, 32], BF16)
        self.idb, self.bidb = sb("sbidb", [128, 128], BF16); self.trin, self.btrin = sb("sbtrin", [128, 128], BF16); self.onen, self.bonen = sb("sbonen", [128, 128], BF16)
        self.idf, self.bidf = sb("sbidf", [64, 64])
        self.stg, self.bstg = sb("sbstg", [64, SEGW]); self.stq, self.bstq = sb("sbstq", [64, QW])
        self.e = [sb("sbe%d" % i, [128, 512]) for i in range(2)]
        self.L = [sb("sbL%d" % i, [128, 512], BF16) for i in range(3)]
        self.R = [sb("sbR%d" % i, [128, 512], BF16) for i in range(3)]
        self.a = [sb("sba%d" % i, [128, 512], BF16) for i in range(3)]
        self.L0, self.bL0 = sb("sbLfirst", [32, 32], BF16)
        self.zero, self.bzero = sb("sbzero", [128, 512], BF16)
        self.oT, self.boT = sb("sboT", [64, SEGW]); self.oS, self.boS = sb("sboS", [64, 16, 64])
        kb.op("pool", nc.gpsimd.memset, writes=[self.boT], ap=self.oT[:], constant=0.0)
        self.kc = [sb("sbkc%d" % i, [64, 4096], BF16) for i in range(2)]; self.vc = [sb("sbvc%d" % i, [128, 32, 64], BF16) for i in range(2)]
        kb.dma("pool", self.MB[:], prm_d["mb"], writes=[self.bMB]); kb.dma("pool", self.DM[:], prm_d["dmask"], writes=[self.bDM])
        kb.dma("pool", self.idb[:], prm_d["identb"], writes=[self.bidb]); kb.dma("pool", self.trin[:], prm_d["trin"], writes=[self.btrin])
        kb.op("pool", nc.gpsimd.memset, writes=[self.bonen], ap=self.onen[:], constant=-1.0)
        kb.op("pool", nc.gpsimd.memset, writes=[self.bzero], ap=self.zero[:], constant=0.0)
        kb.op("pool", nc.gpsimd.memset, writes=[self.boS], ap=self.oS[:], constant=0.0)

    def load_segment(self, s, zg, bzg, zq, bzq, idx, bidx, icol):
        c = self.c; kb = c.kb; nc = c.nc
        stg, bstg = self.stg, self.bstg
        gather_rows(c, stg[:, :], bstg, zg, idx[:64, icol["k"]:icol["k"] + 1], bidx, bzg=bzg)
        kb.op("act", nc.scalar.copy, reads=[bstg], writes=[self.bKT], out=self.KT[:, 2048 * s:2048 * (s + 1)], in_=stg[:, 0:2048])
        kb.op("act", nc.scalar.copy, reads=[bstg], writes=[self.bKS], out=self.KS[:, 2 * s, :], in_=stg[:, 2048:2080])
        kb.op("act", nc.scalar.copy, reads=[bstg], writes=[self.bKS], out=self.KS[:, 2 * s + 1, :], in_=stg[:, 2112:2144])
        gather_rows(c, stg[:, :], bstg, zg, idx[:64, icol["v"]:icol["v"] + 1], bidx, bzg=bzg)
        for g in range(2):
            p, pb = c.ps[4 + g]
            for b in range(8):
                blk = g * 8 + b
                kb.op("pe", nc.tensor.transpose, reads=[bstg, c.bcm], writes=[pb], out=p[:, b * 64:(b + 1) * 64], in_=stg[:, blk * 128:(blk + 1) * 128], identity=c.ident)
            kb.op("act", nc.scalar.copy, reads=[pb], writes=[self.bV], out=self.V[:, 16 * s + g * 8:16 * s + g * 8 + 8, :], in_=p[:, :].rearrange("p (b d) -> p b d", d=64))
        p, pb = c.ps[4]
        for i, c0 in enumerate([2048, 2112]):
            kb.op("pe", nc.tensor.transpose, reads=[bstg, c.bcm], writes=[pb], out=p[:32, i * 64:(i + 1) * 64], in_=stg[:, c0:c0 + 32], identity=c.ident)
        kb.op("act", nc.scalar.copy, reads=[pb], writes=[self.bVS], out=self.VS[:, 2 * s:2 * s + 2, :], in_=p[:32, 0:128].rearrange("p (b d) -> p b d", d=64))
        stq, bstq = self.stq, self.bstq
        gather_rows(c, stq[:, :], bstq, zq, idx[:64, icol["q"]:icol["q"] + 1], bidx, bzg=bzq)
        kb.op("act", nc.scalar.mul, reads=[bstq], writes=[self.bQ], out=self.Q[:, 1024 * s:1024 * (s + 1)], in_=stq[:, 0:1024], mul=0.125)
        kb.op("act", nc.scalar.mul, reads=[bstq], writes=[self.bQS], out=self.QS[:, 2 * s, :], in_=stq[:, 1024:1056], mul=0.125)
        kb.op("act", nc.scalar.mul, reads=[bstq], writes=[self.bQS], out=self.QS[:, 2 * s + 1, :], in_=stq[:, 1088:1120], mul=0.125)

    def attend(self, steps, qap, bq, N, obank, out_ap, bout):
        c = self.c; kb = c.kb; nc = c.nc
        n = len(steps)
        po, pob = c.ps[obank]
        st = {}
        racc = (self.zero, self.bzero)
        first_nk = steps[0]["nk"]
        for i in range(n + 2):
            if i < n:
                S = steps[i]; nk = S["nk"]
                z, zb = c.ps[i % 3]
                kT, bkT = S["kT"]
                kb.op("pe", nc.tensor.matmul, reads=[bkT, bq], writes=[zb], out=z[:nk, :N], lhsT=kT, rhs=qap, start=True, stop=(S["mask"] is None))
                if S["mask"] is not None:
                    m, bm = S["mask"]
                    kb.op("pe", nc.tensor.matmul, reads=[bm, self.bidb], writes=[zb], out=z[:nk, :N], lhsT=self.idb[:nk, :nk], rhs=m, start=False, stop=True)
                e, be = self.e[i % 2]; L, bL = self.L[i % 3]
                if i == 0 and nk != 128: L, bL = self.L0, self.bL0
                kb.op("act", nc.scalar.activation, reads=[zb], writes=[be], out=e[:nk, :N], in_=z[:nk, :N], func=AF.Exp)
                kb.op("act", nc.scalar.activation, reads=[be], writes=[bL], out=L[:nk, :N], in_=e[:nk, :N], func=AF.Ln, bias=1.0)
                st[i] = dict(L=(L, bL), racc=racc, nk=nk)
                if i >= 1 or nk == 128:
                    Rn, bRn = self.R[i % 3]
                    if nk == 128:
                        kb.op("pool", nc.gpsimd.tensor_tensor, reads=[bL, racc[1]], writes=[bRn], out=Rn[:, :N], in0=L[:, :N], in1=racc[0][:, :N], op=ALU.add)
                        racc = (Rn, bRn)
            if 1 <= i <= n:
                j = i - 1; S = steps[j]; nk = S["nk"]; z, zb = c.ps[j % 3]
                L, bL = st[j]["L"]; ra, bra = st[j]["racc"]
                terms = [(self.trin[:nk, :nk], self.btrin, L[:nk, :N], bL)]
                if j >= 1:
                    if first_nk != 128:
                        L0, bL0 = st[0]["L"]
                        terms.append((self.onen[:first_nk, :nk], self.bonen, L0[:first_nk, :N], bL0))
                    if j >= (2 if first_nk != 128 else 1):
                        terms.append((self.onen[:, :nk], self.bonen, ra[:, :N], bra))
                for ti, (lh, blh, rh, brh) in enumerate(terms):
                    kb.op("pe", nc.tensor.matmul, reads=[blh, brh], writes=[zb], out=z[:nk, :N], lhsT=lh, rhs=rh, start=False, stop=(ti == len(terms) - 1))
                a, ba = self.a[j % 3]
                kb.op("act", nc.scalar.activation, reads=[zb], writes=[ba], out=a[:nk, :N], in_=z[:nk, :N], func=AF.Exp)
                st[j]["a"] = (a, ba)
            if i >= 2:
                j = i - 2; S = steps[j]; nk = S["nk"]
                a, ba = st[j]["a"]; v, bv = S["v"]
                kb.op("pe", nc.tensor.matmul, reads=[ba, bv], writes=[pob], out=po[:64, :N], lhsT=v, rhs=a[:nk, :N], start=(j == 0), stop=(j == n - 1))
        kb.op("act", nc.scalar.copy, reads=[pob], writes=[bout], out=out_ap, in_=po[:64, :N])

    def prompt_tile(self, m):
        steps = []
        for kbk in range(8 * m + 7, -1, -1):
            mask = (self.MB[:, kbk - 8 * m, :], self.bMB) if kbk >= 8 * m else None
            steps.append(dict(kT=(self.KT[:, kbk * 128:(kbk + 1) * 128], self.bKT), v=(self.V[:, kbk, :], self.bV), nk=128, mask=mask))
        self.attend(steps, self.Q[:, 512 * m:512 * (m + 1)], self.bQ, 512, 3, self.oT[:, 512 * (m % 2):512 * (m % 2 + 1)], self.boT)

    def sample_seq(self, q):
        c = self.c; kb = c.kb
        kc, bkc = self.kc[q % 2]; vc, bvc = self.vc[q % 2]
        kb.dma("pool", kc[:], self.prm_d["kc"][q], writes=[bkc])
        kb.dma("pool", vc[:], self.prm_d["vc"][q].rearrange("(b p) d -> p b d", p=128), writes=[bvc])
        steps = [dict(kT=(self.KS[:, q, :], self.bKS), v=(self.VS[:, q, :], self.bVS), nk=32, mask=(self.DM[:], self.bDM))]
        for kbk in range(31, -1, -1):
            steps.append(dict(kT=(kc[:, kbk * 128:(kbk + 1) * 128], bkc), v=(vc[:, kbk, :], bvc), nk=128, mask=None))
        self.attend(steps, self.QS[:, q, :], self.bQS, 32, 7, self.oS[:, q, 0:32], self.boS)

import os as _os
D = 2048; DFF = 2048; NIN = 3088; DMIX = 1024
TT = 2112
RPR = 2320
OPR = 1280
TILES = [(0, 512), (512, 512), (1024, 512), (1536, 576)]

def win_chunks():
    ch = []
    for c in range(6): ch.append((128 * c, 128, 'z', 128 * c))
    ch.append((768, 8, 'z', 768)); ch.append((776, 128, 'g', 0)); ch.append((904, 128, 'g', 128))
    for c in range(6): ch.append((1032 + 128 * c, 128, 'z', 776 + 128 * c))
    ch.append((1800, 8, 'z', 1544)); ch.append((1808, 128, 'g', 256)); ch.append((1936, 128, 'g', 384))
    ch.append((2064, 128, 'z', 1552)); ch.append((2192, 128, 'z', 1680))
    ch.append((2320, 128, 'q', 0)); ch.append((2448, 128, 'q', 128))
    ch.append((2576, 128, 'z', 1808)); ch.append((2704, 128, 'z', 1936)); ch.append((2832, 128, 'z', 2064)); ch.append((2960, 128, 'z', 2192))
    return ch
WCH = win_chunks()

class Tab:
    def __init__(self):
        self.cols = {}; self.n = 0
    def add(self, key):
        self.cols[key] = self.n; self.n += 1
def make_tab():
    t = Tab()
    zi = 0
    for k, (c0, w, kind, r0) in enumerate(WCH):
        if kind == 'z':
            for b in range(9): t.add(('az', k, b))
        if kind == 'q':
            for p in range(2):
                for b in range(5): t.add(('aq', k, p, b))
    for s in range(8):
        for nm in ['gq', 'gk', 'gv', 'gg', 'mq', 'mk', 'mv', 'mg', 's5', 'sk', 'sv', 'sq', 'og', 'om', 'os', 'ob']:
            t.add((nm, s))
    for cc in range(6):
        for b in range(9): t.add(('co', cc, b))
    for cc in range(2):
        for p in range(2):
            for b in range(5): t.add(('cs', cc, p, b))
    return t
TAB = make_tab()

def tab_values(core, fused=False):
    h, r = core // 2, core % 2
    cR = core * RPR if fused else 0; cQ = core * 512 if fused else 0; SR_ = RPR if fused else 484; SQ_ = 512 if fused else 64
    cO = core * OPR if fused else 0; OJ = OPR if fused else 160; cC = core * 160 if fused else 0
    T = np.zeros((128, TAB.n), np.int32)
    P = np.arange(128)
    for k, (c0, w, kind, r0) in enumerate(WCH):
        if kind == 'z':
            for b in range(9): T[:, TAB.cols[('az', k, b)]] = (cR + r0 + P) * 9 + b
        if kind == 'q':
            for p in range(2):
                for b in range(5): T[:, TAB.cols[('aq', k, p, b)]] = (cQ + p * 256 + r0 + P) * 5 + b
    for s in range(8):
        base = s * SR_
        if fused:
            oq, ok, ov, og = h * 64, 256 + h * 64, 512 + h * 64 + r * 32, 768 + h
            mq, mk, mv, mg = 776 + h * 64, 776 + 256 + h * 64, 776 + 512 + h * 64 + r * 32, 1544 + h
            o5, osk, osv = 1552 + 32 * core, 1808 + h * 64, 2064 + h * 64; gstep = 4
            sq0 = s * 512 + r * 256 + h * 64
        else:
            oq, ok, ov, og = 0, 64, 128, 160
            mq, mk, mv, mg = 162, 226, 290, 322
            o5, osk, osv = 324, 356, 420; gstep = 1
            sq0 = s * 64
        T[:, TAB.cols[('gq', s)]] = base + oq + P; T[:, TAB.cols[('gk', s)]] = base + ok + P
        T[:, TAB.cols[('gv', s)]] = base + ov + P; T[:, TAB.cols[('gg', s)]] = base + og + gstep * P
        T[:, TAB.cols[('mq', s)]] = base + mq + P; T[:, TAB.cols[('mk', s)]] = base + mk + P
        T[:, TAB.cols[('mv', s)]] = base + mv + P; T[:, TAB.cols[('mg', s)]] = base + mg + gstep * P
        T[:, TAB.cols[('s5', s)]] = base + o5 + P
        T[:, TAB.cols[('sk', s)]] = base + osk + P; T[:, TAB.cols[('sv', s)]] = base + osv + P
        T[:, TAB.cols[('sq', s)]] = sq0 + P
        ob = cO + s * 160
        T[:, TAB.cols[('og', s)]] = ob + P; T[:, TAB.cols[('om', s)]] = ob + 32 + P; T[:, TAB.cols[('os', s)]] = ob + 64 + P; T[:, TAB.cols[('ob', s)]] = ob + 96 + P
    for cc in range(6):
        moff = [0, 0, 32, 32, 64, 64][cc]; half = cc % 2
        j = (half * 128 + P) // 32; i = P % 32
        for b in range(9): T[:, TAB.cols[('co', cc, b)]] = (j * OJ + cC + moff + i) * 9 + b
    for cc in range(2):
        head = cc * 2 + P // 64; i = P % 64
        for p in range(2):
            for b in range(5): T[:, TAB.cols[('cs', cc, p, b)]] = ((2 * head + p) * OJ + cC + 96 + i) * 9 + b
    return np.clip(T, 0, None)

class Rot:
    def __init__(self, items): self.items = items; self.i = 0
    def next(self):
        x = self.items[self.i % len(self.items)]; self.i += 1; return x

def barrier(kb):
    engs = list(kb.eng.keys())
    for e in engs:
        for e2 in engs:
            if e2 != e: kb.wait(e, (kb.cur[e2][0], kb.cur[e2][1]))
        for slot in range(kb.NDMA):
            if kb.dcnt[slot] > 0: kb.wait(e, (kb.dsem[slot], 16 * kb.dcnt[slot]))

def collective(kb, nc, kind, src, bsrc, dst, bdst):
    kb.deps("pool", [bsrc], [bdst])
    c = kb.cur["pool"]
    if c[1] >= kb.EPOCH:
        kb._newsem("pool"); c = kb.cur["pool"]
    ins = nc.gpsimd.collective_compute(kind, ALU.add, replica_groups=[list(range(8))], ins=[src], outs=[dst])
    c[1] += 1; ins.then_inc(c[0], 1)
    kb.mark((c[0], c[1]), [bsrc], [bdst]); kb.ninst += 1

def token_phase(G, lc, la, final):
    nc = G.nc; kb = G.kb; Dm = G.D; ps = G.ps
    has_c = lc is not None; has_a = la is not None
    TMAX = 576
    with ExitStack() as st:
        G.uid += 1; uid = G.uid
        def sb(name, shape, dt=F32): return st.enter_context(nc.sbuf_tensor("t%d_" % uid + name, list(shape), dt)), Buf(name)
        xT, bxT = sb("xT", [128, 16, TMAX]); hT, bhT = sb("hT", [128, 16, TMAX], BF16); aT, baT = sb("aT", [128, 16, TMAX], BF16)
        wbufA = [sb("wA%d" % i, [128, 16, 256], BF16) for i in range(2)]; wbufB = [sb("wB%d" % i, [128, 16, 256], BF16) for i in range(2)]
        sq = [sb("sq%d" % i, [128, 512], BF16) for i in range(2)]
        rstd, brstd = sb("rstd", [128, TMAX])
        sg = [sb("sg%d" % i, [128, 512]) for i in range(2)]
        zst = [sb("zst%d" % i, [128, TMAX]) for i in range(2)]
        ones, bones = sb("ones", [128, 128], BF16); ones64, bones64 = sb("ones64", [128, 128], BF16)
        epsT, beps = sb("epsT", [128, 1]); vecs, bvecs = sb("vecs", [128, 5, 16])
        zblk, bzblk = sb("zblk", [128, 256]); sblk = [sb("sblk%d" % i, [128, 256]) for i in range(2)]
        if has_c:
            oTs, boT = sb("oTs", [128, 8, TMAX]); gTs, bgT = sb("gTs", [128, 4, TMAX]); mT, bmT = sb("mT", [128, 8, TMAX], BF16)
            tmpc = [sb("tmpc%d" % i, [128, 512]) for i in range(3)]; wglu, bwglu = sb("wglu", [128, 2, 256], BF16)
            ostg, bostg = sb("ostg", [128, 256])
        if has_a:
            kvs = [sb("kvs%d" % i, [128, 512]) for i in range(2)]
        V = nc.vector
        kb.op("pool", nc.gpsimd.memset, writes=[bones], ap=ones[:], constant=1.0)
        kb.op("pool", nc.gpsimd.memset, writes=[bones64], ap=ones64[:], constant=0.0)
        kb.op("pool", nc.gpsimd.memset, writes=[bones64], ap=ones64[0:64, 0:64], constant=1.0)
        kb.op("pool", nc.gpsimd.memset, writes=[bones64], ap=ones64[64:128, 64:128], constant=1.0)
        kb.op("pool", nc.gpsimd.memset, writes=[beps], ap=epsT[:], constant=1e-6)
        for t_, b_ in sblk: kb.op("pool", nc.gpsimd.memset, writes=[b_], ap=t_[:], constant=0.0)
        if has_a:
            kb.dma("sp", vecs[:, 0, :], Dm["tvec"][G.lw(la), :, 0, :], writes=[bvecs]); kb.dma("sp", vecs[:, 1, :], Dm["tvec"][G.lw(la), :, 1, :], writes=[bvecs])
        if has_c:
            kb.dma("sp", vecs[:, 2, :], Dm["tvecc"][G.lw(lc), :, 2, :], writes=[bvecs]); kb.dma("sp", vecs[:, 4, :], Dm["tvecc"][G.lw(lc), :, 3, :], writes=[bvecs])
            kb.dma("pool", wglu[:], Dm["wglu"][G.lw(lc)].rearrange("(c p) f -> p c f", p=128), writes=[bwglu])
            if final: kb.dma("sp", vecs[:, 3, :], Dm["nf"], writes=[bvecs])
        sqr = Rot(sq); sgr = Rot(sg); zr = Rot(zst)
        itab = G.itab; bitab = G.bitab
        tcol = lambda key: itab[:, TAB.cols[key]:TAB.cols[key] + 1]

        def sumsq_rstd(src, rds, nchunk, gw, lhs_ones, bl, pst, scale, dst, bdst):
            p, pb = pst
            for c in range(nchunk):
                s, bs = sqr.next()
                kb.op("act", nc.scalar.activation, reads=rds, writes=[bs], out=s[:, :gw], in_=src(c), func=AF.Square)
                kb.op("pe", nc.tensor.matmul, reads=[bs, bl], writes=[pb], out=p[:, :gw], lhsT=lhs_ones, rhs=s[:, :gw], start=(c == 0), stop=(c == nchunk - 1))
            kb.op("act", nc.scalar.activation, reads=[pb, beps], writes=[bdst], out=dst, in_=p[:, :gw], func=AF.Sqrt, scale=scale, bias=epsT[:, 0:1])
            kb.op("dve", V.reciprocal, reads=[bdst], writes=[bdst], out=dst, in_=dst)

        def norm_to(vi, grps, dst, bdst):
            for (g0, gw) in grps:
                sumsq_rstd(lambda c: xT[:, c, g0:g0 + gw], [bxT], 16, gw, ones[:], bones, ps[0], 1.0 / D, rstd[:, g0:g0 + gw], brstd)
                for c in range(16):
                    kb.op("dve", V.scalar_tensor_tensor, reads=[bxT, brstd, bvecs], writes=[bdst], out=dst[:, c, g0:g0 + gw], in0=xT[:, c, g0:g0 + gw],
                          scalar=vecs[:, vi, c:c + 1], in1=rstd[:, g0:g0 + gw], op0=ALU.mult, op1=ALU.mult)

        cur = {"ti": 0}
        def load_w(wt, wb, W, c0, cw, nk, key=None):
            if key is None:
                kb.dma("pool", wt[:, :nk, :cw], W[:, c0:c0 + cw].rearrange("(c p) f -> p c f", p=128), writes=[wb]); return
            if key not in G.wscr:
                G.uid += 1
                G.wscr[key] = (nc.dram_tensor("wscr%d_%s" % (G.uid, key), [W.shape[0], W.shape[1]], BF16, kind="Internal").ap(), {})
            scr, bufs = G.wscr[key]
            bk = bufs.setdefault(c0, Buf("scr"))
            if cur["ti"] == 0:
                kb.dma("pool", wt[:, :nk, :cw], W[:, c0:c0 + cw].rearrange("(c p) f -> p c f", p=128), writes=[wb])
                kb.dma("act", scr[:, c0:c0 + cw].rearrange("(c p) f -> p c f", p=128), wt[:, :nk, :cw], reads=[wb], writes=[bk])
            else:
                kb.dma("sp", wt[:, :nk, :cw], scr[:, c0:c0 + cw].rearrange("(c p) f -> p c f", p=128), reads=[bk], writes=[wb])

        def ffn(wg_d, wu_d, wd_d, vi, grps, kp):
            norm_to(vi, grps, hT, bhT)
            pg = Rot([ps[1], ps[2]]); pu = Rot([ps[3], ps[4]]); pd = Rot([ps[5], ps[6]])
            for b in range(DFF // 256):
                (wgt, wgb), (wut, wub) = wbufA[b % 2], wbufB[b % 2]
                load_w(wgt, wgb, wg_d, b * 256, 256, 16, kp + 'g'); load_w(wut, wub, wu_d, b * 256, 256, 16, kp + 'u')
                for ci in range(2):
                    f = b * 2 + ci
                    for (g0, gw) in grps:
                        (p1, b1), (p2, b2) = pg.next(), pu.next()
                        for k in range(16):
                            kb.op("pe", nc.tensor.matmul, reads=[wgb, bhT], writes=[b1], out=p1[:, :gw], lhsT=wgt[:, k, ci * 128:(ci + 1) * 128], rhs=hT[:, k, g0:g0 + gw], start=(k == 0), stop=(k == 15))
                        for k in range(16):
                            kb.op("pe", nc.tensor.matmul, reads=[wub, bhT], writes=[b2], out=p2[:, :gw], lhsT=wut[:, k, ci * 128:(ci + 1) * 128], rhs=hT[:, k, g0:g0 + gw], start=(k == 0), stop=(k == 15))
                        s, bs = sgr.next()
                        kb.op("act", nc.scalar.activation, reads=[b1], writes=[bs], out=s[:, :gw], in_=p1[:, :gw], func=AF.Silu)
                        kb.op("dve", V.tensor_tensor, reads=[bs, b2], writes=[baT], out=aT[:, f, g0:g0 + gw], in0=s[:, :gw], in1=p2[:, :gw], op=ALU.mult)
            for b in range(D // 256):
                wdt, wdb = (wbufA + wbufB)[b % 4]
                load_w(wdt, wdb, wd_d, b * 256, 256, 16, kp + 'd')
                for ci in range(2):
                    dch = b * 2 + ci
                    for (g0, gw) in grps:
                        p1, b1 = pd.next()
                        for k in range(16):
                            kb.op("pe", nc.tensor.matmul, reads=[wdb, baT], writes=[b1], out=p1[:, :gw], lhsT=wdt[:, k, ci * 128:(ci + 1) * 128], rhs=aT[:, k, g0:g0 + gw], start=(k == 0), stop=(k == 15))
                        kb.op("dve", V.scalar_tensor_tensor, reads=[b1, bxT], writes=[bxT], out=xT[:, dch, g0:g0 + gw], in0=p1[:, :gw], scalar=0.5, in1=xT[:, dch, g0:g0 + gw], op0=ALU.mult, op1=ALU.add)

        def fetch_mix(ti, t0, T):
            kb.dma("sp", gTs[:, :, :T], G.GSr[:, t0:t0 + T].rearrange("(c p) t -> p c t", p=128), reads=[G.bGS], writes=[bgT])
            OR9 = G.OR.rearrange("r (b c) -> (r b) c", c=256)
            for cc in range(6):
                for bi in range(2):
                    gather_rows(G, oTs[:, cc, bi * 256:(bi + 1) * 256], boT, OR9, tcol(('co', cc, 2 * ti + bi)), bitab, bzg=G.bOR)
                if T > 512:
                    gather_rows(G, ostg[:, :], bostg, OR9, tcol(('co', cc, 8)), bitab, bzg=G.bOR)
                    kb.op("pool", nc.gpsimd.tensor_copy, reads=[bostg], writes=[boT], out=oTs[:, cc, 512:544], in_=ostg[:, 0:32])
                    kb.op("pool", nc.gpsimd.tensor_copy, reads=[bostg], writes=[boT], out=oTs[:, cc, 544:576], in_=ostg[:, 64:96])
            for cc in range(2):
                for p in range(2):
                    gather_rows(G, ostg[:, :], bostg, OR9, tcol(('cs', cc, p, ti)), bitab, bzg=G.bOR)
                    dst = oTs[:, 6 + cc, 0:512].rearrange("q (a b c) -> q a b c", a=2, b=2)[:, :, p, :]
                    kb.op("pool", nc.gpsimd.tensor_copy, reads=[bostg], writes=[boT], out=dst, in_=ostg[:, :].rearrange("q (a c) -> q a c", a=2))
                if T > 512:
                    gather_rows(G, ostg[:, :], bostg, OR9, tcol(('cs', cc, 0, 4)), bitab, bzg=G.bOR)
                    kb.op("pool", nc.gpsimd.tensor_copy, reads=[bostg], writes=[boT], out=oTs[:, 6 + cc, 512:544], in_=ostg[:, 0:32])
                    kb.op("pool", nc.gpsimd.tensor_copy, reads=[bostg], writes=[boT], out=oTs[:, 6 + cc, 544:576], in_=ostg[:, 64:96])

        xsrc = G.xsrc
        for ti, (t0, T) in enumerate(TILES):
            cur["ti"] = ti
            grps = [(0, 512)] + ([(512, 64)] if T > 512 else [])
            kb.dma("sp", xT[:, :, :T], xsrc[:, t0:t0 + T].rearrange("(c p) t -> p c t", p=128), reads=[G.bXR], writes=[bxT])
            if has_a and not G.zero_done:
                G.do_zero("sp"); G.zero_done = True
            if has_c:
                if ti == 0: fetch_mix(ti, t0, T)
                cv = lambda j: vecs[:, 4, j:j + 1]
                for (g0, gw) in grps:
                    sl = slice(g0, g0 + gw)
                    for c in range(2):
                        sumsq_rstd(lambda cc_: oTs[:, c, sl], [boT], 1, gw, ones64[:], bones64, ps[0], 1.0 / 64, rstd[:, sl], brstd)
                        t1, bt1 = tmpc[0]; t2, bt2 = tmpc[1]
                        kb.op("act", nc.scalar.activation, reads=[bgT], writes=[bt1], out=t1[:, :gw], in_=gTs[:, c, sl], func=AF.Silu)
                        kb.op("dve", V.scalar_tensor_tensor, reads=[boT, brstd, bvecs], writes=[bt2], out=t2[:, :gw], in0=oTs[:, c, sl], scalar=cv(c), in1=rstd[:, sl], op0=ALU.mult, op1=ALU.mult)
                        kb.op("dve", V.tensor_tensor, reads=[bt1, bt2], writes=[bmT], out=mT[:, c, sl], in0=t1[:, :gw], in1=t2[:, :gw], op=ALU.mult)
                    for c in range(2):
                        sumsq_rstd(lambda cc_: oTs[:, 2 + c, sl], [boT], 1, gw, ones64[:], bones64, ps[0], 1.0 / 64, rstd[:, sl], brstd)
                        t1, bt1 = tmpc[0]; t2, bt2 = tmpc[1]
                        kb.op("act", nc.scalar.activation, reads=[bgT], writes=[bt1], out=t1[:, :gw], in_=gTs[:, 2 + c, sl], func=AF.Sigmoid)
                        kb.op("dve", V.scalar_tensor_tensor, reads=[boT, brstd, bvecs], writes=[bt2], out=t2[:, :gw], in0=oTs[:, 2 + c, sl], scalar=cv(2 + c), in1=rstd[:, sl], op0=ALU.mult, op1=ALU.mult)
                        kb.op("dve", V.tensor_tensor, reads=[bt1, bt2], writes=[bmT], out=mT[:, 2 + c, sl], in0=t1[:, :gw], in1=t2[:, :gw], op=ALU.mult)
                    ych = [sg[0], sg[1]]; ybf = [sq[0], sq[1]]
                    for c in range(2):
                        y, by = ych[c]; t1, bt1 = tmpc[0]; t2, bt2 = tmpc[1]
                        kb.op("act", nc.scalar.activation, reads=[boT], writes=[bt1], out=t1[:, :gw], in_=oTs[:, 4 + c, sl], func=AF.Square)
                        kb.op("dve", V.tensor_scalar, reads=[bt1], writes=[bt1], out=t1[:, :gw], in0=t1[:, :gw], scalar1=0.044715, scalar2=1.0, op0=ALU.mult, op1=ALU.add)
                        kb.op("dve", V.tensor_tensor, reads=[bt1, boT], writes=[bt2], out=t2[:, :gw], in0=t1[:, :gw], in1=oTs[:, 4 + c, sl], op=ALU.mult)
                        kb.op("act", nc.scalar.activation, reads=[bt2], writes=[bt1], out=t1[:, :gw], in_=t2[:, :gw], func=AF.Sigmoid, scale=1.5957691216)
                        kb.op("dve", V.tensor_tensor, reads=[bt1, boT], writes=[by], out=y[:, :gw], in0=t1[:, :gw], in1=oTs[:, 4 + c, sl], op=ALU.mult)
                        yq, byq = ybf[c]
                        kb.op("act", nc.scalar.copy, reads=[by], writes=[byq], out=yq[:, :gw], in_=y[:, :gw])
                    for c in range(2):
                        p1, b1 = ps[1 + c]
                        for k in range(2):
                            kb.op("pe", nc.tensor.matmul, reads=[bwglu, ybf[k][1]], writes=[b1], out=p1[:, :gw], lhsT=wglu[:, k, c * 128:(c + 1) * 128], rhs=ybf[k][0][:, :gw], start=(k == 0), stop=(k == 1))
                        t1, bt1 = tmpc[c]
                        kb.op("act", nc.scalar.activation, reads=[b1, bvecs], writes=[bt1], out=t1[:, :gw], in_=p1[:, :gw], func=AF.Sigmoid, bias=cv(4 + c))
                        kb.op("dve", V.tensor_tensor, reads=[bt1, ych[c][1]], writes=[ych[c][1]], out=ych[c][0][:, :gw], in0=t1[:, :gw], in1=ych[c][0][:, :gw], op=ALU.mult)
                    sumsq_rstd(lambda cc_: ych[cc_][0][:, :gw], [ych[0][1], ych[1][1]], 2, gw, ones[:], bones, ps[0], 1.0 / 256, rstd[:, sl], brstd)
                    for c in range(2):
                        kb.op("dve", V.scalar_tensor_tensor, reads=[ych[c][1], brstd, bvecs], writes=[bmT], out=mT[:, 4 + c, sl], in0=ych[c][0][:, :gw], scalar=cv(6 + c), in1=rstd[:, sl], op0=ALU.mult, op1=ALU.mult)
                    sumsq_rstd(lambda cc_: oTs[:, 6 + cc_, sl], [boT], 2, gw, ones[:], bones, ps[0], 1.0 / 256, rstd[:, sl], brstd)
                    for c in range(2):
                        kb.op("dve", V.scalar_tensor_tensor, reads=[boT, brstd, bvecs], writes=[bmT], out=mT[:, 6 + c, sl], in0=oTs[:, 6 + c, sl], scalar=cv(8 + c), in1=rstd[:, sl], op0=ALU.mult, op1=ALU.mult)
                if ti + 1 < len(TILES): fetch_mix(ti + 1, TILES[ti + 1][0], TILES[ti + 1][1])
                pd = Rot([ps[5], ps[6]])
                for b in range(D // 256):
                    wt, wb = (wbufA + wbufB)[b % 4]
                    load_w(wt, wb, Dm["wout"][G.lw(lc)], b * 256, 256, 8, "wout")
                    for ci in range(2):
                        dch = b * 2 + ci
                        for (g0, gw) in grps:
                            p1, b1 = pd.next()
                            for k in range(8):
                                kb.op("pe", nc.tensor.matmul, reads=[wb, bmT], writes=[b1], out=p1[:, :gw], lhsT=wt[:, k, ci * 128:(ci + 1) * 128], rhs=mT[:, k, g0:g0 + gw], start=(k == 0), stop=(k == 7))
                            kb.op("dve", V.tensor_tensor, reads=[b1, bxT], writes=[bxT], out=xT[:, dch, g0:g0 + gw], in0=p1[:, :gw], in1=xT[:, dch, g0:g0 + gw], op=ALU.add)
                ffn(Dm["wg2"][G.lw(lc)], Dm["wu2"][G.lw(lc)], Dm["wd2"][G.lw(lc)], 2, grps, "f2")
            if has_a:
                ffn(Dm["wg1"][G.lw(la)], Dm["wu1"][G.lw(la)], Dm["wd1"][G.lw(la)], 0, grps, "f1")
                norm_to(1, grps, hT, bhT)
                pd = Rot([ps[5], ps[6]])
                ZS9 = G.ZS.rearrange("r (b c) -> (r b) c", c=256); ZQ5 = G.ZQS.rearrange("r (b c) -> (r b) c", c=256)
                win = Dm["win"][G.lw(la)]
                if ti == 0: kb.deps("pool", G.zero_bufs, [])
                for k, (c0, mw, kind, r0) in enumerate(WCH):
                    wt, wb = (wbufA + wbufB)[k % 4]
                    load_w(wt, wb, win, c0, mw, 16, 'win')
                    z, bzs = zr.next()
                    for (g0, gw) in grps:
                        p1, b1 = pd.next()
                        for kk in range(16):
                            kb.op("pe", nc.tensor.matmul, reads=[wb, bhT], writes=[b1], out=p1[:mw, :gw], lhsT=wt[:, kk, 0:mw], rhs=hT[:, kk, g0:g0 + gw], start=(kk == 0), stop=(kk == 15))
                        kb.op("act", nc.scalar.copy, reads=[b1], writes=[bzs], out=z[:mw, g0:g0 + gw], in_=p1[:mw, :gw])
                    if kind == 'g':
                        kb.dma("sp", G.GSw[r0:r0 + mw, t0:t0 + T], z[:mw, :T], reads=[bzs], writes=[G.bGS])
                    elif kind == 'z':
                        for bi in range(2):
                            scatter_rows(G, z[:mw, bi * 256:(bi + 1) * 256], bzs, ZS9, tcol(('az', k, 2 * ti + bi))[:mw], bitab, G.bZS)
                        if T > 512:
                            sbk, bsbk = sblk[k % 2]
                            kb.op("pool", nc.gpsimd.tensor_copy, reads=[bzs], writes=[bsbk], out=sbk[:mw, 0:32], in_=z[:mw, 512:544])
                            kb.op("pool", nc.gpsimd.tensor_copy, reads=[bzs], writes=[bsbk], out=sbk[:mw, 64:96], in_=z[:mw, 544:576])
                            scatter_rows(G, sbk[:mw, :], bsbk, ZS9, tcol(('az', k, 8))[:mw], bitab, G.bZS)
                        if c0 < 768 and T > 512:
                            for i3, cc0 in enumerate([509, 541, 573]):
                                kb.dma("sp", G.Dout["convT"][G.lo(la), c0:c0 + mw, 3 * i3:3 * i3 + 3], z[:mw, cc0:cc0 + 3], reads=[bzs], writes=[G.bout])
                    else:
                        for p in range(2):
                            kb.op("pool", nc.gpsimd.tensor_copy, reads=[bzs], writes=[bzblk], out=zblk[:, :].rearrange("q (a c) -> q a c", a=2),
                                  in_=z[:, 0:512].rearrange("q (a b c) -> q a b c", a=2, b=2)[:, :, p, :])
                            scatter_rows(G, zblk[:, :], bzblk, ZQ5, tcol(('aq', k, p, ti)), bitab, G.bZQS)
                        if T > 512:
                            sbk, bsbk = sblk[k % 2]
                            kb.op("pool", nc.gpsimd.tensor_copy, reads=[bzs], writes=[bsbk], out=sbk[:, 0:32], in_=z[:, 512:544])
                            kb.op("pool", nc.gpsimd.tensor_copy, reads=[bzs], writes=[bsbk], out=sbk[:, 64:96], in_=z[:, 544:576])
                            for p in range(2):
                                scatter_rows(G, sbk[:, :], bsbk, ZQ5, tcol(('aq', k, p, 4)), bitab, G.bZQS)
                (wkt, wkb), (wvt, wvb) = wbufB[0], wbufB[1]
                load_w(wkt, wkb, win, 2576, 256, 16, 'wink'); load_w(wvt, wvb, win, 2832, 256, 16, 'winv')
                for b0 in range(0, T, 128):
                    bw = min(128, T - b0)
                    p1, b1 = pd.next()
                    for kk in range(16):
                        kb.op("pe", nc.tensor.matmul, reads=[wkb, bhT], writes=[b1], out=p1[:bw, 0:256], lhsT=hT[:, kk, b0:b0 + bw], rhs=wkt[:, kk, :], start=(kk == 0), stop=(kk == 15))
                    for kk in range(16):
                        kb.op("pe", nc.tensor.matmul, reads=[wvb, bhT], writes=[b1], out=p1[:bw, 256:512], lhsT=hT[:, kk, b0:b0 + bw], rhs=wvt[:, kk, :], start=(kk == 0), stop=(kk == 15))
                    kv, bkv = kvs[(b0 // 128) % 2]
                    kb.op("act", nc.scalar.copy, reads=[b1], writes=[bkv], out=kv[:bw, :], in_=p1[:bw, :])
                    kb.dma("sp", G.Dout["kvout"][G.lo(la), t0 + b0:t0 + b0 + bw, :], kv[:bw, :], reads=[bkv], writes=[G.bout])
            if final:
                for (g0, gw) in grps:
                    sumsq_rstd(lambda c: xT[:, c, g0:g0 + gw], [bxT], 16, gw, ones[:], bones, ps[0], 1.0 / D, rstd[:, g0:g0 + gw], brstd)
                    for c in range(16):
                        kb.op("dve", V.scalar_tensor_tensor, reads=[bxT, brstd, bvecs], writes=[bxT], out=xT[:, c, g0:g0 + gw], in0=xT[:, c, g0:g0 + gw],
                              scalar=vecs[:, 3, c:c + 1], in1=rstd[:, g0:g0 + gw], op0=ALU.mult, op1=ALU.mult)
                kb.dma("sp", G.Dout["yT"][:, t0:t0 + T].rearrange("(c p) t -> p c t", p=128), xT[:, :, :T], reads=[bxT], writes=[G.bout])
            else:
                kb.dma("sp", G.xdst[:, t0:t0 + T].rearrange("(c p) t -> p c t", p=128), xT[:, :, :T], reads=[bxT], writes=[G.bout])
        barrier(kb)
    G.first_phase = False

class GCtx:
    pass

def b_phase(G, l):
    nc = G.nc; kb = G.kb; Dm = G.D; Do = G.Dout
    tc = lambda nm, s: TAB.cols[(nm, s)]
    def new_ctx(st):
        c = Ctx(); c.nc = nc; c.st = st; c.kb = kb; c.ps = G.ps
        c.sb = lambda name, shape, dt=F32: (st.enter_context(nc.sbuf_tensor("b%d_" % G.uid + name, list(shape), dt)), Buf(name))
        G.uid += 1
        load_consts(c, Dm["cm"], Dm["rows"])
        return c
    with ExitStack() as st:
        c = new_ctx(st)
        g = GDN(c, {"convw": Dm["g_convw"][G.lw(l)], "alog": Dm["g_alog"][G.lw(l)], "dtb": Dm["g_dtb"][G.lw(l)], "convst": Dm["g_convst"][G.lw(l)], "s0": Dm["g_s0"][G.lw(l)]})
        kb.deps("pool", G.zero_bufs, [])
        for s in range(8):
            g.segment(s, G.ZR, G.bZR[s], G.itab, G.bitab, {"q": tc('gq', s), "k": tc('gk', s), "v": tc('gv', s), "g": tc('gg', s)})
            scatter_rows(c, g.la.oT[:32, :], g.la.boT, G.OS, G.itab[:32, tc('og', s):tc('og', s) + 1], G.bitab, G.bOS)
        kb.dma("sp", Do["gdnS"][G.lo(l)], g.la.S[:], reads=[g.la.bS], writes=[G.bout])
        barrier(kb)
    with ExitStack() as st:
        c = new_ctx(st)
        g = MLSTM(c, {"bi": Dm["m_bi"][G.lw(l)], "bf": Dm["m_bf"][G.lw(l)], "s0": Dm["m_s0"][G.lw(l)], "m0": Dm["m_m0"][G.lw(l)]})
        for s in range(8):
            g.segment(s, G.ZR, G.bZR[s], G.itab, G.bitab, {"q": tc('mq', s), "k": tc('mk', s), "v": tc('mv', s), "g": tc('mg', s)})
            scatter_rows(c, g.hT[:32, :], g.bhT, G.OS, G.itab[:32, tc('om', s):tc('om', s) + 1], G.bitab, G.bOS)
        g.finish()
        kb.dma("sp", Do["mlS"][G.lo(l)], g.Sout[:], reads=[g.bSout], writes=[G.bout])
        kb.dma("sp", Do["mlM"][G.lo(l)], g.mfin[:], reads=[g.bmfin], writes=[G.bout])
        barrier(kb)
    with ExitStack() as st:
        c = new_ctx(st)
        g = S5(c, {"vec": Dm["s_vec"][G.lw(l)], "BreT": Dm["s_BreT"][G.lw(l)], "BimT": Dm["s_BimT"][G.lw(l)], "CreT": Dm["s_CreT"][G.lw(l)], "CimT": Dm["s_CimT"][G.lw(l)], "dvec": Dm["s_dvec"][G.lw(l)], "x0": Dm["s_x0"][G.lw(l)]})
        for s in range(8):
            g.segment(s, G.ZR, G.bZR[s], G.itab, G.bitab, tc('s5', s))
            scatter_rows(c, g.yT[:32, :], g.byT, G.OS, G.itab[:32, tc('os', s):tc('os', s) + 1], G.bitab, G.bOS)
        kb.dma("sp", Do["s5X"][G.lo(l)], g.X[:], reads=[g.bX], writes=[G.bout])
        barrier(kb)
    with ExitStack() as st:
        c = new_ctx(st)
        g = SB(c, {"mb": Dm["b_mb"], "dmask": Dm["b_dmask"], "identb": Dm["b_identb"], "trin": Dm["b_trin"], "kc": Dm["b_kc"][G.lw(l)], "vc": Dm["b_vc"][G.lw(l)]}, 8)
        for s in range(8):
            g.load_segment(s, G.ZR, G.bZR[s], G.ZQR, G.bZQR, G.itab, G.bitab, {"k": tc('sk', s), "v": tc('sv', s), "q": tc('sq', s)})
            g.prompt_tile(2 * s); g.prompt_tile(2 * s + 1)
            g.sample_seq(2 * s); g.sample_seq(2 * s + 1)
            kb.op("pool", nc.gpsimd.tensor_copy, reads=[g.boS], writes=[g.boT], out=g.oT[:, 1024:1056], in_=g.oS[:, 2 * s, 0:32])
            kb.op("pool", nc.gpsimd.tensor_copy, reads=[g.boS], writes=[g.boT], out=g.oT[:, 1088:1120], in_=g.oS[:, 2 * s + 1, 0:32])
            scatter_rows(c, g.oT[:64, :], g.boT, G.OS, G.itab[:64, tc('ob', s):tc('ob', s) + 1], G.bitab, G.bOS)
        barrier(kb)


WSPEC = {"tvec": [128, 4, 16], "wg1": [D, DFF], "wu1": [D, DFF], "wd1": [DFF, D], "win": [D, NIN], "wout": [DMIX, D], "wg2": [D, DFF], "wu2": [D, DFF], "wd2": [DFF, D], "wglu": [256, 256]}
A_W = ["wg1", "wu1", "wd1", "win"]; C_W = ["wout", "wg2", "wu2", "wd2", "wglu"]
BSPEC = {"g_convw": [160, 4], "g_alog": [1, 1], "g_dtb": [1, 1], "g_convst": [160, 16, 3], "g_s0": [64, 16, 32], "m_bi": [1, 1], "m_bf": [1, 1], "m_s0": [64, 16, 33], "m_m0": [1, 16],
         "s_vec": [128, 3], "s_BreT": [32, 128], "s_BimT": [32, 128], "s_CreT": [128, 32], "s_CimT": [128, 32], "s_dvec": [32, 1], "s_x0": [128, 2, 16], "b_kc": [16, 64, 4096], "b_vc": [16, 4096, 64]}
BSHARED = {"cm": [64, 6, 64], "rows": [1, 2, NCOL], "b_mb": [128, 8, 512], "b_dmask": [32, 32], "b_identb": [128, 128], "b_trin": [128, 128]}

def build_launch(kind):
    nc = bass.Bass("TRN2", target_bir_lowering=False)
    G = GCtx(); G.nc = nc; G.uid = 0; G.wscr = {}
    G.lw = lambda l: 0; G.lo = lambda l: 0
    def din(name, shape, dt=F32): return nc.dram_tensor(name, list(shape), dt, kind="ExternalInput").ap()
    def dout(name, shape, dt=F32): return nc.dram_tensor(name, list(shape), dt, kind="ExternalOutput").ap()
    Dm = {}; G.D = Dm; G.Dout = {}
    itab_d = din("itab", [128, TAB.n], I32)
    G.bXR = Buf("XR"); G.bGS = Buf("GS"); G.bZS = Buf("ZS"); G.bZQS = Buf("ZQS"); G.bOS = Buf("OS"); G.bOR = Buf("OR"); G.bout = Buf("out"); G.bZQR = Buf("ZQR")
    zero_list = []
    if kind in ("tpa", "tpca", "tpcf"):
        has_c = kind != "tpa"; has_a = kind != "tpcf"
        G.xsrc = din("xin", [D, TT])
        if has_a:
            Dm["tvec"] = din("tvec", [1, 128, 4, 16])
            for nm in A_W: Dm[nm] = din(nm, [1] + WSPEC[nm])
            G.GSw = dout("GSo", [512, TT]); G.ZS = dout("ZSo", [RPR, SEGW]); G.ZQS = dout("ZQSo", [512, QW])
            G.Dout["kvout"] = dout("kvout", [1, TT, 512]); G.Dout["convT"] = dout("convT", [1, 768, 9])
            zero_list += [(G.ZS, G.bZS, RPR, SEGW), (G.ZQS, G.bZQS, 512, QW)]
        if has_c:
            for nm in C_W: Dm[nm] = din(nm + "c", [1] + WSPEC[nm])
            Dm["tvecc"] = din("tvecc", [1, 128, 4, 16])
            G.GSr = din("GSi", [512, TT]); G.OR = din("ORi", [8 * 160, SEGW])
        if kind == "tpcf":
            Dm["nf"] = din("nf", [128, 16]); G.Dout["yT"] = dout("yT", [D, TT]); G.xdst = None
        else:
            G.xdst = dout("xout", [D, TT])
    else:
        for nm, shp in BSPEC.items(): Dm[nm] = din(nm, [1] + shp)
        for nm, shp in BSHARED.items(): Dm[nm] = din(nm, shp)
        G.ZR = din("ZRi", [8 * 484, SEGW]); G.ZQR = din("ZQRi", [8 * 64, QW]); G.bZR = [Buf("ZR")] * 8
        G.OS = dout("OSo", [8 * 160, SEGW])
        G.Dout.update({"gdnS": dout("gdnS", [1, 64, 17, 32]), "mlS": dout("mlS", [1, 64, 17, 33]), "mlM": dout("mlM", [1, 1, 17]), "s5X": dout("s5X", [1, 128, 2, 17])})
        zero_list += [(G.OS, G.bOS, 8 * 160, SEGW)]
    with ExitStack() as st:
        kb = KB(nc, st); G.kb = kb
        G.ps = [(st.enter_context(nc.psum_tensor("ps%d" % i, [128, 512], F32)), Buf("ps%d" % i)) for i in range(8)]
        G.itab = st.enter_context(nc.sbuf_tensor("itab_s", [128, TAB.n], I32)); G.bitab = Buf("itab")
        kb.dma("sp", G.itab[:], itab_d, writes=[G.bitab])
        zt = st.enter_context(nc.sbuf_tensor("zerot", [128, 576], F32)); bzt = Buf("zt")
        kb.op("pool", nc.gpsimd.memset, writes=[bzt], ap=zt[:], constant=0.0)
        G.zero_bufs = []
        def do_zero(q):
            for (buf, bb, rows, w) in zero_list:
                for r0 in range(0, rows, 128):
                    rr = min(128, rows - r0)
                    for c0 in range(0, w, 576):
                        cw = min(576, w - c0)
                        zb = Buf("z"); G.zero_bufs.append(zb)
                        kb.dma(q, buf[r0:r0 + rr, c0:c0 + cw], zt[:rr, :cw], reads=[bzt], writes=[zb])
        G.do_zero = do_zero; G.zero_done = False
        if kind == "b":
            G.do_zero("sp"); G.zero_done = True
            b_phase(G, 0)
        else:
            token_phase(G, 0 if kind != "tpa" else None, 0 if kind != "tpcf" else None, kind == "tpcf")
        kb.finish([G.bout, G.bZS, G.bZQS, G.bOS, G.bGS])
        barrier(kb)
        print(kind, "instructions", kb.ninst, "sems", kb.nsem)
    return nc

def vecT(v):
    return np.ascontiguousarray(v.reshape(-1, 128).T)

def host_inputs(inp, depth):
    L = depth
    f = lambda a: np.ascontiguousarray(np.asarray(a, dtype=np.float32))
    shared = {}
    tvec = np.zeros((L, 128, 4, 16), np.float32)
    for l in range(L):
        tvec[l, :, 0, :] = vecT(inp["ffn1_norm"][l]); tvec[l, :, 1, :] = vecT(inp["mix_norm"][l]); tvec[l, :, 2, :] = vecT(inp["ffn2_norm"][l])
        cv = np.zeros((128, 16), np.float32)
        g64 = np.tile(inp["gdn_norm"][l], 2)
        cv[:, 0] = g64; cv[:, 1] = g64
        cv[:, 2:4] = vecT(inp["ml_norm"][l]); cv[:, 4:6] = vecT(inp["s5_b_glu"][l]); cv[:, 6:8] = vecT(inp["s5_norm"][l]); cv[:, 8:10] = vecT(inp["sb_norm"][l])
        tvec[l, :, 3, :] = cv
    shared["tvec"] = tvec; shared["nf"] = vecT(inp["final_norm"])
    for nm, src in [("wg1", "ffn1_w_gate"), ("wu1", "ffn1_w_up"), ("wd1", "ffn1_w_down"), ("win", "w_in"), ("wout", "w_out"), ("wg2", "ffn2_w_gate"), ("wu2", "ffn2_w_up"),
                    ("wd2", "ffn2_w_down"), ("wglu", "s5_w_glu")]:
        shared[nm] = f(inp[src][:L])
    cm = np.zeros((64, 6, 64), np.float32)
    jj, ii = np.meshgrid(np.arange(64), np.arange(64), indexing="ij")
    cm[:, 0] = np.eye(64); cm[:, 1] = -1.0 * (ii > jj); cm[:, 2] = -1.0 * (ii < jj); cm[:, 3] = (ii >= jj); cm[:, 4] = 1.0
    rows = np.ones((1, 2, NCOL), np.float32); rows[0, 0, ::64] = 0.0; rows[0, 1, 2080:2112] = 0.0; rows[0, 1, 2144:2176] = 0.0
    shared["cm"] = cm; shared["rows"] = rows
    jj, ii = np.meshgrid(np.arange(32), np.arange(32), indexing="ij")
    shared["b_dmask"] = np.where(jj < ii, 0.0, NEG).astype(np.float32)
    J, Sx = np.meshgrid(np.arange(128), np.arange(128), indexing="ij")
    shared["b_trin"] = (-1.0 * (J >= Sx)).astype(np.float32); shared["b_identb"] = np.eye(128, dtype=np.float32)
    xp = inp["x_prompt"][0]; xs = inp["x_sample"]
    maps = []
    for core in range(8):
        h, r = core // 2, core % 2
        m = dict(shared)
        xt = np.concatenate([xp[2048 * core:2048 * (core + 1)], xs[2 * core], xs[2 * core + 1]], 0)
        m["xT0"] = np.ascontiguousarray(xt.T)
        m["itab"] = tab_values(core)
        cch = list(range(h * 64, h * 64 + 64)) + list(range(256 + h * 64, 256 + h * 64 + 64)) + list(range(512 + h * 64 + r * 32, 512 + h * 64 + r * 32 + 32))
        m["g_convw"] = f(inp["gdn_conv_w"][:L][:, :, cch].transpose(0, 2, 1))
        m["g_alog"] = f(inp["gdn_a_log"][:L, h].reshape(L, 1, 1)); m["g_dtb"] = f(inp["gdn_dt_bias"][:L, h].reshape(L, 1, 1))
        m["g_convst"] = f(inp["state_gdn_conv"][:L][:, :, :, cch].transpose(0, 3, 1, 2))
        m["g_s0"] = f(inp["state_gdn_s"][:L, :, h, :, r * 32:(r + 1) * 32].transpose(0, 2, 1, 3))
        m["m_bi"] = f(inp["ml_b_i"][:L, h].reshape(L, 1, 1)); m["m_bf"] = f(inp["ml_b_f"][:L, h].reshape(L, 1, 1))
        c0 = inp["state_mlstm_c"][:L, :, h, :, r * 32:(r + 1) * 32]; n0 = inp["state_mlstm_n"][:L, :, h, :]
        m["m_s0"] = f(np.concatenate([c0, n0[..., None]], -1).transpose(0, 2, 1, 3))
        m["m_m0"] = f(inp["state_mlstm_m"][:L, :, h].reshape(L, 1, 16))
        gs = [2 * core, 2 * core + 1]
        vec = np.zeros((L, 128, 3), np.float32); BreT = np.zeros((L, 32, 128), np.float32); BimT = np.zeros((L, 32, 128), np.float32)
        CreT = np.zeros((L, 128, 32), np.float32); CimT = np.zeros((L, 128, 32), np.float32)
        for gi, gg in enumerate(gs):
            sl = slice(gi * 64, gi * 64 + 64); cl = slice(gi * 16, gi * 16 + 16)
            vec[:, sl, 0] = inp["s5_a_re"][:L, gg]; vec[:, sl, 1] = inp["s5_a_im"][:L, gg]; vec[:, sl, 2] = inp["s5_log_step"][:L, gg][:, None]
            BreT[:, cl, sl] = inp["s5_b_re"][:L, gg].transpose(0, 2, 1); BimT[:, cl, sl] = inp["s5_b_im"][:L, gg].transpose(0, 2, 1)
            CreT[:, sl, cl] = inp["s5_c_re"][:L, gg].transpose(0, 2, 1); CimT[:, sl, cl] = inp["s5_c_im"][:L, gg].transpose(0, 2, 1)
        m["s_vec"] = vec; m["s_BreT"] = BreT; m["s_BimT"] = BimT; m["s_CreT"] = CreT; m["s_CimT"] = CimT
        m["s_dvec"] = f(inp["s5_d"][:L, 32 * core:32 * core + 32].reshape(L, 32, 1))
        m["s_x0"] = f(np.stack([inp["state_s5_re"][:L][:, :, gs, :].reshape(L, 16, 128).transpose(0, 2, 1), inp["state_s5_im"][:L][:, :, gs, :].reshape(L, 16, 128).transpose(0, 2, 1)], 2))
        p = np.arange(128)[:, None, None, None]; j = np.arange(8)[None, :, None, None]; g4 = np.arange(4)[None, None, :, None]; cc = np.arange(128)[None, None, None, :]
        m["b_mb"] = np.where((128 * j + p) < (128 * (r + 2 * g4) + cc), 0.0, NEG).astype(np.float32).reshape(128, 8, 512)
        m["b_kc"] = f(inp["cache_sb_k"][:L, :, :, h, :].transpose(0, 1, 3, 2)); m["b_vc"] = f(inp["cache_sb_v"][:L, :, :, h, :])
        maps.append(m)
    return maps

def assemble(results, depth):
    L = depth
    yp = np.zeros((1, 16384, D), np.float32); ys = np.zeros((16, 32, D), np.float32)
    pk = np.zeros((L, 1, 16384, 4, 64), np.float32); pv = np.zeros_like(pk); sk = np.zeros((L, 16, 32, 4, 64), np.float32); sv = np.zeros_like(sk)
    pgs = np.zeros((L, 1, 4, 64, 64), np.float32); sgs = np.zeros((L, 16, 4, 64, 64), np.float32)
    pgc = np.zeros((L, 1, 3, 768), np.float32); sgc = np.zeros((L, 16, 3, 768), np.float32)
    pmc = np.zeros((L, 1, 4, 64, 64), np.float32); smc = np.zeros((L, 16, 4, 64, 64), np.float32)
    pmn = np.zeros((L, 1, 4, 64), np.float32); smn = np.zeros((L, 16, 4, 64), np.float32); pmm = np.zeros((L, 1, 4), np.float32); smm = np.zeros((L, 16, 4), np.float32)
    pxr = np.zeros((L, 1, 16, 64), np.float32); pxi = np.zeros_like(pxr); sxr = np.zeros((L, 16, 16, 64), np.float32); sxi = np.zeros_like(sxr)
    for core in range(8):
        R_ = results[core]; h, r = core // 2, core % 2
        yT = R_["yT"]
        yp[0, 2048 * core:2048 * (core + 1)] = yT[:, :2048].T; ys[2 * core] = yT[:, 2048:2080].T; ys[2 * core + 1] = yT[:, 2080:2112].T
        kv = R_["kvout"]
        pk[:, 0, 2048 * core:2048 * (core + 1)] = kv[:, :2048, 0:256].reshape(L, 2048, 4, 64); pv[:, 0, 2048 * core:2048 * (core + 1)] = kv[:, :2048, 256:512].reshape(L, 2048, 4, 64)
        for q in range(2):
            sk[:, 2 * core + q] = kv[:, 2048 + 32 * q:2080 + 32 * q, 0:256].reshape(L, 32, 4, 64); sv[:, 2 * core + q] = kv[:, 2048 + 32 * q:2080 + 32 * q, 256:512].reshape(L, 32, 4, 64)
        cT = R_["convT"]
        if core == 7: pgc[:, 0] = cT[:, :, 0:3].transpose(0, 2, 1)
        sgc[:, 2 * core] = cT[:, :, 3:6].transpose(0, 2, 1); sgc[:, 2 * core + 1] = cT[:, :, 6:9].transpose(0, 2, 1)
        gS = R_["gdnS"]
        pgs[:, 0, h, :, r * 32:(r + 1) * 32] = gS[:, :, 0, :]; sgs[:, :, h, :, r * 32:(r + 1) * 32] = gS[:, :, 1:17, :].transpose(0, 2, 1, 3)
        mS = R_["mlS"]; mM = R_["mlM"]
        pmc[:, 0, h, :, r * 32:(r + 1) * 32] = mS[:, :, 0, :32]; smc[:, :, h, :, r * 32:(r + 1) * 32] = mS[:, :, 1:17, :32].transpose(0, 2, 1, 3)
        if r == 0:
            pmn[:, 0, h] = mS[:, :, 0, 32]; smn[:, :, h] = mS[:, :, 1:17, 32].transpose(0, 2, 1); pmm[:, 0, h] = mM[:, 0, 0]; smm[:, :, h] = mM[:, 0, 1:17]
        X = R_["s5X"]
        gs = [2 * core, 2 * core + 1]
        pxr[:, 0, gs] = X[:, :, 0, 0].reshape(L, 2, 64); pxi[:, 0, gs] = X[:, :, 1, 0].reshape(L, 2, 64)
        sxr[:, :, gs] = X[:, :, 0, 1:17].transpose(0, 2, 1).reshape(L, 16, 2, 64); sxi[:, :, gs] = X[:, :, 1, 1:17].transpose(0, 2, 1).reshape(L, 16, 2, 64)
    return (yp, ys, pk, pv, pgs, pgc, pmc, pmn, pmm, pxr, pxi, sk, sv, sgs, sgc, smc, smn, smm, sxr, sxi)


def e1_rows(core):
    h, r = core // 2, core % 2
    rows = list(range(h * 64, h * 64 + 64)) + list(range(256 + h * 64, 256 + h * 64 + 64)) + list(range(512 + h * 64 + r * 32, 512 + h * 64 + r * 32 + 32)) + [768 + h, 772 + h]
    rows += [776 + x for x in list(range(h * 64, h * 64 + 64)) + list(range(256 + h * 64, 256 + h * 64 + 64)) + list(range(512 + h * 64 + r * 32, 512 + h * 64 + r * 32 + 32))] + [1544 + h, 1548 + h]
    rows += list(range(1552 + 32 * core, 1552 + 32 * core + 32))
    rows += list(range(1808 + h * 64, 1808 + h * 64 + 64)) + list(range(2064 + h * 64, 2064 + h * 64 + 64))
    assert len(rows) == 484
    return np.array(rows)

_PROGS = {}
def prog(kind):
    if kind not in _PROGS: _PROGS[kind] = build_launch(kind)
    return _PROGS[kind]

def run_multi(inp):
    L = 4
    hm = host_inputs(inp, L)
    tabs = [tab_values(c, fused=False) for c in range(8)]
    cores = list(range(8))
    def launch(kind, maps):
        return run_bass_kernel_spmd(prog(kind), maps, core_ids=cores).results
    sl = lambda a, l: np.ascontiguousarray(a[l:l + 1])
    def a_inputs(c, l):
        d = {"tvec": sl(hm[c]["tvec"], l)}
        for nm in A_W: d[nm] = sl(hm[c][nm], l)
        return d
    def c_inputs(c, l):
        d = {"tvecc": sl(hm[c]["tvec"], l)}
        for nm in C_W: d[nm + "c"] = sl(hm[c][nm], l)
        return d
    res = launch("tpa", [dict(itab=tabs[c], xin=hm[c]["xT0"], **a_inputs(c, 0)) for c in cores])
    kv = [[None] * L for _ in cores]; cv = [[None] * L for _ in cores]; st = [[None] * L for _ in cores]
    final = None
    for l in range(L):
        for c in cores: kv[c][l] = res[c]["kvout"][0]; cv[c][l] = res[c]["convT"][0]
        xcur = [res[c]["xout"] for c in cores]; gs = [res[c]["GSo"] for c in cores]
        Z = [res[c]["ZSo"] for c in cores]; ZQ = [res[c]["ZQSo"] for c in cores]
        bmaps = []
        for c in cores:
            h, r = c // 2, c % 2
            rows = e1_rows(c)
            d = {"itab": tabs[c], "ZRi": np.concatenate([Z[s][rows] for s in range(8)], 0), "ZQRi": np.concatenate([ZQ[s][r * 256 + h * 64:r * 256 + h * 64 + 64] for s in range(8)], 0)}
            for nm in BSPEC: d[nm] = sl(hm[c][nm], l)
            for nm in BSHARED: d[nm] = hm[c][nm]
            bmaps.append(d)
        bres = launch("b", bmaps)
        for c in cores: st[c][l] = {k: bres[c][k][0] for k in ["gdnS", "mlS", "mlM", "s5X"]}
        OS = [bres[c]["OSo"] for c in cores]
        tmaps = []
        for c in cores:
            d = {"itab": tabs[c], "xin": xcur[c], "GSi": gs[c], "ORi": np.concatenate([OS[j][c * 160:(c + 1) * 160] for j in range(8)], 0)}
            d.update(c_inputs(c, l))
            if l < L - 1: d.update(a_inputs(c, l + 1))
            else: d["nf"] = hm[c]["nf"]
            tmaps.append(d)
        res = launch("tpca" if l < L - 1 else "tpcf", tmaps)
    results = []
    for c in cores:
        results.append({"yT": res[c]["yT"], "kvout": np.stack(kv[c]), "convT": np.stack(cv[c]), "gdnS": np.stack([st[c][l]["gdnS"] for l in range(L)]),
                        "mlS": np.stack([st[c][l]["mlS"] for l in range(L)]), "mlM": np.stack([st[c][l]["mlM"] for l in range(L)]), "s5X": np.stack([st[c][l]["s5X"] for l in range(L)])})
    return assemble(results, L)

def kernel(**inputs):
    inp = {k: np.asarray(v) for k, v in inputs.items()}
    return run_multi(inp)
```
